# Optimizing a Trainium2 kernel written in Bass

```python
import math
import jax, jax.numpy as jnp
from jax import lax
import numpy as np

D_MODEL = 1024
BATCH = 4
SEQ = 8192
DEPTH = 2

D_MIX = D_MODEL
GLA_WIDTH = 3 * D_MIX // 8
GLA_HEADS = 6
GLA_DV = GLA_WIDTH // GLA_HEADS
GLA_DK = GLA_DV // 2
GLA_LOWRANK = 16
GLA_TAU = 16.0
HG_WIDTH = 3 * D_MIX // 8
HG_HEADS = 6
HG_DV = HG_WIDTH // HG_HEADS
HG_EXPAND = 64
HY_WIDTH = D_MIX - GLA_WIDTH - HG_WIDTH
HY_ORDER = 2
HY_SHORT = 3
HY_EMB = 33
HY_BANDS = (HY_EMB - 1) // 2
HY_FFN = 64
HY_INNER = 2
HY_FAST_DECAY = 0.3
HY_SLOW_DECAY = 1.5
HY_TARGET = 1e-2
CHUNK = 64
N_GROUPS = 4
EXPERTS_PER_GROUP = 4
N_EXPERTS = N_GROUPS * EXPERTS_PER_GROUP
TOP_K = 2
D_EXPERT = D_MODEL // 2
MOE_BLOCK = 256
ALPHA = (2 * DEPTH) ** 0.25
BETA = (8 * DEPTH) ** -0.25
LN_EPS = 1e-5
RMS_EPS = 1e-6
LB_FLOOR = 1e-30
IN_SPLITS = (GLA_HEADS * GLA_DK, GLA_HEADS * GLA_DK, GLA_WIDTH, GLA_WIDTH, 2 * GLA_LOWRANK,
             HG_HEADS * HG_EXPAND, 2 * HG_HEADS * HG_EXPAND, HG_WIDTH, HG_WIDTH, 3 * HY_WIDTH)
D_IN = sum(IN_SPLITS)

kernel_name = 'hybrid_gla_hgrn2_hyena_hmoe_deepnorm'


def layer_norm(x, g, b):
    xf = x.astype(jnp.float32)
    xc = xf - jnp.mean(xf, axis=-1, keepdims=True)
    var = jnp.mean(xc * xc, axis=-1, keepdims=True)
    return (xc * lax.rsqrt(var + LN_EPS) * g.astype(jnp.float32) + b.astype(jnp.float32)).astype(x.dtype)


def head_rms_norm(o, g):
    of = o.astype(jnp.float32)
    of = of * lax.rsqrt(jnp.mean(of * of, axis=-1, keepdims=True) + RMS_EPS)
    return (of.reshape(o.shape[0], o.shape[1], -1) * g.astype(jnp.float32)).astype(o.dtype)


def chunked_gated_scan(q, k, v, log_g):
    out_dtype = v.dtype
    B, L, H, K = q.shape
    V = v.shape[-1]
    N = L // CHUNK

    def to_chunks(a):
        return a.astype(jnp.float32).reshape(B, N, CHUNK, H, a.shape[-1]).transpose(1, 0, 3, 2, 4)

    qc, kc, vc, gc = to_chunks(q), to_chunks(k), to_chunks(v), to_chunks(log_g)
    gc = jnp.cumsum(gc, axis=3)
    lower = jnp.tril(jnp.ones((CHUNK, CHUNK), dtype=bool))[:, :, None]

    def step(state, blk):
        qb, kb, vb, gb = blk
        g_end = gb[:, :, -1:, :]
        inter = jnp.einsum('bhik,bhkv->bhiv', qb * jnp.exp(gb), state)
        diff = gb[:, :, :, None, :] - gb[:, :, None, :, :]
        decay = jnp.where(lower, jnp.exp(jnp.where(lower, diff, 0.0)), 0.0)
        scores = jnp.einsum('bhik,bhjk,bhijk->bhij', qb, kb, decay)
        intra = jnp.einsum('bhij,bhjv->bhiv', scores, vb)
        state = (jnp.exp(g_end[:, :, 0, :])[..., None] * state
                 + jnp.einsum('bhjk,bhjv->bhkv', kb * jnp.exp(g_end - gb), vb))
        return state, inter + intra

    s0 = jnp.zeros((B, H, K, V), jnp.float32)
    _, out = lax.scan(step, s0, (qc, kc, vc, gc))
    return out.transpose(1, 0, 3, 2, 4).reshape(B, L, H, V).astype(out_dtype)


def bidirectional_scan(q, k_fwd, k_bwd, v, g_fwd, g_bwd):
    rev = lambda a: jnp.flip(a, axis=1)
    fwd = chunked_gated_scan(q, k_fwd, v, g_fwd)
    bwd = rev(chunked_gated_scan(rev(q), rev(k_bwd), rev(v), rev(g_bwd)))
    return fwd + bwd


def hyena_filter_spectrum(L, w1, b1, freq, w2, b2, w3):
    f32 = jnp.float32
    t = jnp.linspace(0.0, 1.0, L, dtype=f32)[:, None]
    w = 2.0 * math.pi * jnp.arange(L, dtype=f32)[:, None] / L
    bands = jnp.linspace(1e-4, HY_BANDS - 1, HY_BANDS, dtype=f32)[None, :]
    z = jnp.concatenate([t, jnp.cos(bands * w), -jnp.sin(bands * w)], axis=-1)
    fr = freq.astype(f32)
    h = jnp.sin(fr * (z @ w1.astype(f32) + b1.astype(f32)))
    for i in range(HY_INNER):
        h = jnp.sin(fr * (h @ w2[i].astype(f32) + b2[i].astype(f32)))
    h = (h @ w3.astype(f32)).reshape(L, HY_ORDER, 2, HY_WIDTH)
    max_decay = math.log(HY_TARGET) / HY_FAST_DECAY
    min_decay = math.log(HY_TARGET) / HY_SLOW_DECAY
    deltas = jnp.linspace(min_decay, max_decay, HY_WIDTH, dtype=f32)
    h = h * jnp.exp(-t * jnp.abs(deltas))[:, None, None, :]
    k = jnp.concatenate([h[:, :, 0], jnp.zeros((1, HY_ORDER, HY_WIDTH), f32),
                         jnp.flip(h[1:, :, 1], axis=0)], axis=0)
    k = k / jnp.maximum(jnp.sum(jnp.abs(k), axis=0, keepdims=True), 1e-12)
    return jnp.fft.rfft(k, axis=0)


def hyena_mixer(u, conv_w, conv_b, k_spec, bias):
    out_dtype = u.dtype
    C = u.shape[-1]
    L = u.shape[1]
    pad = (HY_SHORT - 1) // 2
    u = lax.conv_general_dilated(u, conv_w[:, None, :], (1,), ((pad, pad),),
                                 dimension_numbers=('NWC', 'WIO', 'NWC'),
                                 feature_group_count=C) + conv_b
    v, x1, x2 = jnp.split(u.astype(jnp.float32), 3, axis=-1)
    z = v
    for o, gate in enumerate((x1, x2)):
        zf = jnp.fft.rfft(z, n=2 * L, axis=1)
        conv = jnp.fft.irfft(zf * k_spec[:, o][None], n=2 * L, axis=1)[:, :L]
        z = gate * (conv + z * bias[o].astype(jnp.float32))
    return z.astype(out_dtype)


def hybrid_token_mixer(h, w_in, gla_wa2, gla_ba, gla_norm_g, lb, hg_norm_g,
                       hy_conv_w, hy_conv_b, hy_spec, hy_bias, w_out):
    B, L, _ = h.shape
    proj = h @ w_in
    cuts = np.cumsum(IN_SPLITS)[:-1].tolist()
    gq, gk, gv, gg, ga, hq, hf, hi, hgt, hyu = jnp.split(proj, cuts, axis=-1)

    q = gq.reshape(B, L, GLA_HEADS, GLA_DK) * (GLA_DK ** -0.5)
    k = gk.reshape(B, L, GLA_HEADS, GLA_DK)
    v = gv.reshape(B, L, GLA_HEADS, GLA_DV)
    a_logit = jnp.einsum('bldr,drk->bldk', ga.reshape(B, L, 2, GLA_LOWRANK), gla_wa2) + gla_ba
    log_a = (jax.nn.log_sigmoid(a_logit.astype(jnp.float32)) / GLA_TAU).reshape(B, L, 2, GLA_HEADS, GLA_DK)
    o_gla = bidirectional_scan(q, k, k, v, log_a[:, :, 0], log_a[:, :, 1])
    y_gla = head_rms_norm(o_gla, gla_norm_g) * jax.nn.silu(gg)

    qh = jax.nn.silu(hq).reshape(B, L, HG_HEADS, HG_EXPAND)
    zf = hf.reshape(B, L, 2, HG_HEADS * HG_EXPAND).astype(jnp.float32)
    lbf = lb.astype(jnp.float32)
    log_f = jnp.logaddexp(jnp.log1p(-lbf) + jax.nn.log_sigmoid(zf),
                          jnp.log(jnp.maximum(lbf, LB_FLOOR)))
    one_minus_f = (1.0 - lbf) * jax.nn.sigmoid(-zf)
    log_f = log_f.reshape(B, L, 2, HG_HEADS, HG_EXPAND)
    one_minus_f = one_minus_f.reshape(B, L, 2, HG_HEADS, HG_EXPAND)
    vi = hi.reshape(B, L, HG_HEADS, HG_DV)
    o_hg = bidirectional_scan(qh, one_minus_f[:, :, 0], one_minus_f[:, :, 1], vi,
                              log_f[:, :, 0], log_f[:, :, 1])
    y_hg = head_rms_norm(o_hg, hg_norm_g) * jax.nn.sigmoid(hgt)

    y_hy = hyena_mixer(hyu, hy_conv_w, hy_conv_b, hy_spec, hy_bias)

    mix = jnp.concatenate([y_gla.astype(h.dtype), y_hg.astype(h.dtype), y_hy], axis=-1)
    return mix @ w_out


def hierarchical_moe(x, wr_g, br_g, wr_e, br_e, w_gate, w_up, w_down):
    B, L, D = x.shape
    T = B * L
    xt = x.reshape(T, D)
    pg = jax.nn.softmax((xt @ wr_g).astype(jnp.float32) + br_g.astype(jnp.float32), axis=-1)
    g_val, g_idx = lax.top_k(pg, 1)
    le = ((xt @ wr_e).astype(jnp.float32) + br_e.astype(jnp.float32)).reshape(T, N_GROUPS, EXPERTS_PER_GROUP)
    le = le[jnp.arange(T), g_idx[:, 0]]
    pe = jax.nn.softmax(le, axis=-1)
    e_val, e_idx = lax.top_k(pe, TOP_K)
    e_val = e_val / jnp.sum(e_val, axis=-1, keepdims=True)
    weights = g_val * e_val
    experts = g_idx * EXPERTS_PER_GROUP + e_idx

    A = T * TOP_K
    flat_e = experts.reshape(A)
    flat_w = weights.reshape(A)
    flat_t = jnp.repeat(jnp.arange(T, dtype=jnp.int32), TOP_K)
    order = jnp.argsort(flat_e)
    se, sw, st = flat_e[order], flat_w[order], flat_t[order]
    counts = jnp.bincount(flat_e, length=N_EXPERTS)
    padded = ((counts + MOE_BLOCK - 1) // MOE_BLOCK) * MOE_BLOCK
    start = jnp.cumsum(counts) - counts
    pend = jnp.cumsum(padded)
    pstart = pend - padded
    dest = pstart[se] + (jnp.arange(A) - start[se])
    NB = -(-A // MOE_BLOCK) + N_EXPERTS
    slot_tok = jnp.full((NB * MOE_BLOCK,), T, jnp.int32).at[dest].set(st)
    slot_w = jnp.zeros((NB * MOE_BLOCK,), jnp.float32).at[dest].set(sw)
    block_e = jnp.minimum(jnp.searchsorted(pend, jnp.arange(NB) * MOE_BLOCK, side='right'), N_EXPERTS - 1)
    x_pad = jnp.concatenate([xt, jnp.zeros((1, D), xt.dtype)], axis=0)

    def run_block(args):
        tok, e = args
        xb = x_pad[tok]
        hid = jax.nn.silu(xb @ w_gate[e]) * (xb @ w_up[e])
        return hid @ w_down[e]

    yb = lax.map(run_block, (slot_tok.reshape(NB, MOE_BLOCK), block_e))
    y = jax.ops.segment_sum(yb.reshape(NB * MOE_BLOCK, D).astype(jnp.float32) * slot_w[:, None],
                            slot_tok, num_segments=T + 1)[:T]
    return y.reshape(B, L, D).astype(x.dtype)


def setup_inputs(seed: int = 0) -> dict:
    key = jax.random.key(seed)
    ks = iter(jax.random.split(key, 32))
    f32 = jnp.float32

    def nrm(shape, scale):
        return jax.random.normal(next(ks), shape, f32) * scale

    x = nrm((BATCH, SEQ, D_MODEL), 1.0)
    ln_in_g = 1.0 + nrm((D_MODEL,), 0.02)
    ln_in_b = nrm((D_MODEL,), 0.02)
    col_scale = np.ones((D_IN,), np.float32)
    offs = np.concatenate([[0], np.cumsum(IN_SPLITS)])
    col_scale[offs[2]:offs[3]] = BETA
    col_scale[offs[7]:offs[8]] = BETA
    col_scale[offs[9]:offs[9] + HY_WIDTH] = BETA
    w_in = nrm((DEPTH, D_MODEL, D_IN), D_MODEL ** -0.5) * jnp.asarray(col_scale)
    gla_wa2 = nrm((DEPTH, 2, GLA_LOWRANK, GLA_HEADS * GLA_DK), GLA_LOWRANK ** -0.5)
    gla_ba = nrm((DEPTH, 2, GLA_HEADS * GLA_DK), 0.1)
    gla_norm_g = 1.0 + nrm((DEPTH, GLA_WIDTH), 0.02)
    hg_lb_logits = nrm((DEPTH, 2, HG_HEADS * HG_EXPAND), 0.1)
    hg_norm_g = 1.0 + nrm((DEPTH, HG_WIDTH), 0.02)
    hy_conv_w = nrm((DEPTH, HY_SHORT, 3 * HY_WIDTH), HY_SHORT ** -0.5)
    hy_conv_b = nrm((DEPTH, 3 * HY_WIDTH), 0.02)
    hy_w1 = nrm((DEPTH, HY_EMB, HY_FFN), HY_EMB ** -0.5)
    hy_b1 = nrm((DEPTH, HY_FFN), 0.02)
    hy_freq = 1.0 + nrm((DEPTH, HY_FFN), 0.1)
    hy_w2 = nrm((DEPTH, HY_INNER, HY_FFN, HY_FFN), HY_FFN ** -0.5)
    hy_b2 = nrm((DEPTH, HY_INNER, HY_FFN), 0.02)
    hy_w3 = nrm((DEPTH, HY_FFN, HY_ORDER * 2 * HY_WIDTH), HY_FFN ** -0.5)
    hy_bias = nrm((DEPTH, HY_ORDER, HY_WIDTH), 1.0)
    w_out = nrm((DEPTH, D_MIX, D_MODEL), D_MIX ** -0.5 * BETA)
    ln1_g = 1.0 + nrm((DEPTH, D_MODEL), 0.02)
    ln1_b = nrm((DEPTH, D_MODEL), 0.02)
    moe_wr_g = nrm((DEPTH, D_MODEL, N_GROUPS), D_MODEL ** -0.5)
    moe_br_g = nrm((DEPTH, N_GROUPS), 0.01)
    moe_wr_e = nrm((DEPTH, D_MODEL, N_EXPERTS), D_MODEL ** -0.5)
    moe_br_e = nrm((DEPTH, N_EXPERTS), 0.01)
    moe_w_gate = nrm((DEPTH, N_EXPERTS, D_MODEL, D_EXPERT), D_MODEL ** -0.5 * BETA)
    moe_w_up = nrm((DEPTH, N_EXPERTS, D_MODEL, D_EXPERT), D_MODEL ** -0.5 * BETA)
    moe_w_down = nrm((DEPTH, N_EXPERTS, D_EXPERT, D_MODEL), D_EXPERT ** -0.5 * BETA)
    ln2_g = 1.0 + nrm((DEPTH, D_MODEL), 0.02)
    ln2_b = nrm((DEPTH, D_MODEL), 0.02)
    return {'x': x, 'ln_in_g': ln_in_g, 'ln_in_b': ln_in_b, 'w_in': w_in,
            'gla_wa2': gla_wa2, 'gla_ba': gla_ba, 'gla_norm_g': gla_norm_g,
            'hg_lb_logits': hg_lb_logits, 'hg_norm_g': hg_norm_g,
            'hy_conv_w': hy_conv_w, 'hy_conv_b': hy_conv_b, 'hy_w1': hy_w1, 'hy_b1': hy_b1,
            'hy_freq': hy_freq, 'hy_w2': hy_w2, 'hy_b2': hy_b2, 'hy_w3': hy_w3, 'hy_bias': hy_bias,
            'w_out': w_out, 'ln1_g': ln1_g, 'ln1_b': ln1_b,
            'moe_wr_g': moe_wr_g, 'moe_br_g': moe_br_g, 'moe_wr_e': moe_wr_e, 'moe_br_e': moe_br_e,
            'moe_w_gate': moe_w_gate, 'moe_w_up': moe_w_up, 'moe_w_down': moe_w_down,
            'ln2_g': ln2_g, 'ln2_b': ln2_b}


def reference(x, ln_in_g, ln_in_b, w_in, gla_wa2, gla_ba, gla_norm_g, hg_lb_logits, hg_norm_g,
              hy_conv_w, hy_conv_b, hy_w1, hy_b1, hy_freq, hy_w2, hy_b2, hy_w3, hy_bias,
              w_out, ln1_g, ln1_b, moe_wr_g, moe_br_g, moe_wr_e, moe_br_e,
              moe_w_gate, moe_w_up, moe_w_down, ln2_g, ln2_b):
    L = x.shape[1]
    p = jax.nn.softmax(hg_lb_logits.astype(jnp.float32), axis=0)
    lower_bounds = jnp.cumsum(p, axis=0) - p[0:1]
    h = layer_norm(x, ln_in_g, ln_in_b)
    for l in range(DEPTH):
        hy_spec = hyena_filter_spectrum(L, hy_w1[l], hy_b1[l], hy_freq[l], hy_w2[l], hy_b2[l], hy_w3[l])
        mix = hybrid_token_mixer(h, w_in[l], gla_wa2[l], gla_ba[l], gla_norm_g[l], lower_bounds[l],
                                 hg_norm_g[l], hy_conv_w[l], hy_conv_b[l], hy_spec, hy_bias[l], w_out[l])
        h = layer_norm(ALPHA * h + mix, ln1_g[l], ln1_b[l])
        ffn = hierarchical_moe(h, moe_wr_g[l], moe_br_g[l], moe_wr_e[l], moe_br_e[l],
                               moe_w_gate[l], moe_w_up[l], moe_w_down[l])
        h = layer_norm(ALPHA * h + ffn, ln2_g[l], ln2_b[l])
    return h
```

```python
import numpy as np
import concourse.bass as bass
import concourse.mybir as mybir
from concourse.bass_utils import run_bass_kernel_spmd
from contextlib import ExitStack

F32 = mybir.dt.float32
BF16 = mybir.dt.bfloat16
I32 = mybir.dt.int32
AF = mybir.ActivationFunctionType
ALU = mybir.AluOpType
AX = mybir.AxisListType


class Prog:
    def __init__(self, n_dma_sems=20):
        self.nc = bass.Bass("TRN2", target_bir_lowering=False)
        self.es = ExitStack()
        nc = self.nc
        self.eng = {"pe": nc.tensor, "dve": nc.vector, "act": nc.scalar,
                    "pool": nc.gpsimd, "sp": nc.sync}
        self.sem = {}
        for e in self.eng:
            self.sem[("E", e)] = self.es.enter_context(nc.semaphore("s_" + e))
        self.cnt = {("E", e): 0 for e in self.eng}
        self.dpool = {}
        self.dnext = {}
        for q in ("sp", "pool", "act"):
            self.dpool[q] = []
            for i in range(n_dma_sems if q != "act" else 6):
                k = ("D", q, i)
                self.sem[k] = self.es.enter_context(nc.semaphore("d_%s_%d" % (q, i)))
                self.cnt[k] = 0
                self.dpool[q].append(k)
            self.dnext[q] = 0
        self.cur = self.es
        self.uid = 0
        self.scopes = []
        self.seen = {}
        self.lastw = {}
        self.readers = {}
        self.nins = 0

    def sb(self, name, shape, dt=F32):
        self.uid += 1
        return self.cur.enter_context(self.nc.sbuf_tensor("sb%d_%s" % (self.uid, name), list(shape), dt))

    def ps(self, name, shape, dt=F32):
        self.uid += 1
        return self.cur.enter_context(self.nc.psum_tensor("ps%d_%s" % (self.uid, name), list(shape), dt))

    def push_scope(self):
        self.scopes.append(self.cur)
        self.cur = ExitStack()

    def pop_scope(self):
        self.barrier()
        self.cur.close()
        self.cur = self.scopes.pop()

    def barrier(self):
        deps = [(sk, v) for sk, v in self.cnt.items() if v > 0]
        for e in self.eng:
            self._wait(e, deps)

    def dram(self, name, shape, dt=F32, kind="Internal"):
        return self.nc.dram_tensor(name, list(shape), dt, kind=kind).ap()

    def _deps(self, reads, writes):
        deps = []
        for k in reads:
            if k in self.lastw:
                deps.append(self.lastw[k])
        for k in writes:
            if k in self.lastw:
                deps.append(self.lastw[k])
            deps.extend(self.readers.get(k, {}).items())
        return deps

    def _wait(self, e, deps):
        best = {}
        for sk, v in deps:
            if sk == ("E", "pe") and e == "pe":
                continue
            if self.seen.get((e, sk), 0) >= v:
                continue
            if best.get(sk, 0) < v:
                best[sk] = v
        for sk, v in best.items():
            self.eng[e].wait_ge(self.sem[sk], v)
            self.seen[(e, sk)] = v

    def _record(self, tk, reads, writes):
        sk, v = tk
        for k in reads:
            self.readers.setdefault(k, {})[sk] = v
        for k in writes:
            self.lastw[k] = tk
            self.readers[k] = {}

    def op(self, e, fn, reads=(), writes=()):
        self._wait(e, self._deps(reads, writes))
        ins = fn(self.eng[e])
        sk = ("E", e)
        self.cnt[sk] += 1
        ins.then_inc(self.sem[sk], 1)
        self._record((sk, self.cnt[sk]), reads, writes)
        self.nins += 1
        return ins

    def dma(self, q, out, in_, reads=(), writes=(), **kw):
        deps = self._deps(reads, writes)
        sk = self.dpool[q][self.dnext[q] % len(self.dpool[q])]
        self.dnext[q] += 1
        if self.cnt[sk] > 0:
            deps.append((sk, self.cnt[sk]))
        self._wait(q, deps)
        ins = self.eng[q].dma_start(out=out, in_=in_, **kw)
        self.cnt[sk] += 16
        ins.then_inc(self.sem[sk], 16)
        self._record((sk, self.cnt[sk]), reads, writes)
        self.nins += 1
        return ins

    def finish(self, out_keys):
        deps = []
        for k in out_keys:
            if k in self.lastw:
                deps.append(self.lastw[k])
        for q in self.dpool:
            for sk in self.dpool[q]:
                if self.cnt[sk] > 0:
                    deps.append((sk, self.cnt[sk]))
        self._wait("sp", deps)
        self.es.close()
        return self.nc


def _coll(self, kind, groups, out, in_, reads=(), writes=()):
    q = "pool"
    deps = self._deps(reads, writes)
    sk = self.dpool[q][self.dnext[q] % len(self.dpool[q])]
    self.dnext[q] += 1
    if self.cnt[sk] > 0:
        deps.append((sk, self.cnt[sk]))
    self._wait(q, deps)
    ins = self.nc.gpsimd.collective_compute(kind, ALU.bypass, replica_groups=groups, ins=[in_], outs=[out])
    self.cnt[sk] += 16
    ins.then_inc(self.sem[sk], 16)
    self._record((sk, self.cnt[sk]), reads, writes)
    self.nins += 1
    return ins


Prog.coll = _coll

import numpy as np

import os
STOPAT = int(os.environ.get('STOPAT', '99'))
CH = 64
HF = (lambda h: 0) if os.environ.get("HF0") else (lambda h: h)
SEG = 2048
NCS = SEG // CH
NPS = SEG // 128


def scan_consts():
    rm = np.ones((128, SEG), np.float32)
    rm[:, ::CH] = 0.0
    j = np.arange(64)[:, None]
    i = np.arange(64)[None, :]
    mf = (i >= j).astype(np.float32)
    mb = (i <= j).astype(np.float32)
    mf = np.tile(np.concatenate([mf, mf], 0), (1, 8))
    mb = np.tile(np.concatenate([mb, mb], 0), (1, 8))
    return {"c_rm": rm, "c_mf": mf, "c_mb": mb, "c_id": np.eye(128, dtype=np.float32)}


class ScanCtx:
    def __init__(self, P, L, dd=None):
        self.P = P
        self.L = L
        nc = P.nc
        c = {}
        for nm, shp in (("c_rm", [128, SEG]), ("c_mf", [128, 512]), ("c_mb", [128, 512]), ("c_id", [128, 128])):
            c[nm] = dd[nm] if dd is not None else nc.dram_tensor(nm, shp, F32, kind="ExternalInput").ap()
        self.rm = P.sb("rm", [128, SEG], F32)
        self.mf = P.sb("mf", [128, 512], F32)
        self.mb = P.sb("mb", [128, 512], F32)
        idf = P.sb("idf", [128, 128], F32)
        self.idb = P.sb("idb", [128, 128], BF16)
        P.dma("sp", self.rm[:], c["c_rm"], writes=["rm"])
        P.dma("sp", self.mf[:], c["c_mf"], writes=["mf"])
        P.dma("sp", self.mb[:], c["c_mb"], writes=["mb"])
        P.dma("sp", idf[:], c["c_id"], writes=["idf"])
        P.op("dve", lambda e: e.tensor_copy(out=self.idb[:], in_=idf[:]), reads=["idf"], writes=["idb"])
        f = lambda n, w=SEG, dt=F32, p=64: P.sb(n, [p, w], dt)
        self.q = f("s_q"); self.k = f("s_k"); self.x = f("s_x"); self.g = f("s_g")
        self.Pc = f("s_P"); self.Gh = f("s_Gh"); self.t1 = f("s_t1"); self.t2 = f("s_t2")
        self.qt = f("s_qt", SEG, BF16); self.kt = f("s_kt", SEG, BF16)
        self.ga = P.sb("s_ga", [16, SEG], F32)
        self.v32 = P.sb("s_v32", [64, NCS, 64], F32)
        self.vb = P.sb("s_vb", [64, NCS, 64], BF16)
        self.ktok = P.sb("s_ktok", [64, NCS, 64], BF16)
        self.A = P.sb("s_A", [64, NCS, 64], F32)
        self.Sp = P.sb("s_Sp", [64, NCS, 64], BF16)
        self.S = P.sb("s_S", [64, 64], F32)
        self.cc = P.sb("s_cc", [64, 3, NCS], F32)
        self.ct = P.sb("s_ct", [64, 2, NCS], F32)
        self.Pm = P.sb("s_Pm", [64, 8, 64], BF16)
        self.o = P.sb("s_o", [64, NCS, 64], F32)
        self.wa = P.sb("s_wa", [16, 2, 32], F32)
        self.ba = P.sb("s_ba", [64, 4], F32)
        self.lb = P.sb("s_lb", [64, 8], F32)
        P.op("dve", lambda e: e.memset(self.kt[:], 0.0), writes=["kt"])
        self.p_g = P.ps("p_g", [64, 512], F32)
        self.p_t = P.ps("p_t", [64, 1024], BF16)
        self.p_A = [P.ps("p_A%d" % i, [64, 512], F32) for i in range(2)]
        self.p_s = P.ps("p_s", [64, 512], F32)
        self.p_o = [P.ps("p_o%d" % i, [64, 512], F32) for i in range(2)]


def scan_dir(C, kind, layer, K, d, qT_d, kT_d, xg_d, v_d, o_d, prm, rkeys=()):
    P = C.P
    L = C.L
    nseg = L // SEG
    rev = (d == 1)
    segs = list(range(nseg))
    if rev:
        segs = segs[::-1]
    S = C.S
    P.op("dve", lambda e: e.memset(S[0:K, :], 0.0), writes=["S"])
    gscale = (1.0 / 16.0) if kind == "gla" else 1.0
    lbzero = (kind == "gla") or (layer == 0)
    for sg in segs:
        t0 = sg * SEG
        sl = slice(t0, t0 + SEG)
        rk = list(rkeys)
        P.dma("sp", C.q[0:K, :], qT_d[:, sl], reads=rk, writes=["q"])
        P.dma("sp", C.v32[:], v_d[sl, :].rearrange("(n p) v -> p n v", p=64), reads=rk, writes=["v32"])
        P.op("pool", lambda e: e.tensor_copy(out=C.vb[:], in_=C.v32[:]), reads=["v32"], writes=["vb"])
        if kind == "gla":
            P.dma("sp", C.k[0:K, :], kT_d[:, sl], reads=rk, writes=["k"])
            P.dma("sp", C.ga[:], xg_d[:, sl], reads=rk, writes=["ga"])
            for c in range(SEG // 512):
                P.op("pe", lambda e, c=c: e.matmul(C.p_g[0:K, :], lhsT=prm["wa"][:, d, :], rhs=C.ga[:, c * 512:(c + 1) * 512], start=True, stop=True),
                     reads=["ga", "wa"], writes=["p_g"])
                P.op("act", lambda e, c=c: e.activation(out=C.t1[0:K, c * 512:(c + 1) * 512], in_=C.p_g[0:K, :], func=AF.Exp, scale=-1.0, bias=prm["nba"][0:K, d:d + 1]),
                     reads=["p_g", "nba"], writes=[("t1", c)])
            t1k = [("t1", c) for c in range(SEG // 512)]
            P.op("act", lambda e: e.activation(out=C.g[0:K, :], in_=C.t1[0:K, :], func=AF.Ln, bias=1.0, scale=1.0), reads=t1k, writes=["g"])
            P.op("dve", lambda e: e.tensor_scalar(out=C.g[0:K, :], in0=C.g[0:K, :], scalar1=-gscale, scalar2=None, op0=ALU.mult), reads=["g"], writes=["g"])
            P.op("pool", lambda e: e.tensor_scalar(out=C.q[0:K, :], in0=C.q[0:K, :], scalar1=float(K) ** -0.5, scalar2=None, op0=ALU.mult), reads=["q"], writes=["q"])
        else:
            P.dma("sp", C.x[0:K, :], xg_d[:, sl], reads=rk, writes=["x"])
            P.op("act", lambda e: e.activation(out=C.t1[0:K, :], in_=C.q[0:K, :], func=AF.Exp, scale=-1.0), reads=["q"], writes=["t1"])
            P.op("dve", lambda e: e.tensor_scalar(out=C.t1[0:K, :], in0=C.t1[0:K, :], scalar1=1.0, scalar2=None, op0=ALU.add), reads=["t1"], writes=["t1"])
            P.op("dve", lambda e: e.reciprocal(out=C.t1[0:K, :], in_=C.t1[0:K, :]), reads=["t1"], writes=["t1"])
            P.op("dve", lambda e: e.tensor_tensor(out=C.q[0:K, :], in0=C.q[0:K, :], in1=C.t1[0:K, :], op=ALU.mult), reads=["q", "t1"], writes=["q"])
            P.op("act", lambda e: e.activation(out=C.t1[0:K, :], in_=C.x[0:K, :], func=AF.Exp, scale=-1.0), reads=["x"], writes=["t1"])
            P.op("dve", lambda e: e.tensor_scalar(out=C.t2[0:K, :], in0=C.t1[0:K, :], scalar1=1.0, scalar2=None, op0=ALU.add), reads=["t1"], writes=["t2"])
            P.op("dve", lambda e: e.reciprocal(out=C.t2[0:K, :], in_=C.t2[0:K, :]), reads=["t2"], writes=["t2"])
            P.op("dve", lambda e: e.tensor_tensor(out=C.k[0:K, :], in0=C.t1[0:K, :], in1=C.t2[0:K, :], op=ALU.mult), reads=["t1", "t2"], writes=["k"])
            if lbzero:
                P.op("act", lambda e: e.activation(out=C.g[0:K, :], in_=C.t2[0:K, :], func=AF.Ln), reads=["t2"], writes=["g"])
            else:
                P.op("dve", lambda e: e.tensor_scalar(out=C.k[0:K, :], in0=C.k[0:K, :], scalar1=prm["oml"][0:K, d:d + 1], scalar2=None, op0=ALU.mult),
                     reads=["k", "oml"], writes=["k"])
                P.op("dve", lambda e: e.tensor_scalar(out=C.t2[0:K, :], in0=C.t2[0:K, :], scalar1=prm["oml"][0:K, d:d + 1], scalar2=prm["lb"][0:K, d:d + 1],
                                                      op0=ALU.mult, op1=ALU.add), reads=["t2", "oml"], writes=["t2"])
                P.op("act", lambda e: e.activation(out=C.g[0:K, :], in_=C.t2[0:K, :], func=AF.Ln), reads=["t2"], writes=["g"])
        if STOPAT <= 1:
            continue
        P.op("dve", lambda e: e.tensor_tensor_scan(out=C.Pc[0:K, :], data0=C.rm[0:K, :], data1=C.g[0:K, :], initial=0.0, op0=ALU.mult, op1=ALU.add),
             reads=["rm", "g"], writes=["Pc"])
        Pv = C.Pc[0:K, :].rearrange("p (n c) -> p n c", c=CH)
        Ghv = C.Gh[0:K, :].rearrange("p (n c) -> p n c", c=CH)
        gv = C.g[0:K, :].rearrange("p (n c) -> p n c", c=CH)
        cc = C.cc
        ct = C.ct
        if not rev:
            mid = 31
            P.op("dve", lambda e: e.tensor_tensor(out=Ghv, in0=Pv, in1=Pv[:, :, mid:mid + 1].to_broadcast([K, NCS, CH]), op=ALU.subtract),
                 reads=["Pc"], writes=["Gh"])
            P.op("act", lambda e: e.activation(out=cc[0:K, 0, :], in_=Pv[:, :, mid], func=AF.Exp), reads=["Pc"], writes=["cc0"])
            P.op("act", lambda e: e.activation(out=cc[0:K, 1, :], in_=Pv[:, :, CH - 1], func=AF.Exp), reads=["Pc"], writes=["cc1"])
            P.op("act", lambda e: e.activation(out=cc[0:K, 2, :], in_=Ghv[:, :, CH - 1], func=AF.Exp), reads=["Gh"], writes=["cc2"])
        else:
            mid = 32
            Ev = C.t1[0:K, :].rearrange("p (n c) -> p n c", c=CH)
            P.op("dve", lambda e: e.tensor_tensor(out=C.t1[0:K, :], in0=C.Pc[0:K, :], in1=C.g[0:K, :], op=ALU.subtract), reads=["Pc", "g"], writes=["t1"])
            P.op("dve", lambda e: e.tensor_tensor(out=Ghv, in0=Ev[:, :, mid:mid + 1].to_broadcast([K, NCS, CH]), in1=Ev, op=ALU.subtract),
                 reads=["t1"], writes=["Gh"])
            P.op("dve", lambda e: e.tensor_tensor(out=ct[0:K, 0, :], in0=Pv[:, :, CH - 1], in1=Ev[:, :, mid], op=ALU.subtract), reads=["Pc", "t1"], writes=["ct"])
            P.op("act", lambda e: e.activation(out=cc[0:K, 0, :], in_=ct[0:K, 0, :], func=AF.Exp), reads=["ct"], writes=["cc0"])
            P.op("act", lambda e: e.activation(out=cc[0:K, 1, :], in_=Pv[:, :, CH - 1], func=AF.Exp), reads=["Pc"], writes=["cc1"])
            P.op("act", lambda e: e.activation(out=cc[0:K, 2, :], in_=Ev[:, :, mid], func=AF.Exp), reads=["t1"], writes=["cc2"])
        if STOPAT <= 2:
            continue
        P.op("act", lambda e: e.activation(out=C.t2[0:K, :], in_=C.Gh[0:K, :], func=AF.Exp), reads=["Gh"], writes=["t2"])
        P.op("dve", lambda e: e.tensor_tensor(out=C.qt[0:K, :], in0=C.q[0:K, :], in1=C.t2[0:K, :], op=ALU.mult), reads=["q", "t2"], writes=["qt"])
        P.op("act", lambda e: e.activation(out=C.t1[0:K, :], in_=C.Gh[0:K, :], func=AF.Exp, scale=-1.0), reads=["Gh"], writes=["t1"])
        P.op("pool", lambda e: e.tensor_tensor(out=C.kt[0:K, :], in0=C.k[0:K, :], in1=C.t1[0:K, :], op=ALU.mult), reads=["k", "t1"], writes=["kt"])
        if STOPAT <= 3:
            continue
        for half in range(2):
            for cl in range(16):
                n = half * 16 + cl
                P.op("pe", lambda e, cl=cl, n=n: e.transpose(out=C.p_t[:, cl * 64:(cl + 1) * 64], in_=C.kt[0:64, n * 64:(n + 1) * 64], identity=C.idb[0:64, 0:64]),
                     reads=["kt", "idb"], writes=["p_t"])
            P.op("act", lambda e, half=half: e.activation(out=C.ktok[:, half * 16:(half + 1) * 16, :].rearrange("p n k -> p (n k)"), in_=C.p_t[:], func=AF.Copy),
                 reads=["p_t"], writes=[("ktok", half)])
        if STOPAT <= 4:
            continue
        for grp in range(NCS // 8):
            pa = C.p_A[grp % 2]
            pak = ("p_A", grp % 2)
            for ci in range(8):
                n = grp * 8 + ci
                P.op("pe", lambda e, ci=ci, n=n: e.matmul(pa[0:K, ci * 64:(ci + 1) * 64], lhsT=C.ktok[:, n, 0:K], rhs=C.vb[:, n, :], start=True, stop=True),
                     reads=[("ktok", n // 16), "vb"], writes=[pak])
            P.op("dve", lambda e, grp=grp: e.tensor_tensor(out=C.A[0:K, grp * 8:(grp + 1) * 8, :], in0=pa[0:K, :].rearrange("p (n v) -> p n v", v=64),
                                                           in1=cc[0:K, 2, grp * 8:(grp + 1) * 8].unsqueeze(2).to_broadcast([K, 8, 64]), op=ALU.mult),
                 reads=[pak, "cc2"], writes=[("A", grp)])
        if STOPAT <= 5:
            continue
        order = list(range(NCS))
        if rev:
            order = order[::-1]
        for n in order:
            P.op("act", lambda e, n=n: e.activation(out=C.Sp[0:K, n, :], in_=S[0:K, :], func=AF.Copy, scale=cc[0:K, 0, n:n + 1]),
                 reads=["S", "cc0"], writes=[("Sp", n)])
            P.op("dve", lambda e, n=n: e.scalar_tensor_tensor(out=S[0:K, :], in0=S[0:K, :], scalar=cc[0:K, 1, n:n + 1], in1=C.A[0:K, n, :], op0=ALU.mult, op1=ALU.add),
                 reads=["S", "cc1", ("A", n // 8)], writes=["S"])
        if STOPAT <= 6:
            continue
        mask = C.mb if rev else C.mf
        for grp in range(NCS // 8):
            for cl in range(8):
                n = grp * 8 + cl
                P.op("pe", lambda e, cl=cl, n=n: e.matmul(C.p_s[:, cl * 64:(cl + 1) * 64], lhsT=C.kt[0:K, n * 64:(n + 1) * 64],
                                                          rhs=C.qt[0:K, n * 64:(n + 1) * 64], start=True, stop=True),
                     reads=["kt", "qt"], writes=["p_s"])
            P.op("dve", lambda e: e.tensor_tensor(out=C.Pm[:].rearrange("p n c -> p (n c)"), in0=C.p_s[:], in1=mask[0:64, :], op=ALU.mult),
                 reads=["p_s", "mf", "mb"], writes=["Pm"])
            po = C.p_o[grp % 2]
            pok = ("p_o", grp % 2)
            for cl in range(8):
                n = grp * 8 + cl
                P.op("pe", lambda e, cl=cl, n=n: e.matmul(po[:, cl * 64:(cl + 1) * 64], lhsT=C.Pm[:, cl, :], rhs=C.vb[:, n, :], start=True, stop=False),
                     reads=["Pm", "vb"], writes=[pok])
                P.op("pe", lambda e, cl=cl, n=n: e.matmul(po[:, cl * 64:(cl + 1) * 64], lhsT=C.qt[0:K, n * 64:(n + 1) * 64], rhs=C.Sp[0:K, n, :], start=False, stop=True),
                     reads=["qt", ("Sp", n)], writes=[pok])
            P.op("act", lambda e, grp=grp: e.activation(out=C.o[:, grp * 8:(grp + 1) * 8, :].rearrange("p n v -> p (n v)"), in_=po[:], func=AF.Copy),
                 reads=[pok], writes=[("o", grp)])
        P.dma("sp", o_d[sl, :].rearrange("(n p) v -> p n v", p=64), C.o[:], reads=[("o", g_) for g_ in range(NCS // 8)], writes=["o_out"])

import math
import numpy as np

HY_W = 256
TWO_PI_LO = 6.283185


def hyena_consts(L, core):
    f32 = np.float32
    t = np.linspace(0.0, 1.0, L, dtype=f32)[:, None]
    w = (2.0 * math.pi * np.arange(L, dtype=f32)[:, None] / L).astype(f32)
    bands = np.linspace(1e-4, 15, 16, dtype=f32)[None, :]
    z = np.concatenate([t, np.cos(bands * w), -np.sin(bands * w)], axis=-1).astype(f32)
    max_decay = math.log(1e-2) / 0.3
    min_decay = math.log(1e-2) / 1.5
    deltas = np.linspace(min_decay, max_decay, HY_W, dtype=f32)
    dec = np.exp(-t * np.abs(deltas)).astype(f32)
    tidx = np.concatenate([np.arange(L - 1, -1, -1), np.arange(1, L), [0]])
    zR = np.ascontiguousarray(z[tidx].T)
    zR[:, -1] = 0.0
    ch = np.arange(core * 32, core * 32 + 32)
    d = dec[tidx][:, ch].T
    d[:, -1] = 0.0
    decR = np.ascontiguousarray(np.concatenate([d, d], 0))
    J = np.ascontiguousarray(np.eye(128, dtype=f32)[::-1])
    return {"h_zR": zR, "h_decR": decR.astype(f32), "h_J": J, "h_id": np.eye(128, dtype=f32),
            "h_ones": np.ones((64, 128), f32)}


def hyena_stage(P, L, d):
    NB = L // 128
    NT = 2 * L // 512
    HW = 2 * L - 128
    nc = P.nc
    sb = P.sb
    w1 = sb("h_w1", [33, 64]); b1f = sb("h_b1f", [64, 8]); w2 = sb("h_w2", [64, 2, 64]); w3 = sb("h_w3", [64, 2, 64])
    cw = sb("h_cw", [128, 9]); cb = sb("h_cb", [128, 3]); bias = sb("h_bias", [128, 2])
    Jm = sb("h_Jm", [128, 128]); idf = sb("h_idf", [128, 128]); idb = sb("h_idb", [128, 128], BF16); ones = sb("h_onesb", [64, 128])
    P.dma("sp", w1[:], d["w1"], writes=["h_w1"])
    P.dma("sp", b1f[:, 0:4], d["b1f"], writes=["h_b1f"])
    P.dma("sp", w2[:], d["w2"], writes=["h_w2"])
    P.dma("sp", w3[:], d["w3"], writes=["h_w3"])
    P.dma("sp", cw[:], d["cw"].rearrange("p a b -> p (a b)"), writes=["h_cw"])
    P.dma("sp", cb[:], d["cb"], writes=["h_cb"])
    P.dma("sp", bias[:], d["bias"], writes=["h_bias"])
    P.dma("sp", Jm[:], d["h_J"], writes=["h_J"])
    P.dma("sp", idf[:], d["h_id"], writes=["h_idf"])
    P.dma("sp", ones[:], d["h_ones"], writes=["h_ones"])
    P.op("dve", lambda e: e.tensor_copy(out=idb[:], in_=idf[:]), reads=["h_idf"], writes=["h_idb"])
    P.op("dve", lambda e: e.tensor_scalar(out=b1f[:, 4:5], in0=b1f[:, 3:4], scalar1=1.0 / (2 * math.pi), scalar2=None, op0=ALU.mult), reads=["h_b1f"], writes=["h_A"])
    P.op("dve", lambda e: e.tensor_scalar(out=b1f[:, 5:8], in0=b1f[:, 0:3], scalar1=b1f[:, 4:5], scalar2=None, op0=ALU.mult), reads=["h_b1f", "h_A"], writes=["h_B"])
    pA = P.ps("h_pA", [64, 512])
    zt = [sb("h_zt%d" % i, [33, 512]) for i in range(2)]
    dt_ = [sb("h_dt%d" % i, [64, 512]) for i in range(2)]
    u = sb("h_u", [64, 512]); ui = sb("h_ui", [64, 512], I32); hh = sb("h_hh", [64, 512])
    Rt = sb("h_Rt", [64, 512]); Rb = [sb("h_Rb%d" % i, [64, 512], BF16) for i in range(2)]
    asum = sb("h_asum", [64, NT]); rn = sb("h_rn", [64, 2]); dg = sb("h_dg", [64, 64]); rnb = sb("h_rnb", [128, 64])
    scr = d["scr"]
    for t in range(NT):
        b = t % 2
        P.dma("sp", zt[b][:], d["h_zR"][:, t * 512:(t + 1) * 512], writes=[("zt", b)])
        P.dma("sp", dt_[b][:], d["h_decR"][:, t * 512:(t + 1) * 512], writes=[("dt", b)])
        rhs = zt[b]
        rk = ("zt", b)
        for l in range(3):
            lhsT = w1[:, :] if l == 0 else w2[:, l - 1, :]
            kk = 33 if l == 0 else 64
            P.op("pe", lambda e, lhsT=lhsT, rhs=rhs, kk=kk: e.matmul(pA[:, :], lhsT=lhsT, rhs=rhs[0:kk, :], start=True, stop=True),
                 reads=[rk, "h_w1", "h_w2"], writes=["h_pA"])
            P.op("dve", lambda e, l=l: e.tensor_scalar(out=u[:], in0=pA[:], scalar1=b1f[:, 4:5], scalar2=b1f[:, 5 + l:6 + l], op0=ALU.mult, op1=ALU.add),
                 reads=["h_pA", "h_A", "h_B"], writes=["h_u"])
            P.op("dve", lambda e: e.tensor_copy(out=ui[:], in_=u[:]), reads=["h_u"], writes=["h_ui"])
            P.op("dve", lambda e: e.tensor_tensor(out=u[:], in0=u[:], in1=ui[:], op=ALU.subtract), reads=["h_u", "h_ui"], writes=["h_u"])
            P.op("act", lambda e: e.activation(out=hh[:], in_=u[:], func=AF.Sin, scale=TWO_PI_LO), reads=["h_u"], writes=["h_hh"])
            rhs = hh
            rk = "h_hh"
        dr = 0 if t < NT // 2 else 1
        P.op("pe", lambda e, dr=dr: e.matmul(pA[:, :], lhsT=w3[:, dr, :], rhs=hh[:, :], start=True, stop=True), reads=["h_hh", "h_w3"], writes=["h_pA"])
        P.op("dve", lambda e: e.tensor_tensor(out=Rt[:], in0=pA[:], in1=dt_[b][:], op=ALU.mult), reads=["h_pA", ("dt", b)], writes=["h_Rt"])
        P.op("dve", lambda e, t=t: e.tensor_reduce(out=asum[:, t:t + 1], in_=Rt[:], axis=AX.X, op=ALU.add, apply_absolute_value=True), reads=["h_Rt"], writes=[("asum", t)])
        P.op("act", lambda e: e.activation(out=Rb[b][:], in_=Rt[:], func=AF.Copy), reads=["h_Rt"], writes=[("Rb", b)])
        P.dma("sp", scr[:, t * 512:(t + 1) * 512], Rb[b][:], reads=[("Rb", b)], writes=["scr"])
    P.op("dve", lambda e: e.tensor_reduce(out=rn[:, 0:1], in_=asum[:], axis=AX.X, op=ALU.add), reads=[("asum", t) for t in range(NT)], writes=["h_rn"])
    P.op("dve", lambda e: e.tensor_scalar(out=rn[:, 0:1], in0=rn[:, 0:1], scalar1=1e-12, scalar2=None, op0=ALU.max), reads=["h_rn"], writes=["h_rn"])
    P.op("dve", lambda e: e.reciprocal(out=rn[:, 1:2], in_=rn[:, 0:1]), reads=["h_rn"], writes=["h_rn1"])
    P.op("dve", lambda e: e.tensor_scalar(out=dg[:], in0=idf[0:64, 0:64], scalar1=rn[:, 1:2], scalar2=None, op0=ALU.mult), reads=["h_idf", "h_rn1"], writes=["h_dg"])
    pB = P.ps("h_pB", [128, 512])
    P.op("pe", lambda e: e.matmul(pB[:, 0:64], lhsT=ones[:, :], rhs=dg[:, :], start=True, stop=True), reads=["h_ones", "h_dg"], writes=["h_pB"])
    P.op("dve", lambda e: e.tensor_copy(out=rnb[:], in_=pB[:, 0:64]), reads=["h_pB"], writes=["h_rnb"])
    za = sb("h_za", [128, L]); zc = sb("h_zc", [128, L])
    zb = [sb("h_zb%d" % i, [128, 1024], BF16) for i in range(2)]
    ZT = sb("h_ZT", [128, NB, 128], BF16)
    YT = sb("h_YT", [128, NB, 128])
    xr = YT[:].rearrange("p a b -> p (a b)")
    ytk = [("YT", c) for c in range(32)]
    H = [sb("h_H%d" % i, [128, HW], BF16) for i in range(2)]
    pT = P.ps("h_pT", [128, 1024], BF16)
    pC = [P.ps("h_pC%d" % i, [128, 512]) for i in range(2)]

    def short_conv(idx, dst, dk):
        P.dma("sp", xr, d["u3"][idx], writes=ytk)
        P.op("dve", lambda e: e.tensor_scalar(out=dst[:], in0=xr, scalar1=cw[:, idx * 3 + 1:idx * 3 + 2], scalar2=cb[:, idx:idx + 1], op0=ALU.mult, op1=ALU.add),
             reads=ytk + ["h_cw", "h_cb"], writes=[dk])
        P.op("dve", lambda e: e.scalar_tensor_tensor(out=dst[:, 1:L], in0=xr[:, 0:L - 1], scalar=cw[:, idx * 3:idx * 3 + 1], in1=dst[:, 1:L], op0=ALU.mult, op1=ALU.add),
             reads=ytk + ["h_cw", dk], writes=[dk])
        P.op("dve", lambda e: e.scalar_tensor_tensor(out=dst[:, 0:L - 1], in0=xr[:, 1:L], scalar=cw[:, idx * 3 + 2:idx * 3 + 3], in1=dst[:, 0:L - 1], op0=ALU.mult, op1=ALU.add),
             reads=ytk + ["h_cw", dk], writes=[dk])

    short_conv(0, za, "h_za")
    hq = 0
    for o in range(2):
        for g in range(NB // 8):
            zbb = zb[g % 2]
            zbk = ("h_zb", g % 2)
            P.op("act", lambda e, g=g, zbb=zbb: e.activation(out=zbb[:], in_=za[:, g * 1024:(g + 1) * 1024], func=AF.Copy), reads=["h_za"], writes=[zbk])
            for jj in range(8):
                j = g * 8 + jj
                P.op("pe", lambda e, jj=jj, zbb=zbb: e.transpose(out=pT[:, jj * 128:(jj + 1) * 128], in_=zbb[:, jj * 128:(jj + 1) * 128], identity=idb[:]),
                     reads=[zbk, "h_idb"], writes=["h_pT"])
            P.op("dve", lambda e, g=g: e.tensor_copy(out=ZT[:, g * 8:(g + 1) * 8, :].rearrange("p a b -> p (a b)"), in_=pT[:]), reads=["h_pT"], writes=[("ZT", g)])
        ztk = [("ZT", g) for g in range(NB // 8)]
        short_conv(1 + o, zc, "h_zc")
        for c in range(32):
            lane = o * 32 + c
            hb = hq % 2
            hq += 1
            q = "sp" if hb == 0 else "pool"
            P.dma(q, H[hb][:], bass.AP(scr.tensor, lane * 2 * L, [[1, 128], [1, HW]]), reads=["scr"], writes=[("H", hb)])
            pc = pC[hb]
            pck = ("pC", hb)
            ds = [0] + [x for k in range(1, NB) for x in (k, -k)]
            for n_, dd in enumerate(ds):
                j0, j1 = max(0, -dd), min(NB, NB - dd)
                x0 = L - 128 - 128 * dd
                P.op("pe", lambda e, dd=dd, j0=j0, j1=j1, x0=x0, n_=n_: e.matmul(
                    pc[:, (j0 + dd) * 4:(j1 + dd) * 4].rearrange("p (i b) -> p i b", b=4), lhsT=H[hb][:, x0:x0 + 128],
                    rhs=ZT[:, j0:j1, c * 4:(c + 1) * 4], start=(n_ == 0), stop=(n_ == len(ds) - 1)),
                     reads=[("H", hb)] + ztk, writes=[pck])
            eng = "act" if c % 2 == 0 else "dve"
            if eng == "act":
                P.op("act", lambda e, c=c, lane=lane: e.activation(out=YT[:, :, c * 4:(c + 1) * 4], in_=pc[:, 0:NB * 4].rearrange("p (i b) -> p i b", b=4), func=AF.Copy,
                                                                   scale=rnb[:, lane:lane + 1]), reads=[pck, "h_rnb"], writes=[("YT", c)])
            else:
                P.op("dve", lambda e, c=c, lane=lane: e.tensor_scalar(out=YT[:, :, c * 4:(c + 1) * 4], in0=pc[:, 0:NB * 4].rearrange("p (i b) -> p i b", b=4),
                                                                      scalar1=rnb[:, lane:lane + 1], scalar2=None, op0=ALU.mult), reads=[pck, "h_rnb"], writes=[("YT", c)])
        for g in range(NB // 4):
            for ii in range(4):
                i = g * 4 + ii
                P.op("pe", lambda e, ii=ii, i=i: e.matmul(pB[:, ii * 128:(ii + 1) * 128], lhsT=YT[:, i, :], rhs=Jm[:, :], start=True, stop=True),
                     reads=ytk + ["h_J"], writes=["h_pB"])
            sl = slice(g * 512, (g + 1) * 512)
            P.op("dve", lambda e, sl=sl, o=o: e.scalar_tensor_tensor(out=za[:, sl], in0=za[:, sl], scalar=bias[:, o:o + 1], in1=pB[:, :], op0=ALU.mult, op1=ALU.add),
                 reads=["h_za", "h_bias", "h_pB"], writes=["h_za"])
            P.op("pool", lambda e, sl=sl: e.tensor_tensor(out=za[:, sl], in0=za[:, sl], in1=zc[:, sl], op=ALU.mult), reads=["h_za", "h_zc"], writes=["h_za"])
    P.dma("sp", d["y"], za[:], reads=["h_za"], writes=["h_y"])

import math
import numpy as np

import os
STOPC = int(os.environ.get('STOPC', '99'))
D = 1024
ALPHA = 4 ** 0.25
NEXP = 16
DE = 512


def layer_norm_tile(P, x, xk, st, mv, stk, epst, gb, gbk):
    for c in range(2):
        P.op("dve", lambda e, c=c: e.bn_stats(out=st[:, c, :], in_=x[:, c * 512:(c + 1) * 512]), reads=[xk], writes=[(stk, c)])
    P.op("dve", lambda e: e.bn_aggr(out=mv[:, 0:2], in_=st[:].rearrange("p a b -> p (a b)")), reads=[(stk, 0), (stk, 1)], writes=[(stk, "mv")])
    P.op("act", lambda e: e.activation(out=mv[:, 2:3], in_=mv[:, 1:2], func=AF.Sqrt, bias=epst[:, 0:1], scale=1.0), reads=[(stk, "mv"), "eps"], writes=[(stk, "sd")])
    P.op("dve", lambda e: e.reciprocal(out=mv[:, 3:4], in_=mv[:, 2:3]), reads=[(stk, "sd")], writes=[(stk, "rs")])
    P.op("dve", lambda e: e.tensor_scalar(out=x[:], in0=x[:], scalar1=mv[:, 0:1], scalar2=mv[:, 3:4], op0=ALU.subtract, op1=ALU.mult),
         reads=[xk, (stk, "rs"), (stk, "mv")], writes=[xk])
    P.op("pool", lambda e: e.tensor_tensor(out=x[:], in0=x[:], in1=gb[:, 0:D], op=ALU.mult), reads=[xk, gbk], writes=[xk])
    P.op("pool", lambda e: e.tensor_tensor(out=x[:], in0=x[:], in1=gb[:, D:2 * D], op=ALU.add), reads=[xk, gbk], writes=[xk])


def stage_C(P, NT, d):
    nc = P.nc
    sb, ps = P.sb, P.ps
    T = NT * 128
    epst = sb("epst", [128, 1]); eps6 = sb("eps6", [128, 1])
    P.op("dve", lambda e: e.memset(epst[:], 1e-5), writes=["eps"])
    P.op("dve", lambda e: e.memset(eps6[:], 1e-6), writes=["eps6"])
    idf = sb("c_idf", [128, 128]); idb = sb("c_idb", [128, 128], BF16)
    P.dma("sp", idf[:], d["c_id"], writes=["idf"])
    P.op("dve", lambda e: e.tensor_copy(out=idb[:], in_=idf[:]), reads=["idf"], writes=["idb"])
    h1T = sb("h1T", [128, 8, T], BF16)
    wfull = sb("wfull", [128, NT, 16])
    P.push_scope()
    wout = sb("woutb", [128, 8, D], BF16)
    for kc in range(8):
        P.dma("pool", wout[:, kc, :], d["wout"][kc * 128:(kc + 1) * 128, :], writes=[("wout", kc)])
    woutk = [("wout", kc) for kc in range(8)]
    gn = sb("gn", [128, 768]); ln1 = sb("ln1", [128, 2 * D]); wr = sb("wr", [128, 8, 20]); br = sb("br", [128, 20])
    P.dma("sp", gn[:], d["gn"], writes=["gn"])
    P.dma("sp", ln1[:], d["ln1"], writes=["ln1"])
    P.dma("sp", wr[:], d["wr"].rearrange("(k p) n -> p k n", p=128), writes=["wr"])
    P.dma("sp", br[:], d["br"], writes=["br"])
    NBUF = 2
    OF = [sb("OF%d" % i, [128, 768]) for i in range(NBUF)]
    OB = [sb("OB%d" % i, [128, 768]) for i in range(NBUF)]
    GT = [sb("GT%d" % i, [128, 768]) for i in range(NBUF)]
    mix = [sb("mix%d" % i, [128, D]) for i in range(NBUF)]
    mixb = [sb("mixb%d" % i, [128, D], BF16) for i in range(NBUF)]
    mixT = [sb("mixT%d" % i, [128, 8, 128], BF16) for i in range(NBUF)]
    ht = [sb("ht%d" % i, [128, D]) for i in range(NBUF)]
    sq = sb("sq", [128, 768]); ss = sb("ss", [128, 12]); rs = sb("rs_", [128, 12])
    st = sb("st", [128, 2, 6]); mv = sb("mv", [128, 4])
    h1T32 = sb("h1T32", [128, 8, 128])
    lg = sb("lg", [128, 20]); rt = sb("rt", [128, 64])
    pT = ps("pT", [128, 1024], BF16)
    pO = [ps("pO%d" % i, [128, 512]) for i in range(2)]
    pX = [ps("pX%d" % i, [128, 512]) for i in range(2)]
    pL = ps("pL", [128, 512])
    for t in range(NT):
        b = t % NBUF
        rows = slice(t * 128, (t + 1) * 128)
        kOF, kOB, kGT, kmix, kmixb, kmixT, kht = ("OF", b), ("OB", b), ("GT", b), ("mix", b), ("mixb", b), ("mixT", b), ("ht", b)
        P.dma("sp", OF[b][:], d["OF"][rows, :], writes=[kOF])
        P.dma("sp", OB[b][:], d["OB"][rows, :], writes=[kOB])
        P.dma("sp", GT[b][:], d["GT"][rows, :], writes=[kGT])
        P.dma("sp", mix[b][:, 768:1024], d["YH"][rows, :], writes=[(kmix, "hy")])
        P.dma("sp", ht[b][:], d["h"][rows, :], writes=[kht])
        P.op("pool", lambda e: e.tensor_tensor(out=OF[b][:], in0=OF[b][:], in1=OB[b][:], op=ALU.add), reads=[kOF, kOB], writes=[kOF])
        P.op("dve", lambda e: e.tensor_tensor(out=sq[:], in0=OF[b][:], in1=OF[b][:], op=ALU.mult), reads=[kOF], writes=["sq"])
        P.op("dve", lambda e: e.tensor_reduce(out=ss[:], in_=sq[:].rearrange("p (h v) -> p h v", v=64), axis=AX.X, op=ALU.add), reads=["sq"], writes=["ss"])
        P.op("act", lambda e: e.activation(out=rs[:], in_=ss[:], func=AF.Sqrt, bias=eps6[:, 0:1], scale=1.0 / 64.0), reads=["ss", "eps6"], writes=["rs"])
        P.op("dve", lambda e: e.reciprocal(out=rs[:], in_=rs[:]), reads=["rs"], writes=["rs"])
        P.op("dve", lambda e: e.tensor_tensor(out=OF[b][:].rearrange("p (h v) -> p h v", v=64), in0=OF[b][:].rearrange("p (h v) -> p h v", v=64),
                                              in1=rs[:].unsqueeze(2).to_broadcast([128, 12, 64]), op=ALU.mult), reads=[kOF, "rs"], writes=[kOF])
        P.op("pool", lambda e: e.tensor_tensor(out=OF[b][:], in0=OF[b][:], in1=gn[:], op=ALU.mult), reads=[kOF, "gn"], writes=[kOF])
        P.op("act", lambda e: e.activation(out=sq[:], in_=GT[b][:], func=AF.Exp, scale=-1.0), reads=[kGT], writes=["sq"])
        P.op("dve", lambda e: e.tensor_scalar(out=sq[:], in0=sq[:], scalar1=1.0, scalar2=None, op0=ALU.add), reads=["sq"], writes=["sq"])
        P.op("dve", lambda e: e.reciprocal(out=sq[:], in_=sq[:]), reads=["sq"], writes=["sq"])
        P.op("pool", lambda e: e.tensor_tensor(out=sq[:, 0:384], in0=sq[:, 0:384], in1=GT[b][:, 0:384], op=ALU.mult), reads=["sq", kGT], writes=["sq"])
        P.op("dve", lambda e: e.tensor_tensor(out=mix[b][:, 0:768], in0=OF[b][:], in1=sq[:], op=ALU.mult), reads=[kOF, "sq"], writes=[(kmix, "a")])
        P.op("act", lambda e: e.activation(out=mixb[b][:], in_=mix[b][:], func=AF.Copy), reads=[(kmix, "a"), (kmix, "hy")], writes=[kmixb])
        if STOPC <= 1:
            continue
        for kc in range(8):
            P.op("pe", lambda e, kc=kc: e.transpose(out=pT[:, kc * 128:(kc + 1) * 128], in_=mixb[b][:, kc * 128:(kc + 1) * 128], identity=idb[:]),
                 reads=[kmixb, "idb"], writes=["pT"])
        P.op("dve", lambda e: e.tensor_copy(out=mixT[b][:].rearrange("p a b -> p (a b)"), in_=pT[:]), reads=["pT"], writes=[kmixT])
        for hf in range(2):
            for kc in range(8):
                P.op("pe", lambda e, kc=kc, hf=hf: e.matmul(pO[hf][:, :], lhsT=mixT[b][:, kc, :], rhs=wout[:, kc, hf * 512:(hf + 1) * 512], start=(kc == 0), stop=(kc == 7)),
                     reads=[kmixT] + woutk, writes=[("pO", hf)])
            P.op("dve", lambda e, hf=hf: e.scalar_tensor_tensor(out=ht[b][:, hf * 512:(hf + 1) * 512], in0=ht[b][:, hf * 512:(hf + 1) * 512], scalar=ALPHA,
                                                                in1=pO[hf][:, :], op0=ALU.mult, op1=ALU.add), reads=[kht, ("pO", hf)], writes=[kht])
        layer_norm_tile(P, ht[b], kht, st, mv, "st", epst, ln1, "ln1")
        P.dma("sp", d["h1s"][rows, :], ht[b][:], reads=[kht], writes=["h1s"])
        if STOPC <= 2:
            continue
        for kc in range(8):
            P.op("pe", lambda e, kc=kc: e.matmul(pX[kc // 4][:, (kc % 4) * 128:(kc % 4 + 1) * 128], lhsT=ht[b][:, kc * 128:(kc + 1) * 128], rhs=idf[:], start=True, stop=True),
                 reads=[kht, "idf"], writes=[("pX", kc // 4)])
        for hf in range(2):
            if os.environ.get("NOEVAC") == "1":
                continue
            if os.environ.get("NOEVAC") != "act":
                P.op("act", lambda e, hf=hf: e.activation(out=h1T32[:, hf * 4:(hf + 1) * 4, :].rearrange("p a b -> p (a b)"), in_=pX[hf][:, :], func=AF.Copy),
                     reads=[("pX", hf)], writes=[("h1T32", hf)])
            if os.environ.get("NOEVAC") == "dve":
                continue
            P.op("dve", lambda e, hf=hf: e.tensor_copy(out=h1T[:, hf * 4:(hf + 1) * 4, rows], in_=pX[hf][:, :].rearrange("p (a b) -> p a b", b=128)),
                 reads=[("pX", hf), ("h1T32", hf)], writes=[("h1T", t)])
        if STOPC <= 3:
            continue
        for kc in range(8):
            P.op("pe", lambda e, kc=kc: e.matmul(pL[:, 0:20], lhsT=h1T32[:, kc, :], rhs=wr[:, kc, :], start=(kc == 0), stop=(kc == 7)),
                 reads=[("h1T32", kc // 4), "wr"], writes=["pL"])
        P.op("dve", lambda e: e.tensor_tensor(out=lg[:], in0=pL[:, 0:20], in1=br[:], op=ALU.add), reads=["pL", "br"], writes=["lg"])
        if STOPC <= 4:
            continue
        R_ = lambda a, b_: rt[:, a:b_]
        dv = lambda fn, r=("lg", "rt"), w=("rt",): P.op("dve", fn, reads=list(r), writes=list(w))
        dv(lambda e: e.tensor_reduce(out=R_(0, 1), in_=lg[:, 0:4], axis=AX.X, op=ALU.max))
        dv(lambda e: e.tensor_scalar(out=R_(1, 5), in0=lg[:, 0:4], scalar1=R_(0, 1), scalar2=None, op0=ALU.is_equal))
        dv(lambda e: e.tensor_scalar(out=R_(5, 9), in0=lg[:, 0:4], scalar1=R_(0, 1), scalar2=None, op0=ALU.subtract))
        P.op("act", lambda e: e.activation(out=R_(5, 9), in_=R_(5, 9), func=AF.Exp), reads=["rt"], writes=["rt"])
        dv(lambda e: e.tensor_reduce(out=R_(9, 10), in_=R_(5, 9), axis=AX.X, op=ALU.add))
        dv(lambda e: e.reciprocal(out=R_(10, 11), in_=R_(9, 10)))
        dv(lambda e: e.tensor_tensor(out=rt[:, 40:56].rearrange("p (g e) -> p g e", e=4), in0=lg[:, 4:20].rearrange("p (g e) -> p g e", e=4),
                                     in1=R_(1, 5).unsqueeze(2).to_broadcast([128, 4, 4]), op=ALU.mult))
        dv(lambda e: e.tensor_reduce(out=R_(11, 15), in_=rt[:, 40:56].rearrange("p (g e) -> p e g", e=4), axis=AX.X, op=ALU.add))
        dv(lambda e: e.tensor_reduce(out=R_(15, 16), in_=R_(11, 15), axis=AX.X, op=ALU.max))
        dv(lambda e: e.tensor_scalar(out=R_(16, 20), in0=R_(11, 15), scalar1=R_(15, 16), scalar2=None, op0=ALU.is_equal))
        dv(lambda e: e.scalar_tensor_tensor(out=R_(20, 24), in0=R_(16, 20), scalar=-1e30, in1=R_(11, 15), op0=ALU.mult, op1=ALU.add))
        dv(lambda e: e.tensor_reduce(out=R_(24, 25), in_=R_(20, 24), axis=AX.X, op=ALU.max))
        dv(lambda e: e.tensor_scalar(out=R_(28, 32), in0=R_(20, 24), scalar1=R_(24, 25), scalar2=None, op0=ALU.is_equal))
        dv(lambda e: e.tensor_tensor(out=R_(32, 33), in0=R_(24, 25), in1=R_(15, 16), op=ALU.subtract))
        P.op("act", lambda e: e.activation(out=R_(32, 33), in_=R_(32, 33), func=AF.Exp), reads=["rt"], writes=["rt"])
        dv(lambda e: e.tensor_scalar(out=R_(33, 34), in0=R_(32, 33), scalar1=1.0, scalar2=None, op0=ALU.add))
        dv(lambda e: e.reciprocal(out=R_(34, 35), in_=R_(33, 34)))
        dv(lambda e: e.tensor_tensor(out=R_(35, 36), in0=R_(32, 33), in1=R_(34, 35), op=ALU.mult))
        dv(lambda e: e.tensor_scalar(out=R_(36, 40), in0=R_(16, 20), scalar1=R_(34, 35), scalar2=None, op0=ALU.mult))
        dv(lambda e: e.scalar_tensor_tensor(out=R_(36, 40), in0=R_(28, 32), scalar=R_(35, 36), in1=R_(36, 40), op0=ALU.mult, op1=ALU.add))
        dv(lambda e: e.tensor_scalar(out=R_(36, 40), in0=R_(36, 40), scalar1=R_(10, 11), scalar2=None, op0=ALU.mult))
        dv(lambda e: e.tensor_tensor(out=rt[:, 40:56].rearrange("p (g e) -> p g e", e=4), in0=R_(1, 5).unsqueeze(2).to_broadcast([128, 4, 4]),
                                     in1=R_(36, 40).unsqueeze(1).to_broadcast([128, 4, 4]), op=ALU.mult))
        P.op("dve", lambda e, t=t: e.tensor_copy(out=wfull[:, t, :], in_=rt[:, 40:56]), reads=["rt"], writes=[("wfull", t)])
    P.pop_scope()
    if STOPC <= 5:
        return
    P.push_scope()
    ln2 = sb("ln2", [128, 2 * D])
    P.dma("sp", ln2[:], d["ln2"], writes=["ln2"])
    HT = NT // 2 if NT >= 8 else NT
    nhalf = NT // HT
    yacc = sb("yacc", [128, HT, D])
    wg = [sb("wg%d" % i, [128, 8, DE], BF16) for i in range(2)]
    wu = [sb("wu%d" % i, [128, 8, DE], BF16) for i in range(2)]
    wd = [sb("wd%d" % i, [128, 4, D], BF16) for i in range(2)]
    hid = [sb("hid%d" % i, [128, 4, 512], BF16) for i in range(2)]
    sg = [sb("sg%d" % i, [128, 512]) for i in range(2)]
    st2 = sb("st2", [128, 2, 6]); mv2 = sb("mv2", [128, 4])
    xo = [sb("xo%d" % i, [128, D]) for i in range(2)]
    pG = [ps("pG%d" % i, [128, 512]) for i in range(2)]
    pU = [ps("pU%d" % i, [128, 512]) for i in range(2)]
    pD = [ps("pD%d" % i, [128, 512]) for i in range(2)]
    wq = 0
    for hf_ in range(nhalf):
        tbase = hf_ * HT
        P.op("pool", lambda e: e.memset(yacc[:].rearrange("p a b -> p (a b)"), 0.0), writes=[("yacc", i) for i in range(HT)])
        for ex in range(NEXP):
            wb = wq % 2
            wq += 1
            for kc in range(8):
                P.dma("pool", wg[wb][:, kc, :], d["wg"][ex, kc * 128:(kc + 1) * 128, :], writes=[("wg", wb, kc)])
                P.dma("pool", wu[wb][:, kc, :], d["wu"][ex, kc * 128:(kc + 1) * 128, :], writes=[("wu", wb, kc)])
            for fc in range(4):
                P.dma("pool", wd[wb][:, fc, :], d["wd"][ex, fc * 128:(fc + 1) * 128, :], writes=[("wd", wb, fc)])
            ngrp = (HT * 128) // 512 if HT * 128 >= 512 else 1
            gw = min(512, HT * 128)
            for g in range(ngrp):
                tok0 = tbase * 128 + g * gw
                hb = g % 2
                for fc in range(4):
                    pb = fc % 2
                    for kc in range(8):
                        P.op("pe", lambda e, kc=kc, fc=fc, pb=pb: e.matmul(pG[pb][:, 0:gw], lhsT=wg[wb][:, kc, fc * 128:(fc + 1) * 128], rhs=h1T[:, kc, tok0:tok0 + gw],
                                                                           start=(kc == 0), stop=(kc == 7)),
                             reads=[("wg", wb, kc)] + [("h1T", tt) for tt in range(tok0 // 128, (tok0 + gw) // 128)], writes=[("pG", pb)])
                    for kc in range(8):
                        P.op("pe", lambda e, kc=kc, fc=fc, pb=pb: e.matmul(pU[pb][:, 0:gw], lhsT=wu[wb][:, kc, fc * 128:(fc + 1) * 128], rhs=h1T[:, kc, tok0:tok0 + gw],
                                                                           start=(kc == 0), stop=(kc == 7)),
                             reads=[("wu", wb, kc)] + [("h1T", tt) for tt in range(tok0 // 128, (tok0 + gw) // 128)], writes=[("pU", pb)])
                    P.op("act", lambda e, pb=pb: e.activation(out=sg[pb][:, 0:gw], in_=pG[pb][:, 0:gw], func=AF.Silu), reads=[("pG", pb)], writes=[("sg", pb)])
                    P.op("dve", lambda e, pb=pb, fc=fc, hb=hb: e.tensor_tensor(out=hid[hb][:, fc, 0:gw], in0=sg[pb][:, 0:gw], in1=pU[pb][:, 0:gw], op=ALU.mult),
                         reads=[("sg", pb), ("pU", pb)], writes=[("hid", hb, fc)])
                for ts in range(gw // 128):
                    tl = (tok0 - tbase * 128) // 128 + ts
                    tg = tbase + tl
                    for dh in range(2):
                        pb = dh
                        for fc in range(4):
                            P.op("pe", lambda e, fc=fc, ts=ts, dh=dh, pb=pb: e.matmul(pD[pb][:, :], lhsT=hid[hb][:, fc, ts * 128:(ts + 1) * 128], rhs=wd[wb][:, fc, dh * 512:(dh + 1) * 512],
                                                                                      start=(fc == 0), stop=(fc == 3)),
                                 reads=[("hid", hb, fc), ("wd", wb, fc)], writes=[("pD", pb)])
                        P.op("dve", lambda e, tl=tl, tg=tg, dh=dh, pb=pb, ex=ex: e.scalar_tensor_tensor(out=yacc[:, tl, dh * 512:(dh + 1) * 512], in0=pD[pb][:, :],
                                                                                                       scalar=wfull[:, tg, ex:ex + 1], in1=yacc[:, tl, dh * 512:(dh + 1) * 512],
                                                                                                       op0=ALU.mult, op1=ALU.add),
                             reads=[("pD", pb), ("wfull", tg), ("yacc", tl)], writes=[("yacc", tl)])
        for tl in range(HT):
            tg = tbase + tl
            b = tl % 2
            rows = slice(tg * 128, (tg + 1) * 128)
            P.dma("sp", xo[b][:], d["h1s"][rows, :], reads=["h1s"], writes=[("xo", b)])
            P.op("dve", lambda e, tl=tl, b=b: e.scalar_tensor_tensor(out=xo[b][:], in0=xo[b][:], scalar=ALPHA, in1=yacc[:, tl, :], op0=ALU.mult, op1=ALU.add),
                 reads=[("xo", b), ("yacc", tl)], writes=[("xo", b)])
            layer_norm_tile(P, xo[b], ("xo", b), st2, mv2, "st2", epst, ln2, "ln2")
            P.dma("sp", d["out"][rows, :], xo[b][:], reads=[("xo", b)], writes=["out"])
    P.pop_scope()

import math
import numpy as np

L = 8192
D = 1024
DIN = 3872
NFM = 2336
NTM = 1536
NCORES = 8
FM_GQ, FM_GK, FM_GA, FM_HQ, FM_HF, FM_HY = 0, 192, 384, 416, 800, 1568
TM_GV, TM_GG, TM_HI, TM_HGT = 0, 384, 768, 1152
FM_COLS = list(range(0, 384)) + list(range(1152, 1184)) + list(range(1184, 2336)) + list(range(3104, 3872))
TM_COLS = list(range(384, 768)) + list(range(768, 1152)) + list(range(2336, 2720)) + list(range(2720, 3104))


def stage_A2(P, do_ln, x_d, w_d, gb_d, id_d, h_d, FM, TM):
    NG = L // 512
    P.push_scope()
    wsb = P.sb("wsb", [128, 8, DIN], BF16)
    idf = P.sb("a_idf", [128, 128], F32)
    idb = P.sb("a_idb", [128, 128], BF16)
    epst = P.sb("a_epst", [128, 1], F32)
    P.op("dve", lambda e: e.memset(epst[:], 1e-5), writes=["a_eps"])
    P.dma("sp", idf[:], id_d, writes=["a_idf"])
    P.op("dve", lambda e: e.tensor_copy(out=idb[:], in_=idf[:]), reads=["a_idf"], writes=["a_idb"])
    if do_ln:
        gbs = P.sb("a_gbs", [128, 2 * D], F32)
        P.dma("sp", gbs[:], gb_d, writes=["a_gbs"])
    CW = 968
    for kc in range(8):
        for c in range(4):
            P.dma("pool", wsb[:, kc, c * CW:(c + 1) * CW], w_d[kc * 128:(kc + 1) * 128, c * CW:(c + 1) * CW], writes=[("wsb", kc, c)])
    wk = lambda kc: [("wsb", kc, c) for c in range(4)]
    xt = [P.sb("a_xt%d" % i, [128, D], F32) for i in range(2)]
    hb = [P.sb("a_hb%d" % i, [128, D], BF16) for i in range(2)]
    hT = [P.sb("a_hT%d" % i, [128, 8, 512], BF16) for i in range(2)]
    st = P.sb("a_st", [128, 2, 6], F32)
    mv = P.sb("a_mv", [128, 4], F32)
    fo = [P.sb("a_fo%d" % i, [128, 512], F32) for i in range(3)]
    po = [P.sb("a_po%d" % i, [128, NTM], F32) for i in range(2)]
    pT = [P.ps("a_pT%d" % i, [128, 1024], BF16) for i in range(2)]
    pm = [P.ps("a_pm%d" % i, [128, 512], F32) for i in range(4)]
    mmi = 0
    ti = 0
    fi = 0
    for g in range(NG):
        hTg = hT[g % 2]
        khT = ("a_hT", g % 2)
        for tt in range(4):
            t = g * 4 + tt
            b = ti % 2
            ti += 1
            rows = slice(t * 128, (t + 1) * 128)
            kx, kh = ("a_xt", b), ("a_hb", b)
            P.dma("sp", xt[b][:], x_d[rows, :], writes=[kx])
            if do_ln:
                layer_norm_tile(P, xt[b], kx, st, mv, "a_st", epst, gbs, "a_gbs")
                P.dma("sp", h_d[rows, :], xt[b][:], reads=[kx], writes=["hbuf"])
            P.op("act", lambda e, b=b: e.activation(out=hb[b][:], in_=xt[b][:], func=AF.Copy), reads=[kx], writes=[kh])
            ptk = ("a_pT", t % 2)
            for kc in range(8):
                P.op("pe", lambda e, kc=kc, b=b, t=t: e.transpose(out=pT[t % 2][:, kc * 128:(kc + 1) * 128], in_=hb[b][:, kc * 128:(kc + 1) * 128], identity=idb[:]),
                     reads=[kh, "a_idb"], writes=[ptk])
            P.op("dve", lambda e, tt=tt, t=t: e.tensor_copy(out=hTg[:, :, tt * 128:(tt + 1) * 128], in_=pT[t % 2][:].rearrange("p (a b) -> p a b", b=128)),
                 reads=[ptk], writes=[(khT, tt)])
        hk = [(khT, tt) for tt in range(4)]
        for fg in range((NFM + 127) // 128):
            r0 = fg * 128
            rw = min(128, NFM - r0)
            pb = mmi % 4
            mmi += 1
            for kc in range(8):
                P.op("pe", lambda e, kc=kc, pb=pb, r0=r0, rw=rw: e.matmul(pm[pb][0:rw, :], lhsT=wsb[:, kc, r0:r0 + rw], rhs=hTg[:, kc, :], start=(kc == 0), stop=(kc == 7)),
                     reads=hk + wk(kc), writes=[("a_pm", pb)])
            fb = fi % 3
            fi += 1
            if fg % 2 == 0:
                P.op("act", lambda e, pb=pb, fb=fb, rw=rw: e.activation(out=fo[fb][0:rw, :], in_=pm[pb][0:rw, :], func=AF.Copy), reads=[("a_pm", pb)], writes=[("a_fo", fb)])
            else:
                P.op("dve", lambda e, pb=pb, fb=fb, rw=rw: e.tensor_copy(out=fo[fb][0:rw, :], in_=pm[pb][0:rw, :]), reads=[("a_pm", pb)], writes=[("a_fo", fb)])
            P.dma("sp", FM[r0:r0 + rw, g * 512:(g + 1) * 512], fo[fb][0:rw, :], reads=[("a_fo", fb)], writes=["FM"])
        for tt in range(4):
            t = g * 4 + tt
            pbuf = po[t % 2]
            kpo = ("a_po", t % 2)
            for ci in range(3):
                pb = mmi % 4
                mmi += 1
                for kc in range(8):
                    P.op("pe", lambda e, kc=kc, pb=pb, ci=ci, tt=tt: e.matmul(pm[pb][:, :], lhsT=hTg[:, kc, tt * 128:(tt + 1) * 128], rhs=wsb[:, kc, NFM + ci * 512:NFM + (ci + 1) * 512],
                                                                             start=(kc == 0), stop=(kc == 7)), reads=hk + wk(kc), writes=[("a_pm", pb)])
                if ci % 2 == 0:
                    P.op("act", lambda e, pb=pb, ci=ci, pbuf=pbuf: e.activation(out=pbuf[:, ci * 512:(ci + 1) * 512], in_=pm[pb][:, :], func=AF.Copy), reads=[("a_pm", pb)], writes=[(kpo, ci)])
                else:
                    P.op("dve", lambda e, pb=pb, ci=ci, pbuf=pbuf: e.tensor_copy(out=pbuf[:, ci * 512:(ci + 1) * 512], in_=pm[pb][:, :]), reads=[("a_pm", pb)], writes=[(kpo, ci)])
            P.dma("sp", TM[t * 128:(t + 1) * 128, :], pbuf[:], reads=[(kpo, ci) for ci in range(3)], writes=["TM"])
    P.pop_scope()


def scans_all(P, layer, d, FM, TM, OF, OB):
    P.push_scope()
    C = ScanCtx(P, L, d)
    for h in range(6):
        P.dma("sp", C.wa[:], d["gwa"][h], writes=["wa"])
        P.dma("sp", C.ba[0:32, 0:2], d["gba"][h], writes=["ba"])
        P.op("dve", lambda e: e.tensor_scalar(out=C.ba[0:32, 2:4], in0=C.ba[0:32, 0:2], scalar1=-1.0, scalar2=None, op0=ALU.mult), reads=["ba"], writes=["nba"])
        prm = {"wa": C.wa, "nba": C.ba[:, 2:4]}
        for dr in range(2):
            O_ = OF if dr == 0 else OB
            scan_dir(C, "gla", layer, 32, dr, FM[FM_GQ + h * 32:FM_GQ + (h + 1) * 32, :], FM[FM_GK + h * 32:FM_GK + (h + 1) * 32, :],
                     FM[FM_GA + dr * 16:FM_GA + (dr + 1) * 16, :], TM[:, TM_GV + h * 64:TM_GV + (h + 1) * 64], O_[:, h * 64:(h + 1) * 64], prm,
                     rkeys=["FM", "TM"])
    for h in range(6):
        K = 64
        prm = {}
        if layer > 0:
            P.dma("sp", C.lb[0:K, 0:4], d["hlb"][h], writes=["lbl"])
            P.op("dve", lambda e: e.tensor_tensor(out=C.lb[0:K, 4:6], in0=C.lb[0:K, 0:2], in1=C.lb[0:K, 2:4], op=ALU.subtract), reads=["lbl"], writes=["lb1"])
            P.op("act", lambda e: e.activation(out=C.lb[0:K, 4:6], in_=C.lb[0:K, 4:6], func=AF.Exp), reads=["lb1"], writes=["lb1"])
            P.op("dve", lambda e: e.tensor_scalar(out=C.lb[0:K, 4:6], in0=C.lb[0:K, 4:6], scalar1=1.0, scalar2=None, op0=ALU.add), reads=["lb1"], writes=["lb1"])
            P.op("dve", lambda e: e.reciprocal(out=C.lb[0:K, 4:6], in_=C.lb[0:K, 4:6]), reads=["lb1"], writes=["lb"])
            P.op("dve", lambda e: e.tensor_scalar(out=C.lb[0:K, 6:8], in0=C.lb[0:K, 4:6], scalar1=-1.0, scalar2=1.0, op0=ALU.mult, op1=ALU.add), reads=["lb"], writes=["oml"])
            prm = {"lb": C.lb[:, 4:6], "oml": C.lb[:, 6:8]}
        for dr in range(2):
            O_ = OF if dr == 0 else OB
            scan_dir(C, "hg", layer, 64, dr, FM[FM_HQ + h * 64:FM_HQ + (h + 1) * 64, :], None,
                     FM[FM_HF + dr * 384 + h * 64:FM_HF + dr * 384 + (h + 1) * 64, :], TM[:, TM_HI + h * 64:TM_HI + (h + 1) * 64],
                     O_[:, 384 + h * 64:384 + (h + 1) * 64], prm, rkeys=["FM", "TM"])
    P.pop_scope()


def hyena_consts2(Lx):
    f32 = np.float32
    t = np.linspace(0.0, 1.0, Lx, dtype=f32)[:, None]
    w = (2.0 * math.pi * np.arange(Lx, dtype=f32)[:, None] / Lx).astype(f32)
    bands = np.linspace(1e-4, 15, 16, dtype=f32)[None, :]
    z = np.concatenate([t, np.cos(bands * w), -np.sin(bands * w)], axis=-1).astype(f32)
    max_decay = math.log(1e-2) / 0.3
    min_decay = math.log(1e-2) / 1.5
    deltas = np.linspace(min_decay, max_decay, HY_W, dtype=f32)
    dec = np.exp(-t * np.abs(deltas)).astype(f32)
    tidx = np.concatenate([np.arange(Lx - 1, -1, -1), np.arange(1, Lx), [0]])
    zR = np.ascontiguousarray(z[tidx].T)
    zR[:, -1] = 0.0
    dR = np.ascontiguousarray(dec[tidx].T)
    dR[:, -1] = 0.0
    J = np.ascontiguousarray(np.eye(128, dtype=f32)[::-1])
    return {"h_zR": zR, "h_decR": dR.astype(f32), "h_J": J, "h_id": np.eye(128, dtype=f32), "h_ones": np.ones((64, 128), f32)}


def hyena_all(P, Lx, d, FM, YH, scr, fm_hy=FM_HY):
    NB = Lx // 128
    NT = 2 * Lx // 512
    HW = 2 * Lx - 128
    sb = P.sb
    P.push_scope()
    w1 = sb("h_w1", [33, 64]); b1f = sb("h_b1f", [64, 8]); w2 = sb("h_w2", [64, 2, 64]); w3 = sb("h_w3", [64, 2, 2, 256])
    cw = sb("h_cw", [128, 2, 9]); cb = sb("h_cb", [128, 2, 3]); bias = sb("h_bias", [128, 2, 2])
    Jm = sb("h_Jm", [128, 128]); idf = sb("h_idf", [128, 128]); idb = sb("h_idb", [128, 128], BF16); ones = sb("h_onesb", [64, 128])
    P.dma("sp", w1[:], d["w1"], writes=["h_w1"])
    P.dma("sp", b1f[:, 0:4], d["b1f"], writes=["h_b1f"])
    P.dma("sp", w2[:], d["w2"], writes=["h_w2"])
    P.dma("sp", w3[:], d["w3"], writes=["h_w3"])
    for lt in range(2):
        P.dma("sp", cw[:, lt, :], d["cw"][lt], writes=["h_cw"])
        P.dma("sp", cb[:, lt, :], d["cb"][lt], writes=["h_cb"])
        P.dma("sp", bias[:, lt, :], d["bias"][lt], writes=["h_bias"])
    P.dma("sp", Jm[:], d["h_J"], writes=["h_J"])
    P.dma("sp", idf[:], d["h_id"], writes=["h_idf"])
    P.dma("sp", ones[:], d["h_ones"], writes=["h_ones"])
    P.op("dve", lambda e: e.tensor_copy(out=idb[:], in_=idf[:]), reads=["h_idf"], writes=["h_idb"])
    P.op("dve", lambda e: e.tensor_scalar(out=b1f[:, 4:5], in0=b1f[:, 3:4], scalar1=1.0 / (2 * math.pi), scalar2=None, op0=ALU.mult), reads=["h_b1f"], writes=["h_A"])
    P.op("dve", lambda e: e.tensor_scalar(out=b1f[:, 5:8], in0=b1f[:, 0:3], scalar1=b1f[:, 4:5], scalar2=None, op0=ALU.mult), reads=["h_b1f", "h_A"], writes=["h_B"])
    pB = P.ps("h_pB", [128, 512])
    rnb = sb("h_rnb", [128, 512])
    P.push_scope()
    pA = P.ps("h_pA", [64, 512])
    pR = [P.ps("h_pR%d" % i, [64, 512]) for i in range(2)]
    zt = [sb("h_zt%d" % i, [33, 512]) for i in range(2)]
    dt_ = [sb("h_dt%d" % i, [64, 4, 512]) for i in range(2)]
    u = sb("h_u", [64, 512]); ui = sb("h_ui", [64, 512], I32); hh = sb("h_hh", [64, 512]); h3 = [sb("h_h3%d" % i, [64, 512]) for i in range(2)]
    Rt = sb("h_Rt", [64, 512]); Rb = [sb("h_Rb%d" % i, [64, 512], BF16) for i in range(3)]
    asum = sb("h_asum", [64, 8, NT]); rn = sb("h_rn", [64, 16]); dg = sb("h_dg", [64, 64])
    ri = 0
    for t in range(NT):
        b = t % 2
        P.dma("sp", zt[b][:], d["h_zR"][:, t * 512:(t + 1) * 512], writes=[("zt", b)])
        P.dma("sp", dt_[b][:], d["h_decR"][:, t * 512:(t + 1) * 512].rearrange("(a p) n -> p a n", p=64), writes=[("dt", b)])
        rhs = zt[b]
        rk = ("zt", b)
        for l in range(3):
            lhsT = w1[:, :] if l == 0 else w2[:, l - 1, :]
            kk = 33 if l == 0 else 64
            P.op("pe", lambda e, lhsT=lhsT, rhs=rhs, kk=kk: e.matmul(pA[:, :], lhsT=lhsT, rhs=rhs[0:kk, :], start=True, stop=True),
                 reads=[rk, "h_w1", "h_w2"], writes=["h_pA"])
            P.op("dve", lambda e, l=l: e.tensor_scalar(out=u[:], in0=pA[:], scalar1=b1f[:, 4:5], scalar2=b1f[:, 5 + l:6 + l], op0=ALU.mult, op1=ALU.add),
                 reads=["h_pA", "h_A", "h_B"], writes=["h_u"])
            P.op("dve", lambda e: e.tensor_copy(out=ui[:], in_=u[:]), reads=["h_u"], writes=["h_ui"])
            P.op("dve", lambda e: e.tensor_tensor(out=u[:], in0=u[:], in1=ui[:], op=ALU.subtract), reads=["h_u", "h_ui"], writes=["h_u"])
            dst = hh if l < 2 else h3[b]
            dk = "h_hh" if l < 2 else ("h_h3", b)
            P.op("act", lambda e, dst=dst: e.activation(out=dst[:], in_=u[:], func=AF.Sin, scale=TWO_PI_LO), reads=["h_u"], writes=[dk])
            rhs = dst
            rk = dk
        dr = 0 if t < NT // 2 else 1
        for gi in range(8):
            o, cq = gi // 4, gi % 4
            pr = pR[gi % 2]
            prk = ("h_pR", gi % 2)
            P.op("pe", lambda e, dr=dr, o=o, cq=cq, pr=pr, b=b: e.matmul(pr[:, :], lhsT=w3[:, dr, o, cq * 64:(cq + 1) * 64], rhs=h3[b][:, :], start=True, stop=True),
                 reads=[("h_h3", b), "h_w3"], writes=[prk])
            P.op("dve", lambda e, pr=pr, cq=cq, b=b: e.tensor_tensor(out=Rt[:], in0=pr[:], in1=dt_[b][:, cq, :], op=ALU.mult), reads=[prk, ("dt", b)], writes=["h_Rt"])
            P.op("dve", lambda e, gi=gi, t=t: e.tensor_reduce(out=asum[:, gi, t:t + 1], in_=Rt[:], axis=AX.X, op=ALU.add, apply_absolute_value=True), reads=["h_Rt"], writes=[("asum", gi, t)])
            rb = ri % 3
            ri += 1
            P.op("act", lambda e, rb=rb: e.activation(out=Rb[rb][:], in_=Rt[:], func=AF.Copy), reads=["h_Rt"], writes=[("Rb", rb)])
            P.dma("sp", scr[gi * 64:(gi + 1) * 64, t * 512:(t + 1) * 512], Rb[rb][:], reads=[("Rb", rb)], writes=["scr"])
    P.op("dve", lambda e: e.tensor_reduce(out=rn[:, 0:8], in_=asum[:], axis=AX.X, op=ALU.add), reads=[("asum", gi, t) for gi in range(8) for t in range(NT)], writes=["h_rn"])
    P.op("dve", lambda e: e.tensor_scalar(out=rn[:, 0:8], in0=rn[:, 0:8], scalar1=1e-12, scalar2=None, op0=ALU.max), reads=["h_rn"], writes=["h_rn"])
    P.op("dve", lambda e: e.reciprocal(out=rn[:, 8:16], in_=rn[:, 0:8]), reads=["h_rn"], writes=["h_rn1"])
    for gi in range(8):
        P.op("dve", lambda e, gi=gi: e.tensor_scalar(out=dg[:], in0=idf[0:64, 0:64], scalar1=rn[:, 8 + gi:9 + gi], scalar2=None, op0=ALU.mult), reads=["h_idf", "h_rn1"], writes=["h_dg"])
        P.op("pe", lambda e: e.matmul(pB[:, 0:64], lhsT=ones[:, :], rhs=dg[:, :], start=True, stop=True), reads=["h_ones", "h_dg"], writes=["h_pB"])
        P.op("dve", lambda e, gi=gi: e.tensor_copy(out=rnb[:, gi * 64:(gi + 1) * 64], in_=pB[:, 0:64]), reads=["h_pB"], writes=["h_rnb"])
    P.pop_scope()
    za = sb("h_za", [128, Lx]); zc = sb("h_zc", [128, Lx])
    zb = [sb("h_zb%d" % i, [128, 1024], BF16) for i in range(2)]
    ZT = sb("h_ZT", [128, NB, 128], BF16)
    YT = sb("h_YT", [128, NB, 128])
    xr = YT[:].rearrange("p a b -> p (a b)")
    NYK = 8
    ytk = [("YT", c) for c in range(NYK)]
    H = [sb("h_H%d" % i, [128, HW], BF16) for i in range(2)]
    pT = P.ps("h_pT", [128, 1024], BF16)
    pC = [P.ps("h_pC%d" % i, [128, 512]) for i in range(2)]
    yo = [sb("h_yo%d" % i, [128, 512]) for i in range(2)]
    hq = 0
    for lt in range(2):
        def short_conv(idx, dst, dk):
            P.dma("sp", xr, FM[fm_hy + idx * 256 + lt * 128:fm_hy + idx * 256 + (lt + 1) * 128, :], reads=["FM"], writes=ytk)
            P.op("dve", lambda e: e.tensor_scalar(out=dst[:], in0=xr, scalar1=cw[:, lt, idx * 3 + 1:idx * 3 + 2], scalar2=cb[:, lt, idx:idx + 1], op0=ALU.mult, op1=ALU.add),
                 reads=ytk + ["h_cw", "h_cb"], writes=[dk])
            P.op("dve", lambda e: e.scalar_tensor_tensor(out=dst[:, 1:Lx], in0=xr[:, 0:Lx - 1], scalar=cw[:, lt, idx * 3:idx * 3 + 1], in1=dst[:, 1:Lx], op0=ALU.mult, op1=ALU.add),
                 reads=ytk + ["h_cw", dk], writes=[dk])
            P.op("dve", lambda e: e.scalar_tensor_tensor(out=dst[:, 0:Lx - 1], in0=xr[:, 1:Lx], scalar=cw[:, lt, idx * 3 + 2:idx * 3 + 3], in1=dst[:, 0:Lx - 1], op0=ALU.mult, op1=ALU.add),
                 reads=ytk + ["h_cw", dk], writes=[dk])

        short_conv(0, za, "h_za")
        for o in range(2):
            for g in range(NB // 8):
                zbb = zb[g % 2]
                zbk = ("h_zb", g % 2)
                P.op("act", lambda e, g=g, zbb=zbb: e.activation(out=zbb[:], in_=za[:, g * 1024:(g + 1) * 1024], func=AF.Copy), reads=["h_za"], writes=[zbk])
                for jj in range(8):
                    P.op("pe", lambda e, jj=jj, zbb=zbb: e.transpose(out=pT[:, jj * 128:(jj + 1) * 128], in_=zbb[:, jj * 128:(jj + 1) * 128], identity=idb[:]),
                         reads=[zbk, "h_idb"], writes=["h_pT"])
                P.op("dve", lambda e, g=g: e.tensor_copy(out=ZT[:, g * 8:(g + 1) * 8, :].rearrange("p a b -> p (a b)"), in_=pT[:]), reads=["h_pT"], writes=[("ZT", g)])
            ztk = [("ZT", g) for g in range(NB // 8)]
            short_conv(1 + o, zc, "h_zc")
            for c in range(128):
                row = o * 256 + lt * 128 + c
                hb_ = hq % 2
                hq += 1
                q = "sp" if hb_ == 0 else "pool"
                P.dma(q, H[hb_][:], bass.AP(scr.tensor, row * 2 * Lx, [[1, 128], [1, HW]]), reads=["scr"], writes=[("H", hb_)])
                pc = pC[hb_]
                pck = ("pC", hb_)
                ds = [0] + [x for k in range(1, NB) for x in (k, -k)]
                for n_, dd in enumerate(ds):
                    j0, j1 = max(0, -dd), min(NB, NB - dd)
                    x0 = Lx - 128 - 128 * dd
                    P.op("pe", lambda e, dd=dd, j0=j0, j1=j1, x0=x0, n_=n_, pc=pc, hb_=hb_, c=c: e.matmul(
                        pc[:, (j0 + dd):(j1 + dd)], lhsT=H[hb_][:, x0:x0 + 128], rhs=ZT[:, j0:j1, c], start=(n_ == 0), stop=(n_ == len(ds) - 1)),
                         reads=[("H", hb_)] + ztk, writes=[pck])
                yk = ("YT", c % NYK)
                if c % 2 == 0:
                    P.op("act", lambda e, c=c, row=row, pc=pc: e.activation(out=YT[:, :, c], in_=pc[:, 0:NB], func=AF.Copy, scale=rnb[:, row:row + 1]),
                         reads=[pck, "h_rnb"], writes=[yk])
                else:
                    P.op("dve", lambda e, c=c, row=row, pc=pc: e.tensor_scalar(out=YT[:, :, c], in0=pc[:, 0:NB], scalar1=rnb[:, row:row + 1], scalar2=None, op0=ALU.mult),
                         reads=[pck, "h_rnb"], writes=[yk])
            for g in range(NB // 4):
                for ii in range(4):
                    i = g * 4 + ii
                    P.op("pe", lambda e, ii=ii, i=i: e.matmul(pB[:, ii * 128:(ii + 1) * 128], lhsT=YT[:, i, :], rhs=Jm[:, :], start=True, stop=True),
                         reads=ytk + ["h_J"], writes=["h_pB"])
                sl = slice(g * 512, (g + 1) * 512)
                P.op("dve", lambda e, sl=sl, o=o: e.scalar_tensor_tensor(out=za[:, sl], in0=za[:, sl], scalar=bias[:, lt, o:o + 1], in1=pB[:, :], op0=ALU.mult, op1=ALU.add),
                     reads=["h_za", "h_bias", "h_pB"], writes=["h_za"])
                P.op("pool", lambda e, sl=sl: e.tensor_tensor(out=za[:, sl], in0=za[:, sl], in1=zc[:, sl], op=ALU.mult), reads=["h_za", "h_zc"], writes=["h_za"])
        for g in range(NB // 4):
            for ii in range(4):
                i = g * 4 + ii
                P.op("pe", lambda e, ii=ii, i=i: e.matmul(pB[:, ii * 128:(ii + 1) * 128], lhsT=za[:, i * 128:(i + 1) * 128], rhs=idf[:, :], start=True, stop=True),
                     reads=["h_za", "h_idf"], writes=["h_pB"])
            yb = yo[g % 2]
            P.op("act", lambda e, yb=yb: e.activation(out=yb[:], in_=pB[:, :], func=AF.Copy), reads=["h_pB"], writes=[("h_yo", g % 2)])
            P.dma("sp", YH[g * 512:(g + 1) * 512, lt * 128:(lt + 1) * 128].rearrange("(a p) c -> p a c", p=128), yb[:].rearrange("p (a c) -> p a c", c=128),
                  reads=[("h_yo", g % 2)], writes=["YH"])
    P.pop_scope()


def stage_C2(P, d, HB, TM, OF, OB, YH, h1s, OUT, CT=16):
    sb, ps = P.sb, P.ps
    NCH = (L // 128) // CT
    CTOK = CT * 128
    P.push_scope()
    epst = sb("epst", [128, 1]); eps6 = sb("eps6", [128, 1])
    P.op("dve", lambda e: e.memset(epst[:], 1e-5), writes=["eps"])
    P.op("dve", lambda e: e.memset(eps6[:], 1e-6), writes=["eps6"])
    idf = sb("c_idf", [128, 128]); idb = sb("c_idb", [128, 128], BF16)
    P.dma("sp", idf[:], d["c_id"], writes=["idf"])
    P.op("dve", lambda e: e.tensor_copy(out=idb[:], in_=idf[:]), reads=["idf"], writes=["idb"])
    h1T = sb("h1T", [128, 8, CTOK], BF16)
    wfull = sb("wfull", [128, CT, 16])
    for ch in range(NCH):
        tb = ch * CT
        P.push_scope()
        wout = sb("woutb", [128, 8, D], BF16)
        for kc in range(8):
            P.dma("pool", wout[:, kc, :], d["wout"][kc * 128:(kc + 1) * 128, :], writes=[("wout", kc)])
        woutk = [("wout", kc) for kc in range(8)]
        gn = sb("gn", [128, 768]); ln1 = sb("ln1", [128, 2 * D]); wr = sb("wr", [128, 8, 20]); br = sb("br", [128, 20])
        P.dma("sp", gn[:], d["gn"], writes=["gn"])
        P.dma("sp", ln1[:], d["ln1"], writes=["ln1"])
        P.dma("sp", wr[:], d["wr"].rearrange("(k p) n -> p k n", p=128), writes=["wr"])
        P.dma("sp", br[:], d["br"], writes=["br"])
        NBUF = 2
        OFt = [sb("OF%d" % i, [128, 768]) for i in range(NBUF)]
        OBt = [sb("OB%d" % i, [128, 768]) for i in range(NBUF)]
        GT = [sb("GT%d" % i, [128, 768]) for i in range(NBUF)]
        mix = [sb("mix%d" % i, [128, D]) for i in range(NBUF)]
        mixb = [sb("mixb%d" % i, [128, D], BF16) for i in range(NBUF)]
        mixT = [sb("mixT%d" % i, [128, 8, 128], BF16) for i in range(NBUF)]
        ht = [sb("ht%d" % i, [128, D]) for i in range(NBUF)]
        sq = sb("sq", [128, 768]); ss = sb("ss", [128, 12]); rs = sb("rs_", [128, 12])
        st = sb("st", [128, 2, 6]); mv = sb("mv", [128, 4])
        h1T32 = sb("h1T32", [128, 8, 128])
        lg = sb("lg", [128, 20]); rt = sb("rt", [128, 64])
        pT = ps("pT", [128, 1024], BF16)
        pO = [ps("pO%d" % i, [128, 512]) for i in range(2)]
        pX = [ps("pX%d" % i, [128, 512]) for i in range(2)]
        pL = ps("pL", [128, 512])
        for tl in range(CT):
            t = tb + tl
            b = tl % NBUF
            rows = slice(t * 128, (t + 1) * 128)
            lrows = slice(tl * 128, (tl + 1) * 128)
            kOF, kOB, kGT, kmix, kmixb, kmixT, kht = ("OF", b), ("OB", b), ("GT", b), ("mix", b), ("mixb", b), ("mixT", b), ("ht", b)
            P.dma("sp", OFt[b][:], OF[rows, :], writes=[kOF])
            P.dma("sp", OBt[b][:], OB[rows, :], writes=[kOB])
            P.dma("sp", GT[b][:, 0:384], TM[rows, TM_GG:TM_GG + 384], writes=[(kGT, 0)])
            P.dma("sp", GT[b][:, 384:768], TM[rows, TM_HGT:TM_HGT + 384], writes=[(kGT, 1)])
            kGTs = [(kGT, 0), (kGT, 1)]
            P.dma("sp", mix[b][:, 768:1024], YH[rows, :], writes=[(kmix, "hy")])
            P.dma("sp", ht[b][:], HB[rows, :], writes=[kht])
            P.op("pool", lambda e, b=b: e.tensor_tensor(out=OFt[b][:], in0=OFt[b][:], in1=OBt[b][:], op=ALU.add), reads=[kOF, kOB], writes=[kOF])
            P.op("dve", lambda e, b=b: e.tensor_tensor(out=sq[:], in0=OFt[b][:], in1=OFt[b][:], op=ALU.mult), reads=[kOF], writes=["sq"])
            P.op("dve", lambda e: e.tensor_reduce(out=ss[:], in_=sq[:].rearrange("p (h v) -> p h v", v=64), axis=AX.X, op=ALU.add), reads=["sq"], writes=["ss"])
            P.op("act", lambda e: e.activation(out=rs[:], in_=ss[:], func=AF.Sqrt, bias=eps6[:, 0:1], scale=1.0 / 64.0), reads=["ss", "eps6"], writes=["rs"])
            P.op("dve", lambda e: e.reciprocal(out=rs[:], in_=rs[:]), reads=["rs"], writes=["rs"])
            P.op("dve", lambda e, b=b: e.tensor_tensor(out=OFt[b][:].rearrange("p (h v) -> p h v", v=64), in0=OFt[b][:].rearrange("p (h v) -> p h v", v=64),
                                                       in1=rs[:].unsqueeze(2).to_broadcast([128, 12, 64]), op=ALU.mult), reads=[kOF, "rs"], writes=[kOF])
            P.op("pool", lambda e, b=b: e.tensor_tensor(out=OFt[b][:], in0=OFt[b][:], in1=gn[:], op=ALU.mult), reads=[kOF, "gn"], writes=[kOF])
            P.op("act", lambda e, b=b: e.activation(out=sq[:], in_=GT[b][:], func=AF.Exp, scale=-1.0), reads=kGTs, writes=["sq"])
            P.op("dve", lambda e: e.tensor_scalar(out=sq[:], in0=sq[:], scalar1=1.0, scalar2=None, op0=ALU.add), reads=["sq"], writes=["sq"])
            P.op("dve", lambda e: e.reciprocal(out=sq[:], in_=sq[:]), reads=["sq"], writes=["sq"])
            P.op("pool", lambda e, b=b: e.tensor_tensor(out=sq[:, 0:384], in0=sq[:, 0:384], in1=GT[b][:, 0:384], op=ALU.mult), reads=["sq"] + kGTs, writes=["sq"])
            P.op("dve", lambda e, b=b: e.tensor_tensor(out=mix[b][:, 0:768], in0=OFt[b][:], in1=sq[:], op=ALU.mult), reads=[kOF, "sq"], writes=[(kmix, "a")])
            P.op("act", lambda e, b=b: e.activation(out=mixb[b][:], in_=mix[b][:], func=AF.Copy), reads=[(kmix, "a"), (kmix, "hy")], writes=[kmixb])
            for kc in range(8):
                P.op("pe", lambda e, kc=kc, b=b: e.transpose(out=pT[:, kc * 128:(kc + 1) * 128], in_=mixb[b][:, kc * 128:(kc + 1) * 128], identity=idb[:]),
                     reads=[kmixb, "idb"], writes=["pT"])
            P.op("dve", lambda e, b=b: e.tensor_copy(out=mixT[b][:].rearrange("p a b -> p (a b)"), in_=pT[:]), reads=["pT"], writes=[kmixT])
            for hf in range(2):
                for kc in range(8):
                    P.op("pe", lambda e, kc=kc, hf=hf, b=b: e.matmul(pO[hf][:, :], lhsT=mixT[b][:, kc, :], rhs=wout[:, kc, hf * 512:(hf + 1) * 512], start=(kc == 0), stop=(kc == 7)),
                         reads=[kmixT] + woutk, writes=[("pO", hf)])
                P.op("dve", lambda e, hf=hf, b=b: e.scalar_tensor_tensor(out=ht[b][:, hf * 512:(hf + 1) * 512], in0=ht[b][:, hf * 512:(hf + 1) * 512], scalar=ALPHA,
                                                                         in1=pO[hf][:, :], op0=ALU.mult, op1=ALU.add), reads=[kht, ("pO", hf)], writes=[kht])
            layer_norm_tile(P, ht[b], kht, st, mv, "st", epst, ln1, "ln1")
            P.dma("sp", h1s[rows, :], ht[b][:], reads=[kht], writes=["h1s"])
            for kc in range(8):
                P.op("pe", lambda e, kc=kc, b=b: e.matmul(pX[kc // 4][:, (kc % 4) * 128:(kc % 4 + 1) * 128], lhsT=ht[b][:, kc * 128:(kc + 1) * 128], rhs=idf[:], start=True, stop=True),
                     reads=[kht, "idf"], writes=[("pX", kc // 4)])
            for hf in range(2):
                P.op("act", lambda e, hf=hf: e.activation(out=h1T32[:, hf * 4:(hf + 1) * 4, :].rearrange("p a b -> p (a b)"), in_=pX[hf][:, :], func=AF.Copy),
                     reads=[("pX", hf)], writes=[("h1T32", hf)])
                P.op("dve", lambda e, hf=hf, lrows=lrows: e.tensor_copy(out=h1T[:, hf * 4:(hf + 1) * 4, lrows], in_=pX[hf][:, :].rearrange("p (a b) -> p a b", b=128)),
                     reads=[("pX", hf), ("h1T32", hf)], writes=[("h1T", tl)])
            for kc in range(8):
                P.op("pe", lambda e, kc=kc: e.matmul(pL[:, 0:20], lhsT=h1T32[:, kc, :], rhs=wr[:, kc, :], start=(kc == 0), stop=(kc == 7)),
                     reads=[("h1T32", kc // 4), "wr"], writes=["pL"])
            P.op("dve", lambda e: e.tensor_tensor(out=lg[:], in0=pL[:, 0:20], in1=br[:], op=ALU.add), reads=["pL", "br"], writes=["lg"])
            R_ = lambda a, b_: rt[:, a:b_]
            dv = lambda fn, r=("lg", "rt"), w=("rt",): P.op("dve", fn, reads=list(r), writes=list(w))
            dv(lambda e: e.tensor_reduce(out=R_(0, 1), in_=lg[:, 0:4], axis=AX.X, op=ALU.max))
            dv(lambda e: e.tensor_scalar(out=R_(1, 5), in0=lg[:, 0:4], scalar1=R_(0, 1), scalar2=None, op0=ALU.is_equal))
            dv(lambda e: e.tensor_scalar(out=R_(5, 9), in0=lg[:, 0:4], scalar1=R_(0, 1), scalar2=None, op0=ALU.subtract))
            P.op("act", lambda e: e.activation(out=R_(5, 9), in_=R_(5, 9), func=AF.Exp), reads=["rt"], writes=["rt"])
            dv(lambda e: e.tensor_reduce(out=R_(9, 10), in_=R_(5, 9), axis=AX.X, op=ALU.add))
            dv(lambda e: e.reciprocal(out=R_(10, 11), in_=R_(9, 10)))
            dv(lambda e: e.tensor_tensor(out=rt[:, 40:56].rearrange("p (g e) -> p g e", e=4), in0=lg[:, 4:20].rearrange("p (g e) -> p g e", e=4),
                                         in1=R_(1, 5).unsqueeze(2).to_broadcast([128, 4, 4]), op=ALU.mult))
            dv(lambda e: e.tensor_reduce(out=R_(11, 15), in_=rt[:, 40:56].rearrange("p (g e) -> p e g", e=4), axis=AX.X, op=ALU.add))
            dv(lambda e: e.tensor_reduce(out=R_(15, 16), in_=R_(11, 15), axis=AX.X, op=ALU.max))
            dv(lambda e: e.tensor_scalar(out=R_(16, 20), in0=R_(11, 15), scalar1=R_(15, 16), scalar2=None, op0=ALU.is_equal))
            dv(lambda e: e.scalar_tensor_tensor(out=R_(20, 24), in0=R_(16, 20), scalar=-1e30, in1=R_(11, 15), op0=ALU.mult, op1=ALU.add))
            dv(lambda e: e.tensor_reduce(out=R_(24, 25), in_=R_(20, 24), axis=AX.X, op=ALU.max))
            dv(lambda e: e.tensor_scalar(out=R_(28, 32), in0=R_(20, 24), scalar1=R_(24, 25), scalar2=None, op0=ALU.is_equal))
            dv(lambda e: e.tensor_tensor(out=R_(32, 33), in0=R_(24, 25), in1=R_(15, 16), op=ALU.subtract))
            P.op("act", lambda e: e.activation(out=R_(32, 33), in_=R_(32, 33), func=AF.Exp), reads=["rt"], writes=["rt"])
            dv(lambda e: e.tensor_scalar(out=R_(33, 34), in0=R_(32, 33), scalar1=1.0, scalar2=None, op0=ALU.add))
            dv(lambda e: e.reciprocal(out=R_(34, 35), in_=R_(33, 34)))
            dv(lambda e: e.tensor_tensor(out=R_(35, 36), in0=R_(32, 33), in1=R_(34, 35), op=ALU.mult))
            dv(lambda e: e.tensor_scalar(out=R_(36, 40), in0=R_(16, 20), scalar1=R_(34, 35), scalar2=None, op0=ALU.mult))
            dv(lambda e: e.scalar_tensor_tensor(out=R_(36, 40), in0=R_(28, 32), scalar=R_(35, 36), in1=R_(36, 40), op0=ALU.mult, op1=ALU.add))
            dv(lambda e: e.tensor_scalar(out=R_(36, 40), in0=R_(36, 40), scalar1=R_(10, 11), scalar2=None, op0=ALU.mult))
            dv(lambda e: e.tensor_tensor(out=rt[:, 40:56].rearrange("p (g e) -> p g e", e=4), in0=R_(1, 5).unsqueeze(2).to_broadcast([128, 4, 4]),
                                         in1=R_(36, 40).unsqueeze(1).to_broadcast([128, 4, 4]), op=ALU.mult))
            P.op("dve", lambda e, tl=tl: e.tensor_copy(out=wfull[:, tl, :], in_=rt[:, 40:56]), reads=["rt"], writes=[("wfull", tl)])
        P.pop_scope()
        P.push_scope()
        ln2 = sb("ln2", [128, 2 * D])
        P.dma("sp", ln2[:], d["ln2"], writes=["ln2"])
        yacc = sb("yacc", [128, CT, D])
        wg = [sb("wg%d" % i, [128, 8, DE], BF16) for i in range(2)]
        wu = [sb("wu%d" % i, [128, 8, DE], BF16) for i in range(2)]
        wd = [sb("wd%d" % i, [128, 4, D], BF16) for i in range(2)]
        hid = [sb("hid%d" % i, [128, 4, 512], BF16) for i in range(2)]
        sg = [sb("sg%d" % i, [128, 512]) for i in range(2)]
        st2 = sb("st2", [128, 2, 6]); mv2 = sb("mv2", [128, 4])
        xo = [sb("xo%d" % i, [128, D]) for i in range(2)]
        pG = [ps("pG%d" % i, [128, 512]) for i in range(2)]
        pU = [ps("pU%d" % i, [128, 512]) for i in range(2)]
        pD = [ps("pD%d" % i, [128, 512]) for i in range(2)]
        P.op("pool", lambda e: e.memset(yacc[:].rearrange("p a b -> p (a b)"), 0.0), writes=[("yacc", i) for i in range(CT)])
        for ex in range(NEXP):
            wb = ex % 2
            for kc in range(8):
                P.dma("pool", wg[wb][:, kc, :], d["wg"][ex, kc * 128:(kc + 1) * 128, :], writes=[("wg", wb, kc)])
                P.dma("pool", wu[wb][:, kc, :], d["wu"][ex, kc * 128:(kc + 1) * 128, :], writes=[("wu", wb, kc)])
            for fc in range(4):
                P.dma("pool", wd[wb][:, fc, :], d["wd"][ex, fc * 128:(fc + 1) * 128, :], writes=[("wd", wb, fc)])
            gw = min(512, CTOK)
            for g in range(CTOK // gw):
                tok0 = g * gw
                hb = g % 2
                h1k = [("h1T", tt) for tt in range(tok0 // 128, (tok0 + gw) // 128)]
                for fc in range(4):
                    pb = fc % 2
                    for kc in range(8):
                        P.op("pe", lambda e, kc=kc, fc=fc, pb=pb, wb=wb, tok0=tok0: e.matmul(pG[pb][:, 0:gw], lhsT=wg[wb][:, kc, fc * 128:(fc + 1) * 128], rhs=h1T[:, kc, tok0:tok0 + gw],
                                                                                         start=(kc == 0), stop=(kc == 7)), reads=[("wg", wb, kc)] + h1k, writes=[("pG", pb)])
                    for kc in range(8):
                        P.op("pe", lambda e, kc=kc, fc=fc, pb=pb, wb=wb, tok0=tok0: e.matmul(pU[pb][:, 0:gw], lhsT=wu[wb][:, kc, fc * 128:(fc + 1) * 128], rhs=h1T[:, kc, tok0:tok0 + gw],
                                                                                         start=(kc == 0), stop=(kc == 7)), reads=[("wu", wb, kc)] + h1k, writes=[("pU", pb)])
                    P.op("act", lambda e, pb=pb: e.activation(out=sg[pb][:, 0:gw], in_=pG[pb][:, 0:gw], func=AF.Silu), reads=[("pG", pb)], writes=[("sg", pb)])
                    P.op("dve", lambda e, pb=pb, fc=fc, hb=hb: e.tensor_tensor(out=hid[hb][:, fc, 0:gw], in0=sg[pb][:, 0:gw], in1=pU[pb][:, 0:gw], op=ALU.mult),
                         reads=[("sg", pb), ("pU", pb)], writes=[("hid", hb, fc)])
                for ts in range(gw // 128):
                    tl = tok0 // 128 + ts
                    for dh in range(2):
                        pb = dh
                        for fc in range(4):
                            P.op("pe", lambda e, fc=fc, ts=ts, dh=dh, pb=pb, hb=hb, wb=wb: e.matmul(pD[pb][:, :], lhsT=hid[hb][:, fc, ts * 128:(ts + 1) * 128], rhs=wd[wb][:, fc, dh * 512:(dh + 1) * 512],
                                                                                                   start=(fc == 0), stop=(fc == 3)), reads=[("hid", hb, fc), ("wd", wb, fc)], writes=[("pD", pb)])
                        P.op("dve", lambda e, tl=tl, dh=dh, pb=pb, ex=ex: e.scalar_tensor_tensor(out=yacc[:, tl, dh * 512:(dh + 1) * 512], in0=pD[pb][:, :], scalar=wfull[:, tl, ex:ex + 1],
                                                                                                in1=yacc[:, tl, dh * 512:(dh + 1) * 512], op0=ALU.mult, op1=ALU.add),
                             reads=[("pD", pb), ("wfull", tl), ("yacc", tl)], writes=[("yacc", tl)])
        for tl in range(CT):
            t = tb + tl
            b = tl % 2
            rows = slice(t * 128, (t + 1) * 128)
            P.dma("sp", xo[b][:], h1s[rows, :], reads=["h1s"], writes=[("xo", b)])
            P.op("dve", lambda e, tl=tl, b=b: e.scalar_tensor_tensor(out=xo[b][:], in0=xo[b][:], scalar=ALPHA, in1=yacc[:, tl, :], op0=ALU.mult, op1=ALU.add),
                 reads=[("xo", b), ("yacc", tl)], writes=[("xo", b)])
            layer_norm_tile(P, xo[b], ("xo", b), st2, mv2, "st2", epst, ln2, "ln2")
            P.dma("sp", OUT[rows, :], xo[b][:], reads=[("xo", b)], writes=["OUT"])
        P.pop_scope()
    P.pop_scope()


_prog = {}


def build_fused(ins0):
    P = Prog()
    nc = P.nc
    d = {}
    for k, v in ins0.items():
        d[k] = nc.dram_tensor(k, list(v.shape), F32, kind="ExternalInput").ap()
    out = nc.dram_tensor("out", [L, D], F32, kind="ExternalOutput").ap()
    I = lambda n, s, dt=F32: nc.dram_tensor(n, s, dt, kind="Internal").ap()
    HB0 = I("HB0", [L, D]); HB1 = I("HB1", [L, D]); FM = I("FM", [NFM, L]); TM = I("TM", [L, NTM])
    OF = I("OFs", [L, 768]); OB = I("OBs", [L, 768]); YH = I("YHs", [L, 256]); h1s = I("h1s", [L, D]); scr = I("scr", [512, 2 * L], BF16)
    for layer in range(2):
        dl = {k[3:]: v for k, v in d.items() if k.startswith("l%d_" % layer)}
        dl.update({k: v for k, v in d.items() if k.startswith("c_") or k.startswith("h_")})
        xin = d["x"] if layer == 0 else HB1
        stage_A2(P, layer == 0, xin, dl["win"], d["gb"], d["c_id"], HB0, FM, TM)
        scans_all(P, layer, dl, FM, TM, OF, OB)
        hyena_all(P, L, dl, FM, YH, scr)
        stage_C2(P, dl, HB0 if layer == 0 else HB1, TM, OF, OB, YH, h1s, HB1 if layer == 0 else out)
    return P.finish(["OUT"])


def host_inputs(inp, b):
    f32 = np.float32
    rep = lambda v: np.ascontiguousarray(np.tile(np.asarray(v, f32)[None, :], (128, 1)))
    m = {"x": np.ascontiguousarray(inp["x"][b]), "gb": rep(np.concatenate([inp["ln_in_g"], inp["ln_in_b"]]))}
    m.update(scan_consts())
    m.update(hyena_consts2(L))
    for l in range(2):
        p = "l%d_" % l
        w = inp["w_in"][l]
        m[p + "win"] = np.ascontiguousarray(np.concatenate([w[:, FM_COLS], w[:, TM_COLS]], 1))
        wa2, ba, lbl = inp["gla_wa2"][l], inp["gla_ba"][l], inp["hg_lb_logits"]
        m[p + "gwa"] = np.stack([np.ascontiguousarray(wa2[:, :, h * 32:(h + 1) * 32].transpose(1, 0, 2)) for h in range(6)])
        m[p + "gba"] = np.stack([np.ascontiguousarray(ba[:, h * 32:(h + 1) * 32].T) for h in range(6)])
        m[p + "hlb"] = np.stack([np.ascontiguousarray(lbl[:, :, h * 64:(h + 1) * 64].reshape(4, 64).T) for h in range(6)])
        cw, cb, hb_ = inp["hy_conv_w"][l], inp["hy_conv_b"][l], inp["hy_bias"][l]
        m[p + "cw"] = np.ascontiguousarray(np.stack([np.stack([cw[:, k * 256 + lt * 128:k * 256 + (lt + 1) * 128].T for k in range(3)], 1).reshape(128, 9) for lt in range(2)]))
        m[p + "cb"] = np.ascontiguousarray(np.stack([np.stack([cb[k * 256 + lt * 128:k * 256 + (lt + 1) * 128] for k in range(3)], 1) for lt in range(2)]))
        m[p + "bias"] = np.ascontiguousarray(np.stack([np.stack([hb_[o, lt * 128:(lt + 1) * 128] for o in range(2)], 1) for lt in range(2)]))
        m[p + "w1"] = inp["hy_w1"][l]
        m[p + "b1f"] = np.ascontiguousarray(np.stack([inp["hy_b1"][l], inp["hy_b2"][l][0], inp["hy_b2"][l][1], inp["hy_freq"][l]], 1))
        m[p + "w2"] = np.ascontiguousarray(inp["hy_w2"][l].transpose(1, 0, 2))
        m[p + "w3"] = np.ascontiguousarray(inp["hy_w3"][l].reshape(64, 2, 2, 256).transpose(0, 2, 1, 3))
        m[p + "gn"] = rep(np.concatenate([inp["gla_norm_g"][l], inp["hg_norm_g"][l]]))
        m[p + "wout"] = inp["w_out"][l]
        m[p + "ln1"] = rep(np.concatenate([inp["ln1_g"][l], inp["ln1_b"][l]]))
        m[p + "ln2"] = rep(np.concatenate([inp["ln2_g"][l], inp["ln2_b"][l]]))
        m[p + "wr"] = np.ascontiguousarray(np.concatenate([inp["moe_wr_g"][l], inp["moe_wr_e"][l]], 1))
        m[p + "br"] = rep(np.concatenate([inp["moe_br_g"][l], inp["moe_br_e"][l]]))
        m[p + "wg"], m[p + "wu"], m[p + "wd"] = inp["moe_w_gate"][l], inp["moe_w_up"][l], inp["moe_w_down"][l]
    return {k: np.ascontiguousarray(np.asarray(v, f32)) for k, v in m.items()}


def kernel(**inp):
    inp = {k: np.asarray(v) for k, v in inp.items()}
    in_maps = [host_inputs(inp, c % 4) for c in range(4)]
    in_maps = in_maps + in_maps
    if "nc" not in _prog:
        _prog["nc"] = build_fused(in_maps[0])
    res = run_bass_kernel_spmd(_prog["nc"], in_maps, core_ids=list(range(NCORES))).results
    return np.stack([res[b]["out"] for b in range(4)], 0).astype(np.float32)
```

```python
import numpy as np
import concourse.bass as bass
import concourse.mybir as mybir
from concourse.bass_utils import run_bass_kernel_spmd
from contextlib import ExitStack

F32 = mybir.dt.float32
BF16 = mybir.dt.bfloat16
I32 = mybir.dt.int32
AF = mybir.ActivationFunctionType
ALU = mybir.AluOpType
AX = mybir.AxisListType


class Prog:
    def __init__(self, n_dma_sems=20):
        self.nc = bass.Bass("TRN2", target_bir_lowering=False)
        self.es = ExitStack()
        nc = self.nc
        self.eng = {"pe": nc.tensor, "dve": nc.vector, "act": nc.scalar,
                    "pool": nc.gpsimd, "sp": nc.sync}
        self.sem = {}
        for e in self.eng:
            self.sem[("E", e)] = self.es.enter_context(nc.semaphore("s_" + e))
        self.cnt = {("E", e): 0 for e in self.eng}
        self.dpool = {}
        self.dnext = {}
        for q in ("sp", "pool", "act"):
            self.dpool[q] = []
            for i in range(n_dma_sems if q != "act" else 6):
                k = ("D", q, i)
                self.sem[k] = self.es.enter_context(nc.semaphore("d_%s_%d" % (q, i)))
                self.cnt[k] = 0
                self.dpool[q].append(k)
            self.dnext[q] = 0
        self.cur = self.es
        self.uid = 0
        self.scopes = []
        self.seen = {}
        self.lastw = {}
        self.readers = {}
        self.nins = 0

    def sb(self, name, shape, dt=F32):
        self.uid += 1
        return self.cur.enter_context(self.nc.sbuf_tensor("sb%d_%s" % (self.uid, name), list(shape), dt))

    def ps(self, name, shape, dt=F32):
        self.uid += 1
        return self.cur.enter_context(self.nc.psum_tensor("ps%d_%s" % (self.uid, name), list(shape), dt))

    def push_scope(self):
        self.scopes.append(self.cur)
        self.cur = ExitStack()

    def pop_scope(self):
        self.barrier()
        self.cur.close()
        self.cur = self.scopes.pop()

    def barrier(self):
        deps = [(sk, v) for sk, v in self.cnt.items() if v > 0]
        for e in self.eng:
            self._wait(e, deps)

    def dram(self, name, shape, dt=F32, kind="Internal"):
        return self.nc.dram_tensor(name, list(shape), dt, kind=kind).ap()

    def _deps(self, reads, writes):
        deps = []
        for k in reads:
            if k in self.lastw:
                deps.append(self.lastw[k])
        for k in writes:
            if k in self.lastw:
                deps.append(self.lastw[k])
            deps.extend(self.readers.get(k, {}).items())
        return deps

    def _wait(self, e, deps):
        best = {}
        for sk, v in deps:
            if sk == ("E", "pe") and e == "pe":
                continue
            if self.seen.get((e, sk), 0) >= v:
                continue
            if best.get(sk, 0) < v:
                best[sk] = v
        for sk, v in best.items():
            self.eng[e].wait_ge(self.sem[sk], v)
            self.seen[(e, sk)] = v

    def _record(self, tk, reads, writes):
        sk, v = tk
        for k in reads:
            self.readers.setdefault(k, {})[sk] = v
        for k in writes:
            self.lastw[k] = tk
            self.readers[k] = {}

    def op(self, e, fn, reads=(), writes=()):
        self._wait(e, self._deps(reads, writes))
        ins = fn(self.eng[e])
        sk = ("E", e)
        self.cnt[sk] += 1
        ins.then_inc(self.sem[sk], 1)
        self._record((sk, self.cnt[sk]), reads, writes)
        self.nins += 1
        return ins

    def dma(self, q, out, in_, reads=(), writes=(), **kw):
        deps = self._deps(reads, writes)
        sk = self.dpool[q][self.dnext[q] % len(self.dpool[q])]
        self.dnext[q] += 1
        if self.cnt[sk] > 0:
            deps.append((sk, self.cnt[sk]))
        self._wait(q, deps)
        ins = self.eng[q].dma_start(out=out, in_=in_, **kw)
        self.cnt[sk] += 16
        ins.then_inc(self.sem[sk], 16)
        self._record((sk, self.cnt[sk]), reads, writes)
        self.nins += 1
        return ins

    def finish(self, out_keys):
        deps = []
        for k in out_keys:
            if k in self.lastw:
                deps.append(self.lastw[k])
        for q in self.dpool:
            for sk in self.dpool[q]:
                if self.cnt[sk] > 0:
                    deps.append((sk, self.cnt[sk]))
        self._wait("sp", deps)
        self.es.close()
        return self.nc


def _coll(self, kind, groups, out, in_, reads=(), writes=()):
    q = "pool"
    deps = self._deps(reads, writes)
    sk = self.dpool[q][self.dnext[q] % len(self.dpool[q])]
    self.dnext[q] += 1
    if self.cnt[sk] > 0:
        deps.append((sk, self.cnt[sk]))
    self._wait(q, deps)
    ins = self.nc.gpsimd.collective_compute(kind, ALU.bypass, replica_groups=groups, ins=[in_], outs=[out])
    self.cnt[sk] += 16
    ins.then_inc(self.sem[sk], 16)
    self._record((sk, self.cnt[sk]), reads, writes)
    self.nins += 1
    return ins


Prog.coll = _coll

import numpy as np

import os
STOPAT = int(os.environ.get('STOPAT', '99'))
CH = 64
HF = (lambda h: 0) if os.environ.get("HF0") else (lambda h: h)
SEG = 2048
NCS = SEG // CH
NPS = SEG // 128


def scan_consts():
    rm = np.ones((128, SEG), np.float32)
    rm[:, ::CH] = 0.0
    j = np.arange(64)[:, None]
    i = np.arange(64)[None, :]
    mf = (i >= j).astype(np.float32)
    mb = (i <= j).astype(np.float32)
    mf = np.tile(np.concatenate([mf, mf], 0), (1, 8))
    mb = np.tile(np.concatenate([mb, mb], 0), (1, 8))
    return {"c_rm": rm, "c_mf": mf, "c_mb": mb, "c_id": np.eye(128, dtype=np.float32)}


class ScanCtx:
    def __init__(self, P, L, dd=None):
        self.P = P
        self.L = L
        nc = P.nc
        c = {}
        for nm, shp in (("c_rm", [128, SEG]), ("c_mf", [128, 512]), ("c_mb", [128, 512]), ("c_id", [128, 128])):
            c[nm] = dd[nm] if dd is not None else nc.dram_tensor(nm, shp, F32, kind="ExternalInput").ap()
        self.rm = P.sb("rm", [128, SEG], F32)
        self.mf = P.sb("mf", [128, 512], F32)
        self.mb = P.sb("mb", [128, 512], F32)
        idf = P.sb("idf", [128, 128], F32)
        self.idb = P.sb("idb", [128, 128], BF16)
        P.dma("sp", self.rm[:], c["c_rm"], writes=["rm"])
        P.dma("sp", self.mf[:], c["c_mf"], writes=["mf"])
        P.dma("sp", self.mb[:], c["c_mb"], writes=["mb"])
        P.dma("sp", idf[:], c["c_id"], writes=["idf"])
        P.op("dve", lambda e: e.tensor_copy(out=self.idb[:], in_=idf[:]), reads=["idf"], writes=["idb"])
        f = lambda n, w=SEG, dt=F32, p=64: P.sb(n, [p, w], dt)
        self.q = f("s_q"); self.k = f("s_k"); self.x = f("s_x"); self.g = f("s_g")
        self.Pc = f("s_P"); self.Gh = f("s_Gh"); self.t1 = f("s_t1"); self.t2 = f("s_t2")
        self.qt = f("s_qt", SEG, BF16); self.kt = f("s_kt", SEG, BF16)
        self.ga = P.sb("s_ga", [16, SEG], F32)
        self.v32 = P.sb("s_v32", [64, NCS, 64], F32)
        self.vb = P.sb("s_vb", [64, NCS, 64], BF16)
        self.ktok = P.sb("s_ktok", [64, NCS, 64], BF16)
        self.AT = P.sb("s_AT", [64, 64, NCS], F32)
        self.D0 = P.sb("s_D0", [64, 64, NCS], F32)
        self.Sall = P.sb("s_Sall", [64, 64, NCS], F32)
        self.Sp = P.sb("s_Sp", [64, NCS, 64], BF16)
        self.S = P.sb("s_S", [64, 64], F32)
        self.cc = P.sb("s_cc", [64, 3, NCS], F32)
        self.ct = P.sb("s_ct", [64, 2, NCS], F32)
        self.Pm = P.sb("s_Pm", [64, 8, 64], BF16)
        self.o = P.sb("s_o", [64, NCS, 64], F32)
        self.wa = P.sb("s_wa", [16, 2, 32], F32)
        self.ba = P.sb("s_ba", [64, 4], F32)
        self.lb = P.sb("s_lb", [64, 8], F32)
        P.op("dve", lambda e: e.memset(self.kt[:], 0.0), writes=["kt"])
        self.p_g = P.ps("p_g", [64, 512], F32)
        self.p_t = P.ps("p_t", [64, 1024], BF16)
        self.p_A = [P.ps("p_A%d" % i, [64, 512], F32) for i in range(2)]
        self.p_s = P.ps("p_s", [64, 512], F32)
        self.p_o = [P.ps("p_o%d" % i, [64, 512], F32) for i in range(2)]


def scan_dir(C, kind, layer, K, d, qT_d, kT_d, xg_d, v_d, o_d, prm, rkeys=()):
    P = C.P
    L = C.L
    nseg = L // SEG
    rev = (d == 1)
    segs = list(range(nseg))
    if rev:
        segs = segs[::-1]
    S = C.S
    P.op("dve", lambda e: e.memset(S[0:K, :], 0.0), writes=["S"])
    gscale = (1.0 / 16.0) if kind == "gla" else 1.0
    lbzero = (kind == "gla") or (layer == 0)
    for sg in segs:
        t0 = sg * SEG
        sl = slice(t0, t0 + SEG)
        rk = list(rkeys)
        P.dma("sp", C.q[0:K, :], qT_d[:, sl], reads=rk, writes=["q"])
        P.dma("sp", C.v32[:], v_d[sl, :].rearrange("(n p) v -> p n v", p=64), reads=rk, writes=["v32"])
        P.op("pool", lambda e: e.tensor_copy(out=C.vb[:], in_=C.v32[:]), reads=["v32"], writes=["vb"])
        if kind == "gla":
            P.dma("sp", C.k[0:K, :], kT_d[:, sl], reads=rk, writes=["k"])
            P.dma("sp", C.ga[:], xg_d[:, sl], reads=rk, writes=["ga"])
            for c in range(SEG // 512):
                P.op("pe", lambda e, c=c: e.matmul(C.p_g[0:K, :], lhsT=prm["wa"][:, d, :], rhs=C.ga[:, c * 512:(c + 1) * 512], start=True, stop=True),
                     reads=["ga", "wa"], writes=["p_g"])
                P.op("act", lambda e, c=c: e.activation(out=C.t1[0:K, c * 512:(c + 1) * 512], in_=C.p_g[0:K, :], func=AF.Exp, scale=-1.0, bias=prm["nba"][0:K, d:d + 1]),
                     reads=["p_g", "nba"], writes=[("t1", c)])
            t1k = [("t1", c) for c in range(SEG // 512)]
            P.op("act", lambda e: e.activation(out=C.g[0:K, :], in_=C.t1[0:K, :], func=AF.Ln, bias=1.0, scale=1.0), reads=t1k, writes=["g"])
            P.op("dve", lambda e: e.tensor_scalar(out=C.g[0:K, :], in0=C.g[0:K, :], scalar1=-gscale, scalar2=None, op0=ALU.mult), reads=["g"], writes=["g"])
            P.op("pool", lambda e: e.tensor_scalar(out=C.q[0:K, :], in0=C.q[0:K, :], scalar1=float(K) ** -0.5, scalar2=None, op0=ALU.mult), reads=["q"], writes=["q"])
        else:
            P.dma("sp", C.x[0:K, :], xg_d[:, sl], reads=rk, writes=["x"])
            P.op("act", lambda e: e.activation(out=C.t1[0:K, :], in_=C.q[0:K, :], func=AF.Exp, scale=-1.0), reads=["q"], writes=["t1"])
            P.op("dve", lambda e: e.tensor_scalar(out=C.t1[0:K, :], in0=C.t1[0:K, :], scalar1=1.0, scalar2=None, op0=ALU.add), reads=["t1"], writes=["t1"])
            P.op("dve", lambda e: e.reciprocal(out=C.t1[0:K, :], in_=C.t1[0:K, :]), reads=["t1"], writes=["t1"])
            P.op("dve", lambda e: e.tensor_tensor(out=C.q[0:K, :], in0=C.q[0:K, :], in1=C.t1[0:K, :], op=ALU.mult), reads=["q", "t1"], writes=["q"])
            P.op("act", lambda e: e.activation(out=C.t1[0:K, :], in_=C.x[0:K, :], func=AF.Exp, scale=-1.0), reads=["x"], writes=["t1"])
            P.op("dve", lambda e: e.tensor_scalar(out=C.t2[0:K, :], in0=C.t1[0:K, :], scalar1=1.0, scalar2=None, op0=ALU.add), reads=["t1"], writes=["t2"])
            P.op("dve", lambda e: e.reciprocal(out=C.t2[0:K, :], in_=C.t2[0:K, :]), reads=["t2"], writes=["t2"])
            P.op("dve", lambda e: e.tensor_tensor(out=C.k[0:K, :], in0=C.t1[0:K, :], in1=C.t2[0:K, :], op=ALU.mult), reads=["t1", "t2"], writes=["k"])
            if lbzero:
                P.op("act", lambda e: e.activation(out=C.g[0:K, :], in_=C.t2[0:K, :], func=AF.Ln), reads=["t2"], writes=["g"])
            else:
                P.op("dve", lambda e: e.tensor_scalar(out=C.k[0:K, :], in0=C.k[0:K, :], scalar1=prm["oml"][0:K, d:d + 1], scalar2=None, op0=ALU.mult),
                     reads=["k", "oml"], writes=["k"])
                P.op("dve", lambda e: e.tensor_scalar(out=C.t2[0:K, :], in0=C.t2[0:K, :], scalar1=prm["oml"][0:K, d:d + 1], scalar2=prm["lb"][0:K, d:d + 1],
                                                      op0=ALU.mult, op1=ALU.add), reads=["t2", "oml"], writes=["t2"])
                P.op("act", lambda e: e.activation(out=C.g[0:K, :], in_=C.t2[0:K, :], func=AF.Ln), reads=["t2"], writes=["g"])
        if STOPAT <= 1:
            continue
        P.op("dve", lambda e: e.tensor_tensor_scan(out=C.Pc[0:K, :], data0=C.rm[0:K, :], data1=C.g[0:K, :], initial=0.0, op0=ALU.mult, op1=ALU.add),
             reads=["rm", "g"], writes=["Pc"])
        Pv = C.Pc[0:K, :].rearrange("p (n c) -> p n c", c=CH)
        Ghv = C.Gh[0:K, :].rearrange("p (n c) -> p n c", c=CH)
        gv = C.g[0:K, :].rearrange("p (n c) -> p n c", c=CH)
        cc = C.cc
        ct = C.ct
        if not rev:
            mid = 31
            P.op("dve", lambda e: e.tensor_tensor(out=Ghv, in0=Pv, in1=Pv[:, :, mid:mid + 1].to_broadcast([K, NCS, CH]), op=ALU.subtract),
                 reads=["Pc"], writes=["Gh"])
            P.op("act", lambda e: e.activation(out=cc[0:K, 0, :], in_=Pv[:, :, mid], func=AF.Exp), reads=["Pc"], writes=["cc0"])
            P.op("act", lambda e: e.activation(out=cc[0:K, 1, :], in_=Pv[:, :, CH - 1], func=AF.Exp), reads=["Pc"], writes=["cc1"])
            P.op("act", lambda e: e.activation(out=cc[0:K, 2, :], in_=Ghv[:, :, CH - 1], func=AF.Exp), reads=["Gh"], writes=["cc2"])
        else:
            mid = 32
            Ev = C.t1[0:K, :].rearrange("p (n c) -> p n c", c=CH)
            P.op("dve", lambda e: e.tensor_tensor(out=C.t1[0:K, :], in0=C.Pc[0:K, :], in1=C.g[0:K, :], op=ALU.subtract), reads=["Pc", "g"], writes=["t1"])
            P.op("dve", lambda e: e.tensor_tensor(out=Ghv, in0=Ev[:, :, mid:mid + 1].to_broadcast([K, NCS, CH]), in1=Ev, op=ALU.subtract),
                 reads=["t1"], writes=["Gh"])
            P.op("dve", lambda e: e.tensor_tensor(out=ct[0:K, 0, :], in0=Pv[:, :, CH - 1], in1=Ev[:, :, mid], op=ALU.subtract), reads=["Pc", "t1"], writes=["ct"])
            P.op("act", lambda e: e.activation(out=cc[0:K, 0, :], in_=ct[0:K, 0, :], func=AF.Exp), reads=["ct"], writes=["cc0"])
            P.op("act", lambda e: e.activation(out=cc[0:K, 1, :], in_=Pv[:, :, CH - 1], func=AF.Exp), reads=["Pc"], writes=["cc1"])
            P.op("act", lambda e: e.activation(out=cc[0:K, 2, :], in_=Ev[:, :, mid], func=AF.Exp), reads=["t1"], writes=["cc2"])
        if STOPAT <= 2:
            continue
        P.op("act", lambda e: e.activation(out=C.t2[0:K, :], in_=C.Gh[0:K, :], func=AF.Exp), reads=["Gh"], writes=["t2"])
        P.op("dve", lambda e: e.tensor_tensor(out=C.qt[0:K, :], in0=C.q[0:K, :], in1=C.t2[0:K, :], op=ALU.mult), reads=["q", "t2"], writes=["qt"])
        P.op("act", lambda e: e.activation(out=C.t1[0:K, :], in_=C.Gh[0:K, :], func=AF.Exp, scale=-1.0), reads=["Gh"], writes=["t1"])
        P.op("pool", lambda e: e.tensor_tensor(out=C.kt[0:K, :], in0=C.k[0:K, :], in1=C.t1[0:K, :], op=ALU.mult), reads=["k", "t1"], writes=["kt"])
        if STOPAT <= 3:
            continue
        for half in range(2):
            for cl in range(16):
                n = half * 16 + cl
                P.op("pe", lambda e, cl=cl, n=n: e.transpose(out=C.p_t[:, cl * 64:(cl + 1) * 64], in_=C.kt[0:64, n * 64:(n + 1) * 64], identity=C.idb[0:64, 0:64]),
                     reads=["kt", "idb"], writes=["p_t"])
            P.op("act", lambda e, half=half: e.activation(out=C.ktok[:, half * 16:(half + 1) * 16, :].rearrange("p n k -> p (n k)"), in_=C.p_t[:], func=AF.Copy),
                 reads=["p_t"], writes=[("ktok", half)])
        if STOPAT <= 4:
            continue
        for grp in range(NCS // 8):
            pa = C.p_A[grp % 2]
            pak = ("p_A", grp % 2)
            for ci in range(8):
                n = grp * 8 + ci
                P.op("pe", lambda e, ci=ci, n=n: e.matmul(pa[0:K, ci * 64:(ci + 1) * 64], lhsT=C.ktok[:, n, 0:K], rhs=C.vb[:, n, :], start=True, stop=True),
                     reads=[("ktok", n // 16), "vb"], writes=[pak])
            P.op("dve", lambda e, grp=grp: e.tensor_tensor(out=C.AT[0:K, :, grp * 8:(grp + 1) * 8].rearrange("p v n -> p n v"), in0=pa[0:K, :].rearrange("p (n v) -> p n v", v=64),
                                                           in1=cc[0:K, 2, grp * 8:(grp + 1) * 8].unsqueeze(2).to_broadcast([K, 8, 64]), op=ALU.mult),
                 reads=[pak, "cc2"], writes=[("A", grp)])
        if STOPAT <= 5:
            continue
        nf = NCS - 1 if rev else 0
        nl = 0 if rev else NCS - 1
        Ak = [("A", g_) for g_ in range(NCS // 8)]
        P.op("pool", lambda e: e.tensor_copy(out=C.D0[0:K, :, :], in_=cc[0:K, 1, :].unsqueeze(1).to_broadcast([K, 64, NCS])), reads=["cc1"], writes=["D0"])
        P.op("pool", lambda e: e.memset(C.D0[0:K, :, nf:nf + 1], 0.0), reads=["D0"], writes=["D0"])
        P.op("dve", lambda e: e.scalar_tensor_tensor(out=C.AT[0:K, :, nf], in0=S[0:K, :], scalar=cc[0:K, 1, nf:nf + 1], in1=C.AT[0:K, :, nf], op0=ALU.mult, op1=ALU.add),
             reads=["S", "cc1"] + Ak, writes=Ak)
        P.op("act", lambda e: e.activation(out=C.Sp[0:K, nf, :], in_=S[0:K, :], func=AF.Copy, scale=cc[0:K, 0, nf:nf + 1]), reads=["S", "cc0"], writes=[("Sp", "f")])
        AF_ = C.AT[0:K, :, :].rearrange("p v n -> p (v n)")
        DF_ = C.D0[0:K, :, :].rearrange("p v n -> p (v n)")
        SF_ = C.Sall[0:K, :, :].rearrange("p v n -> p (v n)")
        if rev:
            AF_, DF_, SF_ = AF_[:, ::-1], DF_[:, ::-1], SF_[:, ::-1]
        P.op("dve", lambda e: e.tensor_tensor_scan(out=SF_, data0=DF_, data1=AF_, initial=0.0, op0=ALU.mult, op1=ALU.add), reads=Ak + ["D0"], writes=["Sall"])
        if not rev:
            P.op("pool", lambda e: e.tensor_tensor(out=C.Sp[0:K, 1:NCS, :], in0=C.Sall[0:K, :, 0:NCS - 1].rearrange("p v n -> p n v"),
                                                   in1=cc[0:K, 0, 1:NCS].unsqueeze(2).to_broadcast([K, NCS - 1, 64]), op=ALU.mult), reads=["Sall", "cc0"], writes=[("Sp", "r")])
        else:
            P.op("pool", lambda e: e.tensor_tensor(out=C.Sp[0:K, 0:NCS - 1, :], in0=C.Sall[0:K, :, 1:NCS].rearrange("p v n -> p n v"),
                                                   in1=cc[0:K, 0, 0:NCS - 1].unsqueeze(2).to_broadcast([K, NCS - 1, 64]), op=ALU.mult), reads=["Sall", "cc0"], writes=[("Sp", "r")])
        P.op("dve", lambda e: e.tensor_copy(out=S[0:K, :], in_=C.Sall[0:K, :, nl]), reads=["Sall", "S"], writes=["S"])
        if STOPAT <= 6:
            continue
        mask = C.mb if rev else C.mf
        for grp in range(NCS // 8):
            for cl in range(8):
                n = grp * 8 + cl
                P.op("pe", lambda e, cl=cl, n=n: e.matmul(C.p_s[:, cl * 64:(cl + 1) * 64], lhsT=C.kt[0:K, n * 64:(n + 1) * 64],
                                                          rhs=C.qt[0:K, n * 64:(n + 1) * 64], start=True, stop=True),
                     reads=["kt", "qt"], writes=["p_s"])
            P.op("dve", lambda e: e.tensor_tensor(out=C.Pm[:].rearrange("p n c -> p (n c)"), in0=C.p_s[:], in1=mask[0:64, :], op=ALU.mult),
                 reads=["p_s", "mf", "mb"], writes=["Pm"])
            po = C.p_o[grp % 2]
            pok = ("p_o", grp % 2)
            for cl in range(8):
                n = grp * 8 + cl
                P.op("pe", lambda e, cl=cl, n=n: e.matmul(po[:, cl * 64:(cl + 1) * 64], lhsT=C.Pm[:, cl, :], rhs=C.vb[:, n, :], start=True, stop=False),
                     reads=["Pm", "vb"], writes=[pok])
                P.op("pe", lambda e, cl=cl, n=n: e.matmul(po[:, cl * 64:(cl + 1) * 64], lhsT=C.qt[0:K, n * 64:(n + 1) * 64], rhs=C.Sp[0:K, n, :], start=False, stop=True),
                     reads=["qt", ("Sp", "f"), ("Sp", "r")], writes=[pok])
            P.op("act", lambda e, grp=grp: e.activation(out=C.o[:, grp * 8:(grp + 1) * 8, :].rearrange("p n v -> p (n v)"), in_=po[:], func=AF.Copy),
                 reads=[pok], writes=[("o", grp)])
        P.dma("sp", o_d[sl, :].rearrange("(n p) v -> p n v", p=64), C.o[:], reads=[("o", g_) for g_ in range(NCS // 8)], writes=["o_out"])

import math
import numpy as np

HY_W = 256
TWO_PI_LO = 6.283185


def hyena_consts(L, core):
    f32 = np.float32
    t = np.linspace(0.0, 1.0, L, dtype=f32)[:, None]
    w = (2.0 * math.pi * np.arange(L, dtype=f32)[:, None] / L).astype(f32)
    bands = np.linspace(1e-4, 15, 16, dtype=f32)[None, :]
    z = np.concatenate([t, np.cos(bands * w), -np.sin(bands * w)], axis=-1).astype(f32)
    max_decay = math.log(1e-2) / 0.3
    min_decay = math.log(1e-2) / 1.5
    deltas = np.linspace(min_decay, max_decay, HY_W, dtype=f32)
    dec = np.exp(-t * np.abs(deltas)).astype(f32)
    tidx = np.concatenate([np.arange(L - 1, -1, -1), np.arange(1, L), [0]])
    zR = np.ascontiguousarray(z[tidx].T)
    zR[:, -1] = 0.0
    ch = np.arange(core * 32, core * 32 + 32)
    d = dec[tidx][:, ch].T
    d[:, -1] = 0.0
    decR = np.ascontiguousarray(np.concatenate([d, d], 0))
    J = np.ascontiguousarray(np.eye(128, dtype=f32)[::-1])
    return {"h_zR": zR, "h_decR": decR.astype(f32), "h_J": J, "h_id": np.eye(128, dtype=f32),
            "h_ones": np.ones((64, 128), f32)}


def hyena_stage(P, L, d):
    NB = L // 128
    NT = 2 * L // 512
    HW = 2 * L - 128
    nc = P.nc
    sb = P.sb
    w1 = sb("h_w1", [33, 64]); b1f = sb("h_b1f", [64, 8]); w2 = sb("h_w2", [64, 2, 64]); w3 = sb("h_w3", [64, 2, 64])
    cw = sb("h_cw", [128, 9]); cb = sb("h_cb", [128, 3]); bias = sb("h_bias", [128, 2])
    Jm = sb("h_Jm", [128, 128]); idf = sb("h_idf", [128, 128]); idb = sb("h_idb", [128, 128], BF16); ones = sb("h_onesb", [64, 128])
    P.dma("sp", w1[:], d["w1"], writes=["h_w1"])
    P.dma("sp", b1f[:, 0:4], d["b1f"], writes=["h_b1f"])
    P.dma("sp", w2[:], d["w2"], writes=["h_w2"])
    P.dma("sp", w3[:], d["w3"], writes=["h_w3"])
    P.dma("sp", cw[:], d["cw"].rearrange("p a b -> p (a b)"), writes=["h_cw"])
    P.dma("sp", cb[:], d["cb"], writes=["h_cb"])
    P.dma("sp", bias[:], d["bias"], writes=["h_bias"])
    P.dma("sp", Jm[:], d["h_J"], writes=["h_J"])
    P.dma("sp", idf[:], d["h_id"], writes=["h_idf"])
    P.dma("sp", ones[:], d["h_ones"], writes=["h_ones"])
    P.op("dve", lambda e: e.tensor_copy(out=idb[:], in_=idf[:]), reads=["h_idf"], writes=["h_idb"])
    P.op("dve", lambda e: e.tensor_scalar(out=b1f[:, 4:5], in0=b1f[:, 3:4], scalar1=1.0 / (2 * math.pi), scalar2=None, op0=ALU.mult), reads=["h_b1f"], writes=["h_A"])
    P.op("dve", lambda e: e.tensor_scalar(out=b1f[:, 5:8], in0=b1f[:, 0:3], scalar1=b1f[:, 4:5], scalar2=None, op0=ALU.mult), reads=["h_b1f", "h_A"], writes=["h_B"])
    pA = P.ps("h_pA", [64, 512])
    zt = [sb("h_zt%d" % i, [33, 512]) for i in range(2)]
    dt_ = [sb("h_dt%d" % i, [64, 512]) for i in range(2)]
    u = sb("h_u", [64, 512]); ui = sb("h_ui", [64, 512], I32); hh = sb("h_hh", [64, 512])
    Rt = sb("h_Rt", [64, 512]); Rb = [sb("h_Rb%d" % i, [64, 512], BF16) for i in range(2)]
    asum = sb("h_asum", [64, NT]); rn = sb("h_rn", [64, 2]); dg = sb("h_dg", [64, 64]); rnb = sb("h_rnb", [128, 64])
    scr = d["scr"]
    for t in range(NT):
        b = t % 2
        P.dma("sp", zt[b][:], d["h_zR"][:, t * 512:(t + 1) * 512], writes=[("zt", b)])
        P.dma("sp", dt_[b][:], d["h_decR"][:, t * 512:(t + 1) * 512], writes=[("dt", b)])
        rhs = zt[b]
        rk = ("zt", b)
        for l in range(3):
            lhsT = w1[:, :] if l == 0 else w2[:, l - 1, :]
            kk = 33 if l == 0 else 64
            P.op("pe", lambda e, lhsT=lhsT, rhs=rhs, kk=kk: e.matmul(pA[:, :], lhsT=lhsT, rhs=rhs[0:kk, :], start=True, stop=True),
                 reads=[rk, "h_w1", "h_w2"], writes=["h_pA"])
            P.op("dve", lambda e, l=l: e.tensor_scalar(out=u[:], in0=pA[:], scalar1=b1f[:, 4:5], scalar2=b1f[:, 5 + l:6 + l], op0=ALU.mult, op1=ALU.add),
                 reads=["h_pA", "h_A", "h_B"], writes=["h_u"])
            P.op("dve", lambda e: e.tensor_copy(out=ui[:], in_=u[:]), reads=["h_u"], writes=["h_ui"])
            P.op("dve", lambda e: e.tensor_tensor(out=u[:], in0=u[:], in1=ui[:], op=ALU.subtract), reads=["h_u", "h_ui"], writes=["h_u"])
            P.op("act", lambda e: e.activation(out=hh[:], in_=u[:], func=AF.Sin, scale=TWO_PI_LO), reads=["h_u"], writes=["h_hh"])
            rhs = hh
            rk = "h_hh"
        dr = 0 if t < NT // 2 else 1
        P.op("pe", lambda e, dr=dr: e.matmul(pA[:, :], lhsT=w3[:, dr, :], rhs=hh[:, :], start=True, stop=True), reads=["h_hh", "h_w3"], writes=["h_pA"])
        P.op("dve", lambda e: e.tensor_tensor(out=Rt[:], in0=pA[:], in1=dt_[b][:], op=ALU.mult), reads=["h_pA", ("dt", b)], writes=["h_Rt"])
        P.op("dve", lambda e, t=t: e.tensor_reduce(out=asum[:, t:t + 1], in_=Rt[:], axis=AX.X, op=ALU.add, apply_absolute_value=True), reads=["h_Rt"], writes=[("asum", t)])
        P.op("act", lambda e: e.activation(out=Rb[b][:], in_=Rt[:], func=AF.Copy), reads=["h_Rt"], writes=[("Rb", b)])
        P.dma("sp", scr[:, t * 512:(t + 1) * 512], Rb[b][:], reads=[("Rb", b)], writes=["scr"])
    P.op("dve", lambda e: e.tensor_reduce(out=rn[:, 0:1], in_=asum[:], axis=AX.X, op=ALU.add), reads=[("asum", t) for t in range(NT)], writes=["h_rn"])
    P.op("dve", lambda e: e.tensor_scalar(out=rn[:, 0:1], in0=rn[:, 0:1], scalar1=1e-12, scalar2=None, op0=ALU.max), reads=["h_rn"], writes=["h_rn"])
    P.op("dve", lambda e: e.reciprocal(out=rn[:, 1:2], in_=rn[:, 0:1]), reads=["h_rn"], writes=["h_rn1"])
    P.op("dve", lambda e: e.tensor_scalar(out=dg[:], in0=idf[0:64, 0:64], scalar1=rn[:, 1:2], scalar2=None, op0=ALU.mult), reads=["h_idf", "h_rn1"], writes=["h_dg"])
    pB = P.ps("h_pB", [128, 512])
    P.op("pe", lambda e: e.matmul(pB[:, 0:64], lhsT=ones[:, :], rhs=dg[:, :], start=True, stop=True), reads=["h_ones", "h_dg"], writes=["h_pB"])
    P.op("dve", lambda e: e.tensor_copy(out=rnb[:], in_=pB[:, 0:64]), reads=["h_pB"], writes=["h_rnb"])
    za = sb("h_za", [128, L]); zc = sb("h_zc", [128, L])
    zb = [sb("h_zb%d" % i, [128, 1024], BF16) for i in range(2)]
    ZT = sb("h_ZT", [128, NB, 128], BF16)
    YT = sb("h_YT", [128, NB, 128])
    xr = YT[:].rearrange("p a b -> p (a b)")
    ytk = [("YT", c) for c in range(32)]
    H = [sb("h_H%d" % i, [128, HW], BF16) for i in range(2)]
    pT = P.ps("h_pT", [128, 1024], BF16)
    pC = [P.ps("h_pC%d" % i, [128, 512]) for i in range(2)]

    def short_conv(idx, dst, dk):
        P.dma("sp", xr, d["u3"][idx], writes=ytk)
        P.op("dve", lambda e: e.tensor_scalar(out=dst[:], in0=xr, scalar1=cw[:, idx * 3 + 1:idx * 3 + 2], scalar2=cb[:, idx:idx + 1], op0=ALU.mult, op1=ALU.add),
             reads=ytk + ["h_cw", "h_cb"], writes=[dk])
        P.op("dve", lambda e: e.scalar_tensor_tensor(out=dst[:, 1:L], in0=xr[:, 0:L - 1], scalar=cw[:, idx * 3:idx * 3 + 1], in1=dst[:, 1:L], op0=ALU.mult, op1=ALU.add),
             reads=ytk + ["h_cw", dk], writes=[dk])
        P.op("dve", lambda e: e.scalar_tensor_tensor(out=dst[:, 0:L - 1], in0=xr[:, 1:L], scalar=cw[:, idx * 3 + 2:idx * 3 + 3], in1=dst[:, 0:L - 1], op0=ALU.mult, op1=ALU.add),
             reads=ytk + ["h_cw", dk], writes=[dk])

    short_conv(0, za, "h_za")
    hq = 0
    for o in range(2):
        for g in range(NB // 8):
            zbb = zb[g % 2]
            zbk = ("h_zb", g % 2)
            P.op("act", lambda e, g=g, zbb=zbb: e.activation(out=zbb[:], in_=za[:, g * 1024:(g + 1) * 1024], func=AF.Copy), reads=["h_za"], writes=[zbk])
            for jj in range(8):
                j = g * 8 + jj
                P.op("pe", lambda e, jj=jj, zbb=zbb: e.transpose(out=pT[:, jj * 128:(jj + 1) * 128], in_=zbb[:, jj * 128:(jj + 1) * 128], identity=idb[:]),
                     reads=[zbk, "h_idb"], writes=["h_pT"])
            P.op("dve", lambda e, g=g: e.tensor_copy(out=ZT[:, g * 8:(g + 1) * 8, :].rearrange("p a b -> p (a b)"), in_=pT[:]), reads=["h_pT"], writes=[("ZT", g)])
        ztk = [("ZT", g) for g in range(NB // 8)]
        short_conv(1 + o, zc, "h_zc")
        for c in range(32):
            lane = o * 32 + c
            hb = hq % 2
            hq += 1
            q = "sp" if hb == 0 else "pool"
            P.dma(q, H[hb][:], bass.AP(scr.tensor, lane * 2 * L, [[1, 128], [1, HW]]), reads=["scr"], writes=[("H", hb)])
            pc = pC[hb]
            pck = ("pC", hb)
            ds = [0] + [x for k in range(1, NB) for x in (k, -k)]
            for n_, dd in enumerate(ds):
                j0, j1 = max(0, -dd), min(NB, NB - dd)
                x0 = L - 128 - 128 * dd
                P.op("pe", lambda e, dd=dd, j0=j0, j1=j1, x0=x0, n_=n_: e.matmul(
                    pc[:, (j0 + dd) * 4:(j1 + dd) * 4].rearrange("p (i b) -> p i b", b=4), lhsT=H[hb][:, x0:x0 + 128],
                    rhs=ZT[:, j0:j1, c * 4:(c + 1) * 4], start=(n_ == 0), stop=(n_ == len(ds) - 1)),
                     reads=[("H", hb)] + ztk, writes=[pck])
            eng = "act" if c % 2 == 0 else "dve"
            if eng == "act":
                P.op("act", lambda e, c=c, lane=lane: e.activation(out=YT[:, :, c * 4:(c + 1) * 4], in_=pc[:, 0:NB * 4].rearrange("p (i b) -> p i b", b=4), func=AF.Copy,
                                                                   scale=rnb[:, lane:lane + 1]), reads=[pck, "h_rnb"], writes=[("YT", c)])
            else:
                P.op("dve", lambda e, c=c, lane=lane: e.tensor_scalar(out=YT[:, :, c * 4:(c + 1) * 4], in0=pc[:, 0:NB * 4].rearrange("p (i b) -> p i b", b=4),
                                                                      scalar1=rnb[:, lane:lane + 1], scalar2=None, op0=ALU.mult), reads=[pck, "h_rnb"], writes=[("YT", c)])
        for g in range(NB // 4):
            for ii in range(4):
                i = g * 4 + ii
                P.op("pe", lambda e, ii=ii, i=i: e.matmul(pB[:, ii * 128:(ii + 1) * 128], lhsT=YT[:, i, :], rhs=Jm[:, :], start=True, stop=True),
                     reads=ytk + ["h_J"], writes=["h_pB"])
            sl = slice(g * 512, (g + 1) * 512)
            P.op("dve", lambda e, sl=sl, o=o: e.scalar_tensor_tensor(out=za[:, sl], in0=za[:, sl], scalar=bias[:, o:o + 1], in1=pB[:, :], op0=ALU.mult, op1=ALU.add),
                 reads=["h_za", "h_bias", "h_pB"], writes=["h_za"])
            P.op("pool", lambda e, sl=sl: e.tensor_tensor(out=za[:, sl], in0=za[:, sl], in1=zc[:, sl], op=ALU.mult), reads=["h_za", "h_zc"], writes=["h_za"])
    P.dma("sp", d["y"], za[:], reads=["h_za"], writes=["h_y"])

import math
import numpy as np

import os
STOPC = int(os.environ.get('STOPC', '99'))
D = 1024
ALPHA = 4 ** 0.25
NEXP = 16
DE = 512


def layer_norm_tile(P, x, xk, st, mv, stk, epst, gb, gbk):
    for c in range(2):
        P.op("dve", lambda e, c=c: e.bn_stats(out=st[:, c, :], in_=x[:, c * 512:(c + 1) * 512]), reads=[xk], writes=[(stk, c)])
    P.op("dve", lambda e: e.bn_aggr(out=mv[:, 0:2], in_=st[:].rearrange("p a b -> p (a b)")), reads=[(stk, 0), (stk, 1)], writes=[(stk, "mv")])
    P.op("act", lambda e: e.activation(out=mv[:, 2:3], in_=mv[:, 1:2], func=AF.Sqrt, bias=epst[:, 0:1], scale=1.0), reads=[(stk, "mv"), "eps"], writes=[(stk, "sd")])
    P.op("dve", lambda e: e.reciprocal(out=mv[:, 3:4], in_=mv[:, 2:3]), reads=[(stk, "sd")], writes=[(stk, "rs")])
    P.op("dve", lambda e: e.tensor_scalar(out=x[:], in0=x[:], scalar1=mv[:, 0:1], scalar2=mv[:, 3:4], op0=ALU.subtract, op1=ALU.mult),
         reads=[xk, (stk, "rs"), (stk, "mv")], writes=[xk])
    P.op("pool", lambda e: e.tensor_tensor(out=x[:], in0=x[:], in1=gb[:, 0:D], op=ALU.mult), reads=[xk, gbk], writes=[xk])
    P.op("pool", lambda e: e.tensor_tensor(out=x[:], in0=x[:], in1=gb[:, D:2 * D], op=ALU.add), reads=[xk, gbk], writes=[xk])


def stage_C(P, NT, d):
    nc = P.nc
    sb, ps = P.sb, P.ps
    T = NT * 128
    epst = sb("epst", [128, 1]); eps6 = sb("eps6", [128, 1])
    P.op("dve", lambda e: e.memset(epst[:], 1e-5), writes=["eps"])
    P.op("dve", lambda e: e.memset(eps6[:], 1e-6), writes=["eps6"])
    idf = sb("c_idf", [128, 128]); idb = sb("c_idb", [128, 128], BF16)
    P.dma("sp", idf[:], d["c_id"], writes=["idf"])
    P.op("dve", lambda e: e.tensor_copy(out=idb[:], in_=idf[:]), reads=["idf"], writes=["idb"])
    h1T = sb("h1T", [128, 8, T], BF16)
    wfull = sb("wfull", [128, NT, 16])
    P.push_scope()
    wout = sb("woutb", [128, 8, D], BF16)
    for kc in range(8):
        P.dma("pool", wout[:, kc, :], d["wout"][kc * 128:(kc + 1) * 128, :], writes=[("wout", kc)])
    woutk = [("wout", kc) for kc in range(8)]
    gn = sb("gn", [128, 768]); ln1 = sb("ln1", [128, 2 * D]); wr = sb("wr", [128, 8, 20]); br = sb("br", [128, 20])
    P.dma("sp", gn[:], d["gn"], writes=["gn"])
    P.dma("sp", ln1[:], d["ln1"], writes=["ln1"])
    P.dma("sp", wr[:], d["wr"].rearrange("(k p) n -> p k n", p=128), writes=["wr"])
    P.dma("sp", br[:], d["br"], writes=["br"])
    NBUF = 2
    OF = [sb("OF%d" % i, [128, 768]) for i in range(NBUF)]
    OB = [sb("OB%d" % i, [128, 768]) for i in range(NBUF)]
    GT = [sb("GT%d" % i, [128, 768]) for i in range(NBUF)]
    mix = [sb("mix%d" % i, [128, D]) for i in range(NBUF)]
    mixb = [sb("mixb%d" % i, [128, D], BF16) for i in range(NBUF)]
    mixT = [sb("mixT%d" % i, [128, 8, 128], BF16) for i in range(NBUF)]
    ht = [sb("ht%d" % i, [128, D]) for i in range(NBUF)]
    sq = sb("sq", [128, 768]); ss = sb("ss", [128, 12]); rs = sb("rs_", [128, 12])
    st = sb("st", [128, 2, 6]); mv = sb("mv", [128, 4])
    h1T32 = sb("h1T32", [128, 8, 128])
    lg = sb("lg", [128, 20]); rt = sb("rt", [128, 64])
    pT = ps("pT", [128, 1024], BF16)
    pO = [ps("pO%d" % i, [128, 512]) for i in range(2)]
    pX = [ps("pX%d" % i, [128, 512]) for i in range(2)]
    pL = ps("pL", [128, 512])
    for t in range(NT):
        b = t % NBUF
        rows = slice(t * 128, (t + 1) * 128)
        kOF, kOB, kGT, kmix, kmixb, kmixT, kht = ("OF", b), ("OB", b), ("GT", b), ("mix", b), ("mixb", b), ("mixT", b), ("ht", b)
        P.dma("sp", OF[b][:], d["OF"][rows, :], writes=[kOF])
        P.dma("sp", OB[b][:], d["OB"][rows, :], writes=[kOB])
        P.dma("sp", GT[b][:], d["GT"][rows, :], writes=[kGT])
        P.dma("sp", mix[b][:, 768:1024], d["YH"][rows, :], writes=[(kmix, "hy")])
        P.dma("sp", ht[b][:], d["h"][rows, :], writes=[kht])
        P.op("pool", lambda e: e.tensor_tensor(out=OF[b][:], in0=OF[b][:], in1=OB[b][:], op=ALU.add), reads=[kOF, kOB], writes=[kOF])
        P.op("dve", lambda e: e.tensor_tensor(out=sq[:], in0=OF[b][:], in1=OF[b][:], op=ALU.mult), reads=[kOF], writes=["sq"])
        P.op("dve", lambda e: e.tensor_reduce(out=ss[:], in_=sq[:].rearrange("p (h v) -> p h v", v=64), axis=AX.X, op=ALU.add), reads=["sq"], writes=["ss"])
        P.op("act", lambda e: e.activation(out=rs[:], in_=ss[:], func=AF.Sqrt, bias=eps6[:, 0:1], scale=1.0 / 64.0), reads=["ss", "eps6"], writes=["rs"])
        P.op("dve", lambda e: e.reciprocal(out=rs[:], in_=rs[:]), reads=["rs"], writes=["rs"])
        P.op("dve", lambda e: e.tensor_tensor(out=OF[b][:].rearrange("p (h v) -> p h v", v=64), in0=OF[b][:].rearrange("p (h v) -> p h v", v=64),
                                              in1=rs[:].unsqueeze(2).to_broadcast([128, 12, 64]), op=ALU.mult), reads=[kOF, "rs"], writes=[kOF])
        P.op("pool", lambda e: e.tensor_tensor(out=OF[b][:], in0=OF[b][:], in1=gn[:], op=ALU.mult), reads=[kOF, "gn"], writes=[kOF])
        P.op("act", lambda e: e.activation(out=sq[:], in_=GT[b][:], func=AF.Exp, scale=-1.0), reads=[kGT], writes=["sq"])
        P.op("dve", lambda e: e.tensor_scalar(out=sq[:], in0=sq[:], scalar1=1.0, scalar2=None, op0=ALU.add), reads=["sq"], writes=["sq"])
        P.op("dve", lambda e: e.reciprocal(out=sq[:], in_=sq[:]), reads=["sq"], writes=["sq"])
        P.op("pool", lambda e: e.tensor_tensor(out=sq[:, 0:384], in0=sq[:, 0:384], in1=GT[b][:, 0:384], op=ALU.mult), reads=["sq", kGT], writes=["sq"])
        P.op("dve", lambda e: e.tensor_tensor(out=mix[b][:, 0:768], in0=OF[b][:], in1=sq[:], op=ALU.mult), reads=[kOF, "sq"], writes=[(kmix, "a")])
        P.op("act", lambda e: e.activation(out=mixb[b][:], in_=mix[b][:], func=AF.Copy), reads=[(kmix, "a"), (kmix, "hy")], writes=[kmixb])
        if STOPC <= 1:
            continue
        for kc in range(8):
            P.op("pe", lambda e, kc=kc: e.transpose(out=pT[:, kc * 128:(kc + 1) * 128], in_=mixb[b][:, kc * 128:(kc + 1) * 128], identity=idb[:]),
                 reads=[kmixb, "idb"], writes=["pT"])
        P.op("dve", lambda e: e.tensor_copy(out=mixT[b][:].rearrange("p a b -> p (a b)"), in_=pT[:]), reads=["pT"], writes=[kmixT])
        for hf in range(2):
            for kc in range(8):
                P.op("pe", lambda e, kc=kc, hf=hf: e.matmul(pO[hf][:, :], lhsT=mixT[b][:, kc, :], rhs=wout[:, kc, hf * 512:(hf + 1) * 512], start=(kc == 0), stop=(kc == 7)),
                     reads=[kmixT] + woutk, writes=[("pO", hf)])
            P.op("dve", lambda e, hf=hf: e.scalar_tensor_tensor(out=ht[b][:, hf * 512:(hf + 1) * 512], in0=ht[b][:, hf * 512:(hf + 1) * 512], scalar=ALPHA,
                                                                in1=pO[hf][:, :], op0=ALU.mult, op1=ALU.add), reads=[kht, ("pO", hf)], writes=[kht])
        layer_norm_tile(P, ht[b], kht, st, mv, "st", epst, ln1, "ln1")
        P.dma("sp", d["h1s"][rows, :], ht[b][:], reads=[kht], writes=["h1s"])
        if STOPC <= 2:
            continue
        for kc in range(8):
            P.op("pe", lambda e, kc=kc: e.matmul(pX[kc // 4][:, (kc % 4) * 128:(kc % 4 + 1) * 128], lhsT=ht[b][:, kc * 128:(kc + 1) * 128], rhs=idf[:], start=True, stop=True),
                 reads=[kht, "idf"], writes=[("pX", kc // 4)])
        for hf in range(2):
            if os.environ.get("NOEVAC") == "1":
                continue
            if os.environ.get("NOEVAC") != "act":
                P.op("act", lambda e, hf=hf: e.activation(out=h1T32[:, hf * 4:(hf + 1) * 4, :].rearrange("p a b -> p (a b)"), in_=pX[hf][:, :], func=AF.Copy),
                     reads=[("pX", hf)], writes=[("h1T32", hf)])
            if os.environ.get("NOEVAC") == "dve":
                continue
            P.op("dve", lambda e, hf=hf: e.tensor_copy(out=h1T[:, hf * 4:(hf + 1) * 4, rows], in_=pX[hf][:, :].rearrange("p (a b) -> p a b", b=128)),
                 reads=[("pX", hf), ("h1T32", hf)], writes=[("h1T", t)])
        if STOPC <= 3:
            continue
        for kc in range(8):
            P.op("pe", lambda e, kc=kc: e.matmul(pL[:, 0:20], lhsT=h1T32[:, kc, :], rhs=wr[:, kc, :], start=(kc == 0), stop=(kc == 7)),
                 reads=[("h1T32", kc // 4), "wr"], writes=["pL"])
        P.op("dve", lambda e: e.tensor_tensor(out=lg[:], in0=pL[:, 0:20], in1=br[:], op=ALU.add), reads=["pL", "br"], writes=["lg"])
        if STOPC <= 4:
            continue
        R_ = lambda a, b_: rt[:, a:b_]
        dv = lambda fn, r=("lg", "rt"), w=("rt",): P.op("dve", fn, reads=list(r), writes=list(w))
        dv(lambda e: e.tensor_reduce(out=R_(0, 1), in_=lg[:, 0:4], axis=AX.X, op=ALU.max))
        dv(lambda e: e.tensor_scalar(out=R_(1, 5), in0=lg[:, 0:4], scalar1=R_(0, 1), scalar2=None, op0=ALU.is_equal))
        dv(lambda e: e.tensor_scalar(out=R_(5, 9), in0=lg[:, 0:4], scalar1=R_(0, 1), scalar2=None, op0=ALU.subtract))
        P.op("act", lambda e: e.activation(out=R_(5, 9), in_=R_(5, 9), func=AF.Exp), reads=["rt"], writes=["rt"])
        dv(lambda e: e.tensor_reduce(out=R_(9, 10), in_=R_(5, 9), axis=AX.X, op=ALU.add))
        dv(lambda e: e.reciprocal(out=R_(10, 11), in_=R_(9, 10)))
        dv(lambda e: e.tensor_tensor(out=rt[:, 40:56].rearrange("p (g e) -> p g e", e=4), in0=lg[:, 4:20].rearrange("p (g e) -> p g e", e=4),
                                     in1=R_(1, 5).unsqueeze(2).to_broadcast([128, 4, 4]), op=ALU.mult))
        dv(lambda e: e.tensor_reduce(out=R_(11, 15), in_=rt[:, 40:56].rearrange("p (g e) -> p e g", e=4), axis=AX.X, op=ALU.add))
        dv(lambda e: e.tensor_reduce(out=R_(15, 16), in_=R_(11, 15), axis=AX.X, op=ALU.max))
        dv(lambda e: e.tensor_scalar(out=R_(16, 20), in0=R_(11, 15), scalar1=R_(15, 16), scalar2=None, op0=ALU.is_equal))
        dv(lambda e: e.scalar_tensor_tensor(out=R_(20, 24), in0=R_(16, 20), scalar=-1e30, in1=R_(11, 15), op0=ALU.mult, op1=ALU.add))
        dv(lambda e: e.tensor_reduce(out=R_(24, 25), in_=R_(20, 24), axis=AX.X, op=ALU.max))
        dv(lambda e: e.tensor_scalar(out=R_(28, 32), in0=R_(20, 24), scalar1=R_(24, 25), scalar2=None, op0=ALU.is_equal))
        dv(lambda e: e.tensor_tensor(out=R_(32, 33), in0=R_(24, 25), in1=R_(15, 16), op=ALU.subtract))
        P.op("act", lambda e: e.activation(out=R_(32, 33), in_=R_(32, 33), func=AF.Exp), reads=["rt"], writes=["rt"])
        dv(lambda e: e.tensor_scalar(out=R_(33, 34), in0=R_(32, 33), scalar1=1.0, scalar2=None, op0=ALU.add))
        dv(lambda e: e.reciprocal(out=R_(34, 35), in_=R_(33, 34)))
        dv(lambda e: e.tensor_tensor(out=R_(35, 36), in0=R_(32, 33), in1=R_(34, 35), op=ALU.mult))
        dv(lambda e: e.tensor_scalar(out=R_(36, 40), in0=R_(16, 20), scalar1=R_(34, 35), scalar2=None, op0=ALU.mult))
        dv(lambda e: e.scalar_tensor_tensor(out=R_(36, 40), in0=R_(28, 32), scalar=R_(35, 36), in1=R_(36, 40), op0=ALU.mult, op1=ALU.add))
        dv(lambda e: e.tensor_scalar(out=R_(36, 40), in0=R_(36, 40), scalar1=R_(10, 11), scalar2=None, op0=ALU.mult))
        dv(lambda e: e.tensor_tensor(out=rt[:, 40:56].rearrange("p (g e) -> p g e", e=4), in0=R_(1, 5).unsqueeze(2).to_broadcast([128, 4, 4]),
                                     in1=R_(36, 40).unsqueeze(1).to_broadcast([128, 4, 4]), op=ALU.mult))
        P.op("dve", lambda e, t=t: e.tensor_copy(out=wfull[:, t, :], in_=rt[:, 40:56]), reads=["rt"], writes=[("wfull", t)])
    P.pop_scope()
    if STOPC <= 5:
        return
    P.push_scope()
    ln2 = sb("ln2", [128, 2 * D])
    P.dma("sp", ln2[:], d["ln2"], writes=["ln2"])
    HT = NT // 2 if NT >= 8 else NT
    nhalf = NT // HT
    yacc = sb("yacc", [128, HT, D])
    wg = [sb("wg%d" % i, [128, 8, DE], BF16) for i in range(2)]
    wu = [sb("wu%d" % i, [128, 8, DE], BF16) for i in range(2)]
    wd = [sb("wd%d" % i, [128, 4, D], BF16) for i in range(2)]
    hid = [sb("hid%d" % i, [128, 4, 512], BF16) for i in range(2)]
    sg = [sb("sg%d" % i, [128, 512]) for i in range(2)]
    st2 = sb("st2", [128, 2, 6]); mv2 = sb("mv2", [128, 4])
    xo = [sb("xo%d" % i, [128, D]) for i in range(2)]
    pG = [ps("pG%d" % i, [128, 512]) for i in range(2)]
    pU = [ps("pU%d" % i, [128, 512]) for i in range(2)]
    pD = [ps("pD%d" % i, [128, 512]) for i in range(2)]
    wq = 0
    for hf_ in range(nhalf):
        tbase = hf_ * HT
        P.op("pool", lambda e: e.memset(yacc[:].rearrange("p a b -> p (a b)"), 0.0), writes=[("yacc", i) for i in range(HT)])
        for ex in range(NEXP):
            wb = wq % 2
            wq += 1
            for kc in range(8):
                P.dma("pool", wg[wb][:, kc, :], d["wg"][ex, kc * 128:(kc + 1) * 128, :], writes=[("wg", wb, kc)])
                P.dma("pool", wu[wb][:, kc, :], d["wu"][ex, kc * 128:(kc + 1) * 128, :], writes=[("wu", wb, kc)])
            for fc in range(4):
                P.dma("pool", wd[wb][:, fc, :], d["wd"][ex, fc * 128:(fc + 1) * 128, :], writes=[("wd", wb, fc)])
            ngrp = (HT * 128) // 512 if HT * 128 >= 512 else 1
            gw = min(512, HT * 128)
            for g in range(ngrp):
                tok0 = tbase * 128 + g * gw
                hb = g % 2
                for fc in range(4):
                    pb = fc % 2
                    for kc in range(8):
                        P.op("pe", lambda e, kc=kc, fc=fc, pb=pb: e.matmul(pG[pb][:, 0:gw], lhsT=wg[wb][:, kc, fc * 128:(fc + 1) * 128], rhs=h1T[:, kc, tok0:tok0 + gw],
                                                                           start=(kc == 0), stop=(kc == 7)),
                             reads=[("wg", wb, kc)] + [("h1T", tt) for tt in range(tok0 // 128, (tok0 + gw) // 128)], writes=[("pG", pb)])
                    for kc in range(8):
                        P.op("pe", lambda e, kc=kc, fc=fc, pb=pb: e.matmul(pU[pb][:, 0:gw], lhsT=wu[wb][:, kc, fc * 128:(fc + 1) * 128], rhs=h1T[:, kc, tok0:tok0 + gw],
                                                                           start=(kc == 0), stop=(kc == 7)),
                             reads=[("wu", wb, kc)] + [("h1T", tt) for tt in range(tok0 // 128, (tok0 + gw) // 128)], writes=[("pU", pb)])
                    P.op("act", lambda e, pb=pb: e.activation(out=sg[pb][:, 0:gw], in_=pG[pb][:, 0:gw], func=AF.Silu), reads=[("pG", pb)], writes=[("sg", pb)])
                    P.op("dve", lambda e, pb=pb, fc=fc, hb=hb: e.tensor_tensor(out=hid[hb][:, fc, 0:gw], in0=sg[pb][:, 0:gw], in1=pU[pb][:, 0:gw], op=ALU.mult),
                         reads=[("sg", pb), ("pU", pb)], writes=[("hid", hb, fc)])
                for ts in range(gw // 128):
                    tl = (tok0 - tbase * 128) // 128 + ts
                    tg = tbase + tl
                    for dh in range(2):
                        pb = dh
                        for fc in range(4):
                            P.op("pe", lambda e, fc=fc, ts=ts, dh=dh, pb=pb: e.matmul(pD[pb][:, :], lhsT=hid[hb][:, fc, ts * 128:(ts + 1) * 128], rhs=wd[wb][:, fc, dh * 512:(dh + 1) * 512],
                                                                                      start=(fc == 0), stop=(fc == 3)),
                                 reads=[("hid", hb, fc), ("wd", wb, fc)], writes=[("pD", pb)])
                        P.op("dve", lambda e, tl=tl, tg=tg, dh=dh, pb=pb, ex=ex: e.scalar_tensor_tensor(out=yacc[:, tl, dh * 512:(dh + 1) * 512], in0=pD[pb][:, :],
                                                                                                       scalar=wfull[:, tg, ex:ex + 1], in1=yacc[:, tl, dh * 512:(dh + 1) * 512],
                                                                                                       op0=ALU.mult, op1=ALU.add),
                             reads=[("pD", pb), ("wfull", tg), ("yacc", tl)], writes=[("yacc", tl)])
        for tl in range(HT):
            tg = tbase + tl
            b = tl % 2
            rows = slice(tg * 128, (tg + 1) * 128)
            P.dma("sp", xo[b][:], d["h1s"][rows, :], reads=["h1s"], writes=[("xo", b)])
            P.op("dve", lambda e, tl=tl, b=b: e.scalar_tensor_tensor(out=xo[b][:], in0=xo[b][:], scalar=ALPHA, in1=yacc[:, tl, :], op0=ALU.mult, op1=ALU.add),
                 reads=[("xo", b), ("yacc", tl)], writes=[("xo", b)])
            layer_norm_tile(P, xo[b], ("xo", b), st2, mv2, "st2", epst, ln2, "ln2")
            P.dma("sp", d["out"][rows, :], xo[b][:], reads=[("xo", b)], writes=["out"])
    P.pop_scope()

import math
import numpy as np

L = 8192
D = 1024
DIN = 3872
NFM = 2336
NTM = 1536
NCORES = 8
FM_GQ, FM_GK, FM_GA, FM_HQ, FM_HF, FM_HY = 0, 192, 384, 416, 800, 1568
TM_GV, TM_GG, TM_HI, TM_HGT = 0, 384, 768, 1152
FM_COLS = list(range(0, 384)) + list(range(1152, 1184)) + list(range(1184, 2336)) + list(range(3104, 3872))
TM_COLS = list(range(384, 768)) + list(range(768, 1152)) + list(range(2336, 2720)) + list(range(2720, 3104))


def stage_A2(P, do_ln, x_d, w_d, gb_d, id_d, h_d, FM, TM):
    NG = L // 512
    P.push_scope()
    wsb = P.sb("wsb", [128, 8, DIN], BF16)
    idf = P.sb("a_idf", [128, 128], F32)
    idb = P.sb("a_idb", [128, 128], BF16)
    epst = P.sb("a_epst", [128, 1], F32)
    P.op("dve", lambda e: e.memset(epst[:], 1e-5), writes=["a_eps"])
    P.dma("sp", idf[:], id_d, writes=["a_idf"])
    P.op("dve", lambda e: e.tensor_copy(out=idb[:], in_=idf[:]), reads=["a_idf"], writes=["a_idb"])
    if do_ln:
        gbs = P.sb("a_gbs", [128, 2 * D], F32)
        P.dma("sp", gbs[:], gb_d, writes=["a_gbs"])
    CW = 968
    for kc in range(8):
        for c in range(4):
            P.dma("pool", wsb[:, kc, c * CW:(c + 1) * CW], w_d[kc * 128:(kc + 1) * 128, c * CW:(c + 1) * CW], writes=[("wsb", kc, c)])
    wk = lambda kc: [("wsb", kc, c) for c in range(4)]
    xt = [P.sb("a_xt%d" % i, [128, D], F32) for i in range(2)]
    hb = [P.sb("a_hb%d" % i, [128, D], BF16) for i in range(2)]
    hT = [P.sb("a_hT%d" % i, [128, 8, 512], BF16) for i in range(2)]
    st = P.sb("a_st", [128, 2, 6], F32)
    mv = P.sb("a_mv", [128, 4], F32)
    fo = [P.sb("a_fo%d" % i, [128, 512], F32) for i in range(3)]
    po = [P.sb("a_po%d" % i, [128, NTM], F32) for i in range(2)]
    pT = [P.ps("a_pT%d" % i, [128, 1024], BF16) for i in range(2)]
    pm = [P.ps("a_pm%d" % i, [128, 512], F32) for i in range(4)]
    mmi = 0
    ti = 0
    fi = 0
    for g in range(NG):
        hTg = hT[g % 2]
        khT = ("a_hT", g % 2)
        for tt in range(4):
            t = g * 4 + tt
            b = ti % 2
            ti += 1
            rows = slice(t * 128, (t + 1) * 128)
            kx, kh = ("a_xt", b), ("a_hb", b)
            P.dma("sp", xt[b][:], x_d[rows, :], writes=[kx])
            if do_ln:
                layer_norm_tile(P, xt[b], kx, st, mv, "a_st", epst, gbs, "a_gbs")
                P.dma("sp", h_d[rows, :], xt[b][:], reads=[kx], writes=["hbuf"])
            P.op("act", lambda e, b=b: e.activation(out=hb[b][:], in_=xt[b][:], func=AF.Copy), reads=[kx], writes=[kh])
            ptk = ("a_pT", t % 2)
            for kc in range(8):
                P.op("pe", lambda e, kc=kc, b=b, t=t: e.transpose(out=pT[t % 2][:, kc * 128:(kc + 1) * 128], in_=hb[b][:, kc * 128:(kc + 1) * 128], identity=idb[:]),
                     reads=[kh, "a_idb"], writes=[ptk])
            P.op("dve", lambda e, tt=tt, t=t: e.tensor_copy(out=hTg[:, :, tt * 128:(tt + 1) * 128], in_=pT[t % 2][:].rearrange("p (a b) -> p a b", b=128)),
                 reads=[ptk], writes=[(khT, tt)])
        hk = [(khT, tt) for tt in range(4)]
        for fg in range((NFM + 127) // 128):
            r0 = fg * 128
            rw = min(128, NFM - r0)
            pb = mmi % 4
            mmi += 1
            for kc in range(8):
                P.op("pe", lambda e, kc=kc, pb=pb, r0=r0, rw=rw: e.matmul(pm[pb][0:rw, :], lhsT=wsb[:, kc, r0:r0 + rw], rhs=hTg[:, kc, :], start=(kc == 0), stop=(kc == 7)),
                     reads=hk + wk(kc), writes=[("a_pm", pb)])
            fb = fi % 3
            fi += 1
            if fg % 2 == 0:
                P.op("act", lambda e, pb=pb, fb=fb, rw=rw: e.activation(out=fo[fb][0:rw, :], in_=pm[pb][0:rw, :], func=AF.Copy), reads=[("a_pm", pb)], writes=[("a_fo", fb)])
            else:
                P.op("dve", lambda e, pb=pb, fb=fb, rw=rw: e.tensor_copy(out=fo[fb][0:rw, :], in_=pm[pb][0:rw, :]), reads=[("a_pm", pb)], writes=[("a_fo", fb)])
            P.dma("sp", FM[r0:r0 + rw, g * 512:(g + 1) * 512], fo[fb][0:rw, :], reads=[("a_fo", fb)], writes=["FM"])
        for tt in range(4):
            t = g * 4 + tt
            pbuf = po[t % 2]
            kpo = ("a_po", t % 2)
            for ci in range(3):
                pb = mmi % 4
                mmi += 1
                for kc in range(8):
                    P.op("pe", lambda e, kc=kc, pb=pb, ci=ci, tt=tt: e.matmul(pm[pb][:, :], lhsT=hTg[:, kc, tt * 128:(tt + 1) * 128], rhs=wsb[:, kc, NFM + ci * 512:NFM + (ci + 1) * 512],
                                                                             start=(kc == 0), stop=(kc == 7)), reads=hk + wk(kc), writes=[("a_pm", pb)])
                if ci % 2 == 0:
                    P.op("act", lambda e, pb=pb, ci=ci, pbuf=pbuf: e.activation(out=pbuf[:, ci * 512:(ci + 1) * 512], in_=pm[pb][:, :], func=AF.Copy), reads=[("a_pm", pb)], writes=[(kpo, ci)])
                else:
                    P.op("dve", lambda e, pb=pb, ci=ci, pbuf=pbuf: e.tensor_copy(out=pbuf[:, ci * 512:(ci + 1) * 512], in_=pm[pb][:, :]), reads=[("a_pm", pb)], writes=[(kpo, ci)])
            P.dma("sp", TM[t * 128:(t + 1) * 128, :], pbuf[:], reads=[(kpo, ci) for ci in range(3)], writes=["TM"])
    P.pop_scope()


def scans_all(P, layer, d, FM, TM, OF, OB):
    P.push_scope()
    C = ScanCtx(P, L, d)
    for h in range(6):
        P.dma("sp", C.wa[:], d["gwa"][h], writes=["wa"])
        P.dma("sp", C.ba[0:32, 0:2], d["gba"][h], writes=["ba"])
        P.op("dve", lambda e: e.tensor_scalar(out=C.ba[0:32, 2:4], in0=C.ba[0:32, 0:2], scalar1=-1.0, scalar2=None, op0=ALU.mult), reads=["ba"], writes=["nba"])
        prm = {"wa": C.wa, "nba": C.ba[:, 2:4]}
        for dr in range(2):
            O_ = OF if dr == 0 else OB
            scan_dir(C, "gla", layer, 32, dr, FM[FM_GQ + h * 32:FM_GQ + (h + 1) * 32, :], FM[FM_GK + h * 32:FM_GK + (h + 1) * 32, :],
                     FM[FM_GA + dr * 16:FM_GA + (dr + 1) * 16, :], TM[:, TM_GV + h * 64:TM_GV + (h + 1) * 64], O_[:, h * 64:(h + 1) * 64], prm,
                     rkeys=["FM", "TM"])
    for h in range(6):
        K = 64
        prm = {}
        if layer > 0:
            P.dma("sp", C.lb[0:K, 0:4], d["hlb"][h], writes=["lbl"])
            P.op("dve", lambda e: e.tensor_tensor(out=C.lb[0:K, 4:6], in0=C.lb[0:K, 0:2], in1=C.lb[0:K, 2:4], op=ALU.subtract), reads=["lbl"], writes=["lb1"])
            P.op("act", lambda e: e.activation(out=C.lb[0:K, 4:6], in_=C.lb[0:K, 4:6], func=AF.Exp), reads=["lb1"], writes=["lb1"])
            P.op("dve", lambda e: e.tensor_scalar(out=C.lb[0:K, 4:6], in0=C.lb[0:K, 4:6], scalar1=1.0, scalar2=None, op0=ALU.add), reads=["lb1"], writes=["lb1"])
            P.op("dve", lambda e: e.reciprocal(out=C.lb[0:K, 4:6], in_=C.lb[0:K, 4:6]), reads=["lb1"], writes=["lb"])
            P.op("dve", lambda e: e.tensor_scalar(out=C.lb[0:K, 6:8], in0=C.lb[0:K, 4:6], scalar1=-1.0, scalar2=1.0, op0=ALU.mult, op1=ALU.add), reads=["lb"], writes=["oml"])
            prm = {"lb": C.lb[:, 4:6], "oml": C.lb[:, 6:8]}
        for dr in range(2):
            O_ = OF if dr == 0 else OB
            scan_dir(C, "hg", layer, 64, dr, FM[FM_HQ + h * 64:FM_HQ + (h + 1) * 64, :], None,
                     FM[FM_HF + dr * 384 + h * 64:FM_HF + dr * 384 + (h + 1) * 64, :], TM[:, TM_HI + h * 64:TM_HI + (h + 1) * 64],
                     O_[:, 384 + h * 64:384 + (h + 1) * 64], prm, rkeys=["FM", "TM"])
    P.pop_scope()


def hyena_consts2(Lx):
    f32 = np.float32
    t = np.linspace(0.0, 1.0, Lx, dtype=f32)[:, None]
    w = (2.0 * math.pi * np.arange(Lx, dtype=f32)[:, None] / Lx).astype(f32)
    bands = np.linspace(1e-4, 15, 16, dtype=f32)[None, :]
    z = np.concatenate([t, np.cos(bands * w), -np.sin(bands * w)], axis=-1).astype(f32)
    max_decay = math.log(1e-2) / 0.3
    min_decay = math.log(1e-2) / 1.5
    deltas = np.linspace(min_decay, max_decay, HY_W, dtype=f32)
    dec = np.exp(-t * np.abs(deltas)).astype(f32)
    tidx = np.concatenate([np.arange(Lx - 1, -1, -1), np.arange(1, Lx), [0]])
    zR = np.ascontiguousarray(z[tidx].T)
    zR[:, -1] = 0.0
    dR = np.ascontiguousarray(dec[tidx].T)
    dR[:, -1] = 0.0
    J = np.ascontiguousarray(np.eye(128, dtype=f32)[::-1])
    return {"h_zR": zR, "h_decR": dR.astype(f32), "h_J": J, "h_id": np.eye(128, dtype=f32), "h_ones": np.ones((64, 128), f32)}


def hyena_all(P, Lx, d, FM, YH, scr, fm_hy=FM_HY):
    NB = Lx // 128
    NT = 2 * Lx // 512
    HW = 2 * Lx - 128
    sb = P.sb
    P.push_scope()
    w1 = sb("h_w1", [33, 64]); b1f = sb("h_b1f", [64, 8]); w2 = sb("h_w2", [64, 2, 64]); w3 = sb("h_w3", [64, 2, 2, 256])
    cw = sb("h_cw", [128, 2, 9]); cb = sb("h_cb", [128, 2, 3]); bias = sb("h_bias", [128, 2, 2])
    Jm = sb("h_Jm", [128, 128]); idf = sb("h_idf", [128, 128]); idb = sb("h_idb", [128, 128], BF16); ones = sb("h_onesb", [64, 128])
    P.dma("sp", w1[:], d["w1"], writes=["h_w1"])
    P.dma("sp", b1f[:, 0:4], d["b1f"], writes=["h_b1f"])
    P.dma("sp", w2[:], d["w2"], writes=["h_w2"])
    P.dma("sp", w3[:], d["w3"], writes=["h_w3"])
    for lt in range(2):
        P.dma("sp", cw[:, lt, :], d["cw"][lt], writes=["h_cw"])
        P.dma("sp", cb[:, lt, :], d["cb"][lt], writes=["h_cb"])
        P.dma("sp", bias[:, lt, :], d["bias"][lt], writes=["h_bias"])
    P.dma("sp", Jm[:], d["h_J"], writes=["h_J"])
    P.dma("sp", idf[:], d["h_id"], writes=["h_idf"])
    P.dma("sp", ones[:], d["h_ones"], writes=["h_ones"])
    P.op("dve", lambda e: e.tensor_copy(out=idb[:], in_=idf[:]), reads=["h_idf"], writes=["h_idb"])
    P.op("dve", lambda e: e.tensor_scalar(out=b1f[:, 4:5], in0=b1f[:, 3:4], scalar1=1.0 / (2 * math.pi), scalar2=None, op0=ALU.mult), reads=["h_b1f"], writes=["h_A"])
    P.op("dve", lambda e: e.tensor_scalar(out=b1f[:, 5:8], in0=b1f[:, 0:3], scalar1=b1f[:, 4:5], scalar2=None, op0=ALU.mult), reads=["h_b1f", "h_A"], writes=["h_B"])
    pB = P.ps("h_pB", [128, 512])
    rnb = sb("h_rnb", [128, 512])
    P.push_scope()
    pA = P.ps("h_pA", [64, 512])
    pR = [P.ps("h_pR%d" % i, [64, 512]) for i in range(2)]
    zt = [sb("h_zt%d" % i, [33, 512]) for i in range(2)]
    dt_ = [sb("h_dt%d" % i, [64, 4, 512]) for i in range(2)]
    u = sb("h_u", [64, 512]); ui = sb("h_ui", [64, 512], I32); hh = sb("h_hh", [64, 512]); h3 = [sb("h_h3%d" % i, [64, 512]) for i in range(2)]
    Rt = sb("h_Rt", [64, 512]); Rb = [sb("h_Rb%d" % i, [64, 512], BF16) for i in range(3)]
    asum = sb("h_asum", [64, 8, NT]); rn = sb("h_rn", [64, 16]); dg = sb("h_dg", [64, 64])
    ri = 0
    for t in range(NT):
        b = t % 2
        P.dma("sp", zt[b][:], d["h_zR"][:, t * 512:(t + 1) * 512], writes=[("zt", b)])
        P.dma("sp", dt_[b][:], d["h_decR"][:, t * 512:(t + 1) * 512].rearrange("(a p) n -> p a n", p=64), writes=[("dt", b)])
        rhs = zt[b]
        rk = ("zt", b)
        for l in range(3):
            lhsT = w1[:, :] if l == 0 else w2[:, l - 1, :]
            kk = 33 if l == 0 else 64
            P.op("pe", lambda e, lhsT=lhsT, rhs=rhs, kk=kk: e.matmul(pA[:, :], lhsT=lhsT, rhs=rhs[0:kk, :], start=True, stop=True),
                 reads=[rk, "h_w1", "h_w2"], writes=["h_pA"])
            P.op("dve", lambda e, l=l: e.tensor_scalar(out=u[:], in0=pA[:], scalar1=b1f[:, 4:5], scalar2=b1f[:, 5 + l:6 + l], op0=ALU.mult, op1=ALU.add),
                 reads=["h_pA", "h_A", "h_B"], writes=["h_u"])
            P.op("dve", lambda e: e.tensor_copy(out=ui[:], in_=u[:]), reads=["h_u"], writes=["h_ui"])
            P.op("dve", lambda e: e.tensor_tensor(out=u[:], in0=u[:], in1=ui[:], op=ALU.subtract), reads=["h_u", "h_ui"], writes=["h_u"])
            dst = hh if l < 2 else h3[b]
            dk = "h_hh" if l < 2 else ("h_h3", b)
            P.op("act", lambda e, dst=dst: e.activation(out=dst[:], in_=u[:], func=AF.Sin, scale=TWO_PI_LO), reads=["h_u"], writes=[dk])
            rhs = dst
            rk = dk
        dr = 0 if t < NT // 2 else 1
        for gi in range(8):
            o, cq = gi // 4, gi % 4
            pr = pR[gi % 2]
            prk = ("h_pR", gi % 2)
            P.op("pe", lambda e, dr=dr, o=o, cq=cq, pr=pr, b=b: e.matmul(pr[:, :], lhsT=w3[:, dr, o, cq * 64:(cq + 1) * 64], rhs=h3[b][:, :], start=True, stop=True),
                 reads=[("h_h3", b), "h_w3"], writes=[prk])
            P.op("dve", lambda e, pr=pr, cq=cq, b=b: e.tensor_tensor(out=Rt[:], in0=pr[:], in1=dt_[b][:, cq, :], op=ALU.mult), reads=[prk, ("dt", b)], writes=["h_Rt"])
            P.op("dve", lambda e, gi=gi, t=t: e.tensor_reduce(out=asum[:, gi, t:t + 1], in_=Rt[:], axis=AX.X, op=ALU.add, apply_absolute_value=True), reads=["h_Rt"], writes=[("asum", gi, t)])
            rb = ri % 3
            ri += 1
            P.op("act", lambda e, rb=rb: e.activation(out=Rb[rb][:], in_=Rt[:], func=AF.Copy), reads=["h_Rt"], writes=[("Rb", rb)])
            P.dma("sp", scr[gi * 64:(gi + 1) * 64, t * 512:(t + 1) * 512], Rb[rb][:], reads=[("Rb", rb)], writes=["scr"])
    P.op("dve", lambda e: e.tensor_reduce(out=rn[:, 0:8], in_=asum[:], axis=AX.X, op=ALU.add), reads=[("asum", gi, t) for gi in range(8) for t in range(NT)], writes=["h_rn"])
    P.op("dve", lambda e: e.tensor_scalar(out=rn[:, 0:8], in0=rn[:, 0:8], scalar1=1e-12, scalar2=None, op0=ALU.max), reads=["h_rn"], writes=["h_rn"])
    P.op("dve", lambda e: e.reciprocal(out=rn[:, 8:16], in_=rn[:, 0:8]), reads=["h_rn"], writes=["h_rn1"])
    for gi in range(8):
        P.op("dve", lambda e, gi=gi: e.tensor_scalar(out=dg[:], in0=idf[0:64, 0:64], scalar1=rn[:, 8 + gi:9 + gi], scalar2=None, op0=ALU.mult), reads=["h_idf", "h_rn1"], writes=["h_dg"])
        P.op("pe", lambda e: e.matmul(pB[:, 0:64], lhsT=ones[:, :], rhs=dg[:, :], start=True, stop=True), reads=["h_ones", "h_dg"], writes=["h_pB"])
        P.op("dve", lambda e, gi=gi: e.tensor_copy(out=rnb[:, gi * 64:(gi + 1) * 64], in_=pB[:, 0:64]), reads=["h_pB"], writes=["h_rnb"])
    P.pop_scope()
    za = sb("h_za", [128, Lx]); zc = sb("h_zc", [128, Lx])
    zb = [sb("h_zb%d" % i, [128, 1024], BF16) for i in range(2)]
    ZT = sb("h_ZT", [128, NB, 128], BF16)
    YT = sb("h_YT", [128, NB, 128])
    xr = YT[:].rearrange("p a b -> p (a b)")
    NYK = 8
    ytk = [("YT", c) for c in range(NYK)]
    H = [sb("h_H%d" % i, [128, HW], BF16) for i in range(2)]
    pT = P.ps("h_pT", [128, 1024], BF16)
    pC = [P.ps("h_pC%d" % i, [128, 512]) for i in range(2)]
    yo = [sb("h_yo%d" % i, [128, 512]) for i in range(2)]
    hq = 0
    for lt in range(2):
        def short_conv(idx, dst, dk):
            P.dma("sp", xr, FM[fm_hy + idx * 256 + lt * 128:fm_hy + idx * 256 + (lt + 1) * 128, :], reads=["FM"], writes=ytk)
            P.op("dve", lambda e: e.tensor_scalar(out=dst[:], in0=xr, scalar1=cw[:, lt, idx * 3 + 1:idx * 3 + 2], scalar2=cb[:, lt, idx:idx + 1], op0=ALU.mult, op1=ALU.add),
                 reads=ytk + ["h_cw", "h_cb"], writes=[dk])
            P.op("dve", lambda e: e.scalar_tensor_tensor(out=dst[:, 1:Lx], in0=xr[:, 0:Lx - 1], scalar=cw[:, lt, idx * 3:idx * 3 + 1], in1=dst[:, 1:Lx], op0=ALU.mult, op1=ALU.add),
                 reads=ytk + ["h_cw", dk], writes=[dk])
            P.op("dve", lambda e: e.scalar_tensor_tensor(out=dst[:, 0:Lx - 1], in0=xr[:, 1:Lx], scalar=cw[:, lt, idx * 3 + 2:idx * 3 + 3], in1=dst[:, 0:Lx - 1], op0=ALU.mult, op1=ALU.add),
                 reads=ytk + ["h_cw", dk], writes=[dk])

        short_conv(0, za, "h_za")
        for o in range(2):
            for g in range(NB // 8):
                zbb = zb[g % 2]
                zbk = ("h_zb", g % 2)
                P.op("act", lambda e, g=g, zbb=zbb: e.activation(out=zbb[:], in_=za[:, g * 1024:(g + 1) * 1024], func=AF.Copy), reads=["h_za"], writes=[zbk])
                for jj in range(8):
                    P.op("pe", lambda e, jj=jj, zbb=zbb: e.transpose(out=pT[:, jj * 128:(jj + 1) * 128], in_=zbb[:, jj * 128:(jj + 1) * 128], identity=idb[:]),
                         reads=[zbk, "h_idb"], writes=["h_pT"])
                P.op("dve", lambda e, g=g: e.tensor_copy(out=ZT[:, g * 8:(g + 1) * 8, :].rearrange("p a b -> p (a b)"), in_=pT[:]), reads=["h_pT"], writes=[("ZT", g)])
            ztk = [("ZT", g) for g in range(NB // 8)]
            short_conv(1 + o, zc, "h_zc")
            for c in range(128):
                row = o * 256 + lt * 128 + c
                hb_ = hq % 2
                hq += 1
                q = "sp" if hb_ == 0 else "pool"
                P.dma(q, H[hb_][:], bass.AP(scr.tensor, row * 2 * Lx, [[1, 128], [1, HW]]), reads=["scr"], writes=[("H", hb_)])
                pc = pC[hb_]
                pck = ("pC", hb_)
                ds = [0] + [x for k in range(1, NB) for x in (k, -k)]
                for n_, dd in enumerate(ds):
                    j0, j1 = max(0, -dd), min(NB, NB - dd)
                    x0 = Lx - 128 - 128 * dd
                    P.op("pe", lambda e, dd=dd, j0=j0, j1=j1, x0=x0, n_=n_, pc=pc, hb_=hb_, c=c: e.matmul(
                        pc[:, (j0 + dd):(j1 + dd)], lhsT=H[hb_][:, x0:x0 + 128], rhs=ZT[:, j0:j1, c], start=(n_ == 0), stop=(n_ == len(ds) - 1)),
                         reads=[("H", hb_)] + ztk, writes=[pck])
                yk = ("YT", c % NYK)
                if c % 2 == 0:
                    P.op("act", lambda e, c=c, row=row, pc=pc: e.activation(out=YT[:, :, c], in_=pc[:, 0:NB], func=AF.Copy, scale=rnb[:, row:row + 1]),
                         reads=[pck, "h_rnb"], writes=[yk])
                else:
                    P.op("dve", lambda e, c=c, row=row, pc=pc: e.tensor_scalar(out=YT[:, :, c], in0=pc[:, 0:NB], scalar1=rnb[:, row:row + 1], scalar2=None, op0=ALU.mult),
                         reads=[pck, "h_rnb"], writes=[yk])
            for g in range(NB // 4):
                for ii in range(4):
                    i = g * 4 + ii
                    P.op("pe", lambda e, ii=ii, i=i: e.matmul(pB[:, ii * 128:(ii + 1) * 128], lhsT=YT[:, i, :], rhs=Jm[:, :], start=True, stop=True),
                         reads=ytk + ["h_J"], writes=["h_pB"])
                sl = slice(g * 512, (g + 1) * 512)
                P.op("dve", lambda e, sl=sl, o=o: e.scalar_tensor_tensor(out=za[:, sl], in0=za[:, sl], scalar=bias[:, lt, o:o + 1], in1=pB[:, :], op0=ALU.mult, op1=ALU.add),
                     reads=["h_za", "h_bias", "h_pB"], writes=["h_za"])
                P.op("pool", lambda e, sl=sl: e.tensor_tensor(out=za[:, sl], in0=za[:, sl], in1=zc[:, sl], op=ALU.mult), reads=["h_za", "h_zc"], writes=["h_za"])
        for g in range(NB // 4):
            for ii in range(4):
                i = g * 4 + ii
                P.op("pe", lambda e, ii=ii, i=i: e.matmul(pB[:, ii * 128:(ii + 1) * 128], lhsT=za[:, i * 128:(i + 1) * 128], rhs=idf[:, :], start=True, stop=True),
                     reads=["h_za", "h_idf"], writes=["h_pB"])
            yb = yo[g % 2]
            P.op("act", lambda e, yb=yb: e.activation(out=yb[:], in_=pB[:, :], func=AF.Copy), reads=["h_pB"], writes=[("h_yo", g % 2)])
            P.dma("sp", YH[g * 512:(g + 1) * 512, lt * 128:(lt + 1) * 128].rearrange("(a p) c -> p a c", p=128), yb[:].rearrange("p (a c) -> p a c", c=128),
                  reads=[("h_yo", g % 2)], writes=["YH"])
    P.pop_scope()


def stage_C2(P, d, HB, TM, OF, OB, YH, h1s, OUT, CT=16):
    sb, ps = P.sb, P.ps
    NCH = (L // 128) // CT
    CTOK = CT * 128
    P.push_scope()
    epst = sb("epst", [128, 1]); eps6 = sb("eps6", [128, 1])
    P.op("dve", lambda e: e.memset(epst[:], 1e-5), writes=["eps"])
    P.op("dve", lambda e: e.memset(eps6[:], 1e-6), writes=["eps6"])
    idf = sb("c_idf", [128, 128]); idb = sb("c_idb", [128, 128], BF16)
    P.dma("sp", idf[:], d["c_id"], writes=["idf"])
    P.op("dve", lambda e: e.tensor_copy(out=idb[:], in_=idf[:]), reads=["idf"], writes=["idb"])
    h1T = sb("h1T", [128, 8, CTOK], BF16)
    wfull = sb("wfull", [128, CT, 16])
    for ch in range(NCH):
        tb = ch * CT
        P.push_scope()
        wout = sb("woutb", [128, 8, D], BF16)
        for kc in range(8):
            P.dma("pool", wout[:, kc, :], d["wout"][kc * 128:(kc + 1) * 128, :], writes=[("wout", kc)])
        woutk = [("wout", kc) for kc in range(8)]
        gn = sb("gn", [128, 768]); ln1 = sb("ln1", [128, 2 * D]); wr = sb("wr", [128, 8, 20]); br = sb("br", [128, 20])
        P.dma("sp", gn[:], d["gn"], writes=["gn"])
        P.dma("sp", ln1[:], d["ln1"], writes=["ln1"])
        P.dma("sp", wr[:], d["wr"].rearrange("(k p) n -> p k n", p=128), writes=["wr"])
        P.dma("sp", br[:], d["br"], writes=["br"])
        NBUF = 2
        OFt = [sb("OF%d" % i, [128, 768]) for i in range(NBUF)]
        OBt = [sb("OB%d" % i, [128, 768]) for i in range(NBUF)]
        GT = [sb("GT%d" % i, [128, 768]) for i in range(NBUF)]
        mix = [sb("mix%d" % i, [128, D]) for i in range(NBUF)]
        mixb = [sb("mixb%d" % i, [128, D], BF16) for i in range(NBUF)]
        mixT = [sb("mixT%d" % i, [128, 8, 128], BF16) for i in range(NBUF)]
        ht = [sb("ht%d" % i, [128, D]) for i in range(NBUF)]
        sq = sb("sq", [128, 768]); ss = sb("ss", [128, 12]); rs = sb("rs_", [128, 12])
        st = sb("st", [128, 2, 6]); mv = sb("mv", [128, 4])
        h1T32 = sb("h1T32", [128, 8, 128])
        lg = sb("lg", [128, 20]); rt = sb("rt", [128, 64])
        pT = ps("pT", [128, 1024], BF16)
        pO = [ps("pO%d" % i, [128, 512]) for i in range(2)]
        pX = [ps("pX%d" % i, [128, 512]) for i in range(2)]
        pL = ps("pL", [128, 512])
        for tl in range(CT):
            t = tb + tl
            b = tl % NBUF
            rows = slice(t * 128, (t + 1) * 128)
            lrows = slice(tl * 128, (tl + 1) * 128)
            kOF, kOB, kGT, kmix, kmixb, kmixT, kht = ("OF", b), ("OB", b), ("GT", b), ("mix", b), ("mixb", b), ("mixT", b), ("ht", b)
            P.dma("sp", OFt[b][:], OF[rows, :], writes=[kOF])
            P.dma("sp", OBt[b][:], OB[rows, :], writes=[kOB])
            P.dma("sp", GT[b][:, 0:384], TM[rows, TM_GG:TM_GG + 384], writes=[(kGT, 0)])
            P.dma("sp", GT[b][:, 384:768], TM[rows, TM_HGT:TM_HGT + 384], writes=[(kGT, 1)])
            kGTs = [(kGT, 0), (kGT, 1)]
            P.dma("sp", mix[b][:, 768:1024], YH[rows, :], writes=[(kmix, "hy")])
            P.dma("sp", ht[b][:], HB[rows, :], writes=[kht])
            P.op("pool", lambda e, b=b: e.tensor_tensor(out=OFt[b][:], in0=OFt[b][:], in1=OBt[b][:], op=ALU.add), reads=[kOF, kOB], writes=[kOF])
            P.op("dve", lambda e, b=b: e.tensor_tensor(out=sq[:], in0=OFt[b][:], in1=OFt[b][:], op=ALU.mult), reads=[kOF], writes=["sq"])
            P.op("dve", lambda e: e.tensor_reduce(out=ss[:], in_=sq[:].rearrange("p (h v) -> p h v", v=64), axis=AX.X, op=ALU.add), reads=["sq"], writes=["ss"])
            P.op("act", lambda e: e.activation(out=rs[:], in_=ss[:], func=AF.Sqrt, bias=eps6[:, 0:1], scale=1.0 / 64.0), reads=["ss", "eps6"], writes=["rs"])
            P.op("dve", lambda e: e.reciprocal(out=rs[:], in_=rs[:]), reads=["rs"], writes=["rs"])
            P.op("dve", lambda e, b=b: e.tensor_tensor(out=OFt[b][:].rearrange("p (h v) -> p h v", v=64), in0=OFt[b][:].rearrange("p (h v) -> p h v", v=64),
                                                       in1=rs[:].unsqueeze(2).to_broadcast([128, 12, 64]), op=ALU.mult), reads=[kOF, "rs"], writes=[kOF])
            P.op("pool", lambda e, b=b: e.tensor_tensor(out=OFt[b][:], in0=OFt[b][:], in1=gn[:], op=ALU.mult), reads=[kOF, "gn"], writes=[kOF])
            P.op("act", lambda e, b=b: e.activation(out=sq[:], in_=GT[b][:], func=AF.Exp, scale=-1.0), reads=kGTs, writes=["sq"])
            P.op("dve", lambda e: e.tensor_scalar(out=sq[:], in0=sq[:], scalar1=1.0, scalar2=None, op0=ALU.add), reads=["sq"], writes=["sq"])
            P.op("dve", lambda e: e.reciprocal(out=sq[:], in_=sq[:]), reads=["sq"], writes=["sq"])
            P.op("pool", lambda e, b=b: e.tensor_tensor(out=sq[:, 0:384], in0=sq[:, 0:384], in1=GT[b][:, 0:384], op=ALU.mult), reads=["sq"] + kGTs, writes=["sq"])
            P.op("dve", lambda e, b=b: e.tensor_tensor(out=mix[b][:, 0:768], in0=OFt[b][:], in1=sq[:], op=ALU.mult), reads=[kOF, "sq"], writes=[(kmix, "a")])
            P.op("act", lambda e, b=b: e.activation(out=mixb[b][:], in_=mix[b][:], func=AF.Copy), reads=[(kmix, "a"), (kmix, "hy")], writes=[kmixb])
            for kc in range(8):
                P.op("pe", lambda e, kc=kc, b=b: e.transpose(out=pT[:, kc * 128:(kc + 1) * 128], in_=mixb[b][:, kc * 128:(kc + 1) * 128], identity=idb[:]),
                     reads=[kmixb, "idb"], writes=["pT"])
            P.op("dve", lambda e, b=b: e.tensor_copy(out=mixT[b][:].rearrange("p a b -> p (a b)"), in_=pT[:]), reads=["pT"], writes=[kmixT])
            for hf in range(2):
                for kc in range(8):
                    P.op("pe", lambda e, kc=kc, hf=hf, b=b: e.matmul(pO[hf][:, :], lhsT=mixT[b][:, kc, :], rhs=wout[:, kc, hf * 512:(hf + 1) * 512], start=(kc == 0), stop=(kc == 7)),
                         reads=[kmixT] + woutk, writes=[("pO", hf)])
                P.op("dve", lambda e, hf=hf, b=b: e.scalar_tensor_tensor(out=ht[b][:, hf * 512:(hf + 1) * 512], in0=ht[b][:, hf * 512:(hf + 1) * 512], scalar=ALPHA,
                                                                         in1=pO[hf][:, :], op0=ALU.mult, op1=ALU.add), reads=[kht, ("pO", hf)], writes=[kht])
            layer_norm_tile(P, ht[b], kht, st, mv, "st", epst, ln1, "ln1")
            P.dma("sp", h1s[rows, :], ht[b][:], reads=[kht], writes=["h1s"])
            for kc in range(8):
                P.op("pe", lambda e, kc=kc, b=b: e.matmul(pX[kc // 4][:, (kc % 4) * 128:(kc % 4 + 1) * 128], lhsT=ht[b][:, kc * 128:(kc + 1) * 128], rhs=idf[:], start=True, stop=True),
                     reads=[kht, "idf"], writes=[("pX", kc // 4)])
            for hf in range(2):
                P.op("act", lambda e, hf=hf: e.activation(out=h1T32[:, hf * 4:(hf + 1) * 4, :].rearrange("p a b -> p (a b)"), in_=pX[hf][:, :], func=AF.Copy),
                     reads=[("pX", hf)], writes=[("h1T32", hf)])
                P.op("dve", lambda e, hf=hf, lrows=lrows: e.tensor_copy(out=h1T[:, hf * 4:(hf + 1) * 4, lrows], in_=pX[hf][:, :].rearrange("p (a b) -> p a b", b=128)),
                     reads=[("pX", hf), ("h1T32", hf)], writes=[("h1T", tl)])
            for kc in range(8):
                P.op("pe", lambda e, kc=kc: e.matmul(pL[:, 0:20], lhsT=h1T32[:, kc, :], rhs=wr[:, kc, :], start=(kc == 0), stop=(kc == 7)),
                     reads=[("h1T32", kc // 4), "wr"], writes=["pL"])
            P.op("dve", lambda e: e.tensor_tensor(out=lg[:], in0=pL[:, 0:20], in1=br[:], op=ALU.add), reads=["pL", "br"], writes=["lg"])
            R_ = lambda a, b_: rt[:, a:b_]
            dv = lambda fn, r=("lg", "rt"), w=("rt",): P.op("dve", fn, reads=list(r), writes=list(w))
            dv(lambda e: e.tensor_reduce(out=R_(0, 1), in_=lg[:, 0:4], axis=AX.X, op=ALU.max))
            dv(lambda e: e.tensor_scalar(out=R_(1, 5), in0=lg[:, 0:4], scalar1=R_(0, 1), scalar2=None, op0=ALU.is_equal))
            dv(lambda e: e.tensor_scalar(out=R_(5, 9), in0=lg[:, 0:4], scalar1=R_(0, 1), scalar2=None, op0=ALU.subtract))
            P.op("act", lambda e: e.activation(out=R_(5, 9), in_=R_(5, 9), func=AF.Exp), reads=["rt"], writes=["rt"])
            dv(lambda e: e.tensor_reduce(out=R_(9, 10), in_=R_(5, 9), axis=AX.X, op=ALU.add))
            dv(lambda e: e.reciprocal(out=R_(10, 11), in_=R_(9, 10)))
            dv(lambda e: e.tensor_tensor(out=rt[:, 40:56].rearrange("p (g e) -> p g e", e=4), in0=lg[:, 4:20].rearrange("p (g e) -> p g e", e=4),
                                         in1=R_(1, 5).unsqueeze(2).to_broadcast([128, 4, 4]), op=ALU.mult))
            dv(lambda e: e.tensor_reduce(out=R_(11, 15), in_=rt[:, 40:56].rearrange("p (g e) -> p e g", e=4), axis=AX.X, op=ALU.add))
            dv(lambda e: e.tensor_reduce(out=R_(15, 16), in_=R_(11, 15), axis=AX.X, op=ALU.max))
            dv(lambda e: e.tensor_scalar(out=R_(16, 20), in0=R_(11, 15), scalar1=R_(15, 16), scalar2=None, op0=ALU.is_equal))
            dv(lambda e: e.scalar_tensor_tensor(out=R_(20, 24), in0=R_(16, 20), scalar=-1e30, in1=R_(11, 15), op0=ALU.mult, op1=ALU.add))
            dv(lambda e: e.tensor_reduce(out=R_(24, 25), in_=R_(20, 24), axis=AX.X, op=ALU.max))
            dv(lambda e: e.tensor_scalar(out=R_(28, 32), in0=R_(20, 24), scalar1=R_(24, 25), scalar2=None, op0=ALU.is_equal))
            dv(lambda e: e.tensor_tensor(out=R_(32, 33), in0=R_(24, 25), in1=R_(15, 16), op=ALU.subtract))
            P.op("act", lambda e: e.activation(out=R_(32, 33), in_=R_(32, 33), func=AF.Exp), reads=["rt"], writes=["rt"])
            dv(lambda e: e.tensor_scalar(out=R_(33, 34), in0=R_(32, 33), scalar1=1.0, scalar2=None, op0=ALU.add))
            dv(lambda e: e.reciprocal(out=R_(34, 35), in_=R_(33, 34)))
            dv(lambda e: e.tensor_tensor(out=R_(35, 36), in0=R_(32, 33), in1=R_(34, 35), op=ALU.mult))
            dv(lambda e: e.tensor_scalar(out=R_(36, 40), in0=R_(16, 20), scalar1=R_(34, 35), scalar2=None, op0=ALU.mult))
            dv(lambda e: e.scalar_tensor_tensor(out=R_(36, 40), in0=R_(28, 32), scalar=R_(35, 36), in1=R_(36, 40), op0=ALU.mult, op1=ALU.add))
            dv(lambda e: e.tensor_scalar(out=R_(36, 40), in0=R_(36, 40), scalar1=R_(10, 11), scalar2=None, op0=ALU.mult))
            dv(lambda e: e.tensor_tensor(out=rt[:, 40:56].rearrange("p (g e) -> p g e", e=4), in0=R_(1, 5).unsqueeze(2).to_broadcast([128, 4, 4]),
                                         in1=R_(36, 40).unsqueeze(1).to_broadcast([128, 4, 4]), op=ALU.mult))
            P.op("dve", lambda e, tl=tl: e.tensor_copy(out=wfull[:, tl, :], in_=rt[:, 40:56]), reads=["rt"], writes=[("wfull", tl)])
        P.pop_scope()
        P.push_scope()
        ln2 = sb("ln2", [128, 2 * D])
        P.dma("sp", ln2[:], d["ln2"], writes=["ln2"])
        yacc = sb("yacc", [128, CT, D])
        wg = [sb("wg%d" % i, [128, 8, DE], BF16) for i in range(2)]
        wu = [sb("wu%d" % i, [128, 8, DE], BF16) for i in range(2)]
        wd = [sb("wd%d" % i, [128, 4, D], BF16) for i in range(2)]
        hid = [sb("hid%d" % i, [128, 4, 512], BF16) for i in range(2)]
        sg = [sb("sg%d" % i, [128, 512]) for i in range(2)]
        st2 = sb("st2", [128, 2, 6]); mv2 = sb("mv2", [128, 4])
        xo = [sb("xo%d" % i, [128, D]) for i in range(2)]
        pG = [ps("pG%d" % i, [128, 512]) for i in range(2)]
        pU = [ps("pU%d" % i, [128, 512]) for i in range(2)]
        pD = [ps("pD%d" % i, [128, 512]) for i in range(2)]
        P.op("pool", lambda e: e.memset(yacc[:].rearrange("p a b -> p (a b)"), 0.0), writes=[("yacc", i) for i in range(CT)])
        for ex in range(NEXP):
            wb = ex % 2
            for kc in range(8):
                P.dma("pool", wg[wb][:, kc, :], d["wg"][ex, kc * 128:(kc + 1) * 128, :], writes=[("wg", wb, kc)])
                P.dma("pool", wu[wb][:, kc, :], d["wu"][ex, kc * 128:(kc + 1) * 128, :], writes=[("wu", wb, kc)])
            for fc in range(4):
                P.dma("pool", wd[wb][:, fc, :], d["wd"][ex, fc * 128:(fc + 1) * 128, :], writes=[("wd", wb, fc)])
            gw = min(512, CTOK)
            for g in range(CTOK // gw):
                tok0 = g * gw
                hb = g % 2
                h1k = [("h1T", tt) for tt in range(tok0 // 128, (tok0 + gw) // 128)]
                for fc in range(4):
                    pb = fc % 2
                    for kc in range(8):
                        P.op("pe", lambda e, kc=kc, fc=fc, pb=pb, wb=wb, tok0=tok0: e.matmul(pG[pb][:, 0:gw], lhsT=wg[wb][:, kc, fc * 128:(fc + 1) * 128], rhs=h1T[:, kc, tok0:tok0 + gw],
                                                                                         start=(kc == 0), stop=(kc == 7)), reads=[("wg", wb, kc)] + h1k, writes=[("pG", pb)])
                    for kc in range(8):
                        P.op("pe", lambda e, kc=kc, fc=fc, pb=pb, wb=wb, tok0=tok0: e.matmul(pU[pb][:, 0:gw], lhsT=wu[wb][:, kc, fc * 128:(fc + 1) * 128], rhs=h1T[:, kc, tok0:tok0 + gw],
                                                                                         start=(kc == 0), stop=(kc == 7)), reads=[("wu", wb, kc)] + h1k, writes=[("pU", pb)])
                    P.op("act", lambda e, pb=pb: e.activation(out=sg[pb][:, 0:gw], in_=pG[pb][:, 0:gw], func=AF.Silu), reads=[("pG", pb)], writes=[("sg", pb)])
                    P.op("dve", lambda e, pb=pb, fc=fc, hb=hb: e.tensor_tensor(out=hid[hb][:, fc, 0:gw], in0=sg[pb][:, 0:gw], in1=pU[pb][:, 0:gw], op=ALU.mult),
                         reads=[("sg", pb), ("pU", pb)], writes=[("hid", hb, fc)])
                for ts in range(gw // 128):
                    tl = tok0 // 128 + ts
                    for dh in range(2):
                        pb = dh
                        for fc in range(4):
                            P.op("pe", lambda e, fc=fc, ts=ts, dh=dh, pb=pb, hb=hb, wb=wb: e.matmul(pD[pb][:, :], lhsT=hid[hb][:, fc, ts * 128:(ts + 1) * 128], rhs=wd[wb][:, fc, dh * 512:(dh + 1) * 512],
                                                                                                   start=(fc == 0), stop=(fc == 3)), reads=[("hid", hb, fc), ("wd", wb, fc)], writes=[("pD", pb)])
                        P.op("dve", lambda e, tl=tl, dh=dh, pb=pb, ex=ex: e.scalar_tensor_tensor(out=yacc[:, tl, dh * 512:(dh + 1) * 512], in0=pD[pb][:, :], scalar=wfull[:, tl, ex:ex + 1],
                                                                                                in1=yacc[:, tl, dh * 512:(dh + 1) * 512], op0=ALU.mult, op1=ALU.add),
                             reads=[("pD", pb), ("wfull", tl), ("yacc", tl)], writes=[("yacc", tl)])
        for tl in range(CT):
            t = tb + tl
            b = tl % 2
            rows = slice(t * 128, (t + 1) * 128)
            P.dma("sp", xo[b][:], h1s[rows, :], reads=["h1s"], writes=[("xo", b)])
            P.op("dve", lambda e, tl=tl, b=b: e.scalar_tensor_tensor(out=xo[b][:], in0=xo[b][:], scalar=ALPHA, in1=yacc[:, tl, :], op0=ALU.mult, op1=ALU.add),
                 reads=[("xo", b), ("yacc", tl)], writes=[("xo", b)])
            layer_norm_tile(P, xo[b], ("xo", b), st2, mv2, "st2", epst, ln2, "ln2")
            P.dma("sp", OUT[rows, :], xo[b][:], reads=[("xo", b)], writes=["OUT"])
        P.pop_scope()
    P.pop_scope()


_prog = {}


def build_fused(ins0):
    P = Prog()
    nc = P.nc
    d = {}
    for k, v in ins0.items():
        d[k] = nc.dram_tensor(k, list(v.shape), F32, kind="ExternalInput").ap()
    out = nc.dram_tensor("out", [L, D], F32, kind="ExternalOutput").ap()
    I = lambda n, s, dt=F32: nc.dram_tensor(n, s, dt, kind="Internal").ap()
    HB0 = I("HB0", [L, D]); HB1 = I("HB1", [L, D]); FM = I("FM", [NFM, L]); TM = I("TM", [L, NTM])
    OF = I("OFs", [L, 768]); OB = I("OBs", [L, 768]); YH = I("YHs", [L, 256]); h1s = I("h1s", [L, D]); scr = I("scr", [512, 2 * L], BF16)
    for layer in range(2):
        dl = {k[3:]: v for k, v in d.items() if k.startswith("l%d_" % layer)}
        dl.update({k: v for k, v in d.items() if k.startswith("c_") or k.startswith("h_")})
        xin = d["x"] if layer == 0 else HB1
        stage_A2(P, layer == 0, xin, dl["win"], d["gb"], d["c_id"], HB0, FM, TM)
        scans_all(P, layer, dl, FM, TM, OF, OB)
        hyena_all(P, L, dl, FM, YH, scr)
        stage_C2(P, dl, HB0 if layer == 0 else HB1, TM, OF, OB, YH, h1s, HB1 if layer == 0 else out)
    return P.finish(["OUT"])


def host_inputs(inp, b):
    f32 = np.float32
    rep = lambda v: np.ascontiguousarray(np.tile(np.asarray(v, f32)[None, :], (128, 1)))
    m = {"x": np.ascontiguousarray(inp["x"][b]), "gb": rep(np.concatenate([inp["ln_in_g"], inp["ln_in_b"]]))}
    m.update(scan_consts())
    m.update(hyena_consts2(L))
    for l in range(2):
        p = "l%d_" % l
        w = inp["w_in"][l]
        m[p + "win"] = np.ascontiguousarray(np.concatenate([w[:, FM_COLS], w[:, TM_COLS]], 1))
        wa2, ba, lbl = inp["gla_wa2"][l], inp["gla_ba"][l], inp["hg_lb_logits"]
        m[p + "gwa"] = np.stack([np.ascontiguousarray(wa2[:, :, h * 32:(h + 1) * 32].transpose(1, 0, 2)) for h in range(6)])
        m[p + "gba"] = np.stack([np.ascontiguousarray(ba[:, h * 32:(h + 1) * 32].T) for h in range(6)])
        m[p + "hlb"] = np.stack([np.ascontiguousarray(lbl[:, :, h * 64:(h + 1) * 64].reshape(4, 64).T) for h in range(6)])
        cw, cb, hb_ = inp["hy_conv_w"][l], inp["hy_conv_b"][l], inp["hy_bias"][l]
        m[p + "cw"] = np.ascontiguousarray(np.stack([np.stack([cw[:, k * 256 + lt * 128:k * 256 + (lt + 1) * 128].T for k in range(3)], 1).reshape(128, 9) for lt in range(2)]))
        m[p + "cb"] = np.ascontiguousarray(np.stack([np.stack([cb[k * 256 + lt * 128:k * 256 + (lt + 1) * 128] for k in range(3)], 1) for lt in range(2)]))
        m[p + "bias"] = np.ascontiguousarray(np.stack([np.stack([hb_[o, lt * 128:(lt + 1) * 128] for o in range(2)], 1) for lt in range(2)]))
        m[p + "w1"] = inp["hy_w1"][l]
        m[p + "b1f"] = np.ascontiguousarray(np.stack([inp["hy_b1"][l], inp["hy_b2"][l][0], inp["hy_b2"][l][1], inp["hy_freq"][l]], 1))
        m[p + "w2"] = np.ascontiguousarray(inp["hy_w2"][l].transpose(1, 0, 2))
        m[p + "w3"] = np.ascontiguousarray(inp["hy_w3"][l].reshape(64, 2, 2, 256).transpose(0, 2, 1, 3))
        m[p + "gn"] = rep(np.concatenate([inp["gla_norm_g"][l], inp["hg_norm_g"][l]]))
        m[p + "wout"] = inp["w_out"][l]
        m[p + "ln1"] = rep(np.concatenate([inp["ln1_g"][l], inp["ln1_b"][l]]))
        m[p + "ln2"] = rep(np.concatenate([inp["ln2_g"][l], inp["ln2_b"][l]]))
        m[p + "wr"] = np.ascontiguousarray(np.concatenate([inp["moe_wr_g"][l], inp["moe_wr_e"][l]], 1))
        m[p + "br"] = rep(np.concatenate([inp["moe_br_g"][l], inp["moe_br_e"][l]]))
        m[p + "wg"], m[p + "wu"], m[p + "wd"] = inp["moe_w_gate"][l], inp["moe_w_up"][l], inp["moe_w_down"][l]
    return {k: np.ascontiguousarray(np.asarray(v, f32)) for k, v in m.items()}


def kernel(**inp):
    inp = {k: np.asarray(v) for k, v in inp.items()}
    in_maps = [host_inputs(inp, c % 4) for c in range(4)]
    in_maps = in_maps + in_maps
    if "nc" not in _prog:
        _prog["nc"] = build_fused(in_maps[0])
    res = run_bass_kernel_spmd(_prog["nc"], in_maps, core_ids=list(range(NCORES))).results
    return np.stack([res[b]["out"] for b in range(4)], 0).astype(np.float32)
```

```python
import numpy as np
import concourse.bass as bass
import concourse.mybir as mybir
from concourse.bass_utils import run_bass_kernel_spmd
from contextlib import ExitStack

F32 = mybir.dt.float32
BF16 = mybir.dt.bfloat16
I32 = mybir.dt.int32
AF = mybir.ActivationFunctionType
ALU = mybir.AluOpType
AX = mybir.AxisListType


class Prog:
    def __init__(self, n_dma_sems=20):
        self.nc = bass.Bass("TRN2", target_bir_lowering=False)
        self.es = ExitStack()
        nc = self.nc
        self.eng = {"pe": nc.tensor, "dve": nc.vector, "act": nc.scalar,
                    "pool": nc.gpsimd, "sp": nc.sync}
        self.sem = {}
        for e in self.eng:
            self.sem[("E", e)] = self.es.enter_context(nc.semaphore("s_" + e))
        self.cnt = {("E", e): 0 for e in self.eng}
        self.dpool = {}
        self.dnext = {}
        for q in ("sp", "pool", "act"):
            self.dpool[q] = []
            for i in range(n_dma_sems if q != "act" else 6):
                k = ("D", q, i)
                self.sem[k] = self.es.enter_context(nc.semaphore("d_%s_%d" % (q, i)))
                self.cnt[k] = 0
                self.dpool[q].append(k)
            self.dnext[q] = 0
        self.cur = self.es
        self.uid = 0
        self.scopes = []
        self.seen = {}
        self.lastw = {}
        self.readers = {}
        self.nins = 0

    def sb(self, name, shape, dt=F32):
        self.uid += 1
        return self.cur.enter_context(self.nc.sbuf_tensor("sb%d_%s" % (self.uid, name), list(shape), dt))

    def ps(self, name, shape, dt=F32):
        self.uid += 1
        return self.cur.enter_context(self.nc.psum_tensor("ps%d_%s" % (self.uid, name), list(shape), dt))

    def push_scope(self):
        self.scopes.append(self.cur)
        self.cur = ExitStack()

    def pop_scope(self):
        self.barrier()
        self.cur.close()
        self.cur = self.scopes.pop()

    def barrier(self):
        deps = [(sk, v) for sk, v in self.cnt.items() if v > 0]
        for e in self.eng:
            self._wait(e, deps)

    def dram(self, name, shape, dt=F32, kind="Internal"):
        return self.nc.dram_tensor(name, list(shape), dt, kind=kind).ap()

    def _deps(self, reads, writes):
        deps = []
        for k in reads:
            if k in self.lastw:
                deps.append(self.lastw[k])
        for k in writes:
            if k in self.lastw:
                deps.append(self.lastw[k])
            deps.extend(self.readers.get(k, {}).items())
        return deps

    def _wait(self, e, deps):
        best = {}
        for sk, v in deps:
            if sk == ("E", "pe") and e == "pe":
                continue
            if self.seen.get((e, sk), 0) >= v:
                continue
            if best.get(sk, 0) < v:
                best[sk] = v
        for sk, v in best.items():
            self.eng[e].wait_ge(self.sem[sk], v)
            self.seen[(e, sk)] = v

    def _record(self, tk, reads, writes):
        sk, v = tk
        for k in reads:
            self.readers.setdefault(k, {})[sk] = v
        for k in writes:
            self.lastw[k] = tk
            self.readers[k] = {}

    def op(self, e, fn, reads=(), writes=()):
        self._wait(e, self._deps(reads, writes))
        ins = fn(self.eng[e])
        sk = ("E", e)
        self.cnt[sk] += 1
        ins.then_inc(self.sem[sk], 1)
        self._record((sk, self.cnt[sk]), reads, writes)
        self.nins += 1
        return ins

    def dma(self, q, out, in_, reads=(), writes=(), **kw):
        deps = self._deps(reads, writes)
        sk = self.dpool[q][self.dnext[q] % len(self.dpool[q])]
        self.dnext[q] += 1
        if self.cnt[sk] > 0:
            deps.append((sk, self.cnt[sk]))
        self._wait(q, deps)
        ins = self.eng[q].dma_start(out=out, in_=in_, **kw)
        self.cnt[sk] += 16
        ins.then_inc(self.sem[sk], 16)
        self._record((sk, self.cnt[sk]), reads, writes)
        self.nins += 1
        return ins

    def finish(self, out_keys):
        deps = []
        for k in out_keys:
            if k in self.lastw:
                deps.append(self.lastw[k])
        for q in self.dpool:
            for sk in self.dpool[q]:
                if self.cnt[sk] > 0:
                    deps.append((sk, self.cnt[sk]))
        self._wait("sp", deps)
        self.es.close()
        return self.nc


def _coll(self, kind, groups, out, in_, reads=(), writes=()):
    q = "pool"
    deps = self._deps(reads, writes)
    sk = self.dpool[q][self.dnext[q] % len(self.dpool[q])]
    self.dnext[q] += 1
    if self.cnt[sk] > 0:
        deps.append((sk, self.cnt[sk]))
    self._wait(q, deps)
    ins = self.nc.gpsimd.collective_compute(kind, ALU.bypass, replica_groups=groups, ins=[in_], outs=[out])
    self.cnt[sk] += 16
    ins.then_inc(self.sem[sk], 16)
    self._record((sk, self.cnt[sk]), reads, writes)
    self.nins += 1
    return ins


Prog.coll = _coll

import math
import numpy as np

import os
STOPAT = int(os.environ.get('STOPAT', '99'))
CH = 64
HF = (lambda h: 0) if os.environ.get("HF0") else (lambda h: h)
SEG = 1024
NCS = SEG // CH
NPS = SEG // 128


def scan_consts():
    rm = np.ones((128, SEG), np.float32)
    rm[:, ::CH] = 0.0
    j = np.arange(64)[:, None]
    i = np.arange(64)[None, :]
    mf = (i >= j).astype(np.float32)
    mb = (i <= j).astype(np.float32)
    mf = np.tile(np.concatenate([mf, mf], 0), (1, 8))
    mb = np.tile(np.concatenate([mb, mb], 0), (1, 8))
    return {"c_rm": rm, "c_mf": mf, "c_mb": mb, "c_id": np.eye(128, dtype=np.float32)}


class ScanCtx:
    def __init__(self, P, L, dd=None, cid=0, share=None):
        self.P = P
        self.L = L
        self.id = "c%d" % cid
        nc = P.nc
        c = {}
        for nm, shp in (("c_rm", [128, SEG]), ("c_mf", [128, 512]), ("c_mb", [128, 512]), ("c_id", [128, 128])):
            if share is not None:
                continue
            c[nm] = dd[nm] if dd is not None else nc.dram_tensor(nm, shp, F32, kind="ExternalInput").ap()
        if share is None:
            self.rm = P.sb("rm", [128, SEG], F32)
            self.mf = P.sb("mf", [128, 512], F32)
            self.mb = P.sb("mb", [128, 512], F32)
            idf = P.sb("idf", [128, 128], F32)
            self.idb = P.sb("idb", [128, 128], BF16)
            P.dma("sp", self.rm[:], c["c_rm"], writes=["rm"])
            P.dma("sp", self.mf[:], c["c_mf"], writes=["mf"])
            P.dma("sp", self.mb[:], c["c_mb"], writes=["mb"])
            P.dma("sp", idf[:], c["c_id"], writes=["idf"])
            P.op("dve", lambda e: e.tensor_copy(out=self.idb[:], in_=idf[:]), reads=["idf"], writes=["idb"])
        else:
            self.rm, self.mf, self.mb, self.idb = share.rm, share.mf, share.mb, share.idb
        f = lambda n, w=SEG, dt=F32, p=64: P.sb(n, [p, w], dt)
        self.q = f("s_q"); self.k = f("s_k"); self.x = f("s_x"); self.g = f("s_g")
        self.Pc = f("s_P"); self.Gh = f("s_Gh"); self.t1 = f("s_t1"); self.t2 = f("s_t2")
        self.qt = f("s_qt", SEG, BF16); self.kt = f("s_kt", SEG, BF16)
        self.ga = P.sb("s_ga", [16, SEG], F32)
        self.v32 = P.sb("s_v32", [64, NCS, 64], F32)
        self.vb = P.sb("s_vb", [64, NCS, 64], BF16)
        self.ktok = P.sb("s_ktok", [64, NCS, 64], BF16)
        self.AT = P.sb("s_AT", [64, 64, NCS], F32)
        self.D0 = P.sb("s_D0", [64, 64, NCS], F32)
        self.Sall = P.sb("s_Sall", [64, 64, NCS], F32)
        self.Sp = P.sb("s_Sp", [64, NCS, 64], BF16)
        self.S = P.sb("s_S", [64, 64], F32)
        self.cc = P.sb("s_cc", [64, 3, NCS], F32)
        self.ct = P.sb("s_ct", [64, 2, NCS], F32)
        self.Pm = P.sb("s_Pm", [64, 8, 64], BF16)
        self.o = P.sb("s_o", [64, NCS, 64], F32)
        self.wa = P.sb("s_wa", [16, 2, 32], F32)
        self.ba = P.sb("s_ba", [64, 4], F32)
        self.lb = P.sb("s_lb", [64, 10], F32)
        P.op("dve", lambda e: e.memset(self.kt[:], 0.0), writes=[(self.id, "kt")])
        self.p_g = P.ps("p_g", [64, 512], F32)
        self.p_t = self.p_g[:].bitcast(BF16)
        self.p_A = [P.ps("p_A", [64, 512], F32)]
        self.p_s = P.ps("p_s", [64, 512], F32)
        self.p_o = [P.ps("p_o", [64, 512], F32)]


def scan_dir(C, kind, layer, K, d, qT_d, kT_d, xg_d, v_d, o_d, prm, rkeys=()):
    P0 = C.P
    GLOBALK = ("rm", "mf", "mb", "idb", "FM", "TM", "o_out")

    class _PX:
        nins = 0

        @staticmethod
        def kx(k):
            return k if (isinstance(k, str) and k in GLOBALK) else (C.id, k)

        @staticmethod
        def op(e, fn, reads=(), writes=()):
            return P0.op(e, fn, [_PX.kx(k) for k in reads], [_PX.kx(k) for k in writes])

        @staticmethod
        def dma(q, out, in_, reads=(), writes=(), **kw):
            return P0.dma(q, out, in_, [_PX.kx(k) for k in reads], [_PX.kx(k) for k in writes], **kw)

    P = _PX
    L = C.L
    nseg = L // SEG
    rev = (d == 1)
    segs = list(range(nseg))
    if rev:
        segs = segs[::-1]
    S = C.S
    P.op("dve", lambda e: e.memset(S[0:K, :], 0.0), writes=["S"])
    gscale = (1.0 / 16.0) if kind == "gla" else 1.0
    sgn = -gscale if kind == "gla" else 1.0
    qbias = math.log(float(K) ** -0.5) if kind == "gla" else 0.0
    lbzero = (kind == "gla") or (layer == 0)
    for sg in segs:
        t0 = sg * SEG
        sl = slice(t0, t0 + SEG)
        rk = list(rkeys)
        P.dma("sp", C.q[0:K, :], qT_d[:, sl], reads=rk, writes=["q"])
        P.dma("sp", C.v32[:], v_d[sl, :].rearrange("(n p) v -> p n v", p=64), reads=rk, writes=["v32"])
        P.op("act", lambda e: e.activation(out=C.vb[:].rearrange("p n v -> p (n v)"), in_=C.v32[:].rearrange("p n v -> p (n v)"), func=AF.Copy), reads=["v32"], writes=["vb"])
        if kind == "gla":
            P.dma("sp", C.k[0:K, :], kT_d[:, sl], reads=rk, writes=["k"])
            P.dma("sp", C.ga[:], xg_d[:, sl], reads=rk, writes=["ga"])
            for c in range(SEG // 512):
                P.op("pe", lambda e, c=c: e.matmul(C.p_g[0:K, :], lhsT=prm["wa"][:, d, :], rhs=C.ga[:, c * 512:(c + 1) * 512], start=True, stop=True),
                     reads=["ga", "wa"], writes=["p_g"])
                P.op("act", lambda e, c=c: e.activation(out=C.t1[0:K, c * 512:(c + 1) * 512], in_=C.p_g[0:K, :], func=AF.Exp, scale=-1.0, bias=prm["nba"][0:K, d:d + 1]),
                     reads=["p_g", "nba"], writes=[("t1", c)])
            t1k = [("t1", c) for c in range(SEG // 512)]
            P.op("act", lambda e: e.activation(out=C.g[0:K, :], in_=C.t1[0:K, :], func=AF.Ln, bias=1.0, scale=1.0), reads=t1k, writes=["g"])
        else:
            P.dma("sp", C.x[0:K, :], xg_d[:, sl], reads=rk, writes=["x"])
            P.op("act", lambda e: e.activation(out=C.q[0:K, :], in_=C.q[0:K, :], func=AF.Silu), reads=["q"], writes=["q"])
            P.op("act", lambda e: e.activation(out=C.t2[0:K, :], in_=C.x[0:K, :], func=AF.Sigmoid), reads=["x"], writes=["t2"])
            if lbzero:
                P.op("dve", lambda e: e.tensor_scalar(out=C.k[0:K, :], in0=C.t2[0:K, :], scalar1=-1.0, scalar2=1.0, op0=ALU.mult, op1=ALU.add), reads=["t2"], writes=["k"])
                P.op("act", lambda e: e.activation(out=C.g[0:K, :], in_=C.t2[0:K, :], func=AF.Ln), reads=["t2"], writes=["g"])
            else:
                P.op("dve", lambda e: e.tensor_scalar(out=C.k[0:K, :], in0=C.t2[0:K, :], scalar1=prm["noml"][0:K, d:d + 1], scalar2=prm["oml"][0:K, d:d + 1],
                                                      op0=ALU.mult, op1=ALU.add), reads=["t2", "oml"], writes=["k"])
                P.op("dve", lambda e: e.tensor_scalar(out=C.t2[0:K, :], in0=C.t2[0:K, :], scalar1=prm["oml"][0:K, d:d + 1], scalar2=prm["lb"][0:K, d:d + 1],
                                                      op0=ALU.mult, op1=ALU.add), reads=["t2", "oml"], writes=["t2"])
                P.op("act", lambda e: e.activation(out=C.g[0:K, :], in_=C.t2[0:K, :], func=AF.Ln), reads=["t2"], writes=["g"])
        yield
        P.op("dve", lambda e: e.tensor_tensor_scan(out=C.Pc[0:K, :], data0=C.rm[0:K, :], data1=C.g[0:K, :], initial=0.0, op0=ALU.mult, op1=ALU.add),
             reads=["rm", "g"], writes=["Pc"])
        Pv = C.Pc[0:K, :].rearrange("p (n c) -> p n c", c=CH)
        Ghv = C.Gh[0:K, :].rearrange("p (n c) -> p n c", c=CH)
        gv = C.g[0:K, :].rearrange("p (n c) -> p n c", c=CH)
        cc = C.cc
        ct = C.ct
        if not rev:
            mid = 31
            P.op("dve", lambda e: e.tensor_tensor(out=Ghv, in0=Pv, in1=Pv[:, :, mid:mid + 1].to_broadcast([K, NCS, CH]), op=ALU.subtract),
                 reads=["Pc"], writes=["Gh"])
            P.op("act", lambda e: e.activation(out=cc[0:K, 0, :], in_=Pv[:, :, mid], func=AF.Exp, scale=sgn), reads=["Pc"], writes=["cc0"])
            P.op("act", lambda e: e.activation(out=cc[0:K, 1, :], in_=Pv[:, :, CH - 1], func=AF.Exp, scale=sgn), reads=["Pc"], writes=["cc1"])
            P.op("act", lambda e: e.activation(out=cc[0:K, 2, :], in_=Ghv[:, :, CH - 1], func=AF.Exp, scale=sgn), reads=["Gh"], writes=["cc2"])
        else:
            mid = 32
            Ev = C.t1[0:K, :].rearrange("p (n c) -> p n c", c=CH)
            P.op("dve", lambda e: e.tensor_tensor(out=C.t1[0:K, :], in0=C.Pc[0:K, :], in1=C.g[0:K, :], op=ALU.subtract), reads=["Pc", "g"], writes=["t1"])
            P.op("dve", lambda e: e.tensor_tensor(out=Ghv, in0=Ev[:, :, mid:mid + 1].to_broadcast([K, NCS, CH]), in1=Ev, op=ALU.subtract),
                 reads=["t1"], writes=["Gh"])
            P.op("dve", lambda e: e.tensor_tensor(out=ct[0:K, 0, :], in0=Pv[:, :, CH - 1], in1=Ev[:, :, mid], op=ALU.subtract), reads=["Pc", "t1"], writes=["ct"])
            P.op("act", lambda e: e.activation(out=cc[0:K, 0, :], in_=ct[0:K, 0, :], func=AF.Exp, scale=sgn), reads=["ct"], writes=["cc0"])
            P.op("act", lambda e: e.activation(out=cc[0:K, 1, :], in_=Pv[:, :, CH - 1], func=AF.Exp, scale=sgn), reads=["Pc"], writes=["cc1"])
            P.op("act", lambda e: e.activation(out=cc[0:K, 2, :], in_=Ev[:, :, mid], func=AF.Exp, scale=sgn), reads=["t1"], writes=["cc2"])
        yield
        P.op("act", lambda e: e.activation(out=C.t2[0:K, :], in_=C.Gh[0:K, :], func=AF.Exp, scale=sgn, bias=qbias), reads=["Gh"], writes=["t2"])
        P.op("pool", lambda e: e.tensor_tensor(out=C.qt[0:K, :], in0=C.q[0:K, :], in1=C.t2[0:K, :], op=ALU.mult), reads=["q", "t2"], writes=["qt"])
        P.op("act", lambda e: e.activation(out=C.t1[0:K, :], in_=C.Gh[0:K, :], func=AF.Exp, scale=-sgn), reads=["Gh"], writes=["t1"])
        P.op("dve", lambda e: e.tensor_tensor(out=C.kt[0:K, :], in0=C.k[0:K, :], in1=C.t1[0:K, :], op=ALU.mult), reads=["k", "t1"], writes=["kt"])
        yield
        for half in range(NCS // 16):
            for cl in range(16):
                n = half * 16 + cl
                P.op("pe", lambda e, cl=cl, n=n: e.transpose(out=C.p_t[:, cl * 64:(cl + 1) * 64], in_=C.kt[0:64, n * 64:(n + 1) * 64], identity=C.idb[0:64, 0:64]),
                     reads=["kt", "idb"], writes=["p_g"])
            P.op("act", lambda e, half=half: e.activation(out=C.ktok[:, half * 16:(half + 1) * 16, :].rearrange("p n k -> p (n k)"), in_=C.p_t[:], func=AF.Copy),
                 reads=["p_g"], writes=[("ktok", half)])
        yield
        for grp in range(NCS // 8):
            pa = C.p_A[0]
            pak = ("p_A", 0)
            for ci in range(8):
                n = grp * 8 + ci
                P.op("pe", lambda e, ci=ci, n=n: e.matmul(pa[0:K, ci * 64:(ci + 1) * 64], lhsT=C.ktok[:, n, 0:K], rhs=C.vb[:, n, :], start=True, stop=True),
                     reads=[("ktok", n // 16), "vb"], writes=[pak])
            P.op("dve", lambda e, grp=grp: e.tensor_tensor(out=C.AT[0:K, :, grp * 8:(grp + 1) * 8].rearrange("p v n -> p n v"), in0=pa[0:K, :].rearrange("p (n v) -> p n v", v=64),
                                                           in1=cc[0:K, 2, grp * 8:(grp + 1) * 8].unsqueeze(2).to_broadcast([K, 8, 64]), op=ALU.mult),
                 reads=[pak, "cc2"], writes=[("A", grp)])
        yield
        nf = NCS - 1 if rev else 0
        nl = 0 if rev else NCS - 1
        Ak = [("A", g_) for g_ in range(NCS // 8)]
        P.op("dve", lambda e: e.tensor_copy(out=C.D0[0:K, :, :], in_=cc[0:K, 1, :].unsqueeze(1).to_broadcast([K, 64, NCS])), reads=["cc1"], writes=["D0"])
        P.op("dve", lambda e: e.memset(C.D0[0:K, :, nf:nf + 1], 0.0), reads=["D0"], writes=["D0"])
        P.op("dve", lambda e: e.scalar_tensor_tensor(out=C.AT[0:K, :, nf], in0=S[0:K, :], scalar=cc[0:K, 1, nf:nf + 1], in1=C.AT[0:K, :, nf], op0=ALU.mult, op1=ALU.add),
             reads=["S", "cc1"] + Ak, writes=Ak)
        P.op("act", lambda e: e.activation(out=C.Sp[0:K, nf, :], in_=S[0:K, :], func=AF.Copy, scale=cc[0:K, 0, nf:nf + 1]), reads=["S", "cc0"], writes=[("Sp", "f")])
        AF_ = C.AT[0:K, :, :].rearrange("p v n -> p (v n)")
        DF_ = C.D0[0:K, :, :].rearrange("p v n -> p (v n)")
        SF_ = C.Sall[0:K, :, :].rearrange("p v n -> p (v n)")
        if rev:
            AF_, DF_, SF_ = AF_[:, ::-1], DF_[:, ::-1], SF_[:, ::-1]
        P.op("dve", lambda e: e.tensor_tensor_scan(out=SF_, data0=DF_, data1=AF_, initial=0.0, op0=ALU.mult, op1=ALU.add), reads=Ak + ["D0"], writes=["Sall"])
        if not rev:
            P.op("dve", lambda e: e.tensor_tensor(out=C.Sp[0:K, 1:NCS, :], in0=C.Sall[0:K, :, 0:NCS - 1].rearrange("p v n -> p n v"),
                                                   in1=cc[0:K, 0, 1:NCS].unsqueeze(2).to_broadcast([K, NCS - 1, 64]), op=ALU.mult), reads=["Sall", "cc0"], writes=[("Sp", "r")])
        else:
            P.op("dve", lambda e: e.tensor_tensor(out=C.Sp[0:K, 0:NCS - 1, :], in0=C.Sall[0:K, :, 1:NCS].rearrange("p v n -> p n v"),
                                                   in1=cc[0:K, 0, 0:NCS - 1].unsqueeze(2).to_broadcast([K, NCS - 1, 64]), op=ALU.mult), reads=["Sall", "cc0"], writes=[("Sp", "r")])
        P.op("dve", lambda e: e.tensor_copy(out=S[0:K, :], in_=C.Sall[0:K, :, nl]), reads=["Sall", "S"], writes=["S"])
        yield
        mask = C.mb if rev else C.mf
        for grp in range(NCS // 8):
            for cl in range(8):
                n = grp * 8 + cl
                P.op("pe", lambda e, cl=cl, n=n: e.matmul(C.p_s[:, cl * 64:(cl + 1) * 64], lhsT=C.kt[0:K, n * 64:(n + 1) * 64],
                                                          rhs=C.qt[0:K, n * 64:(n + 1) * 64], start=True, stop=True),
                     reads=["kt", "qt"], writes=["p_s"])
            P.op("dve", lambda e: e.tensor_tensor(out=C.Pm[:].rearrange("p n c -> p (n c)"), in0=C.p_s[:], in1=mask[0:64, :], op=ALU.mult),
                 reads=["p_s", "mf", "mb"], writes=["Pm"])
            po = C.p_o[0]
            pok = ("p_o", 0)
            for cl in range(8):
                n = grp * 8 + cl
                P.op("pe", lambda e, cl=cl, n=n: e.matmul(po[:, cl * 64:(cl + 1) * 64], lhsT=C.Pm[:, cl, :], rhs=C.vb[:, n, :], start=True, stop=False),
                     reads=["Pm", "vb"], writes=[pok])
                P.op("pe", lambda e, cl=cl, n=n: e.matmul(po[:, cl * 64:(cl + 1) * 64], lhsT=C.qt[0:K, n * 64:(n + 1) * 64], rhs=C.Sp[0:K, n, :], start=False, stop=True),
                     reads=["qt", ("Sp", "f"), ("Sp", "r")], writes=[pok])
            P.op("act", lambda e, grp=grp: e.activation(out=C.o[:, grp * 8:(grp + 1) * 8, :].rearrange("p n v -> p (n v)"), in_=po[:], func=AF.Copy),
                 reads=[pok], writes=[("o", grp)])
            yield
        P.dma("sp", o_d[sl, :].rearrange("(n p) v -> p n v", p=64), C.o[:], reads=[("o", g_) for g_ in range(NCS // 8)], writes=["o_out"])

import math
import numpy as np

HY_W = 256
TWO_PI_LO = 6.283185


def hyena_consts(L, core):
    f32 = np.float32
    t = np.linspace(0.0, 1.0, L, dtype=f32)[:, None]
    w = (2.0 * math.pi * np.arange(L, dtype=f32)[:, None] / L).astype(f32)
    bands = np.linspace(1e-4, 15, 16, dtype=f32)[None, :]
    z = np.concatenate([t, np.cos(bands * w), -np.sin(bands * w)], axis=-1).astype(f32)
    max_decay = math.log(1e-2) / 0.3
    min_decay = math.log(1e-2) / 1.5
    deltas = np.linspace(min_decay, max_decay, HY_W, dtype=f32)
    dec = np.exp(-t * np.abs(deltas)).astype(f32)
    tidx = np.concatenate([np.arange(L - 1, -1, -1), np.arange(1, L), [0]])
    zR = np.ascontiguousarray(z[tidx].T)
    zR[:, -1] = 0.0
    ch = np.arange(core * 32, core * 32 + 32)
    d = dec[tidx][:, ch].T
    d[:, -1] = 0.0
    decR = np.ascontiguousarray(np.concatenate([d, d], 0))
    J = np.ascontiguousarray(np.eye(128, dtype=f32)[::-1])
    return {"h_zR": zR, "h_decR": decR.astype(f32), "h_J": J, "h_id": np.eye(128, dtype=f32),
            "h_ones": np.ones((64, 128), f32)}


def hyena_stage(P, L, d):
    NB = L // 128
    NT = 2 * L // 512
    HW = 2 * L - 128
    nc = P.nc
    sb = P.sb
    w1 = sb("h_w1", [33, 64]); b1f = sb("h_b1f", [64, 8]); w2 = sb("h_w2", [64, 2, 64]); w3 = sb("h_w3", [64, 2, 64])
    cw = sb("h_cw", [128, 9]); cb = sb("h_cb", [128, 3]); bias = sb("h_bias", [128, 2])
    Jm = sb("h_Jm", [128, 128]); idf = sb("h_idf", [128, 128]); idb = sb("h_idb", [128, 128], BF16); ones = sb("h_onesb", [64, 128])
    P.dma("sp", w1[:], d["w1"], writes=["h_w1"])
    P.dma("sp", b1f[:, 0:4], d["b1f"], writes=["h_b1f"])
    P.dma("sp", w2[:], d["w2"], writes=["h_w2"])
    P.dma("sp", w3[:], d["w3"], writes=["h_w3"])
    P.dma("sp", cw[:], d["cw"].rearrange("p a b -> p (a b)"), writes=["h_cw"])
    P.dma("sp", cb[:], d["cb"], writes=["h_cb"])
    P.dma("sp", bias[:], d["bias"], writes=["h_bias"])
    P.dma("sp", Jm[:], d["h_J"], writes=["h_J"])
    P.dma("sp", idf[:], d["h_id"], writes=["h_idf"])
    P.dma("sp", ones[:], d["h_ones"], writes=["h_ones"])
    P.op("dve", lambda e: e.tensor_copy(out=idb[:], in_=idf[:]), reads=["h_idf"], writes=["h_idb"])
    P.op("dve", lambda e: e.tensor_scalar(out=b1f[:, 4:5], in0=b1f[:, 3:4], scalar1=1.0 / (2 * math.pi), scalar2=None, op0=ALU.mult), reads=["h_b1f"], writes=["h_A"])
    P.op("dve", lambda e: e.tensor_scalar(out=b1f[:, 5:8], in0=b1f[:, 0:3], scalar1=b1f[:, 4:5], scalar2=None, op0=ALU.mult), reads=["h_b1f", "h_A"], writes=["h_B"])
    pA = P.ps("h_pA", [64, 512])
    zt = [sb("h_zt%d" % i, [33, 512]) for i in range(2)]
    dt_ = [sb("h_dt%d" % i, [64, 512]) for i in range(2)]
    u = sb("h_u", [64, 512]); ui = sb("h_ui", [64, 512], I32); hh = sb("h_hh", [64, 512])
    Rt = sb("h_Rt", [64, 512]); Rb = [sb("h_Rb%d" % i, [64, 512], BF16) for i in range(2)]
    asum = sb("h_asum", [64, NT]); rn = sb("h_rn", [64, 2]); dg = sb("h_dg", [64, 64]); rnb = sb("h_rnb", [128, 64])
    scr = d["scr"]
    for t in range(NT):
        b = t % 2
        P.dma("sp", zt[b][:], d["h_zR"][:, t * 512:(t + 1) * 512], writes=[("zt", b)])
        P.dma("sp", dt_[b][:], d["h_decR"][:, t * 512:(t + 1) * 512], writes=[("dt", b)])
        rhs = zt[b]
        rk = ("zt", b)
        for l in range(3):
            lhsT = w1[:, :] if l == 0 else w2[:, l - 1, :]
            kk = 33 if l == 0 else 64
            P.op("pe", lambda e, lhsT=lhsT, rhs=rhs, kk=kk: e.matmul(pA[:, :], lhsT=lhsT, rhs=rhs[0:kk, :], start=True, stop=True),
                 reads=[rk, "h_w1", "h_w2"], writes=["h_pA"])
            P.op("dve", lambda e, l=l: e.tensor_scalar(out=u[:], in0=pA[:], scalar1=b1f[:, 4:5], scalar2=b1f[:, 5 + l:6 + l], op0=ALU.mult, op1=ALU.add),
                 reads=["h_pA", "h_A", "h_B"], writes=["h_u"])
            P.op("dve", lambda e: e.tensor_copy(out=ui[:], in_=u[:]), reads=["h_u"], writes=["h_ui"])
            P.op("dve", lambda e: e.tensor_tensor(out=u[:], in0=u[:], in1=ui[:], op=ALU.subtract), reads=["h_u", "h_ui"], writes=["h_u"])
            P.op("act", lambda e: e.activation(out=hh[:], in_=u[:], func=AF.Sin, scale=TWO_PI_LO), reads=["h_u"], writes=["h_hh"])
            rhs = hh
            rk = "h_hh"
        dr = 0 if t < NT // 2 else 1
        P.op("pe", lambda e, dr=dr: e.matmul(pA[:, :], lhsT=w3[:, dr, :], rhs=hh[:, :], start=True, stop=True), reads=["h_hh", "h_w3"], writes=["h_pA"])
        P.op("dve", lambda e: e.tensor_tensor(out=Rt[:], in0=pA[:], in1=dt_[b][:], op=ALU.mult), reads=["h_pA", ("dt", b)], writes=["h_Rt"])
        P.op("dve", lambda e, t=t: e.tensor_reduce(out=asum[:, t:t + 1], in_=Rt[:], axis=AX.X, op=ALU.add, apply_absolute_value=True), reads=["h_Rt"], writes=[("asum", t)])
        P.op("act", lambda e: e.activation(out=Rb[b][:], in_=Rt[:], func=AF.Copy), reads=["h_Rt"], writes=[("Rb", b)])
        P.dma("sp", scr[:, t * 512:(t + 1) * 512], Rb[b][:], reads=[("Rb", b)], writes=["scr"])
    P.op("dve", lambda e: e.tensor_reduce(out=rn[:, 0:1], in_=asum[:], axis=AX.X, op=ALU.add), reads=[("asum", t) for t in range(NT)], writes=["h_rn"])
    P.op("dve", lambda e: e.tensor_scalar(out=rn[:, 0:1], in0=rn[:, 0:1], scalar1=1e-12, scalar2=None, op0=ALU.max), reads=["h_rn"], writes=["h_rn"])
    P.op("dve", lambda e: e.reciprocal(out=rn[:, 1:2], in_=rn[:, 0:1]), reads=["h_rn"], writes=["h_rn1"])
    P.op("dve", lambda e: e.tensor_scalar(out=dg[:], in0=idf[0:64, 0:64], scalar1=rn[:, 1:2], scalar2=None, op0=ALU.mult), reads=["h_idf", "h_rn1"], writes=["h_dg"])
    pB = P.ps("h_pB", [128, 512])
    P.op("pe", lambda e: e.matmul(pB[:, 0:64], lhsT=ones[:, :], rhs=dg[:, :], start=True, stop=True), reads=["h_ones", "h_dg"], writes=["h_pB"])
    P.op("dve", lambda e: e.tensor_copy(out=rnb[:], in_=pB[:, 0:64]), reads=["h_pB"], writes=["h_rnb"])
    za = sb("h_za", [128, L]); zc = sb("h_zc", [128, L])
    zb = [sb("h_zb%d" % i, [128, 1024], BF16) for i in range(2)]
    ZT = sb("h_ZT", [128, NB, 128], BF16)
    YT = sb("h_YT", [128, NB, 128])
    xr = YT[:].rearrange("p a b -> p (a b)")
    ytk = [("YT", c) for c in range(32)]
    H = [sb("h_H%d" % i, [128, HW], BF16) for i in range(2)]
    pT = P.ps("h_pT", [128, 1024], BF16)
    pC = [P.ps("h_pC%d" % i, [128, 512]) for i in range(2)]

    def short_conv(idx, dst, dk):
        P.dma("sp", xr, d["u3"][idx], writes=ytk)
        P.op("dve", lambda e: e.tensor_scalar(out=dst[:], in0=xr, scalar1=cw[:, idx * 3 + 1:idx * 3 + 2], scalar2=cb[:, idx:idx + 1], op0=ALU.mult, op1=ALU.add),
             reads=ytk + ["h_cw", "h_cb"], writes=[dk])
        P.op("dve", lambda e: e.scalar_tensor_tensor(out=dst[:, 1:L], in0=xr[:, 0:L - 1], scalar=cw[:, idx * 3:idx * 3 + 1], in1=dst[:, 1:L], op0=ALU.mult, op1=ALU.add),
             reads=ytk + ["h_cw", dk], writes=[dk])
        P.op("dve", lambda e: e.scalar_tensor_tensor(out=dst[:, 0:L - 1], in0=xr[:, 1:L], scalar=cw[:, idx * 3 + 2:idx * 3 + 3], in1=dst[:, 0:L - 1], op0=ALU.mult, op1=ALU.add),
             reads=ytk + ["h_cw", dk], writes=[dk])

    short_conv(0, za, "h_za")
    hq = 0
    for o in range(2):
        for g in range(NB // 8):
            zbb = zb[g % 2]
            zbk = ("h_zb", g % 2)
            P.op("act", lambda e, g=g, zbb=zbb: e.activation(out=zbb[:], in_=za[:, g * 1024:(g + 1) * 1024], func=AF.Copy), reads=["h_za"], writes=[zbk])
            for jj in range(8):
                j = g * 8 + jj
                P.op("pe", lambda e, jj=jj, zbb=zbb: e.transpose(out=pT[:, jj * 128:(jj + 1) * 128], in_=zbb[:, jj * 128:(jj + 1) * 128], identity=idb[:]),
                     reads=[zbk, "h_idb"], writes=["h_pT"])
            P.op("dve", lambda e, g=g: e.tensor_copy(out=ZT[:, g * 8:(g + 1) * 8, :].rearrange("p a b -> p (a b)"), in_=pT[:]), reads=["h_pT"], writes=[("ZT", g)])
        ztk = [("ZT", g) for g in range(NB // 8)]
        short_conv(1 + o, zc, "h_zc")
        for c in range(32):
            lane = o * 32 + c
            hb = hq % 2
            hq += 1
            q = "sp" if hb == 0 else "pool"
            P.dma(q, H[hb][:], bass.AP(scr.tensor, lane * 2 * L, [[1, 128], [1, HW]]), reads=["scr"], writes=[("H", hb)])
            pc = pC[hb]
            pck = ("pC", hb)
            ds = [0] + [x for k in range(1, NB) for x in (k, -k)]
            for n_, dd in enumerate(ds):
                j0, j1 = max(0, -dd), min(NB, NB - dd)
                x0 = L - 128 - 128 * dd
                P.op("pe", lambda e, dd=dd, j0=j0, j1=j1, x0=x0, n_=n_: e.matmul(
                    pc[:, (j0 + dd) * 4:(j1 + dd) * 4].rearrange("p (i b) -> p i b", b=4), lhsT=H[hb][:, x0:x0 + 128],
                    rhs=ZT[:, j0:j1, c * 4:(c + 1) * 4], start=(n_ == 0), stop=(n_ == len(ds) - 1)),
                     reads=[("H", hb)] + ztk, writes=[pck])
            eng = "act" if c % 2 == 0 else "dve"
            if eng == "act":
                P.op("act", lambda e, c=c, lane=lane: e.activation(out=YT[:, :, c * 4:(c + 1) * 4], in_=pc[:, 0:NB * 4].rearrange("p (i b) -> p i b", b=4), func=AF.Copy,
                                                                   scale=rnb[:, lane:lane + 1]), reads=[pck, "h_rnb"], writes=[("YT", c)])
            else:
                P.op("dve", lambda e, c=c, lane=lane: e.tensor_scalar(out=YT[:, :, c * 4:(c + 1) * 4], in0=pc[:, 0:NB * 4].rearrange("p (i b) -> p i b", b=4),
                                                                      scalar1=rnb[:, lane:lane + 1], scalar2=None, op0=ALU.mult), reads=[pck, "h_rnb"], writes=[("YT", c)])
        for g in range(NB // 4):
            for ii in range(4):
                i = g * 4 + ii
                P.op("pe", lambda e, ii=ii, i=i: e.matmul(pB[:, ii * 128:(ii + 1) * 128], lhsT=YT[:, i, :], rhs=Jm[:, :], start=True, stop=True),
                     reads=ytk + ["h_J"], writes=["h_pB"])
            sl = slice(g * 512, (g + 1) * 512)
            P.op("dve", lambda e, sl=sl, o=o: e.scalar_tensor_tensor(out=za[:, sl], in0=za[:, sl], scalar=bias[:, o:o + 1], in1=pB[:, :], op0=ALU.mult, op1=ALU.add),
                 reads=["h_za", "h_bias", "h_pB"], writes=["h_za"])
            P.op("pool", lambda e, sl=sl: e.tensor_tensor(out=za[:, sl], in0=za[:, sl], in1=zc[:, sl], op=ALU.mult), reads=["h_za", "h_zc"], writes=["h_za"])
    P.dma("sp", d["y"], za[:], reads=["h_za"], writes=["h_y"])

import math
import numpy as np

import os
STOPC = int(os.environ.get('STOPC', '99'))
D = 1024
ALPHA = 4 ** 0.25
NEXP = 16
DE = 512


def layer_norm_tile(P, x, xk, st, mv, stk, epst, gb, gbk):
    for c in range(2):
        P.op("dve", lambda e, c=c: e.bn_stats(out=st[:, c, :], in_=x[:, c * 512:(c + 1) * 512]), reads=[xk], writes=[(stk, c)])
    P.op("dve", lambda e: e.bn_aggr(out=mv[:, 0:2], in_=st[:].rearrange("p a b -> p (a b)")), reads=[(stk, 0), (stk, 1)], writes=[(stk, "mv")])
    P.op("act", lambda e: e.activation(out=mv[:, 2:3], in_=mv[:, 1:2], func=AF.Sqrt, bias=epst[:, 0:1], scale=1.0), reads=[(stk, "mv"), "eps"], writes=[(stk, "sd")])
    P.op("dve", lambda e: e.reciprocal(out=mv[:, 3:4], in_=mv[:, 2:3]), reads=[(stk, "sd")], writes=[(stk, "rs")])
    P.op("dve", lambda e: e.tensor_scalar(out=x[:], in0=x[:], scalar1=mv[:, 0:1], scalar2=mv[:, 3:4], op0=ALU.subtract, op1=ALU.mult),
         reads=[xk, (stk, "rs"), (stk, "mv")], writes=[xk])
    P.op("pool", lambda e: e.tensor_tensor(out=x[:], in0=x[:], in1=gb[:, 0:D], op=ALU.mult), reads=[xk, gbk], writes=[xk])
    P.op("pool", lambda e: e.tensor_tensor(out=x[:], in0=x[:], in1=gb[:, D:2 * D], op=ALU.add), reads=[xk, gbk], writes=[xk])


def stage_C(P, NT, d):
    nc = P.nc
    sb, ps = P.sb, P.ps
    T = NT * 128
    epst = sb("epst", [128, 1]); eps6 = sb("eps6", [128, 1])
    P.op("dve", lambda e: e.memset(epst[:], 1e-5), writes=["eps"])
    P.op("dve", lambda e: e.memset(eps6[:], 1e-6), writes=["eps6"])
    idf = sb("c_idf", [128, 128]); idb = sb("c_idb", [128, 128], BF16)
    P.dma("sp", idf[:], d["c_id"], writes=["idf"])
    P.op("dve", lambda e: e.tensor_copy(out=idb[:], in_=idf[:]), reads=["idf"], writes=["idb"])
    h1T = sb("h1T", [128, 8, T], BF16)
    wfull = sb("wfull", [128, NT, 16])
    P.push_scope()
    wout = sb("woutb", [128, 8, D], BF16)
    for kc in range(8):
        P.dma("pool", wout[:, kc, :], d["wout"][kc * 128:(kc + 1) * 128, :], writes=[("wout", kc)])
    woutk = [("wout", kc) for kc in range(8)]
    gn = sb("gn", [128, 768]); ln1 = sb("ln1", [128, 2 * D]); wr = sb("wr", [128, 8, 20]); br = sb("br", [128, 20])
    P.dma("sp", gn[:], d["gn"], writes=["gn"])
    P.dma("sp", ln1[:], d["ln1"], writes=["ln1"])
    P.dma("sp", wr[:], d["wr"].rearrange("(k p) n -> p k n", p=128), writes=["wr"])
    P.dma("sp", br[:], d["br"], writes=["br"])
    NBUF = 2
    OF = [sb("OF%d" % i, [128, 768]) for i in range(NBUF)]
    OB = [sb("OB%d" % i, [128, 768]) for i in range(NBUF)]
    GT = [sb("GT%d" % i, [128, 768]) for i in range(NBUF)]
    mix = [sb("mix%d" % i, [128, D]) for i in range(NBUF)]
    mixb = [sb("mixb%d" % i, [128, D], BF16) for i in range(NBUF)]
    mixT = [sb("mixT%d" % i, [128, 8, 128], BF16) for i in range(NBUF)]
    ht = [sb("ht%d" % i, [128, D]) for i in range(NBUF)]
    sq = sb("sq", [128, 768]); ss = sb("ss", [128, 12]); rs = sb("rs_", [128, 12])
    st = sb("st", [128, 2, 6]); mv = sb("mv", [128, 4])
    h1T32 = sb("h1T32", [128, 8, 128])
    lg = sb("lg", [128, 20]); rt = sb("rt", [128, 64])
    pT = ps("pT", [128, 1024], BF16)
    pO = [ps("pO%d" % i, [128, 512]) for i in range(2)]
    pX = [ps("pX%d" % i, [128, 512]) for i in range(2)]
    pL = ps("pL", [128, 512])
    for t in range(NT):
        b = t % NBUF
        rows = slice(t * 128, (t + 1) * 128)
        kOF, kOB, kGT, kmix, kmixb, kmixT, kht = ("OF", b), ("OB", b), ("GT", b), ("mix", b), ("mixb", b), ("mixT", b), ("ht", b)
        P.dma("sp", OF[b][:], d["OF"][rows, :], writes=[kOF])
        P.dma("sp", OB[b][:], d["OB"][rows, :], writes=[kOB])
        P.dma("sp", GT[b][:], d["GT"][rows, :], writes=[kGT])
        P.dma("sp", mix[b][:, 768:1024], d["YH"][rows, :], writes=[(kmix, "hy")])
        P.dma("sp", ht[b][:], d["h"][rows, :], writes=[kht])
        P.op("pool", lambda e: e.tensor_tensor(out=OF[b][:], in0=OF[b][:], in1=OB[b][:], op=ALU.add), reads=[kOF, kOB], writes=[kOF])
        P.op("dve", lambda e: e.tensor_tensor(out=sq[:], in0=OF[b][:], in1=OF[b][:], op=ALU.mult), reads=[kOF], writes=["sq"])
        P.op("dve", lambda e: e.tensor_reduce(out=ss[:], in_=sq[:].rearrange("p (h v) -> p h v", v=64), axis=AX.X, op=ALU.add), reads=["sq"], writes=["ss"])
        P.op("act", lambda e: e.activation(out=rs[:], in_=ss[:], func=AF.Sqrt, bias=eps6[:, 0:1], scale=1.0 / 64.0), reads=["ss", "eps6"], writes=["rs"])
        P.op("dve", lambda e: e.reciprocal(out=rs[:], in_=rs[:]), reads=["rs"], writes=["rs"])
        P.op("dve", lambda e: e.tensor_tensor(out=OF[b][:].rearrange("p (h v) -> p h v", v=64), in0=OF[b][:].rearrange("p (h v) -> p h v", v=64),
                                              in1=rs[:].unsqueeze(2).to_broadcast([128, 12, 64]), op=ALU.mult), reads=[kOF, "rs"], writes=[kOF])
        P.op("pool", lambda e: e.tensor_tensor(out=OF[b][:], in0=OF[b][:], in1=gn[:], op=ALU.mult), reads=[kOF, "gn"], writes=[kOF])
        P.op("act", lambda e: e.activation(out=sq[:], in_=GT[b][:], func=AF.Exp, scale=-1.0), reads=[kGT], writes=["sq"])
        P.op("dve", lambda e: e.tensor_scalar(out=sq[:], in0=sq[:], scalar1=1.0, scalar2=None, op0=ALU.add), reads=["sq"], writes=["sq"])
        P.op("dve", lambda e: e.reciprocal(out=sq[:], in_=sq[:]), reads=["sq"], writes=["sq"])
        P.op("pool", lambda e: e.tensor_tensor(out=sq[:, 0:384], in0=sq[:, 0:384], in1=GT[b][:, 0:384], op=ALU.mult), reads=["sq", kGT], writes=["sq"])
        P.op("dve", lambda e: e.tensor_tensor(out=mix[b][:, 0:768], in0=OF[b][:], in1=sq[:], op=ALU.mult), reads=[kOF, "sq"], writes=[(kmix, "a")])
        P.op("act", lambda e: e.activation(out=mixb[b][:], in_=mix[b][:], func=AF.Copy), reads=[(kmix, "a"), (kmix, "hy")], writes=[kmixb])
        if STOPC <= 1:
            continue
        for kc in range(8):
            P.op("pe", lambda e, kc=kc: e.transpose(out=pT[:, kc * 128:(kc + 1) * 128], in_=mixb[b][:, kc * 128:(kc + 1) * 128], identity=idb[:]),
                 reads=[kmixb, "idb"], writes=["pT"])
        P.op("dve", lambda e: e.tensor_copy(out=mixT[b][:].rearrange("p a b -> p (a b)"), in_=pT[:]), reads=["pT"], writes=[kmixT])
        for hf in range(2):
            for kc in range(8):
                P.op("pe", lambda e, kc=kc, hf=hf: e.matmul(pO[hf][:, :], lhsT=mixT[b][:, kc, :], rhs=wout[:, kc, hf * 512:(hf + 1) * 512], start=(kc == 0), stop=(kc == 7)),
                     reads=[kmixT] + woutk, writes=[("pO", hf)])
            P.op("dve", lambda e, hf=hf: e.scalar_tensor_tensor(out=ht[b][:, hf * 512:(hf + 1) * 512], in0=ht[b][:, hf * 512:(hf + 1) * 512], scalar=ALPHA,
                                                                in1=pO[hf][:, :], op0=ALU.mult, op1=ALU.add), reads=[kht, ("pO", hf)], writes=[kht])
        layer_norm_tile(P, ht[b], kht, st, mv, "st", epst, ln1, "ln1")
        P.dma("sp", d["h1s"][rows, :], ht[b][:], reads=[kht], writes=["h1s"])
        if STOPC <= 2:
            continue
        for kc in range(8):
            P.op("pe", lambda e, kc=kc: e.matmul(pX[kc // 4][:, (kc % 4) * 128:(kc % 4 + 1) * 128], lhsT=ht[b][:, kc * 128:(kc + 1) * 128], rhs=idf[:], start=True, stop=True),
                 reads=[kht, "idf"], writes=[("pX", kc // 4)])
        for hf in range(2):
            if os.environ.get("NOEVAC") == "1":
                continue
            if os.environ.get("NOEVAC") != "act":
                P.op("act", lambda e, hf=hf: e.activation(out=h1T32[:, hf * 4:(hf + 1) * 4, :].rearrange("p a b -> p (a b)"), in_=pX[hf][:, :], func=AF.Copy),
                     reads=[("pX", hf)], writes=[("h1T32", hf)])
            if os.environ.get("NOEVAC") == "dve":
                continue
            P.op("dve", lambda e, hf=hf: e.tensor_copy(out=h1T[:, hf * 4:(hf + 1) * 4, rows], in_=pX[hf][:, :].rearrange("p (a b) -> p a b", b=128)),
                 reads=[("pX", hf), ("h1T32", hf)], writes=[("h1T", t)])
        if STOPC <= 3:
            continue
        for kc in range(8):
            P.op("pe", lambda e, kc=kc: e.matmul(pL[:, 0:20], lhsT=h1T32[:, kc, :], rhs=wr[:, kc, :], start=(kc == 0), stop=(kc == 7)),
                 reads=[("h1T32", kc // 4), "wr"], writes=["pL"])
        P.op("dve", lambda e: e.tensor_tensor(out=lg[:], in0=pL[:, 0:20], in1=br[:], op=ALU.add), reads=["pL", "br"], writes=["lg"])
        if STOPC <= 4:
            continue
        R_ = lambda a, b_: rt[:, a:b_]
        dv = lambda fn, r=("lg", "rt"), w=("rt",): P.op("dve", fn, reads=list(r), writes=list(w))
        dv(lambda e: e.tensor_reduce(out=R_(0, 1), in_=lg[:, 0:4], axis=AX.X, op=ALU.max))
        dv(lambda e: e.tensor_scalar(out=R_(1, 5), in0=lg[:, 0:4], scalar1=R_(0, 1), scalar2=None, op0=ALU.is_equal))
        dv(lambda e: e.tensor_scalar(out=R_(5, 9), in0=lg[:, 0:4], scalar1=R_(0, 1), scalar2=None, op0=ALU.subtract))
        P.op("act", lambda e: e.activation(out=R_(5, 9), in_=R_(5, 9), func=AF.Exp), reads=["rt"], writes=["rt"])
        dv(lambda e: e.tensor_reduce(out=R_(9, 10), in_=R_(5, 9), axis=AX.X, op=ALU.add))
        dv(lambda e: e.reciprocal(out=R_(10, 11), in_=R_(9, 10)))
        dv(lambda e: e.tensor_tensor(out=rt[:, 40:56].rearrange("p (g e) -> p g e", e=4), in0=lg[:, 4:20].rearrange("p (g e) -> p g e", e=4),
                                     in1=R_(1, 5).unsqueeze(2).to_broadcast([128, 4, 4]), op=ALU.mult))
        dv(lambda e: e.tensor_reduce(out=R_(11, 15), in_=rt[:, 40:56].rearrange("p (g e) -> p e g", e=4), axis=AX.X, op=ALU.add))
        dv(lambda e: e.tensor_reduce(out=R_(15, 16), in_=R_(11, 15), axis=AX.X, op=ALU.max))
        dv(lambda e: e.tensor_scalar(out=R_(16, 20), in0=R_(11, 15), scalar1=R_(15, 16), scalar2=None, op0=ALU.is_equal))
        dv(lambda e: e.scalar_tensor_tensor(out=R_(20, 24), in0=R_(16, 20), scalar=-1e30, in1=R_(11, 15), op0=ALU.mult, op1=ALU.add))
        dv(lambda e: e.tensor_reduce(out=R_(24, 25), in_=R_(20, 24), axis=AX.X, op=ALU.max))
        dv(lambda e: e.tensor_scalar(out=R_(28, 32), in0=R_(20, 24), scalar1=R_(24, 25), scalar2=None, op0=ALU.is_equal))
        dv(lambda e: e.tensor_tensor(out=R_(32, 33), in0=R_(24, 25), in1=R_(15, 16), op=ALU.subtract))
        P.op("act", lambda e: e.activation(out=R_(32, 33), in_=R_(32, 33), func=AF.Exp), reads=["rt"], writes=["rt"])
        dv(lambda e: e.tensor_scalar(out=R_(33, 34), in0=R_(32, 33), scalar1=1.0, scalar2=None, op0=ALU.add))
        dv(lambda e: e.reciprocal(out=R_(34, 35), in_=R_(33, 34)))
        dv(lambda e: e.tensor_tensor(out=R_(35, 36), in0=R_(32, 33), in1=R_(34, 35), op=ALU.mult))
        dv(lambda e: e.tensor_scalar(out=R_(36, 40), in0=R_(16, 20), scalar1=R_(34, 35), scalar2=None, op0=ALU.mult))
        dv(lambda e: e.scalar_tensor_tensor(out=R_(36, 40), in0=R_(28, 32), scalar=R_(35, 36), in1=R_(36, 40), op0=ALU.mult, op1=ALU.add))
        dv(lambda e: e.tensor_scalar(out=R_(36, 40), in0=R_(36, 40), scalar1=R_(10, 11), scalar2=None, op0=ALU.mult))
        dv(lambda e: e.tensor_tensor(out=rt[:, 40:56].rearrange("p (g e) -> p g e", e=4), in0=R_(1, 5).unsqueeze(2).to_broadcast([128, 4, 4]),
                                     in1=R_(36, 40).unsqueeze(1).to_broadcast([128, 4, 4]), op=ALU.mult))
        P.op("dve", lambda e, t=t: e.tensor_copy(out=wfull[:, t, :], in_=rt[:, 40:56]), reads=["rt"], writes=[("wfull", t)])
    P.pop_scope()
    if STOPC <= 5:
        return
    P.push_scope()
    ln2 = sb("ln2", [128, 2 * D])
    P.dma("sp", ln2[:], d["ln2"], writes=["ln2"])
    HT = NT // 2 if NT >= 8 else NT
    nhalf = NT // HT
    yacc = sb("yacc", [128, HT, D])
    wg = [sb("wg%d" % i, [128, 8, DE], BF16) for i in range(2)]
    wu = [sb("wu%d" % i, [128, 8, DE], BF16) for i in range(2)]
    wd = [sb("wd%d" % i, [128, 4, D], BF16) for i in range(2)]
    hid = [sb("hid%d" % i, [128, 4, 512], BF16) for i in range(2)]
    sg = [sb("sg%d" % i, [128, 512]) for i in range(2)]
    st2 = sb("st2", [128, 2, 6]); mv2 = sb("mv2", [128, 4])
    xo = [sb("xo%d" % i, [128, D]) for i in range(2)]
    pG = [ps("pG%d" % i, [128, 512]) for i in range(2)]
    pU = [ps("pU%d" % i, [128, 512]) for i in range(2)]
    pD = [ps("pD%d" % i, [128, 512]) for i in range(2)]
    wq = 0
    for hf_ in range(nhalf):
        tbase = hf_ * HT
        P.op("pool", lambda e: e.memset(yacc[:].rearrange("p a b -> p (a b)"), 0.0), writes=[("yacc", i) for i in range(HT)])
        for ex in range(NEXP):
            wb = wq % 2
            wq += 1
            for kc in range(8):
                P.dma("pool", wg[wb][:, kc, :], d["wg"][ex, kc * 128:(kc + 1) * 128, :], writes=[("wg", wb, kc)])
                P.dma("pool", wu[wb][:, kc, :], d["wu"][ex, kc * 128:(kc + 1) * 128, :], writes=[("wu", wb, kc)])
            for fc in range(4):
                P.dma("pool", wd[wb][:, fc, :], d["wd"][ex, fc * 128:(fc + 1) * 128, :], writes=[("wd", wb, fc)])
            ngrp = (HT * 128) // 512 if HT * 128 >= 512 else 1
            gw = min(512, HT * 128)
            for g in range(ngrp):
                tok0 = tbase * 128 + g * gw
                hb = g % 2
                for fc in range(4):
                    pb = fc % 2
                    for kc in range(8):
                        P.op("pe", lambda e, kc=kc, fc=fc, pb=pb: e.matmul(pG[pb][:, 0:gw], lhsT=wg[wb][:, kc, fc * 128:(fc + 1) * 128], rhs=h1T[:, kc, tok0:tok0 + gw],
                                                                           start=(kc == 0), stop=(kc == 7)),
                             reads=[("wg", wb, kc)] + [("h1T", tt) for tt in range(tok0 // 128, (tok0 + gw) // 128)], writes=[("pG", pb)])
                    for kc in range(8):
                        P.op("pe", lambda e, kc=kc, fc=fc, pb=pb: e.matmul(pU[pb][:, 0:gw], lhsT=wu[wb][:, kc, fc * 128:(fc + 1) * 128], rhs=h1T[:, kc, tok0:tok0 + gw],
                                                                           start=(kc == 0), stop=(kc == 7)),
                             reads=[("wu", wb, kc)] + [("h1T", tt) for tt in range(tok0 // 128, (tok0 + gw) // 128)], writes=[("pU", pb)])
                    P.op("act", lambda e, pb=pb: e.activation(out=sg[pb][:, 0:gw], in_=pG[pb][:, 0:gw], func=AF.Silu), reads=[("pG", pb)], writes=[("sg", pb)])
                    P.op("dve", lambda e, pb=pb, fc=fc, hb=hb: e.tensor_tensor(out=hid[hb][:, fc, 0:gw], in0=sg[pb][:, 0:gw], in1=pU[pb][:, 0:gw], op=ALU.mult),
                         reads=[("sg", pb), ("pU", pb)], writes=[("hid", hb, fc)])
                for ts in range(gw // 128):
                    tl = (tok0 - tbase * 128) // 128 + ts
                    tg = tbase + tl
                    for dh in range(2):
                        pb = dh
                        for fc in range(4):
                            P.op("pe", lambda e, fc=fc, ts=ts, dh=dh, pb=pb: e.matmul(pD[pb][:, :], lhsT=hid[hb][:, fc, ts * 128:(ts + 1) * 128], rhs=wd[wb][:, fc, dh * 512:(dh + 1) * 512],
                                                                                      start=(fc == 0), stop=(fc == 3)),
                                 reads=[("hid", hb, fc), ("wd", wb, fc)], writes=[("pD", pb)])
                        P.op("dve", lambda e, tl=tl, tg=tg, dh=dh, pb=pb, ex=ex: e.scalar_tensor_tensor(out=yacc[:, tl, dh * 512:(dh + 1) * 512], in0=pD[pb][:, :],
                                                                                                       scalar=wfull[:, tg, ex:ex + 1], in1=yacc[:, tl, dh * 512:(dh + 1) * 512],
                                                                                                       op0=ALU.mult, op1=ALU.add),
                             reads=[("pD", pb), ("wfull", tg), ("yacc", tl)], writes=[("yacc", tl)])
        for tl in range(HT):
            tg = tbase + tl
            b = tl % 2
            rows = slice(tg * 128, (tg + 1) * 128)
            P.dma("sp", xo[b][:], d["h1s"][rows, :], reads=["h1s"], writes=[("xo", b)])
            P.op("dve", lambda e, tl=tl, b=b: e.scalar_tensor_tensor(out=xo[b][:], in0=xo[b][:], scalar=ALPHA, in1=yacc[:, tl, :], op0=ALU.mult, op1=ALU.add),
                 reads=[("xo", b), ("yacc", tl)], writes=[("xo", b)])
            layer_norm_tile(P, xo[b], ("xo", b), st2, mv2, "st2", epst, ln2, "ln2")
            P.dma("sp", d["out"][rows, :], xo[b][:], reads=[("xo", b)], writes=["out"])
    P.pop_scope()

import math
import numpy as np

L = 8192
D = 1024
DIN = 3872
NFM = 2336
NTM = 1536
NCORES = 8
FM_GQ, FM_GK, FM_GA, FM_HQ, FM_HF, FM_HY = 0, 192, 384, 416, 800, 1568
TM_GV, TM_GG, TM_HI, TM_HGT = 0, 384, 768, 1152
FM_COLS = list(range(0, 384)) + list(range(1152, 1184)) + list(range(1184, 2336)) + list(range(3104, 3872))
TM_COLS = list(range(384, 768)) + list(range(768, 1152)) + list(range(2336, 2720)) + list(range(2720, 3104))


def stage_A2(P, do_ln, x_d, w_d, gb_d, id_d, h_d, FM, TM):
    NG = L // 512
    P.push_scope()
    wsb = P.sb("wsb", [128, 8, DIN], BF16)
    idf = P.sb("a_idf", [128, 128], F32)
    idb = P.sb("a_idb", [128, 128], BF16)
    epst = P.sb("a_epst", [128, 1], F32)
    P.op("dve", lambda e: e.memset(epst[:], 1e-5), writes=["a_eps"])
    P.dma("sp", idf[:], id_d, writes=["a_idf"])
    P.op("dve", lambda e: e.tensor_copy(out=idb[:], in_=idf[:]), reads=["a_idf"], writes=["a_idb"])
    if do_ln:
        gbs = P.sb("a_gbs", [128, 2 * D], F32)
        P.dma("sp", gbs[:], gb_d, writes=["a_gbs"])
    CW = 968
    for kc in range(8):
        for c in range(4):
            P.dma("pool", wsb[:, kc, c * CW:(c + 1) * CW], w_d[kc * 128:(kc + 1) * 128, c * CW:(c + 1) * CW], writes=[("wsb", kc, c)])
    wk = lambda kc: [("wsb", kc, c) for c in range(4)]
    xt = [P.sb("a_xt%d" % i, [128, D], F32) for i in range(2)]
    hb = [P.sb("a_hb%d" % i, [128, D], BF16) for i in range(2)]
    hT = [P.sb("a_hT%d" % i, [128, 8, 512], BF16) for i in range(2)]
    st = P.sb("a_st", [128, 2, 6], F32)
    mv = P.sb("a_mv", [128, 4], F32)
    fo = [P.sb("a_fo%d" % i, [128, 512], F32) for i in range(3)]
    po = [P.sb("a_po%d" % i, [128, NTM], F32) for i in range(2)]
    pT = [P.ps("a_pT%d" % i, [128, 1024], BF16) for i in range(2)]
    pm = [P.ps("a_pm%d" % i, [128, 512], F32) for i in range(4)]
    mmi = 0
    ti = 0
    fi = 0
    for g in range(NG):
        hTg = hT[g % 2]
        khT = ("a_hT", g % 2)
        for tt in range(4):
            t = g * 4 + tt
            b = ti % 2
            ti += 1
            rows = slice(t * 128, (t + 1) * 128)
            kx, kh = ("a_xt", b), ("a_hb", b)
            P.dma("sp", xt[b][:], x_d[rows, :], writes=[kx])
            if do_ln:
                layer_norm_tile(P, xt[b], kx, st, mv, "a_st", epst, gbs, "a_gbs")
                P.dma("sp", h_d[rows, :], xt[b][:], reads=[kx], writes=["hbuf"])
            P.op("act", lambda e, b=b: e.activation(out=hb[b][:], in_=xt[b][:], func=AF.Copy), reads=[kx], writes=[kh])
            ptk = ("a_pT", t % 2)
            for kc in range(8):
                P.op("pe", lambda e, kc=kc, b=b, t=t: e.transpose(out=pT[t % 2][:, kc * 128:(kc + 1) * 128], in_=hb[b][:, kc * 128:(kc + 1) * 128], identity=idb[:]),
                     reads=[kh, "a_idb"], writes=[ptk])
            P.op("dve", lambda e, tt=tt, t=t: e.tensor_copy(out=hTg[:, :, tt * 128:(tt + 1) * 128], in_=pT[t % 2][:].rearrange("p (a b) -> p a b", b=128)),
                 reads=[ptk], writes=[(khT, tt)])
        hk = [(khT, tt) for tt in range(4)]
        for fg in range((NFM + 127) // 128):
            r0 = fg * 128
            rw = min(128, NFM - r0)
            pb = mmi % 4
            mmi += 1
            for kc in range(8):
                P.op("pe", lambda e, kc=kc, pb=pb, r0=r0, rw=rw: e.matmul(pm[pb][0:rw, :], lhsT=wsb[:, kc, r0:r0 + rw], rhs=hTg[:, kc, :], start=(kc == 0), stop=(kc == 7)),
                     reads=hk + wk(kc), writes=[("a_pm", pb)])
            fb = fi % 3
            fi += 1
            if fg % 2 == 0:
                P.op("act", lambda e, pb=pb, fb=fb, rw=rw: e.activation(out=fo[fb][0:rw, :], in_=pm[pb][0:rw, :], func=AF.Copy), reads=[("a_pm", pb)], writes=[("a_fo", fb)])
            else:
                P.op("dve", lambda e, pb=pb, fb=fb, rw=rw: e.tensor_copy(out=fo[fb][0:rw, :], in_=pm[pb][0:rw, :]), reads=[("a_pm", pb)], writes=[("a_fo", fb)])
            P.dma("sp", FM[r0:r0 + rw, g * 512:(g + 1) * 512], fo[fb][0:rw, :], reads=[("a_fo", fb)], writes=["FM"])
        for tt in range(4):
            t = g * 4 + tt
            pbuf = po[t % 2]
            kpo = ("a_po", t % 2)
            for ci in range(3):
                pb = mmi % 4
                mmi += 1
                for kc in range(8):
                    P.op("pe", lambda e, kc=kc, pb=pb, ci=ci, tt=tt: e.matmul(pm[pb][:, :], lhsT=hTg[:, kc, tt * 128:(tt + 1) * 128], rhs=wsb[:, kc, NFM + ci * 512:NFM + (ci + 1) * 512],
                                                                             start=(kc == 0), stop=(kc == 7)), reads=hk + wk(kc), writes=[("a_pm", pb)])
                if ci % 2 == 0:
                    P.op("act", lambda e, pb=pb, ci=ci, pbuf=pbuf: e.activation(out=pbuf[:, ci * 512:(ci + 1) * 512], in_=pm[pb][:, :], func=AF.Copy), reads=[("a_pm", pb)], writes=[(kpo, ci)])
                else:
                    P.op("dve", lambda e, pb=pb, ci=ci, pbuf=pbuf: e.tensor_copy(out=pbuf[:, ci * 512:(ci + 1) * 512], in_=pm[pb][:, :]), reads=[("a_pm", pb)], writes=[(kpo, ci)])
            P.dma("sp", TM[t * 128:(t + 1) * 128, :], pbuf[:], reads=[(kpo, ci) for ci in range(3)], writes=["TM"])
    P.pop_scope()


def scan_params(P, C, kind, layer, h, d):
    k = lambda n: (C.id, n)
    if kind == "gla":
        P.dma("sp", C.wa[:], d["gwa"][h], writes=[k("wa")])
        P.dma("sp", C.ba[0:32, 0:2], d["gba"][h], writes=[k("ba")])
        P.op("dve", lambda e: e.tensor_scalar(out=C.ba[0:32, 2:4], in0=C.ba[0:32, 0:2], scalar1=-1.0, scalar2=None, op0=ALU.mult), reads=[k("ba")], writes=[k("nba")])
        return {"wa": C.wa, "nba": C.ba[:, 2:4]}
    K = 64
    if layer == 0:
        return {}
    P.dma("sp", C.lb[0:K, 0:4], d["hlb"][h], writes=[k("lbl")])
    P.op("dve", lambda e: e.tensor_tensor(out=C.lb[0:K, 4:6], in0=C.lb[0:K, 0:2], in1=C.lb[0:K, 2:4], op=ALU.subtract), reads=[k("lbl")], writes=[k("lb1")])
    P.op("act", lambda e: e.activation(out=C.lb[0:K, 4:6], in_=C.lb[0:K, 4:6], func=AF.Exp), reads=[k("lb1")], writes=[k("lb1")])
    P.op("dve", lambda e: e.tensor_scalar(out=C.lb[0:K, 4:6], in0=C.lb[0:K, 4:6], scalar1=1.0, scalar2=None, op0=ALU.add), reads=[k("lb1")], writes=[k("lb1")])
    P.op("dve", lambda e: e.reciprocal(out=C.lb[0:K, 4:6], in_=C.lb[0:K, 4:6]), reads=[k("lb1")], writes=[k("lb")])
    P.op("dve", lambda e: e.tensor_scalar(out=C.lb[0:K, 6:8], in0=C.lb[0:K, 4:6], scalar1=-1.0, scalar2=1.0, op0=ALU.mult, op1=ALU.add), reads=[k("lb")], writes=[k("oml")])
    P.op("dve", lambda e: e.tensor_scalar(out=C.lb[0:K, 8:10], in0=C.lb[0:K, 6:8], scalar1=-1.0, scalar2=None, op0=ALU.mult), reads=[k("oml")], writes=[k("oml")])
    return {"lb": C.lb[:, 4:6], "oml": C.lb[:, 6:8], "noml": C.lb[:, 8:10]}


def run_gens(gens):
    gens = list(gens)
    while gens:
        for g in list(gens):
            try:
                next(g)
            except StopIteration:
                gens.remove(g)


def scans_all(P, layer, d, FM, TM, OF, OB):
    P.push_scope()
    C0 = ScanCtx(P, L, d, cid=0)
    C1 = ScanCtx(P, L, d, cid=1, share=C0)
    ctxs = [C0, C1]
    for kind, h in [("gla", hh) for hh in range(6)] + [("hg", hh) for hh in range(6)]:
        gens = []
        for dr in range(2):
            C = ctxs[dr]
            O_ = OF if dr == 0 else OB
            prm = scan_params(P, C, kind, layer, h, d)
            if kind == "gla":
                gens.append(scan_dir(C, "gla", layer, 32, dr, FM[FM_GQ + h * 32:FM_GQ + (h + 1) * 32, :], FM[FM_GK + h * 32:FM_GK + (h + 1) * 32, :],
                                     FM[FM_GA + dr * 16:FM_GA + (dr + 1) * 16, :], TM[:, TM_GV + h * 64:TM_GV + (h + 1) * 64], O_[:, h * 64:(h + 1) * 64], prm,
                                     rkeys=["FM", "TM"]))
            else:
                gens.append(scan_dir(C, "hg", layer, 64, dr, FM[FM_HQ + h * 64:FM_HQ + (h + 1) * 64, :], None,
                                     FM[FM_HF + dr * 384 + h * 64:FM_HF + dr * 384 + (h + 1) * 64, :], TM[:, TM_HI + h * 64:TM_HI + (h + 1) * 64],
                                     O_[:, 384 + h * 64:384 + (h + 1) * 64], prm, rkeys=["FM", "TM"]))
        run_gens(gens)
    P.pop_scope()


def hyena_consts2(Lx):
    f32 = np.float32
    t = np.linspace(0.0, 1.0, Lx, dtype=f32)[:, None]
    w = (2.0 * math.pi * np.arange(Lx, dtype=f32)[:, None] / Lx).astype(f32)
    bands = np.linspace(1e-4, 15, 16, dtype=f32)[None, :]
    z = np.concatenate([t, np.cos(bands * w), -np.sin(bands * w)], axis=-1).astype(f32)
    max_decay = math.log(1e-2) / 0.3
    min_decay = math.log(1e-2) / 1.5
    deltas = np.linspace(min_decay, max_decay, HY_W, dtype=f32)
    dec = np.exp(-t * np.abs(deltas)).astype(f32)
    tidx = np.concatenate([np.arange(Lx - 1, -1, -1), np.arange(1, Lx), [0]])
    zR = np.ascontiguousarray(z[tidx].T)
    zR[:, -1] = 0.0
    dR = np.ascontiguousarray(dec[tidx].T)
    dR[:, -1] = 0.0
    J = np.ascontiguousarray(np.eye(128, dtype=f32)[::-1])
    return {"h_zR": zR, "h_decR": dR.astype(f32), "h_J": J, "h_id": np.eye(128, dtype=f32), "h_ones": np.ones((64, 128), f32)}


def hyena_all(P, Lx, d, FM, YH, scr, fm_hy=FM_HY):
    NB = Lx // 128
    NT = 2 * Lx // 512
    HW = 2 * Lx - 128
    sb = P.sb
    P.push_scope()
    w1 = sb("h_w1", [33, 64]); b1f = sb("h_b1f", [64, 8]); w2 = sb("h_w2", [64, 2, 64]); w3 = sb("h_w3", [64, 2, 2, 256])
    cw = sb("h_cw", [128, 2, 9]); cb = sb("h_cb", [128, 2, 3]); bias = sb("h_bias", [128, 2, 2])
    Jm = sb("h_Jm", [128, 128]); idf = sb("h_idf", [128, 128]); idb = sb("h_idb", [128, 128], BF16); ones = sb("h_onesb", [64, 128])
    P.dma("sp", w1[:], d["w1"], writes=["h_w1"])
    P.dma("sp", b1f[:, 0:4], d["b1f"], writes=["h_b1f"])
    P.dma("sp", w2[:], d["w2"], writes=["h_w2"])
    P.dma("sp", w3[:], d["w3"], writes=["h_w3"])
    for lt in range(2):
        P.dma("sp", cw[:, lt, :], d["cw"][lt], writes=["h_cw"])
        P.dma("sp", cb[:, lt, :], d["cb"][lt], writes=["h_cb"])
        P.dma("sp", bias[:, lt, :], d["bias"][lt], writes=["h_bias"])
    P.dma("sp", Jm[:], d["h_J"], writes=["h_J"])
    P.dma("sp", idf[:], d["h_id"], writes=["h_idf"])
    P.dma("sp", ones[:], d["h_ones"], writes=["h_ones"])
    P.op("dve", lambda e: e.tensor_copy(out=idb[:], in_=idf[:]), reads=["h_idf"], writes=["h_idb"])
    P.op("dve", lambda e: e.tensor_scalar(out=b1f[:, 4:5], in0=b1f[:, 3:4], scalar1=1.0 / (2 * math.pi), scalar2=None, op0=ALU.mult), reads=["h_b1f"], writes=["h_A"])
    P.op("dve", lambda e: e.tensor_scalar(out=b1f[:, 5:8], in0=b1f[:, 0:3], scalar1=b1f[:, 4:5], scalar2=None, op0=ALU.mult), reads=["h_b1f", "h_A"], writes=["h_B"])
    pB = P.ps("h_pB", [128, 512])
    rnb = sb("h_rnb", [128, 512])
    P.push_scope()
    pA = P.ps("h_pA", [64, 512])
    pR = [P.ps("h_pR%d" % i, [64, 512]) for i in range(2)]
    zt = [sb("h_zt%d" % i, [33, 512]) for i in range(2)]
    dt_ = [sb("h_dt%d" % i, [64, 4, 512]) for i in range(2)]
    u = sb("h_u", [64, 512]); ui = sb("h_ui", [64, 512], I32); hh = sb("h_hh", [64, 512]); h3 = [sb("h_h3%d" % i, [64, 512]) for i in range(2)]
    Rt = sb("h_Rt", [64, 512]); Rb = [sb("h_Rb%d" % i, [64, 512], BF16) for i in range(3)]
    asum = sb("h_asum", [64, 8, NT]); rn = sb("h_rn", [64, 16]); dg = sb("h_dg", [64, 64])
    ri = 0
    for t in range(NT):
        b = t % 2
        P.dma("sp", zt[b][:], d["h_zR"][:, t * 512:(t + 1) * 512], writes=[("zt", b)])
        P.dma("sp", dt_[b][:], d["h_decR"][:, t * 512:(t + 1) * 512].rearrange("(a p) n -> p a n", p=64), writes=[("dt", b)])
        rhs = zt[b]
        rk = ("zt", b)
        for l in range(3):
            lhsT = w1[:, :] if l == 0 else w2[:, l - 1, :]
            kk = 33 if l == 0 else 64
            P.op("pe", lambda e, lhsT=lhsT, rhs=rhs, kk=kk: e.matmul(pA[:, :], lhsT=lhsT, rhs=rhs[0:kk, :], start=True, stop=True),
                 reads=[rk, "h_w1", "h_w2"], writes=["h_pA"])
            P.op("dve", lambda e, l=l: e.tensor_scalar(out=u[:], in0=pA[:], scalar1=b1f[:, 4:5], scalar2=b1f[:, 5 + l:6 + l], op0=ALU.mult, op1=ALU.add),
                 reads=["h_pA", "h_A", "h_B"], writes=["h_u"])
            P.op("dve", lambda e: e.tensor_copy(out=ui[:], in_=u[:]), reads=["h_u"], writes=["h_ui"])
            P.op("dve", lambda e: e.tensor_tensor(out=u[:], in0=u[:], in1=ui[:], op=ALU.subtract), reads=["h_u", "h_ui"], writes=["h_u"])
            dst = hh if l < 2 else h3[b]
            dk = "h_hh" if l < 2 else ("h_h3", b)
            P.op("act", lambda e, dst=dst: e.activation(out=dst[:], in_=u[:], func=AF.Sin, scale=TWO_PI_LO), reads=["h_u"], writes=[dk])
            rhs = dst
            rk = dk
        dr = 0 if t < NT // 2 else 1
        for gi in range(8):
            o, cq = gi // 4, gi % 4
            pr = pR[gi % 2]
            prk = ("h_pR", gi % 2)
            P.op("pe", lambda e, dr=dr, o=o, cq=cq, pr=pr, b=b: e.matmul(pr[:, :], lhsT=w3[:, dr, o, cq * 64:(cq + 1) * 64], rhs=h3[b][:, :], start=True, stop=True),
                 reads=[("h_h3", b), "h_w3"], writes=[prk])
            P.op("dve", lambda e, pr=pr, cq=cq, b=b: e.tensor_tensor(out=Rt[:], in0=pr[:], in1=dt_[b][:, cq, :], op=ALU.mult), reads=[prk, ("dt", b)], writes=["h_Rt"])
            P.op("dve", lambda e, gi=gi, t=t: e.tensor_reduce(out=asum[:, gi, t:t + 1], in_=Rt[:], axis=AX.X, op=ALU.add, apply_absolute_value=True), reads=["h_Rt"], writes=[("asum", gi, t)])
            rb = ri % 3
            ri += 1
            P.op("act", lambda e, rb=rb: e.activation(out=Rb[rb][:], in_=Rt[:], func=AF.Copy), reads=["h_Rt"], writes=[("Rb", rb)])
            P.dma("sp", scr[gi * 64:(gi + 1) * 64, t * 512:(t + 1) * 512], Rb[rb][:], reads=[("Rb", rb)], writes=["scr"])
    P.op("dve", lambda e: e.tensor_reduce(out=rn[:, 0:8], in_=asum[:], axis=AX.X, op=ALU.add), reads=[("asum", gi, t) for gi in range(8) for t in range(NT)], writes=["h_rn"])
    P.op("dve", lambda e: e.tensor_scalar(out=rn[:, 0:8], in0=rn[:, 0:8], scalar1=1e-12, scalar2=None, op0=ALU.max), reads=["h_rn"], writes=["h_rn"])
    P.op("dve", lambda e: e.reciprocal(out=rn[:, 8:16], in_=rn[:, 0:8]), reads=["h_rn"], writes=["h_rn1"])
    for gi in range(8):
        P.op("dve", lambda e, gi=gi: e.tensor_scalar(out=dg[:], in0=idf[0:64, 0:64], scalar1=rn[:, 8 + gi:9 + gi], scalar2=None, op0=ALU.mult), reads=["h_idf", "h_rn1"], writes=["h_dg"])
        P.op("pe", lambda e: e.matmul(pB[:, 0:64], lhsT=ones[:, :], rhs=dg[:, :], start=True, stop=True), reads=["h_ones", "h_dg"], writes=["h_pB"])
        P.op("dve", lambda e, gi=gi: e.tensor_copy(out=rnb[:, gi * 64:(gi + 1) * 64], in_=pB[:, 0:64]), reads=["h_pB"], writes=["h_rnb"])
    P.pop_scope()
    za = sb("h_za", [128, Lx]); zc = sb("h_zc", [128, Lx])
    zb = [sb("h_zb%d" % i, [128, 1024], BF16) for i in range(2)]
    ZT = sb("h_ZT", [128, NB, 128], BF16)
    YT = sb("h_YT", [128, NB, 128])
    xr = YT[:].rearrange("p a b -> p (a b)")
    NYK = 8
    ytk = [("YT", c) for c in range(NYK)]
    H = [sb("h_H%d" % i, [128, HW], BF16) for i in range(2)]
    pT = P.ps("h_pT", [128, 1024], BF16)
    pC = [P.ps("h_pC%d" % i, [128, 512]) for i in range(2)]
    yo = [sb("h_yo%d" % i, [128, 512]) for i in range(2)]
    hq = 0
    for lt in range(2):
        def short_conv(idx, dst, dk):
            P.dma("sp", xr, FM[fm_hy + idx * 256 + lt * 128:fm_hy + idx * 256 + (lt + 1) * 128, :], reads=["FM"], writes=ytk)
            P.op("dve", lambda e: e.tensor_scalar(out=dst[:], in0=xr, scalar1=cw[:, lt, idx * 3 + 1:idx * 3 + 2], scalar2=cb[:, lt, idx:idx + 1], op0=ALU.mult, op1=ALU.add),
                 reads=ytk + ["h_cw", "h_cb"], writes=[dk])
            P.op("dve", lambda e: e.scalar_tensor_tensor(out=dst[:, 1:Lx], in0=xr[:, 0:Lx - 1], scalar=cw[:, lt, idx * 3:idx * 3 + 1], in1=dst[:, 1:Lx], op0=ALU.mult, op1=ALU.add),
                 reads=ytk + ["h_cw", dk], writes=[dk])
            P.op("dve", lambda e: e.scalar_tensor_tensor(out=dst[:, 0:Lx - 1], in0=xr[:, 1:Lx], scalar=cw[:, lt, idx * 3 + 2:idx * 3 + 3], in1=dst[:, 0:Lx - 1], op0=ALU.mult, op1=ALU.add),
                 reads=ytk + ["h_cw", dk], writes=[dk])

        short_conv(0, za, "h_za")
        for o in range(2):
            for g in range(NB // 8):
                zbb = zb[g % 2]
                zbk = ("h_zb", g % 2)
                P.op("act", lambda e, g=g, zbb=zbb: e.activation(out=zbb[:], in_=za[:, g * 1024:(g + 1) * 1024], func=AF.Copy), reads=["h_za"], writes=[zbk])
                for jj in range(8):
                    P.op("pe", lambda e, jj=jj, zbb=zbb: e.transpose(out=pT[:, jj * 128:(jj + 1) * 128], in_=zbb[:, jj * 128:(jj + 1) * 128], identity=idb[:]),
                         reads=[zbk, "h_idb"], writes=["h_pT"])
                P.op("dve", lambda e, g=g: e.tensor_copy(out=ZT[:, g * 8:(g + 1) * 8, :].rearrange("p a b -> p (a b)"), in_=pT[:]), reads=["h_pT"], writes=[("ZT", g)])
            ztk = [("ZT", g) for g in range(NB // 8)]
            short_conv(1 + o, zc, "h_zc")
            for c in range(128):
                row = o * 256 + lt * 128 + c
                hb_ = hq % 2
                hq += 1
                q = "sp" if hb_ == 0 else "pool"
                P.dma(q, H[hb_][:], bass.AP(scr.tensor, row * 2 * Lx, [[1, 128], [1, HW]]), reads=["scr"], writes=[("H", hb_)])
                pc = pC[hb_]
                pck = ("pC", hb_)
                ds = [0] + [x for k in range(1, NB) for x in (k, -k)]
                for n_, dd in enumerate(ds):
                    j0, j1 = max(0, -dd), min(NB, NB - dd)
                    x0 = Lx - 128 - 128 * dd
                    P.op("pe", lambda e, dd=dd, j0=j0, j1=j1, x0=x0, n_=n_, pc=pc, hb_=hb_, c=c: e.matmul(
                        pc[:, (j0 + dd):(j1 + dd)], lhsT=H[hb_][:, x0:x0 + 128], rhs=ZT[:, j0:j1, c], start=(n_ == 0), stop=(n_ == len(ds) - 1)),
                         reads=[("H", hb_)] + ztk, writes=[pck])
                yk = ("YT", c % NYK)
                if c % 2 == 0:
                    P.op("act", lambda e, c=c, row=row, pc=pc: e.activation(out=YT[:, :, c], in_=pc[:, 0:NB], func=AF.Copy, scale=rnb[:, row:row + 1]),
                         reads=[pck, "h_rnb"], writes=[yk])
                else:
                    P.op("dve", lambda e, c=c, row=row, pc=pc: e.tensor_scalar(out=YT[:, :, c], in0=pc[:, 0:NB], scalar1=rnb[:, row:row + 1], scalar2=None, op0=ALU.mult),
                         reads=[pck, "h_rnb"], writes=[yk])
            for g in range(NB // 4):
                for ii in range(4):
                    i = g * 4 + ii
                    P.op("pe", lambda e, ii=ii, i=i: e.matmul(pB[:, ii * 128:(ii + 1) * 128], lhsT=YT[:, i, :], rhs=Jm[:, :], start=True, stop=True),
                         reads=ytk + ["h_J"], writes=["h_pB"])
                sl = slice(g * 512, (g + 1) * 512)
                P.op("dve", lambda e, sl=sl, o=o: e.scalar_tensor_tensor(out=za[:, sl], in0=za[:, sl], scalar=bias[:, lt, o:o + 1], in1=pB[:, :], op0=ALU.mult, op1=ALU.add),
                     reads=["h_za", "h_bias", "h_pB"], writes=["h_za"])
                P.op("pool", lambda e, sl=sl: e.tensor_tensor(out=za[:, sl], in0=za[:, sl], in1=zc[:, sl], op=ALU.mult), reads=["h_za", "h_zc"], writes=["h_za"])
        for g in range(NB // 4):
            for ii in range(4):
                i = g * 4 + ii
                P.op("pe", lambda e, ii=ii, i=i: e.matmul(pB[:, ii * 128:(ii + 1) * 128], lhsT=za[:, i * 128:(i + 1) * 128], rhs=idf[:, :], start=True, stop=True),
                     reads=["h_za", "h_idf"], writes=["h_pB"])
            yb = yo[g % 2]
            P.op("act", lambda e, yb=yb: e.activation(out=yb[:], in_=pB[:, :], func=AF.Copy), reads=["h_pB"], writes=[("h_yo", g % 2)])
            P.dma("sp", YH[g * 512:(g + 1) * 512, lt * 128:(lt + 1) * 128].rearrange("(a p) c -> p a c", p=128), yb[:].rearrange("p (a c) -> p a c", c=128),
                  reads=[("h_yo", g % 2)], writes=["YH"])
    P.pop_scope()


def stage_C2(P, d, HB, TM, OF, OB, YH, h1s, OUT, CT=16):
    sb, ps = P.sb, P.ps
    NCH = (L // 128) // CT
    CTOK = CT * 128
    P.push_scope()
    epst = sb("epst", [128, 1]); eps6 = sb("eps6", [128, 1])
    P.op("dve", lambda e: e.memset(epst[:], 1e-5), writes=["eps"])
    P.op("dve", lambda e: e.memset(eps6[:], 1e-6), writes=["eps6"])
    idf = sb("c_idf", [128, 128]); idb = sb("c_idb", [128, 128], BF16)
    P.dma("sp", idf[:], d["c_id"], writes=["idf"])
    P.op("dve", lambda e: e.tensor_copy(out=idb[:], in_=idf[:]), reads=["idf"], writes=["idb"])
    h1T = sb("h1T", [128, 8, CTOK], BF16)
    wfull = sb("wfull", [128, CT, 16])
    for ch in range(NCH):
        tb = ch * CT
        P.push_scope()
        wout = sb("woutb", [128, 8, D], BF16)
        for kc in range(8):
            P.dma("pool", wout[:, kc, :], d["wout"][kc * 128:(kc + 1) * 128, :], writes=[("wout", kc)])
        woutk = [("wout", kc) for kc in range(8)]
        gn = sb("gn", [128, 768]); ln1 = sb("ln1", [128, 2 * D]); wr = sb("wr", [128, 8, 20]); br = sb("br", [128, 20])
        P.dma("sp", gn[:], d["gn"], writes=["gn"])
        P.dma("sp", ln1[:], d["ln1"], writes=["ln1"])
        P.dma("sp", wr[:], d["wr"].rearrange("(k p) n -> p k n", p=128), writes=["wr"])
        P.dma("sp", br[:], d["br"], writes=["br"])
        NBUF = 2
        OFt = [sb("OF%d" % i, [128, 768]) for i in range(NBUF)]
        OBt = [sb("OB%d" % i, [128, 768]) for i in range(NBUF)]
        GT = [sb("GT%d" % i, [128, 768]) for i in range(NBUF)]
        mix = [sb("mix%d" % i, [128, D]) for i in range(NBUF)]
        mixb = [sb("mixb%d" % i, [128, D], BF16) for i in range(NBUF)]
        mixT = [sb("mixT%d" % i, [128, 8, 128], BF16) for i in range(NBUF)]
        ht = [sb("ht%d" % i, [128, D]) for i in range(NBUF)]
        sq = sb("sq", [128, 768]); ss = sb("ss", [128, 12]); rs = sb("rs_", [128, 12])
        st = sb("st", [128, 2, 6]); mv = sb("mv", [128, 4])
        h1T32 = sb("h1T32", [128, 8, 128])
        lg = sb("lg", [128, 20]); rt = sb("rt", [128, 64])
        pT = ps("pT", [128, 1024], BF16)
        pO = [ps("pO%d" % i, [128, 512]) for i in range(2)]
        pX = [ps("pX%d" % i, [128, 512]) for i in range(2)]
        pL = ps("pL", [128, 512])
        for tl in range(CT):
            t = tb + tl
            b = tl % NBUF
            rows = slice(t * 128, (t + 1) * 128)
            lrows = slice(tl * 128, (tl + 1) * 128)
            kOF, kOB, kGT, kmix, kmixb, kmixT, kht = ("OF", b), ("OB", b), ("GT", b), ("mix", b), ("mixb", b), ("mixT", b), ("ht", b)
            P.dma("sp", OFt[b][:], OF[rows, :], writes=[kOF])
            P.dma("sp", OBt[b][:], OB[rows, :], writes=[kOB])
            P.dma("sp", GT[b][:, 0:384], TM[rows, TM_GG:TM_GG + 384], writes=[(kGT, 0)])
            P.dma("sp", GT[b][:, 384:768], TM[rows, TM_HGT:TM_HGT + 384], writes=[(kGT, 1)])
            kGTs = [(kGT, 0), (kGT, 1)]
            P.dma("sp", mix[b][:, 768:1024], YH[rows, :], writes=[(kmix, "hy")])
            P.dma("sp", ht[b][:], HB[rows, :], writes=[kht])
            P.op("pool", lambda e, b=b: e.tensor_tensor(out=OFt[b][:], in0=OFt[b][:], in1=OBt[b][:], op=ALU.add), reads=[kOF, kOB], writes=[kOF])
            P.op("dve", lambda e, b=b: e.tensor_tensor(out=sq[:], in0=OFt[b][:], in1=OFt[b][:], op=ALU.mult), reads=[kOF], writes=["sq"])
            P.op("dve", lambda e: e.tensor_reduce(out=ss[:], in_=sq[:].rearrange("p (h v) -> p h v", v=64), axis=AX.X, op=ALU.add), reads=["sq"], writes=["ss"])
            P.op("act", lambda e: e.activation(out=rs[:], in_=ss[:], func=AF.Sqrt, bias=eps6[:, 0:1], scale=1.0 / 64.0), reads=["ss", "eps6"], writes=["rs"])
            P.op("dve", lambda e: e.reciprocal(out=rs[:], in_=rs[:]), reads=["rs"], writes=["rs"])
            P.op("dve", lambda e, b=b: e.tensor_tensor(out=OFt[b][:].rearrange("p (h v) -> p h v", v=64), in0=OFt[b][:].rearrange("p (h v) -> p h v", v=64),
                                                       in1=rs[:].unsqueeze(2).to_broadcast([128, 12, 64]), op=ALU.mult), reads=[kOF, "rs"], writes=[kOF])
            P.op("pool", lambda e, b=b: e.tensor_tensor(out=OFt[b][:], in0=OFt[b][:], in1=gn[:], op=ALU.mult), reads=[kOF, "gn"], writes=[kOF])
            P.op("act", lambda e, b=b: e.activation(out=sq[:], in_=GT[b][:], func=AF.Exp, scale=-1.0), reads=kGTs, writes=["sq"])
            P.op("dve", lambda e: e.tensor_scalar(out=sq[:], in0=sq[:], scalar1=1.0, scalar2=None, op0=ALU.add), reads=["sq"], writes=["sq"])
            P.op("dve", lambda e: e.reciprocal(out=sq[:], in_=sq[:]), reads=["sq"], writes=["sq"])
            P.op("pool", lambda e, b=b: e.tensor_tensor(out=sq[:, 0:384], in0=sq[:, 0:384], in1=GT[b][:, 0:384], op=ALU.mult), reads=["sq"] + kGTs, writes=["sq"])
            P.op("dve", lambda e, b=b: e.tensor_tensor(out=mix[b][:, 0:768], in0=OFt[b][:], in1=sq[:], op=ALU.mult), reads=[kOF, "sq"], writes=[(kmix, "a")])
            P.op("act", lambda e, b=b: e.activation(out=mixb[b][:], in_=mix[b][:], func=AF.Copy), reads=[(kmix, "a"), (kmix, "hy")], writes=[kmixb])
            for kc in range(8):
                P.op("pe", lambda e, kc=kc, b=b: e.transpose(out=pT[:, kc * 128:(kc + 1) * 128], in_=mixb[b][:, kc * 128:(kc + 1) * 128], identity=idb[:]),
                     reads=[kmixb, "idb"], writes=["pT"])
            P.op("dve", lambda e, b=b: e.tensor_copy(out=mixT[b][:].rearrange("p a b -> p (a b)"), in_=pT[:]), reads=["pT"], writes=[kmixT])
            for hf in range(2):
                for kc in range(8):
                    P.op("pe", lambda e, kc=kc, hf=hf, b=b: e.matmul(pO[hf][:, :], lhsT=mixT[b][:, kc, :], rhs=wout[:, kc, hf * 512:(hf + 1) * 512], start=(kc == 0), stop=(kc == 7)),
                         reads=[kmixT] + woutk, writes=[("pO", hf)])
                P.op("dve", lambda e, hf=hf, b=b: e.scalar_tensor_tensor(out=ht[b][:, hf * 512:(hf + 1) * 512], in0=ht[b][:, hf * 512:(hf + 1) * 512], scalar=ALPHA,
                                                                         in1=pO[hf][:, :], op0=ALU.mult, op1=ALU.add), reads=[kht, ("pO", hf)], writes=[kht])
            layer_norm_tile(P, ht[b], kht, st, mv, "st", epst, ln1, "ln1")
            P.dma("sp", h1s[rows, :], ht[b][:], reads=[kht], writes=["h1s"])
            for kc in range(8):
                P.op("pe", lambda e, kc=kc, b=b: e.matmul(pX[kc // 4][:, (kc % 4) * 128:(kc % 4 + 1) * 128], lhsT=ht[b][:, kc * 128:(kc + 1) * 128], rhs=idf[:], start=True, stop=True),
                     reads=[kht, "idf"], writes=[("pX", kc // 4)])
            for hf in range(2):
                P.op("act", lambda e, hf=hf: e.activation(out=h1T32[:, hf * 4:(hf + 1) * 4, :].rearrange("p a b -> p (a b)"), in_=pX[hf][:, :], func=AF.Copy),
                     reads=[("pX", hf)], writes=[("h1T32", hf)])
                P.op("dve", lambda e, hf=hf, lrows=lrows: e.tensor_copy(out=h1T[:, hf * 4:(hf + 1) * 4, lrows], in_=pX[hf][:, :].rearrange("p (a b) -> p a b", b=128)),
                     reads=[("pX", hf), ("h1T32", hf)], writes=[("h1T", tl)])
            for kc in range(8):
                P.op("pe", lambda e, kc=kc: e.matmul(pL[:, 0:20], lhsT=h1T32[:, kc, :], rhs=wr[:, kc, :], start=(kc == 0), stop=(kc == 7)),
                     reads=[("h1T32", kc // 4), "wr"], writes=["pL"])
            P.op("dve", lambda e: e.tensor_tensor(out=lg[:], in0=pL[:, 0:20], in1=br[:], op=ALU.add), reads=["pL", "br"], writes=["lg"])
            R_ = lambda a, b_: rt[:, a:b_]
            dv = lambda fn, r=("lg", "rt"), w=("rt",): P.op("dve", fn, reads=list(r), writes=list(w))
            dv(lambda e: e.tensor_reduce(out=R_(0, 1), in_=lg[:, 0:4], axis=AX.X, op=ALU.max))
            dv(lambda e: e.tensor_scalar(out=R_(1, 5), in0=lg[:, 0:4], scalar1=R_(0, 1), scalar2=None, op0=ALU.is_equal))
            dv(lambda e: e.tensor_scalar(out=R_(5, 9), in0=lg[:, 0:4], scalar1=R_(0, 1), scalar2=None, op0=ALU.subtract))
            P.op("act", lambda e: e.activation(out=R_(5, 9), in_=R_(5, 9), func=AF.Exp), reads=["rt"], writes=["rt"])
            dv(lambda e: e.tensor_reduce(out=R_(9, 10), in_=R_(5, 9), axis=AX.X, op=ALU.add))
            dv(lambda e: e.reciprocal(out=R_(10, 11), in_=R_(9, 10)))
            dv(lambda e: e.tensor_tensor(out=rt[:, 40:56].rearrange("p (g e) -> p g e", e=4), in0=lg[:, 4:20].rearrange("p (g e) -> p g e", e=4),
                                         in1=R_(1, 5).unsqueeze(2).to_broadcast([128, 4, 4]), op=ALU.mult))
            dv(lambda e: e.tensor_reduce(out=R_(11, 15), in_=rt[:, 40:56].rearrange("p (g e) -> p e g", e=4), axis=AX.X, op=ALU.add))
            dv(lambda e: e.tensor_reduce(out=R_(15, 16), in_=R_(11, 15), axis=AX.X, op=ALU.max))
            dv(lambda e: e.tensor_scalar(out=R_(16, 20), in0=R_(11, 15), scalar1=R_(15, 16), scalar2=None, op0=ALU.is_equal))
            dv(lambda e: e.scalar_tensor_tensor(out=R_(20, 24), in0=R_(16, 20), scalar=-1e30, in1=R_(11, 15), op0=ALU.mult, op1=ALU.add))
            dv(lambda e: e.tensor_reduce(out=R_(24, 25), in_=R_(20, 24), axis=AX.X, op=ALU.max))
            dv(lambda e: e.tensor_scalar(out=R_(28, 32), in0=R_(20, 24), scalar1=R_(24, 25), scalar2=None, op0=ALU.is_equal))
            dv(lambda e: e.tensor_tensor(out=R_(32, 33), in0=R_(24, 25), in1=R_(15, 16), op=ALU.subtract))
            P.op("act", lambda e: e.activation(out=R_(32, 33), in_=R_(32, 33), func=AF.Exp), reads=["rt"], writes=["rt"])
            dv(lambda e: e.tensor_scalar(out=R_(33, 34), in0=R_(32, 33), scalar1=1.0, scalar2=None, op0=ALU.add))
            dv(lambda e: e.reciprocal(out=R_(34, 35), in_=R_(33, 34)))
            dv(lambda e: e.tensor_tensor(out=R_(35, 36), in0=R_(32, 33), in1=R_(34, 35), op=ALU.mult))
            dv(lambda e: e.tensor_scalar(out=R_(36, 40), in0=R_(16, 20), scalar1=R_(34, 35), scalar2=None, op0=ALU.mult))
            dv(lambda e: e.scalar_tensor_tensor(out=R_(36, 40), in0=R_(28, 32), scalar=R_(35, 36), in1=R_(36, 40), op0=ALU.mult, op1=ALU.add))
            dv(lambda e: e.tensor_scalar(out=R_(36, 40), in0=R_(36, 40), scalar1=R_(10, 11), scalar2=None, op0=ALU.mult))
            dv(lambda e: e.tensor_tensor(out=rt[:, 40:56].rearrange("p (g e) -> p g e", e=4), in0=R_(1, 5).unsqueeze(2).to_broadcast([128, 4, 4]),
                                         in1=R_(36, 40).unsqueeze(1).to_broadcast([128, 4, 4]), op=ALU.mult))
            P.op("dve", lambda e, tl=tl: e.tensor_copy(out=wfull[:, tl, :], in_=rt[:, 40:56]), reads=["rt"], writes=[("wfull", tl)])
        P.pop_scope()
        P.push_scope()
        ln2 = sb("ln2", [128, 2 * D])
        P.dma("sp", ln2[:], d["ln2"], writes=["ln2"])
        yacc = sb("yacc", [128, CT, D])
        wg = [sb("wg%d" % i, [128, 8, DE], BF16) for i in range(2)]
        wu = [sb("wu%d" % i, [128, 8, DE], BF16) for i in range(2)]
        wd = [sb("wd%d" % i, [128, 4, D], BF16) for i in range(2)]
        hid = [sb("hid%d" % i, [128, 4, 512], BF16) for i in range(2)]
        sg = [sb("sg%d" % i, [128, 512]) for i in range(2)]
        st2 = sb("st2", [128, 2, 6]); mv2 = sb("mv2", [128, 4])
        xo = [sb("xo%d" % i, [128, D]) for i in range(2)]
        pG = [ps("pG%d" % i, [128, 512]) for i in range(2)]
        pU = [ps("pU%d" % i, [128, 512]) for i in range(2)]
        pD = [ps("pD%d" % i, [128, 512]) for i in range(2)]
        P.op("pool", lambda e: e.memset(yacc[:].rearrange("p a b -> p (a b)"), 0.0), writes=[("yacc", i) for i in range(CT)])
        for ex in range(NEXP):
            wb = ex % 2
            for kc in range(8):
                P.dma("pool", wg[wb][:, kc, :], d["wg"][ex, kc * 128:(kc + 1) * 128, :], writes=[("wg", wb, kc)])
                P.dma("pool", wu[wb][:, kc, :], d["wu"][ex, kc * 128:(kc + 1) * 128, :], writes=[("wu", wb, kc)])
            for fc in range(4):
                P.dma("pool", wd[wb][:, fc, :], d["wd"][ex, fc * 128:(fc + 1) * 128, :], writes=[("wd", wb, fc)])
            gw = min(512, CTOK)
            for g in range(CTOK // gw):
                tok0 = g * gw
                hb = g % 2
                h1k = [("h1T", tt) for tt in range(tok0 // 128, (tok0 + gw) // 128)]
                for fc in range(4):
                    pb = fc % 2
                    for kc in range(8):
                        P.op("pe", lambda e, kc=kc, fc=fc, pb=pb, wb=wb, tok0=tok0: e.matmul(pG[pb][:, 0:gw], lhsT=wg[wb][:, kc, fc * 128:(fc + 1) * 128], rhs=h1T[:, kc, tok0:tok0 + gw],
                                                                                         start=(kc == 0), stop=(kc == 7)), reads=[("wg", wb, kc)] + h1k, writes=[("pG", pb)])
                    for kc in range(8):
                        P.op("pe", lambda e, kc=kc, fc=fc, pb=pb, wb=wb, tok0=tok0: e.matmul(pU[pb][:, 0:gw], lhsT=wu[wb][:, kc, fc * 128:(fc + 1) * 128], rhs=h1T[:, kc, tok0:tok0 + gw],
                                                                                         start=(kc == 0), stop=(kc == 7)), reads=[("wu", wb, kc)] + h1k, writes=[("pU", pb)])
                    P.op("act", lambda e, pb=pb: e.activation(out=sg[pb][:, 0:gw], in_=pG[pb][:, 0:gw], func=AF.Silu), reads=[("pG", pb)], writes=[("sg", pb)])
                    P.op("dve", lambda e, pb=pb, fc=fc, hb=hb: e.tensor_tensor(out=hid[hb][:, fc, 0:gw], in0=sg[pb][:, 0:gw], in1=pU[pb][:, 0:gw], op=ALU.mult),
                         reads=[("sg", pb), ("pU", pb)], writes=[("hid", hb, fc)])
                for ts in range(gw // 128):
                    tl = tok0 // 128 + ts
                    for dh in range(2):
                        pb = dh
                        for fc in range(4):
                            P.op("pe", lambda e, fc=fc, ts=ts, dh=dh, pb=pb, hb=hb, wb=wb: e.matmul(pD[pb][:, :], lhsT=hid[hb][:, fc, ts * 128:(ts + 1) * 128], rhs=wd[wb][:, fc, dh * 512:(dh + 1) * 512],
                                                                                                   start=(fc == 0), stop=(fc == 3)), reads=[("hid", hb, fc), ("wd", wb, fc)], writes=[("pD", pb)])
                        P.op("dve", lambda e, tl=tl, dh=dh, pb=pb, ex=ex: e.scalar_tensor_tensor(out=yacc[:, tl, dh * 512:(dh + 1) * 512], in0=pD[pb][:, :], scalar=wfull[:, tl, ex:ex + 1],
                                                                                                in1=yacc[:, tl, dh * 512:(dh + 1) * 512], op0=ALU.mult, op1=ALU.add),
                             reads=[("pD", pb), ("wfull", tl), ("yacc", tl)], writes=[("yacc", tl)])
        for tl in range(CT):
            t = tb + tl
            b = tl % 2
            rows = slice(t * 128, (t + 1) * 128)
            P.dma("sp", xo[b][:], h1s[rows, :], reads=["h1s"], writes=[("xo", b)])
            P.op("dve", lambda e, tl=tl, b=b: e.scalar_tensor_tensor(out=xo[b][:], in0=xo[b][:], scalar=ALPHA, in1=yacc[:, tl, :], op0=ALU.mult, op1=ALU.add),
                 reads=[("xo", b), ("yacc", tl)], writes=[("xo", b)])
            layer_norm_tile(P, xo[b], ("xo", b), st2, mv2, "st2", epst, ln2, "ln2")
            P.dma("sp", OUT[rows, :], xo[b][:], reads=[("xo", b)], writes=["OUT"])
        P.pop_scope()
    P.pop_scope()


_prog = {}


def build_fused(ins0):
    P = Prog()
    nc = P.nc
    d = {}
    for k, v in ins0.items():
        d[k] = nc.dram_tensor(k, list(v.shape), F32, kind="ExternalInput").ap()
    out = nc.dram_tensor("out", [L, D], F32, kind="ExternalOutput").ap()
    I = lambda n, s, dt=F32: nc.dram_tensor(n, s, dt, kind="Internal").ap()
    HB0 = I("HB0", [L, D]); HB1 = I("HB1", [L, D]); FM = I("FM", [NFM, L]); TM = I("TM", [L, NTM])
    OF = I("OFs", [L, 768]); OB = I("OBs", [L, 768]); YH = I("YHs", [L, 256]); h1s = I("h1s", [L, D]); scr = I("scr", [512, 2 * L], BF16)
    for layer in range(2):
        dl = {k[3:]: v for k, v in d.items() if k.startswith("l%d_" % layer)}
        dl.update({k: v for k, v in d.items() if k.startswith("c_") or k.startswith("h_")})
        xin = d["x"] if layer == 0 else HB1
        stage_A2(P, layer == 0, xin, dl["win"], d["gb"], d["c_id"], HB0, FM, TM)
        scans_all(P, layer, dl, FM, TM, OF, OB)
        hyena_all(P, L, dl, FM, YH, scr)
        stage_C2(P, dl, HB0 if layer == 0 else HB1, TM, OF, OB, YH, h1s, HB1 if layer == 0 else out)
    return P.finish(["OUT"])


def host_inputs(inp, b):
    f32 = np.float32
    rep = lambda v: np.ascontiguousarray(np.tile(np.asarray(v, f32)[None, :], (128, 1)))
    m = {"x": np.ascontiguousarray(inp["x"][b]), "gb": rep(np.concatenate([inp["ln_in_g"], inp["ln_in_b"]]))}
    m.update(scan_consts())
    m.update(hyena_consts2(L))
    for l in range(2):
        p = "l%d_" % l
        w = inp["w_in"][l]
        m[p + "win"] = np.ascontiguousarray(np.concatenate([w[:, FM_COLS], w[:, TM_COLS]], 1))
        wa2, ba, lbl = inp["gla_wa2"][l], inp["gla_ba"][l], inp["hg_lb_logits"]
        m[p + "gwa"] = np.stack([np.ascontiguousarray(wa2[:, :, h * 32:(h + 1) * 32].transpose(1, 0, 2)) for h in range(6)])
        m[p + "gba"] = np.stack([np.ascontiguousarray(ba[:, h * 32:(h + 1) * 32].T) for h in range(6)])
        m[p + "hlb"] = np.stack([np.ascontiguousarray(lbl[:, :, h * 64:(h + 1) * 64].reshape(4, 64).T) for h in range(6)])
        cw, cb, hb_ = inp["hy_conv_w"][l], inp["hy_conv_b"][l], inp["hy_bias"][l]
        m[p + "cw"] = np.ascontiguousarray(np.stack([np.stack([cw[:, k * 256 + lt * 128:k * 256 + (lt + 1) * 128].T for k in range(3)], 1).reshape(128, 9) for lt in range(2)]))
        m[p + "cb"] = np.ascontiguousarray(np.stack([np.stack([cb[k * 256 + lt * 128:k * 256 + (lt + 1) * 128] for k in range(3)], 1) for lt in range(2)]))
        m[p + "bias"] = np.ascontiguousarray(np.stack([np.stack([hb_[o, lt * 128:(lt + 1) * 128] for o in range(2)], 1) for lt in range(2)]))
        m[p + "w1"] = inp["hy_w1"][l]
        m[p + "b1f"] = np.ascontiguousarray(np.stack([inp["hy_b1"][l], inp["hy_b2"][l][0], inp["hy_b2"][l][1], inp["hy_freq"][l]], 1))
        m[p + "w2"] = np.ascontiguousarray(inp["hy_w2"][l].transpose(1, 0, 2))
        m[p + "w3"] = np.ascontiguousarray(inp["hy_w3"][l].reshape(64, 2, 2, 256).transpose(0, 2, 1, 3))
        m[p + "gn"] = rep(np.concatenate([inp["gla_norm_g"][l], inp["hg_norm_g"][l]]))
        m[p + "wout"] = inp["w_out"][l]
        m[p + "ln1"] = rep(np.concatenate([inp["ln1_g"][l], inp["ln1_b"][l]]))
        m[p + "ln2"] = rep(np.concatenate([inp["ln2_g"][l], inp["ln2_b"][l]]))
        m[p + "wr"] = np.ascontiguousarray(np.concatenate([inp["moe_wr_g"][l], inp["moe_wr_e"][l]], 1))
        m[p + "br"] = rep(np.concatenate([inp["moe_br_g"][l], inp["moe_br_e"][l]]))
        m[p + "wg"], m[p + "wu"], m[p + "wd"] = inp["moe_w_gate"][l], inp["moe_w_up"][l], inp["moe_w_down"][l]
    return {k: np.ascontiguousarray(np.asarray(v, f32)) for k, v in m.items()}


def kernel(**inp):
    inp = {k: np.asarray(v) for k, v in inp.items()}
    in_maps = [host_inputs(inp, c % 4) for c in range(4)]
    in_maps = in_maps + in_maps
    if "nc" not in _prog:
        _prog["nc"] = build_fused(in_maps[0])
    res = run_bass_kernel_spmd(_prog["nc"], in_maps, core_ids=list(range(NCORES))).results
    return np.stack([res[b]["out"] for b in range(4)], 0).astype(np.float32)
```

```python
import numpy as np
import concourse.bass as bass
import concourse.mybir as mybir
from concourse.bass_utils import run_bass_kernel_spmd
from contextlib import ExitStack

F32 = mybir.dt.float32
BF16 = mybir.dt.bfloat16
I32 = mybir.dt.int32
AF = mybir.ActivationFunctionType
ALU = mybir.AluOpType
AX = mybir.AxisListType


class Prog:
    def __init__(self, n_dma_sems=20):
        self.nc = bass.Bass("TRN2", target_bir_lowering=False)
        self.es = ExitStack()
        nc = self.nc
        self.eng = {"pe": nc.tensor, "dve": nc.vector, "act": nc.scalar,
                    "pool": nc.gpsimd, "sp": nc.sync}
        self.sem = {}
        for e in self.eng:
            self.sem[("E", e)] = self.es.enter_context(nc.semaphore("s_" + e))
        self.cnt = {("E", e): 0 for e in self.eng}
        self.dpool = {}
        self.dnext = {}
        for q in ("sp", "pool", "act"):
            self.dpool[q] = []
            for i in range(n_dma_sems if q != "act" else 6):
                k = ("D", q, i)
                self.sem[k] = self.es.enter_context(nc.semaphore("d_%s_%d" % (q, i)))
                self.cnt[k] = 0
                self.dpool[q].append(k)
            self.dnext[q] = 0
        self.cur = self.es
        self.uid = 0
        self.scopes = []
        self.seen = {}
        self.lastw = {}
        self.readers = {}
        self.nins = 0

    def sb(self, name, shape, dt=F32):
        self.uid += 1
        return self.cur.enter_context(self.nc.sbuf_tensor("sb%d_%s" % (self.uid, name), list(shape), dt))

    def ps(self, name, shape, dt=F32):
        self.uid += 1
        return self.cur.enter_context(self.nc.psum_tensor("ps%d_%s" % (self.uid, name), list(shape), dt))

    def push_scope(self):
        self.scopes.append(self.cur)
        self.cur = ExitStack()

    def pop_scope(self):
        self.barrier()
        self.cur.close()
        self.cur = self.scopes.pop()

    def barrier(self):
        deps = [(sk, v) for sk, v in self.cnt.items() if v > 0]
        for e in self.eng:
            self._wait(e, deps)

    def dram(self, name, shape, dt=F32, kind="Internal"):
        return self.nc.dram_tensor(name, list(shape), dt, kind=kind).ap()

    def _deps(self, reads, writes):
        deps = []
        for k in reads:
            if k in self.lastw:
                deps.append(self.lastw[k])
        for k in writes:
            if k in self.lastw:
                deps.append(self.lastw[k])
            deps.extend(self.readers.get(k, {}).items())
        return deps

    def _wait(self, e, deps):
        best = {}
        for sk, v in deps:
            if sk == ("E", "pe") and e == "pe":
                continue
            if self.seen.get((e, sk), 0) >= v:
                continue
            if best.get(sk, 0) < v:
                best[sk] = v
        for sk, v in best.items():
            self.eng[e].wait_ge(self.sem[sk], v)
            self.seen[(e, sk)] = v

    def _record(self, tk, reads, writes):
        sk, v = tk
        for k in reads:
            self.readers.setdefault(k, {})[sk] = v
        for k in writes:
            self.lastw[k] = tk
            self.readers[k] = {}

    def op(self, e, fn, reads=(), writes=()):
        self._wait(e, self._deps(reads, writes))
        ins = fn(self.eng[e])
        sk = ("E", e)
        self.cnt[sk] += 1
        ins.then_inc(self.sem[sk], 1)
        self._record((sk, self.cnt[sk]), reads, writes)
        self.nins += 1
        return ins

    def dma(self, q, out, in_, reads=(), writes=(), **kw):
        deps = self._deps(reads, writes)
        sk = self.dpool[q][self.dnext[q] % len(self.dpool[q])]
        self.dnext[q] += 1
        if self.cnt[sk] > 0:
            deps.append((sk, self.cnt[sk]))
        self._wait(q, deps)
        ins = self.eng[q].dma_start(out=out, in_=in_, **kw)
        self.cnt[sk] += 16
        ins.then_inc(self.sem[sk], 16)
        self._record((sk, self.cnt[sk]), reads, writes)
        self.nins += 1
        return ins

    def finish(self, out_keys):
        deps = []
        for k in out_keys:
            if k in self.lastw:
                deps.append(self.lastw[k])
        for q in self.dpool:
            for sk in self.dpool[q]:
                if self.cnt[sk] > 0:
                    deps.append((sk, self.cnt[sk]))
        self._wait("sp", deps)
        self.es.close()
        return self.nc


def _coll(self, kind, groups, out, in_, reads=(), writes=()):
    q = "pool"
    deps = self._deps(reads, writes)
    sk = self.dpool[q][self.dnext[q] % len(self.dpool[q])]
    self.dnext[q] += 1
    if self.cnt[sk] > 0:
        deps.append((sk, self.cnt[sk]))
    self._wait(q, deps)
    ins = self.nc.gpsimd.collective_compute(kind, ALU.bypass, replica_groups=groups, ins=[in_], outs=[out])
    self.cnt[sk] += 16
    ins.then_inc(self.sem[sk], 16)
    self._record((sk, self.cnt[sk]), reads, writes)
    self.nins += 1
    return ins


Prog.coll = _coll

import math
import numpy as np

import os
STOPAT = int(os.environ.get('STOPAT', '99'))
CH = 64
HF = (lambda h: 0) if os.environ.get("HF0") else (lambda h: h)
SEG = 1024
NCS = SEG // CH
NPS = SEG // 128


def scan_consts():
    rm = np.ones((128, SEG), np.float32)
    rm[:, ::CH] = 0.0
    j = np.arange(64)[:, None]
    i = np.arange(64)[None, :]
    mf = (i >= j).astype(np.float32)
    mb = (i <= j).astype(np.float32)
    mf = np.tile(np.concatenate([mf, mf], 0), (1, 8))
    mb = np.tile(np.concatenate([mb, mb], 0), (1, 8))
    return {"c_rm": rm, "c_mf": mf, "c_mb": mb, "c_id": np.eye(128, dtype=np.float32)}


class ScanCtx:
    def __init__(self, P, L, dd=None, cid=0, share=None):
        self.P = P
        self.L = L
        self.id = "c%d" % cid
        nc = P.nc
        c = {}
        for nm, shp in (("c_rm", [128, SEG]), ("c_mf", [128, 512]), ("c_mb", [128, 512]), ("c_id", [128, 128])):
            if share is not None:
                continue
            c[nm] = dd[nm] if dd is not None else nc.dram_tensor(nm, shp, F32, kind="ExternalInput").ap()
        if share is None:
            self.rm = P.sb("rm", [128, SEG], F32)
            self.mf = P.sb("mf", [128, 512], F32)
            self.mb = P.sb("mb", [128, 512], F32)
            idf = P.sb("idf", [128, 128], F32)
            self.idb = P.sb("idb", [128, 128], BF16)
            P.dma("sp", self.rm[:], c["c_rm"], writes=["rm"])
            P.dma("sp", self.mf[:], c["c_mf"], writes=["mf"])
            P.dma("sp", self.mb[:], c["c_mb"], writes=["mb"])
            P.dma("sp", idf[:], c["c_id"], writes=["idf"])
            P.op("dve", lambda e: e.tensor_copy(out=self.idb[:], in_=idf[:]), reads=["idf"], writes=["idb"])
        else:
            self.rm, self.mf, self.mb, self.idb = share.rm, share.mf, share.mb, share.idb
        f = lambda n, w=SEG, dt=F32, p=64: P.sb(n, [p, w], dt)
        self.q = f("s_q"); self.k = f("s_k"); self.x = f("s_x"); self.g = f("s_g")
        self.Pc = f("s_P"); self.Gh = f("s_Gh"); self.t1 = f("s_t1"); self.t2 = f("s_t2")
        self.qt = f("s_qt", SEG, BF16); self.kt = f("s_kt", SEG, BF16)
        self.ga = self.x[0:16, :]
        self.v32 = P.sb("s_v32", [64, NCS, 64], F32)
        self.vb = P.sb("s_vb", [64, NCS, 64], BF16)
        self.ktok = P.sb("s_ktok", [64, NCS, 64], BF16)
        self.AT = P.sb("s_AT", [64, 64, NCS], F32)
        self.D0 = P.sb("s_D0", [64, 64, NCS], F32)
        self.Sall = P.sb("s_Sall", [64, 64, NCS], F32)
        self.Sp = P.sb("s_Sp", [64, NCS, 64], BF16)
        self.S = P.sb("s_S", [64, 64], F32)
        self.cc = P.sb("s_cc", [64, 3, NCS], F32)
        self.ct = P.sb("s_ct", [64, 2, NCS], F32)
        self.Pm = P.sb("s_Pm", [64, 8, 64], BF16)
        self.o = P.sb("s_o", [64, NCS, 64], F32)
        self.wa = P.sb("s_wa", [16, 2, 32], F32)
        self.ba = P.sb("s_ba", [64, 4], F32)
        self.lb = P.sb("s_lb", [64, 10], F32)
        P.op("dve", lambda e: e.memset(self.kt[:], 0.0), writes=[(self.id, "kt")])
        self.p_g = P.ps("p_g", [64, 512], F32)
        self.p_t = self.p_g[:].bitcast(BF16)
        self.p_A = [P.ps("p_A", [64, 512], F32)]
        self.p_s = P.ps("p_s", [64, 512], F32)
        self.p_o = [P.ps("p_o", [64, 512], F32)]


def scan_dir(C, kind, layer, K, d, qT_d, kT_d, xg_d, v_d, o_d, prm, rkeys=()):
    P0 = C.P
    GLOBALK = ("rm", "mf", "mb", "idb", "FM", "TM", "o_out")

    class _PX:
        nins = 0

        @staticmethod
        def kx(k):
            return k if (isinstance(k, str) and k in GLOBALK) else (C.id, k)

        @staticmethod
        def op(e, fn, reads=(), writes=()):
            return P0.op(e, fn, [_PX.kx(k) for k in reads], [_PX.kx(k) for k in writes])

        @staticmethod
        def dma(q, out, in_, reads=(), writes=(), **kw):
            return P0.dma(q, out, in_, [_PX.kx(k) for k in reads], [_PX.kx(k) for k in writes], **kw)

    P = _PX
    L = C.L
    nseg = L // SEG
    rev = (d == 1)
    segs = list(range(nseg))
    if rev:
        segs = segs[::-1]
    S = C.S
    P.op("dve", lambda e: e.memset(S[0:K, :], 0.0), writes=["S"])
    gscale = (1.0 / 16.0) if kind == "gla" else 1.0
    sgn = -gscale if kind == "gla" else 1.0
    qbias = math.log(float(K) ** -0.5) if kind == "gla" else 0.0
    lbzero = (kind == "gla") or (layer == 0)
    for sg in segs:
        t0 = sg * SEG
        sl = slice(t0, t0 + SEG)
        rk = list(rkeys)
        P.dma("sp", C.q[0:K, :], qT_d[:, sl], reads=rk, writes=["q"])
        P.dma("sp", C.v32[:], v_d[sl, :].rearrange("(n p) v -> p n v", p=64), reads=rk, writes=["v32"])
        P.op("act", lambda e: e.activation(out=C.vb[:].rearrange("p n v -> p (n v)"), in_=C.v32[:].rearrange("p n v -> p (n v)"), func=AF.Copy), reads=["v32"], writes=["vb"])
        if kind == "gla":
            P.dma("sp", C.k[0:K, :], kT_d[:, sl], reads=rk, writes=["k"])
            P.dma("sp", C.ga[:], xg_d[:, sl], reads=rk, writes=["x"])
            for c in range(SEG // 512):
                P.op("pe", lambda e, c=c: e.matmul(C.p_g[0:K, :], lhsT=prm["wa"][:, d, :], rhs=C.ga[:, c * 512:(c + 1) * 512], start=True, stop=True),
                     reads=["x", "wa"], writes=["p_g"])
                P.op("act", lambda e, c=c: e.activation(out=C.t1[0:K, c * 512:(c + 1) * 512], in_=C.p_g[0:K, :], func=AF.Exp, scale=-1.0, bias=prm["nba"][0:K, d:d + 1]),
                     reads=["p_g", "nba"], writes=[("t1", c)])
            t1k = [("t1", c) for c in range(SEG // 512)]
            P.op("act", lambda e: e.activation(out=C.g[0:K, :], in_=C.t1[0:K, :], func=AF.Ln, bias=1.0, scale=1.0), reads=t1k, writes=["g"])
        else:
            P.dma("sp", C.x[0:K, :], xg_d[:, sl], reads=rk, writes=["x"])
            P.op("act", lambda e: e.activation(out=C.q[0:K, :], in_=C.q[0:K, :], func=AF.Silu), reads=["q"], writes=["q"])
            P.op("act", lambda e: e.activation(out=C.t2[0:K, :], in_=C.x[0:K, :], func=AF.Sigmoid), reads=["x"], writes=["t2"])
            if lbzero:
                P.op("dve", lambda e: e.tensor_scalar(out=C.k[0:K, :], in0=C.t2[0:K, :], scalar1=-1.0, scalar2=1.0, op0=ALU.mult, op1=ALU.add), reads=["t2"], writes=["k"])
                P.op("act", lambda e: e.activation(out=C.g[0:K, :], in_=C.t2[0:K, :], func=AF.Ln), reads=["t2"], writes=["g"])
            else:
                P.op("dve", lambda e: e.tensor_scalar(out=C.k[0:K, :], in0=C.t2[0:K, :], scalar1=prm["noml"][0:K, d:d + 1], scalar2=prm["oml"][0:K, d:d + 1],
                                                      op0=ALU.mult, op1=ALU.add), reads=["t2", "oml"], writes=["k"])
                P.op("dve", lambda e: e.tensor_scalar(out=C.t2[0:K, :], in0=C.t2[0:K, :], scalar1=prm["oml"][0:K, d:d + 1], scalar2=prm["lb"][0:K, d:d + 1],
                                                      op0=ALU.mult, op1=ALU.add), reads=["t2", "oml"], writes=["t2"])
                P.op("act", lambda e: e.activation(out=C.g[0:K, :], in_=C.t2[0:K, :], func=AF.Ln), reads=["t2"], writes=["g"])
        yield
        P.op("dve", lambda e: e.tensor_tensor_scan(out=C.Pc[0:K, :], data0=C.rm[0:K, :], data1=C.g[0:K, :], initial=0.0, op0=ALU.mult, op1=ALU.add),
             reads=["rm", "g"], writes=["Pc"])
        Pv = C.Pc[0:K, :].rearrange("p (n c) -> p n c", c=CH)
        Ghv = C.Gh[0:K, :].rearrange("p (n c) -> p n c", c=CH)
        gv = C.g[0:K, :].rearrange("p (n c) -> p n c", c=CH)
        cc = C.cc
        ct = C.ct
        if not rev:
            mid = 31
            P.op("dve", lambda e: e.tensor_tensor(out=Ghv, in0=Pv, in1=Pv[:, :, mid:mid + 1].to_broadcast([K, NCS, CH]), op=ALU.subtract),
                 reads=["Pc"], writes=["Gh"])
            P.op("act", lambda e: e.activation(out=cc[0:K, 0, :], in_=Pv[:, :, mid], func=AF.Exp, scale=sgn), reads=["Pc"], writes=["cc0"])
            P.op("act", lambda e: e.activation(out=cc[0:K, 1, :], in_=Pv[:, :, CH - 1], func=AF.Exp, scale=sgn), reads=["Pc"], writes=["cc1"])
            P.op("act", lambda e: e.activation(out=cc[0:K, 2, :], in_=Ghv[:, :, CH - 1], func=AF.Exp, scale=sgn), reads=["Gh"], writes=["cc2"])
        else:
            mid = 32
            Ev = C.t1[0:K, :].rearrange("p (n c) -> p n c", c=CH)
            P.op("dve", lambda e: e.tensor_tensor(out=C.t1[0:K, :], in0=C.Pc[0:K, :], in1=C.g[0:K, :], op=ALU.subtract), reads=["Pc", "g"], writes=["t1"])
            P.op("dve", lambda e: e.tensor_tensor(out=Ghv, in0=Ev[:, :, mid:mid + 1].to_broadcast([K, NCS, CH]), in1=Ev, op=ALU.subtract),
                 reads=["t1"], writes=["Gh"])
            P.op("dve", lambda e: e.tensor_tensor(out=ct[0:K, 0, :], in0=Pv[:, :, CH - 1], in1=Ev[:, :, mid], op=ALU.subtract), reads=["Pc", "t1"], writes=["ct"])
            P.op("act", lambda e: e.activation(out=cc[0:K, 0, :], in_=ct[0:K, 0, :], func=AF.Exp, scale=sgn), reads=["ct"], writes=["cc0"])
            P.op("act", lambda e: e.activation(out=cc[0:K, 1, :], in_=Pv[:, :, CH - 1], func=AF.Exp, scale=sgn), reads=["Pc"], writes=["cc1"])
            P.op("act", lambda e: e.activation(out=cc[0:K, 2, :], in_=Ev[:, :, mid], func=AF.Exp, scale=sgn), reads=["t1"], writes=["cc2"])
        yield
        P.op("act", lambda e: e.activation(out=C.t2[0:K, :], in_=C.Gh[0:K, :], func=AF.Exp, scale=sgn, bias=qbias), reads=["Gh"], writes=["t2"])
        P.op("pool", lambda e: e.tensor_tensor(out=C.qt[0:K, :], in0=C.q[0:K, :], in1=C.t2[0:K, :], op=ALU.mult), reads=["q", "t2"], writes=["qt"])
        P.op("act", lambda e: e.activation(out=C.t1[0:K, :], in_=C.Gh[0:K, :], func=AF.Exp, scale=-sgn), reads=["Gh"], writes=["t1"])
        P.op("dve", lambda e: e.tensor_tensor(out=C.kt[0:K, :], in0=C.k[0:K, :], in1=C.t1[0:K, :], op=ALU.mult), reads=["k", "t1"], writes=["kt"])
        yield
        for half in range(NCS // 16):
            for cl in range(16):
                n = half * 16 + cl
                P.op("pe", lambda e, cl=cl, n=n: e.transpose(out=C.p_t[:, cl * 64:(cl + 1) * 64], in_=C.kt[0:64, n * 64:(n + 1) * 64], identity=C.idb[0:64, 0:64]),
                     reads=["kt", "idb"], writes=["p_g"])
            P.op("act", lambda e, half=half: e.activation(out=C.ktok[:, half * 16:(half + 1) * 16, :].rearrange("p n k -> p (n k)"), in_=C.p_t[:], func=AF.Copy),
                 reads=["p_g"], writes=[("ktok", half)])
        yield
        for grp in range(NCS // 8):
            pa = C.p_A[0]
            pak = ("p_A", 0)
            for ci in range(8):
                n = grp * 8 + ci
                P.op("pe", lambda e, ci=ci, n=n: e.matmul(pa[0:K, ci * 64:(ci + 1) * 64], lhsT=C.ktok[:, n, 0:K], rhs=C.vb[:, n, :], start=True, stop=True),
                     reads=[("ktok", n // 16), "vb"], writes=[pak])
            P.op("dve", lambda e, grp=grp: e.tensor_tensor(out=C.AT[0:K, :, grp * 8:(grp + 1) * 8].rearrange("p v n -> p n v"), in0=pa[0:K, :].rearrange("p (n v) -> p n v", v=64),
                                                           in1=cc[0:K, 2, grp * 8:(grp + 1) * 8].unsqueeze(2).to_broadcast([K, 8, 64]), op=ALU.mult),
                 reads=[pak, "cc2"], writes=[("A", grp)])
        yield
        nf = NCS - 1 if rev else 0
        nl = 0 if rev else NCS - 1
        Ak = [("A", g_) for g_ in range(NCS // 8)]
        P.op("dve", lambda e: e.tensor_copy(out=C.D0[0:K, :, :], in_=cc[0:K, 1, :].unsqueeze(1).to_broadcast([K, 64, NCS])), reads=["cc1"], writes=["D0"])
        P.op("dve", lambda e: e.memset(C.D0[0:K, :, nf:nf + 1], 0.0), reads=["D0"], writes=["D0"])
        P.op("dve", lambda e: e.scalar_tensor_tensor(out=C.AT[0:K, :, nf], in0=S[0:K, :], scalar=cc[0:K, 1, nf:nf + 1], in1=C.AT[0:K, :, nf], op0=ALU.mult, op1=ALU.add),
             reads=["S", "cc1"] + Ak, writes=Ak)
        P.op("act", lambda e: e.activation(out=C.Sp[0:K, nf, :], in_=S[0:K, :], func=AF.Copy, scale=cc[0:K, 0, nf:nf + 1]), reads=["S", "cc0"], writes=[("Sp", "f")])
        AF_ = C.AT[0:K, :, :].rearrange("p v n -> p (v n)")
        DF_ = C.D0[0:K, :, :].rearrange("p v n -> p (v n)")
        SF_ = C.Sall[0:K, :, :].rearrange("p v n -> p (v n)")
        if rev:
            AF_, DF_, SF_ = AF_[:, ::-1], DF_[:, ::-1], SF_[:, ::-1]
        P.op("dve", lambda e: e.tensor_tensor_scan(out=SF_, data0=DF_, data1=AF_, initial=0.0, op0=ALU.mult, op1=ALU.add), reads=Ak + ["D0"], writes=["Sall"])
        if not rev:
            P.op("dve", lambda e: e.tensor_tensor(out=C.Sp[0:K, 1:NCS, :], in0=C.Sall[0:K, :, 0:NCS - 1].rearrange("p v n -> p n v"),
                                                   in1=cc[0:K, 0, 1:NCS].unsqueeze(2).to_broadcast([K, NCS - 1, 64]), op=ALU.mult), reads=["Sall", "cc0"], writes=[("Sp", "r")])
        else:
            P.op("dve", lambda e: e.tensor_tensor(out=C.Sp[0:K, 0:NCS - 1, :], in0=C.Sall[0:K, :, 1:NCS].rearrange("p v n -> p n v"),
                                                   in1=cc[0:K, 0, 0:NCS - 1].unsqueeze(2).to_broadcast([K, NCS - 1, 64]), op=ALU.mult), reads=["Sall", "cc0"], writes=[("Sp", "r")])
        P.op("dve", lambda e: e.tensor_copy(out=S[0:K, :], in_=C.Sall[0:K, :, nl]), reads=["Sall", "S"], writes=["S"])
        yield
        mask = C.mb if rev else C.mf
        for grp in range(NCS // 8):
            for cl in range(8):
                n = grp * 8 + cl
                P.op("pe", lambda e, cl=cl, n=n: e.matmul(C.p_s[:, cl * 64:(cl + 1) * 64], lhsT=C.kt[0:K, n * 64:(n + 1) * 64],
                                                          rhs=C.qt[0:K, n * 64:(n + 1) * 64], start=True, stop=True),
                     reads=["kt", "qt"], writes=["p_s"])
            P.op("dve", lambda e: e.tensor_tensor(out=C.Pm[:].rearrange("p n c -> p (n c)"), in0=C.p_s[:], in1=mask[0:64, :], op=ALU.mult),
                 reads=["p_s", "mf", "mb"], writes=["Pm"])
            po = C.p_o[0]
            pok = ("p_o", 0)
            for cl in range(8):
                n = grp * 8 + cl
                P.op("pe", lambda e, cl=cl, n=n: e.matmul(po[:, cl * 64:(cl + 1) * 64], lhsT=C.Pm[:, cl, :], rhs=C.vb[:, n, :], start=True, stop=False),
                     reads=["Pm", "vb"], writes=[pok])
                P.op("pe", lambda e, cl=cl, n=n: e.matmul(po[:, cl * 64:(cl + 1) * 64], lhsT=C.qt[0:K, n * 64:(n + 1) * 64], rhs=C.Sp[0:K, n, :], start=False, stop=True),
                     reads=["qt", ("Sp", "f"), ("Sp", "r")], writes=[pok])
            P.op("act", lambda e, grp=grp: e.activation(out=C.o[:, grp * 8:(grp + 1) * 8, :].rearrange("p n v -> p (n v)"), in_=po[:], func=AF.Copy),
                 reads=[pok], writes=[("o", grp)])
            yield
        P.dma("sp", o_d[sl, :].rearrange("(n p) v -> p n v", p=64), C.o[:], reads=[("o", g_) for g_ in range(NCS // 8)], writes=["o_out"])

import math
import numpy as np

HY_W = 256
TWO_PI_LO = 6.283185


def hyena_consts(L, core):
    f32 = np.float32
    t = np.linspace(0.0, 1.0, L, dtype=f32)[:, None]
    w = (2.0 * math.pi * np.arange(L, dtype=f32)[:, None] / L).astype(f32)
    bands = np.linspace(1e-4, 15, 16, dtype=f32)[None, :]
    z = np.concatenate([t, np.cos(bands * w), -np.sin(bands * w)], axis=-1).astype(f32)
    max_decay = math.log(1e-2) / 0.3
    min_decay = math.log(1e-2) / 1.5
    deltas = np.linspace(min_decay, max_decay, HY_W, dtype=f32)
    dec = np.exp(-t * np.abs(deltas)).astype(f32)
    tidx = np.concatenate([np.arange(L - 1, -1, -1), np.arange(1, L), [0]])
    zR = np.ascontiguousarray(z[tidx].T)
    zR[:, -1] = 0.0
    ch = np.arange(core * 32, core * 32 + 32)
    d = dec[tidx][:, ch].T
    d[:, -1] = 0.0
    decR = np.ascontiguousarray(np.concatenate([d, d], 0))
    J = np.ascontiguousarray(np.eye(128, dtype=f32)[::-1])
    return {"h_zR": zR, "h_decR": decR.astype(f32), "h_J": J, "h_id": np.eye(128, dtype=f32),
            "h_ones": np.ones((64, 128), f32)}


def hyena_stage(P, L, d):
    NB = L // 128
    NT = 2 * L // 512
    HW = 2 * L - 128
    nc = P.nc
    sb = P.sb
    w1 = sb("h_w1", [33, 64]); b1f = sb("h_b1f", [64, 8]); w2 = sb("h_w2", [64, 2, 64]); w3 = sb("h_w3", [64, 2, 64])
    cw = sb("h_cw", [128, 9]); cb = sb("h_cb", [128, 3]); bias = sb("h_bias", [128, 2])
    Jm = sb("h_Jm", [128, 128]); idf = sb("h_idf", [128, 128]); idb = sb("h_idb", [128, 128], BF16); ones = sb("h_onesb", [64, 128])
    P.dma("sp", w1[:], d["w1"], writes=["h_w1"])
    P.dma("sp", b1f[:, 0:4], d["b1f"], writes=["h_b1f"])
    P.dma("sp", w2[:], d["w2"], writes=["h_w2"])
    P.dma("sp", w3[:], d["w3"], writes=["h_w3"])
    P.dma("sp", cw[:], d["cw"].rearrange("p a b -> p (a b)"), writes=["h_cw"])
    P.dma("sp", cb[:], d["cb"], writes=["h_cb"])
    P.dma("sp", bias[:], d["bias"], writes=["h_bias"])
    P.dma("sp", Jm[:], d["h_J"], writes=["h_J"])
    P.dma("sp", idf[:], d["h_id"], writes=["h_idf"])
    P.dma("sp", ones[:], d["h_ones"], writes=["h_ones"])
    P.op("dve", lambda e: e.tensor_copy(out=idb[:], in_=idf[:]), reads=["h_idf"], writes=["h_idb"])
    P.op("dve", lambda e: e.tensor_scalar(out=b1f[:, 4:5], in0=b1f[:, 3:4], scalar1=1.0 / (2 * math.pi), scalar2=None, op0=ALU.mult), reads=["h_b1f"], writes=["h_A"])
    P.op("dve", lambda e: e.tensor_scalar(out=b1f[:, 5:8], in0=b1f[:, 0:3], scalar1=b1f[:, 4:5], scalar2=None, op0=ALU.mult), reads=["h_b1f", "h_A"], writes=["h_B"])
    pA = P.ps("h_pA", [64, 512])
    zt = [sb("h_zt%d" % i, [33, 512]) for i in range(2)]
    dt_ = [sb("h_dt%d" % i, [64, 512]) for i in range(2)]
    u = sb("h_u", [64, 512]); ui = sb("h_ui", [64, 512], I32); hh = sb("h_hh", [64, 512])
    Rt = sb("h_Rt", [64, 512]); Rb = [sb("h_Rb%d" % i, [64, 512], BF16) for i in range(2)]
    asum = sb("h_asum", [64, NT]); rn = sb("h_rn", [64, 2]); dg = sb("h_dg", [64, 64]); rnb = sb("h_rnb", [128, 64])
    scr = d["scr"]
    for t in range(NT):
        b = t % 2
        P.dma("sp", zt[b][:], d["h_zR"][:, t * 512:(t + 1) * 512], writes=[("zt", b)])
        P.dma("sp", dt_[b][:], d["h_decR"][:, t * 512:(t + 1) * 512], writes=[("dt", b)])
        rhs = zt[b]
        rk = ("zt", b)
        for l in range(3):
            lhsT = w1[:, :] if l == 0 else w2[:, l - 1, :]
            kk = 33 if l == 0 else 64
            P.op("pe", lambda e, lhsT=lhsT, rhs=rhs, kk=kk: e.matmul(pA[:, :], lhsT=lhsT, rhs=rhs[0:kk, :], start=True, stop=True),
                 reads=[rk, "h_w1", "h_w2"], writes=["h_pA"])
            P.op("dve", lambda e, l=l: e.tensor_scalar(out=u[:], in0=pA[:], scalar1=b1f[:, 4:5], scalar2=b1f[:, 5 + l:6 + l], op0=ALU.mult, op1=ALU.add),
                 reads=["h_pA", "h_A", "h_B"], writes=["h_u"])
            P.op("dve", lambda e: e.tensor_copy(out=ui[:], in_=u[:]), reads=["h_u"], writes=["h_ui"])
            P.op("dve", lambda e: e.tensor_tensor(out=u[:], in0=u[:], in1=ui[:], op=ALU.subtract), reads=["h_u", "h_ui"], writes=["h_u"])
            P.op("act", lambda e: e.activation(out=hh[:], in_=u[:], func=AF.Sin, scale=TWO_PI_LO), reads=["h_u"], writes=["h_hh"])
            rhs = hh
            rk = "h_hh"
        dr = 0 if t < NT // 2 else 1
        P.op("pe", lambda e, dr=dr: e.matmul(pA[:, :], lhsT=w3[:, dr, :], rhs=hh[:, :], start=True, stop=True), reads=["h_hh", "h_w3"], writes=["h_pA"])
        P.op("dve", lambda e: e.tensor_tensor(out=Rt[:], in0=pA[:], in1=dt_[b][:], op=ALU.mult), reads=["h_pA", ("dt", b)], writes=["h_Rt"])
        P.op("dve", lambda e, t=t: e.tensor_reduce(out=asum[:, t:t + 1], in_=Rt[:], axis=AX.X, op=ALU.add, apply_absolute_value=True), reads=["h_Rt"], writes=[("asum", t)])
        P.op("act", lambda e: e.activation(out=Rb[b][:], in_=Rt[:], func=AF.Copy), reads=["h_Rt"], writes=[("Rb", b)])
        P.dma("sp", scr[:, t * 512:(t + 1) * 512], Rb[b][:], reads=[("Rb", b)], writes=["scr"])
    P.op("dve", lambda e: e.tensor_reduce(out=rn[:, 0:1], in_=asum[:], axis=AX.X, op=ALU.add), reads=[("asum", t) for t in range(NT)], writes=["h_rn"])
    P.op("dve", lambda e: e.tensor_scalar(out=rn[:, 0:1], in0=rn[:, 0:1], scalar1=1e-12, scalar2=None, op0=ALU.max), reads=["h_rn"], writes=["h_rn"])
    P.op("dve", lambda e: e.reciprocal(out=rn[:, 1:2], in_=rn[:, 0:1]), reads=["h_rn"], writes=["h_rn1"])
    P.op("dve", lambda e: e.tensor_scalar(out=dg[:], in0=idf[0:64, 0:64], scalar1=rn[:, 1:2], scalar2=None, op0=ALU.mult), reads=["h_idf", "h_rn1"], writes=["h_dg"])
    pB = P.ps("h_pB", [128, 512])
    P.op("pe", lambda e: e.matmul(pB[:, 0:64], lhsT=ones[:, :], rhs=dg[:, :], start=True, stop=True), reads=["h_ones", "h_dg"], writes=["h_pB"])
    P.op("dve", lambda e: e.tensor_copy(out=rnb[:], in_=pB[:, 0:64]), reads=["h_pB"], writes=["h_rnb"])
    za = sb("h_za", [128, L]); zc = sb("h_zc", [128, L])
    zb = [sb("h_zb%d" % i, [128, 1024], BF16) for i in range(2)]
    ZT = sb("h_ZT", [128, NB, 128], BF16)
    YT = sb("h_YT", [128, NB, 128])
    xr = YT[:].rearrange("p a b -> p (a b)")
    ytk = [("YT", c) for c in range(32)]
    H = [sb("h_H%d" % i, [128, HW], BF16) for i in range(2)]
    pT = P.ps("h_pT", [128, 1024], BF16)
    pC = [P.ps("h_pC%d" % i, [128, 512]) for i in range(2)]

    def short_conv(idx, dst, dk):
        P.dma("sp", xr, d["u3"][idx], writes=ytk)
        P.op("dve", lambda e: e.tensor_scalar(out=dst[:], in0=xr, scalar1=cw[:, idx * 3 + 1:idx * 3 + 2], scalar2=cb[:, idx:idx + 1], op0=ALU.mult, op1=ALU.add),
             reads=ytk + ["h_cw", "h_cb"], writes=[dk])
        P.op("dve", lambda e: e.scalar_tensor_tensor(out=dst[:, 1:L], in0=xr[:, 0:L - 1], scalar=cw[:, idx * 3:idx * 3 + 1], in1=dst[:, 1:L], op0=ALU.mult, op1=ALU.add),
             reads=ytk + ["h_cw", dk], writes=[dk])
        P.op("dve", lambda e: e.scalar_tensor_tensor(out=dst[:, 0:L - 1], in0=xr[:, 1:L], scalar=cw[:, idx * 3 + 2:idx * 3 + 3], in1=dst[:, 0:L - 1], op0=ALU.mult, op1=ALU.add),
             reads=ytk + ["h_cw", dk], writes=[dk])

    short_conv(0, za, "h_za")
    hq = 0
    for o in range(2):
        for g in range(NB // 8):
            zbb = zb[g % 2]
            zbk = ("h_zb", g % 2)
            P.op("act", lambda e, g=g, zbb=zbb: e.activation(out=zbb[:], in_=za[:, g * 1024:(g + 1) * 1024], func=AF.Copy), reads=["h_za"], writes=[zbk])
            for jj in range(8):
                j = g * 8 + jj
                P.op("pe", lambda e, jj=jj, zbb=zbb: e.transpose(out=pT[:, jj * 128:(jj + 1) * 128], in_=zbb[:, jj * 128:(jj + 1) * 128], identity=idb[:]),
                     reads=[zbk, "h_idb"], writes=["h_pT"])
            P.op("dve", lambda e, g=g: e.tensor_copy(out=ZT[:, g * 8:(g + 1) * 8, :].rearrange("p a b -> p (a b)"), in_=pT[:]), reads=["h_pT"], writes=[("ZT", g)])
        ztk = [("ZT", g) for g in range(NB // 8)]
        short_conv(1 + o, zc, "h_zc")
        for c in range(32):
            lane = o * 32 + c
            hb = hq % 2
            hq += 1
            q = "sp" if hb == 0 else "pool"
            P.dma(q, H[hb][:], bass.AP(scr.tensor, lane * 2 * L, [[1, 128], [1, HW]]), reads=["scr"], writes=[("H", hb)])
            pc = pC[hb]
            pck = ("pC", hb)
            ds = [0] + [x for k in range(1, NB) for x in (k, -k)]
            for n_, dd in enumerate(ds):
                j0, j1 = max(0, -dd), min(NB, NB - dd)
                x0 = L - 128 - 128 * dd
                P.op("pe", lambda e, dd=dd, j0=j0, j1=j1, x0=x0, n_=n_: e.matmul(
                    pc[:, (j0 + dd) * 4:(j1 + dd) * 4].rearrange("p (i b) -> p i b", b=4), lhsT=H[hb][:, x0:x0 + 128],
                    rhs=ZT[:, j0:j1, c * 4:(c + 1) * 4], start=(n_ == 0), stop=(n_ == len(ds) - 1)),
                     reads=[("H", hb)] + ztk, writes=[pck])
            eng = "act" if c % 2 == 0 else "dve"
            if eng == "act":
                P.op("act", lambda e, c=c, lane=lane: e.activation(out=YT[:, :, c * 4:(c + 1) * 4], in_=pc[:, 0:NB * 4].rearrange("p (i b) -> p i b", b=4), func=AF.Copy,
                                                                   scale=rnb[:, lane:lane + 1]), reads=[pck, "h_rnb"], writes=[("YT", c)])
            else:
                P.op("dve", lambda e, c=c, lane=lane: e.tensor_scalar(out=YT[:, :, c * 4:(c + 1) * 4], in0=pc[:, 0:NB * 4].rearrange("p (i b) -> p i b", b=4),
                                                                      scalar1=rnb[:, lane:lane + 1], scalar2=None, op0=ALU.mult), reads=[pck, "h_rnb"], writes=[("YT", c)])
        for g in range(NB // 4):
            for ii in range(4):
                i = g * 4 + ii
                P.op("pe", lambda e, ii=ii, i=i: e.matmul(pB[:, ii * 128:(ii + 1) * 128], lhsT=YT[:, i, :], rhs=Jm[:, :], start=True, stop=True),
                     reads=ytk + ["h_J"], writes=["h_pB"])
            sl = slice(g * 512, (g + 1) * 512)
            P.op("dve", lambda e, sl=sl, o=o: e.scalar_tensor_tensor(out=za[:, sl], in0=za[:, sl], scalar=bias[:, o:o + 1], in1=pB[:, :], op0=ALU.mult, op1=ALU.add),
                 reads=["h_za", "h_bias", "h_pB"], writes=["h_za"])
            P.op("pool", lambda e, sl=sl: e.tensor_tensor(out=za[:, sl], in0=za[:, sl], in1=zc[:, sl], op=ALU.mult), reads=["h_za", "h_zc"], writes=["h_za"])
    P.dma("sp", d["y"], za[:], reads=["h_za"], writes=["h_y"])

import math
import numpy as np

import os
STOPC = int(os.environ.get('STOPC', '99'))
D = 1024
ALPHA = 4 ** 0.25
NEXP = 16
DE = 512


def layer_norm_tile(P, x, xk, st, mv, stk, epst, gb, gbk):
    for c in range(2):
        P.op("dve", lambda e, c=c: e.bn_stats(out=st[:, c, :], in_=x[:, c * 512:(c + 1) * 512]), reads=[xk], writes=[(stk, c)])
    P.op("dve", lambda e: e.bn_aggr(out=mv[:, 0:2], in_=st[:].rearrange("p a b -> p (a b)")), reads=[(stk, 0), (stk, 1)], writes=[(stk, "mv")])
    P.op("act", lambda e: e.activation(out=mv[:, 2:3], in_=mv[:, 1:2], func=AF.Sqrt, bias=epst[:, 0:1], scale=1.0), reads=[(stk, "mv"), "eps"], writes=[(stk, "sd")])
    P.op("dve", lambda e: e.reciprocal(out=mv[:, 3:4], in_=mv[:, 2:3]), reads=[(stk, "sd")], writes=[(stk, "rs")])
    P.op("dve", lambda e: e.tensor_scalar(out=x[:], in0=x[:], scalar1=mv[:, 0:1], scalar2=mv[:, 3:4], op0=ALU.subtract, op1=ALU.mult),
         reads=[xk, (stk, "rs"), (stk, "mv")], writes=[xk])
    P.op("pool", lambda e: e.tensor_tensor(out=x[:], in0=x[:], in1=gb[:, 0:D], op=ALU.mult), reads=[xk, gbk], writes=[xk])
    P.op("pool", lambda e: e.tensor_tensor(out=x[:], in0=x[:], in1=gb[:, D:2 * D], op=ALU.add), reads=[xk, gbk], writes=[xk])


def stage_C(P, NT, d):
    nc = P.nc
    sb, ps = P.sb, P.ps
    T = NT * 128
    epst = sb("epst", [128, 1]); eps6 = sb("eps6", [128, 1])
    P.op("dve", lambda e: e.memset(epst[:], 1e-5), writes=["eps"])
    P.op("dve", lambda e: e.memset(eps6[:], 1e-6), writes=["eps6"])
    idf = sb("c_idf", [128, 128]); idb = sb("c_idb", [128, 128], BF16)
    P.dma("sp", idf[:], d["c_id"], writes=["idf"])
    P.op("dve", lambda e: e.tensor_copy(out=idb[:], in_=idf[:]), reads=["idf"], writes=["idb"])
    h1T = sb("h1T", [128, 8, T], BF16)
    wfull = sb("wfull", [128, NT, 16])
    P.push_scope()
    wout = sb("woutb", [128, 8, D], BF16)
    for kc in range(8):
        P.dma("pool", wout[:, kc, :], d["wout"][kc * 128:(kc + 1) * 128, :], writes=[("wout", kc)])
    woutk = [("wout", kc) for kc in range(8)]
    gn = sb("gn", [128, 768]); ln1 = sb("ln1", [128, 2 * D]); wr = sb("wr", [128, 8, 20]); br = sb("br", [128, 20])
    P.dma("sp", gn[:], d["gn"], writes=["gn"])
    P.dma("sp", ln1[:], d["ln1"], writes=["ln1"])
    P.dma("sp", wr[:], d["wr"].rearrange("(k p) n -> p k n", p=128), writes=["wr"])
    P.dma("sp", br[:], d["br"], writes=["br"])
    NBUF = 2
    OF = [sb("OF%d" % i, [128, 768]) for i in range(NBUF)]
    OB = [sb("OB%d" % i, [128, 768]) for i in range(NBUF)]
    GT = [sb("GT%d" % i, [128, 768]) for i in range(NBUF)]
    mix = [sb("mix%d" % i, [128, D]) for i in range(NBUF)]
    mixb = [sb("mixb%d" % i, [128, D], BF16) for i in range(NBUF)]
    mixT = [sb("mixT%d" % i, [128, 8, 128], BF16) for i in range(NBUF)]
    ht = [sb("ht%d" % i, [128, D]) for i in range(NBUF)]
    sq = sb("sq", [128, 768]); ss = sb("ss", [128, 12]); rs = sb("rs_", [128, 12])
    st = sb("st", [128, 2, 6]); mv = sb("mv", [128, 4])
    h1T32 = sb("h1T32", [128, 8, 128])
    lg = sb("lg", [128, 20]); rt = sb("rt", [128, 64])
    pT = ps("pT", [128, 1024], BF16)
    pO = [ps("pO%d" % i, [128, 512]) for i in range(2)]
    pX = [ps("pX%d" % i, [128, 512]) for i in range(2)]
    pL = ps("pL", [128, 512])
    for t in range(NT):
        b = t % NBUF
        rows = slice(t * 128, (t + 1) * 128)
        kOF, kOB, kGT, kmix, kmixb, kmixT, kht = ("OF", b), ("OB", b), ("GT", b), ("mix", b), ("mixb", b), ("mixT", b), ("ht", b)
        P.dma("sp", OF[b][:], d["OF"][rows, :], writes=[kOF])
        P.dma("sp", OB[b][:], d["OB"][rows, :], writes=[kOB])
        P.dma("sp", GT[b][:], d["GT"][rows, :], writes=[kGT])
        P.dma("sp", mix[b][:, 768:1024], d["YH"][rows, :], writes=[(kmix, "hy")])
        P.dma("sp", ht[b][:], d["h"][rows, :], writes=[kht])
        P.op("pool", lambda e: e.tensor_tensor(out=OF[b][:], in0=OF[b][:], in1=OB[b][:], op=ALU.add), reads=[kOF, kOB], writes=[kOF])
        P.op("dve", lambda e: e.tensor_tensor(out=sq[:], in0=OF[b][:], in1=OF[b][:], op=ALU.mult), reads=[kOF], writes=["sq"])
        P.op("dve", lambda e: e.tensor_reduce(out=ss[:], in_=sq[:].rearrange("p (h v) -> p h v", v=64), axis=AX.X, op=ALU.add), reads=["sq"], writes=["ss"])
        P.op("act", lambda e: e.activation(out=rs[:], in_=ss[:], func=AF.Sqrt, bias=eps6[:, 0:1], scale=1.0 / 64.0), reads=["ss", "eps6"], writes=["rs"])
        P.op("dve", lambda e: e.reciprocal(out=rs[:], in_=rs[:]), reads=["rs"], writes=["rs"])
        P.op("dve", lambda e: e.tensor_tensor(out=OF[b][:].rearrange("p (h v) -> p h v", v=64), in0=OF[b][:].rearrange("p (h v) -> p h v", v=64),
                                              in1=rs[:].unsqueeze(2).to_broadcast([128, 12, 64]), op=ALU.mult), reads=[kOF, "rs"], writes=[kOF])
        P.op("pool", lambda e: e.tensor_tensor(out=OF[b][:], in0=OF[b][:], in1=gn[:], op=ALU.mult), reads=[kOF, "gn"], writes=[kOF])
        P.op("act", lambda e: e.activation(out=sq[:], in_=GT[b][:], func=AF.Exp, scale=-1.0), reads=[kGT], writes=["sq"])
        P.op("dve", lambda e: e.tensor_scalar(out=sq[:], in0=sq[:], scalar1=1.0, scalar2=None, op0=ALU.add), reads=["sq"], writes=["sq"])
        P.op("dve", lambda e: e.reciprocal(out=sq[:], in_=sq[:]), reads=["sq"], writes=["sq"])
        P.op("pool", lambda e: e.tensor_tensor(out=sq[:, 0:384], in0=sq[:, 0:384], in1=GT[b][:, 0:384], op=ALU.mult), reads=["sq", kGT], writes=["sq"])
        P.op("dve", lambda e: e.tensor_tensor(out=mix[b][:, 0:768], in0=OF[b][:], in1=sq[:], op=ALU.mult), reads=[kOF, "sq"], writes=[(kmix, "a")])
        P.op("act", lambda e: e.activation(out=mixb[b][:], in_=mix[b][:], func=AF.Copy), reads=[(kmix, "a"), (kmix, "hy")], writes=[kmixb])
        if STOPC <= 1:
            continue
        for kc in range(8):
            P.op("pe", lambda e, kc=kc: e.transpose(out=pT[:, kc * 128:(kc + 1) * 128], in_=mixb[b][:, kc * 128:(kc + 1) * 128], identity=idb[:]),
                 reads=[kmixb, "idb"], writes=["pT"])
        P.op("dve", lambda e: e.tensor_copy(out=mixT[b][:].rearrange("p a b -> p (a b)"), in_=pT[:]), reads=["pT"], writes=[kmixT])
        for hf in range(2):
            for kc in range(8):
                P.op("pe", lambda e, kc=kc, hf=hf: e.matmul(pO[hf][:, :], lhsT=mixT[b][:, kc, :], rhs=wout[:, kc, hf * 512:(hf + 1) * 512], start=(kc == 0), stop=(kc == 7)),
                     reads=[kmixT] + woutk, writes=[("pO", hf)])
            P.op("dve", lambda e, hf=hf: e.scalar_tensor_tensor(out=ht[b][:, hf * 512:(hf + 1) * 512], in0=ht[b][:, hf * 512:(hf + 1) * 512], scalar=ALPHA,
                                                                in1=pO[hf][:, :], op0=ALU.mult, op1=ALU.add), reads=[kht, ("pO", hf)], writes=[kht])
        layer_norm_tile(P, ht[b], kht, st, mv, "st", epst, ln1, "ln1")
        P.dma("sp", d["h1s"][rows, :], ht[b][:], reads=[kht], writes=["h1s"])
        if STOPC <= 2:
            continue
        for kc in range(8):
            P.op("pe", lambda e, kc=kc: e.matmul(pX[kc // 4][:, (kc % 4) * 128:(kc % 4 + 1) * 128], lhsT=ht[b][:, kc * 128:(kc + 1) * 128], rhs=idf[:], start=True, stop=True),
                 reads=[kht, "idf"], writes=[("pX", kc // 4)])
        for hf in range(2):
            if os.environ.get("NOEVAC") == "1":
                continue
            if os.environ.get("NOEVAC") != "act":
                P.op("act", lambda e, hf=hf: e.activation(out=h1T32[:, hf * 4:(hf + 1) * 4, :].rearrange("p a b -> p (a b)"), in_=pX[hf][:, :], func=AF.Copy),
                     reads=[("pX", hf)], writes=[("h1T32", hf)])
            if os.environ.get("NOEVAC") == "dve":
                continue
            P.op("dve", lambda e, hf=hf: e.tensor_copy(out=h1T[:, hf * 4:(hf + 1) * 4, rows], in_=pX[hf][:, :].rearrange("p (a b) -> p a b", b=128)),
                 reads=[("pX", hf), ("h1T32", hf)], writes=[("h1T", t)])
        if STOPC <= 3:
            continue
        for kc in range(8):
            P.op("pe", lambda e, kc=kc: e.matmul(pL[:, 0:20], lhsT=h1T32[:, kc, :], rhs=wr[:, kc, :], start=(kc == 0), stop=(kc == 7)),
                 reads=[("h1T32", kc // 4), "wr"], writes=["pL"])
        P.op("dve", lambda e: e.tensor_tensor(out=lg[:], in0=pL[:, 0:20], in1=br[:], op=ALU.add), reads=["pL", "br"], writes=["lg"])
        if STOPC <= 4:
            continue
        R_ = lambda a, b_: rt[:, a:b_]
        dv = lambda fn, r=("lg", "rt"), w=("rt",): P.op("dve", fn, reads=list(r), writes=list(w))
        dv(lambda e: e.tensor_reduce(out=R_(0, 1), in_=lg[:, 0:4], axis=AX.X, op=ALU.max))
        dv(lambda e: e.tensor_scalar(out=R_(1, 5), in0=lg[:, 0:4], scalar1=R_(0, 1), scalar2=None, op0=ALU.is_equal))
        dv(lambda e: e.tensor_scalar(out=R_(5, 9), in0=lg[:, 0:4], scalar1=R_(0, 1), scalar2=None, op0=ALU.subtract))
        P.op("act", lambda e: e.activation(out=R_(5, 9), in_=R_(5, 9), func=AF.Exp), reads=["rt"], writes=["rt"])
        dv(lambda e: e.tensor_reduce(out=R_(9, 10), in_=R_(5, 9), axis=AX.X, op=ALU.add))
        dv(lambda e: e.reciprocal(out=R_(10, 11), in_=R_(9, 10)))
        dv(lambda e: e.tensor_tensor(out=rt[:, 40:56].rearrange("p (g e) -> p g e", e=4), in0=lg[:, 4:20].rearrange("p (g e) -> p g e", e=4),
                                     in1=R_(1, 5).unsqueeze(2).to_broadcast([128, 4, 4]), op=ALU.mult))
        dv(lambda e: e.tensor_reduce(out=R_(11, 15), in_=rt[:, 40:56].rearrange("p (g e) -> p e g", e=4), axis=AX.X, op=ALU.add))
        dv(lambda e: e.tensor_reduce(out=R_(15, 16), in_=R_(11, 15), axis=AX.X, op=ALU.max))
        dv(lambda e: e.tensor_scalar(out=R_(16, 20), in0=R_(11, 15), scalar1=R_(15, 16), scalar2=None, op0=ALU.is_equal))
        dv(lambda e: e.scalar_tensor_tensor(out=R_(20, 24), in0=R_(16, 20), scalar=-1e30, in1=R_(11, 15), op0=ALU.mult, op1=ALU.add))
        dv(lambda e: e.tensor_reduce(out=R_(24, 25), in_=R_(20, 24), axis=AX.X, op=ALU.max))
        dv(lambda e: e.tensor_scalar(out=R_(28, 32), in0=R_(20, 24), scalar1=R_(24, 25), scalar2=None, op0=ALU.is_equal))
        dv(lambda e: e.tensor_tensor(out=R_(32, 33), in0=R_(24, 25), in1=R_(15, 16), op=ALU.subtract))
        P.op("act", lambda e: e.activation(out=R_(32, 33), in_=R_(32, 33), func=AF.Exp), reads=["rt"], writes=["rt"])
        dv(lambda e: e.tensor_scalar(out=R_(33, 34), in0=R_(32, 33), scalar1=1.0, scalar2=None, op0=ALU.add))
        dv(lambda e: e.reciprocal(out=R_(34, 35), in_=R_(33, 34)))
        dv(lambda e: e.tensor_tensor(out=R_(35, 36), in0=R_(32, 33), in1=R_(34, 35), op=ALU.mult))
        dv(lambda e: e.tensor_scalar(out=R_(36, 40), in0=R_(16, 20), scalar1=R_(34, 35), scalar2=None, op0=ALU.mult))
        dv(lambda e: e.scalar_tensor_tensor(out=R_(36, 40), in0=R_(28, 32), scalar=R_(35, 36), in1=R_(36, 40), op0=ALU.mult, op1=ALU.add))
        dv(lambda e: e.tensor_scalar(out=R_(36, 40), in0=R_(36, 40), scalar1=R_(10, 11), scalar2=None, op0=ALU.mult))
        dv(lambda e: e.tensor_tensor(out=rt[:, 40:56].rearrange("p (g e) -> p g e", e=4), in0=R_(1, 5).unsqueeze(2).to_broadcast([128, 4, 4]),
                                     in1=R_(36, 40).unsqueeze(1).to_broadcast([128, 4, 4]), op=ALU.mult))
        P.op("dve", lambda e, t=t: e.tensor_copy(out=wfull[:, t, :], in_=rt[:, 40:56]), reads=["rt"], writes=[("wfull", t)])
    P.pop_scope()
    if STOPC <= 5:
        return
    P.push_scope()
    ln2 = sb("ln2", [128, 2 * D])
    P.dma("sp", ln2[:], d["ln2"], writes=["ln2"])
    HT = NT // 2 if NT >= 8 else NT
    nhalf = NT // HT
    yacc = sb("yacc", [128, HT, D])
    wg = [sb("wg%d" % i, [128, 8, DE], BF16) for i in range(2)]
    wu = [sb("wu%d" % i, [128, 8, DE], BF16) for i in range(2)]
    wd = [sb("wd%d" % i, [128, 4, D], BF16) for i in range(2)]
    hid = [sb("hid%d" % i, [128, 4, 512], BF16) for i in range(2)]
    sg = [sb("sg%d" % i, [128, 512]) for i in range(2)]
    st2 = sb("st2", [128, 2, 6]); mv2 = sb("mv2", [128, 4])
    xo = [sb("xo%d" % i, [128, D]) for i in range(2)]
    pG = [ps("pG%d" % i, [128, 512]) for i in range(2)]
    pU = [ps("pU%d" % i, [128, 512]) for i in range(2)]
    pD = [ps("pD%d" % i, [128, 512]) for i in range(2)]
    wq = 0
    for hf_ in range(nhalf):
        tbase = hf_ * HT
        P.op("pool", lambda e: e.memset(yacc[:].rearrange("p a b -> p (a b)"), 0.0), writes=[("yacc", i) for i in range(HT)])
        for ex in range(NEXP):
            wb = wq % 2
            wq += 1
            for kc in range(8):
                P.dma("pool", wg[wb][:, kc, :], d["wg"][ex, kc * 128:(kc + 1) * 128, :], writes=[("wg", wb, kc)])
                P.dma("pool", wu[wb][:, kc, :], d["wu"][ex, kc * 128:(kc + 1) * 128, :], writes=[("wu", wb, kc)])
            for fc in range(4):
                P.dma("pool", wd[wb][:, fc, :], d["wd"][ex, fc * 128:(fc + 1) * 128, :], writes=[("wd", wb, fc)])
            ngrp = (HT * 128) // 512 if HT * 128 >= 512 else 1
            gw = min(512, HT * 128)
            for g in range(ngrp):
                tok0 = tbase * 128 + g * gw
                hb = g % 2
                for fc in range(4):
                    pb = fc % 2
                    for kc in range(8):
                        P.op("pe", lambda e, kc=kc, fc=fc, pb=pb: e.matmul(pG[pb][:, 0:gw], lhsT=wg[wb][:, kc, fc * 128:(fc + 1) * 128], rhs=h1T[:, kc, tok0:tok0 + gw],
                                                                           start=(kc == 0), stop=(kc == 7)),
                             reads=[("wg", wb, kc)] + [("h1T", tt) for tt in range(tok0 // 128, (tok0 + gw) // 128)], writes=[("pG", pb)])
                    for kc in range(8):
                        P.op("pe", lambda e, kc=kc, fc=fc, pb=pb: e.matmul(pU[pb][:, 0:gw], lhsT=wu[wb][:, kc, fc * 128:(fc + 1) * 128], rhs=h1T[:, kc, tok0:tok0 + gw],
                                                                           start=(kc == 0), stop=(kc == 7)),
                             reads=[("wu", wb, kc)] + [("h1T", tt) for tt in range(tok0 // 128, (tok0 + gw) // 128)], writes=[("pU", pb)])
                    P.op("act", lambda e, pb=pb: e.activation(out=sg[pb][:, 0:gw], in_=pG[pb][:, 0:gw], func=AF.Silu), reads=[("pG", pb)], writes=[("sg", pb)])
                    P.op("dve", lambda e, pb=pb, fc=fc, hb=hb: e.tensor_tensor(out=hid[hb][:, fc, 0:gw], in0=sg[pb][:, 0:gw], in1=pU[pb][:, 0:gw], op=ALU.mult),
                         reads=[("sg", pb), ("pU", pb)], writes=[("hid", hb, fc)])
                for ts in range(gw // 128):
                    tl = (tok0 - tbase * 128) // 128 + ts
                    tg = tbase + tl
                    for dh in range(2):
                        pb = dh
                        for fc in range(4):
                            P.op("pe", lambda e, fc=fc, ts=ts, dh=dh, pb=pb: e.matmul(pD[pb][:, :], lhsT=hid[hb][:, fc, ts * 128:(ts + 1) * 128], rhs=wd[wb][:, fc, dh * 512:(dh + 1) * 512],
                                                                                      start=(fc == 0), stop=(fc == 3)),
                                 reads=[("hid", hb, fc), ("wd", wb, fc)], writes=[("pD", pb)])
                        P.op("dve", lambda e, tl=tl, tg=tg, dh=dh, pb=pb, ex=ex: e.scalar_tensor_tensor(out=yacc[:, tl, dh * 512:(dh + 1) * 512], in0=pD[pb][:, :],
                                                                                                       scalar=wfull[:, tg, ex:ex + 1], in1=yacc[:, tl, dh * 512:(dh + 1) * 512],
                                                                                                       op0=ALU.mult, op1=ALU.add),
                             reads=[("pD", pb), ("wfull", tg), ("yacc", tl)], writes=[("yacc", tl)])
        for tl in range(HT):
            tg = tbase + tl
            b = tl % 2
            rows = slice(tg * 128, (tg + 1) * 128)
            P.dma("sp", xo[b][:], d["h1s"][rows, :], reads=["h1s"], writes=[("xo", b)])
            P.op("dve", lambda e, tl=tl, b=b: e.scalar_tensor_tensor(out=xo[b][:], in0=xo[b][:], scalar=ALPHA, in1=yacc[:, tl, :], op0=ALU.mult, op1=ALU.add),
                 reads=[("xo", b), ("yacc", tl)], writes=[("xo", b)])
            layer_norm_tile(P, xo[b], ("xo", b), st2, mv2, "st2", epst, ln2, "ln2")
            P.dma("sp", d["out"][rows, :], xo[b][:], reads=[("xo", b)], writes=["out"])
    P.pop_scope()

import math
import numpy as np

L = 8192
D = 1024
DIN = 3872
NFM = 2336
NTM = 1536
NCORES = 8
import os
NOWDMA = bool(os.environ.get('NOWDMA'))
FM_GQ, FM_GK, FM_GA, FM_HQ, FM_HF, FM_HY = 0, 192, 384, 416, 800, 1568
TM_GV, TM_GG, TM_HI, TM_HGT = 0, 384, 768, 1152
FM_COLS = list(range(0, 384)) + list(range(1152, 1184)) + list(range(1184, 2336)) + list(range(3104, 3872))
TM_COLS = list(range(384, 768)) + list(range(768, 1152)) + list(range(2336, 2720)) + list(range(2720, 3104))


def stage_A2(P, do_ln, x_d, w_d, gb_d, id_d, h_d, FM, TM):
    NG = L // 512
    P.push_scope()
    wsb = P.sb("wsb", [128, 8, DIN], BF16)
    idf = P.sb("a_idf", [128, 128], F32)
    idb = P.sb("a_idb", [128, 128], BF16)
    epst = P.sb("a_epst", [128, 1], F32)
    P.op("dve", lambda e: e.memset(epst[:], 1e-5), writes=["a_eps"])
    P.dma("sp", idf[:], id_d, writes=["a_idf"])
    P.op("dve", lambda e: e.tensor_copy(out=idb[:], in_=idf[:]), reads=["a_idf"], writes=["a_idb"])
    if do_ln:
        gbs = P.sb("a_gbs", [128, 2 * D], F32)
        P.dma("sp", gbs[:], gb_d, writes=["a_gbs"])
    CW = 968
    for kc in range(8):
        for c in range(4):
            P.dma("pool", wsb[:, kc, c * CW:(c + 1) * CW], w_d[kc * 128:(kc + 1) * 128, c * CW:(c + 1) * CW], writes=[("wsb", kc, c)])
    wk = lambda kc: [("wsb", kc, c) for c in range(4)]
    xt = [P.sb("a_xt%d" % i, [128, D], F32) for i in range(2)]
    hb = [P.sb("a_hb%d" % i, [128, D], BF16) for i in range(2)]
    hT = [P.sb("a_hT%d" % i, [128, 8, 512], BF16) for i in range(2)]
    st = P.sb("a_st", [128, 2, 6], F32)
    mv = P.sb("a_mv", [128, 4], F32)
    fo = [P.sb("a_fo%d" % i, [128, 512], F32) for i in range(3)]
    po = [P.sb("a_po%d" % i, [128, NTM], F32) for i in range(2)]
    pT = [P.ps("a_pT%d" % i, [128, 1024], BF16) for i in range(2)]
    pm = [P.ps("a_pm%d" % i, [128, 512], F32) for i in range(4)]
    mmi = 0
    ti = 0
    fi = 0
    for g in range(NG):
        hTg = hT[g % 2]
        khT = ("a_hT", g % 2)
        for tt in range(4):
            t = g * 4 + tt
            b = ti % 2
            ti += 1
            rows = slice(t * 128, (t + 1) * 128)
            kx, kh = ("a_xt", b), ("a_hb", b)
            P.dma("sp", xt[b][:], x_d[rows, :], writes=[kx])
            if do_ln:
                layer_norm_tile(P, xt[b], kx, st, mv, "a_st", epst, gbs, "a_gbs")
                P.dma("sp", h_d[rows, :], xt[b][:], reads=[kx], writes=["hbuf"])
            P.op("act", lambda e, b=b: e.activation(out=hb[b][:], in_=xt[b][:], func=AF.Copy), reads=[kx], writes=[kh])
            ptk = ("a_pT", t % 2)
            for kc in range(8):
                P.op("pe", lambda e, kc=kc, b=b, t=t: e.transpose(out=pT[t % 2][:, kc * 128:(kc + 1) * 128], in_=hb[b][:, kc * 128:(kc + 1) * 128], identity=idb[:]),
                     reads=[kh, "a_idb"], writes=[ptk])
            P.op("dve", lambda e, tt=tt, t=t: e.tensor_copy(out=hTg[:, :, tt * 128:(tt + 1) * 128], in_=pT[t % 2][:].rearrange("p (a b) -> p a b", b=128)),
                 reads=[ptk], writes=[(khT, tt)])
        hk = [(khT, tt) for tt in range(4)]
        for fg in range((NFM + 127) // 128):
            r0 = fg * 128
            rw = min(128, NFM - r0)
            pb = mmi % 4
            mmi += 1
            for kc in range(8):
                P.op("pe", lambda e, kc=kc, pb=pb, r0=r0, rw=rw: e.matmul(pm[pb][0:rw, :], lhsT=wsb[:, kc, r0:r0 + rw], rhs=hTg[:, kc, :], start=(kc == 0), stop=(kc == 7)),
                     reads=hk + wk(kc), writes=[("a_pm", pb)])
            fb = fi % 3
            fi += 1
            if fg % 2 == 0:
                P.op("act", lambda e, pb=pb, fb=fb, rw=rw: e.activation(out=fo[fb][0:rw, :], in_=pm[pb][0:rw, :], func=AF.Copy), reads=[("a_pm", pb)], writes=[("a_fo", fb)])
            else:
                P.op("dve", lambda e, pb=pb, fb=fb, rw=rw: e.tensor_copy(out=fo[fb][0:rw, :], in_=pm[pb][0:rw, :]), reads=[("a_pm", pb)], writes=[("a_fo", fb)])
            P.dma("sp", FM[r0:r0 + rw, g * 512:(g + 1) * 512], fo[fb][0:rw, :], reads=[("a_fo", fb)], writes=["FM"])
        for tt in range(4):
            t = g * 4 + tt
            pbuf = po[t % 2]
            kpo = ("a_po", t % 2)
            for ci in range(3):
                pb = mmi % 4
                mmi += 1
                for kc in range(8):
                    P.op("pe", lambda e, kc=kc, pb=pb, ci=ci, tt=tt: e.matmul(pm[pb][:, :], lhsT=hTg[:, kc, tt * 128:(tt + 1) * 128], rhs=wsb[:, kc, NFM + ci * 512:NFM + (ci + 1) * 512],
                                                                             start=(kc == 0), stop=(kc == 7)), reads=hk + wk(kc), writes=[("a_pm", pb)])
                if ci % 2 == 0:
                    P.op("act", lambda e, pb=pb, ci=ci, pbuf=pbuf: e.activation(out=pbuf[:, ci * 512:(ci + 1) * 512], in_=pm[pb][:, :], func=AF.Copy), reads=[("a_pm", pb)], writes=[(kpo, ci)])
                else:
                    P.op("dve", lambda e, pb=pb, ci=ci, pbuf=pbuf: e.tensor_copy(out=pbuf[:, ci * 512:(ci + 1) * 512], in_=pm[pb][:, :]), reads=[("a_pm", pb)], writes=[(kpo, ci)])
            P.dma("sp", TM[t * 128:(t + 1) * 128, :], pbuf[:], reads=[(kpo, ci) for ci in range(3)], writes=["TM"])
    P.pop_scope()


def scan_params(P, C, kind, layer, h, d):
    k = lambda n: (C.id, n)
    if kind == "gla":
        P.dma("sp", C.wa[:], d["gwa"][h], writes=[k("wa")])
        P.dma("sp", C.ba[0:32, 0:2], d["gba"][h], writes=[k("ba")])
        P.op("dve", lambda e: e.tensor_scalar(out=C.ba[0:32, 2:4], in0=C.ba[0:32, 0:2], scalar1=-1.0, scalar2=None, op0=ALU.mult), reads=[k("ba")], writes=[k("nba")])
        return {"wa": C.wa, "nba": C.ba[:, 2:4]}
    K = 64
    if layer == 0:
        return {}
    P.dma("sp", C.lb[0:K, 0:4], d["hlb"][h], writes=[k("lbl")])
    P.op("dve", lambda e: e.tensor_tensor(out=C.lb[0:K, 4:6], in0=C.lb[0:K, 0:2], in1=C.lb[0:K, 2:4], op=ALU.subtract), reads=[k("lbl")], writes=[k("lb1")])
    P.op("act", lambda e: e.activation(out=C.lb[0:K, 4:6], in_=C.lb[0:K, 4:6], func=AF.Exp), reads=[k("lb1")], writes=[k("lb1")])
    P.op("dve", lambda e: e.tensor_scalar(out=C.lb[0:K, 4:6], in0=C.lb[0:K, 4:6], scalar1=1.0, scalar2=None, op0=ALU.add), reads=[k("lb1")], writes=[k("lb1")])
    P.op("dve", lambda e: e.reciprocal(out=C.lb[0:K, 4:6], in_=C.lb[0:K, 4:6]), reads=[k("lb1")], writes=[k("lb")])
    P.op("dve", lambda e: e.tensor_scalar(out=C.lb[0:K, 6:8], in0=C.lb[0:K, 4:6], scalar1=-1.0, scalar2=1.0, op0=ALU.mult, op1=ALU.add), reads=[k("lb")], writes=[k("oml")])
    P.op("dve", lambda e: e.tensor_scalar(out=C.lb[0:K, 8:10], in0=C.lb[0:K, 6:8], scalar1=-1.0, scalar2=None, op0=ALU.mult), reads=[k("oml")], writes=[k("oml")])
    return {"lb": C.lb[:, 4:6], "oml": C.lb[:, 6:8], "noml": C.lb[:, 8:10]}


def run_gens(gens):
    gens = list(gens)
    while gens:
        for g in list(gens):
            try:
                next(g)
            except StopIteration:
                gens.remove(g)


def scans_all(P, layer, d, FM, TM, OF, OB):
    P.push_scope()
    C0 = ScanCtx(P, L, d, cid=0)
    C1 = ScanCtx(P, L, d, cid=1, share=C0)
    ctxs = [C0, C1]
    for kind, h in [("gla", hh) for hh in range(6)] + [("hg", hh) for hh in range(6)]:
        gens = []
        for dr in range(2):
            C = ctxs[dr]
            O_ = OF if dr == 0 else OB
            prm = scan_params(P, C, kind, layer, h, d)
            if kind == "gla":
                gens.append(scan_dir(C, "gla", layer, 32, dr, FM[FM_GQ + h * 32:FM_GQ + (h + 1) * 32, :], FM[FM_GK + h * 32:FM_GK + (h + 1) * 32, :],
                                     FM[FM_GA + dr * 16:FM_GA + (dr + 1) * 16, :], TM[:, TM_GV + h * 64:TM_GV + (h + 1) * 64], O_[:, h * 64:(h + 1) * 64], prm,
                                     rkeys=["FM", "TM"]))
            else:
                gens.append(scan_dir(C, "hg", layer, 64, dr, FM[FM_HQ + h * 64:FM_HQ + (h + 1) * 64, :], None,
                                     FM[FM_HF + dr * 384 + h * 64:FM_HF + dr * 384 + (h + 1) * 64, :], TM[:, TM_HI + h * 64:TM_HI + (h + 1) * 64],
                                     O_[:, 384 + h * 64:384 + (h + 1) * 64], prm, rkeys=["FM", "TM"]))
        run_gens(gens)
    P.pop_scope()


def hyena_consts2(Lx):
    f32 = np.float32
    t = np.linspace(0.0, 1.0, Lx, dtype=f32)[:, None]
    w = (2.0 * math.pi * np.arange(Lx, dtype=f32)[:, None] / Lx).astype(f32)
    bands = np.linspace(1e-4, 15, 16, dtype=f32)[None, :]
    z = np.concatenate([t, np.cos(bands * w), -np.sin(bands * w)], axis=-1).astype(f32)
    max_decay = math.log(1e-2) / 0.3
    min_decay = math.log(1e-2) / 1.5
    deltas = np.linspace(min_decay, max_decay, HY_W, dtype=f32)
    dec = np.exp(-t * np.abs(deltas)).astype(f32)
    tidx = np.concatenate([np.arange(Lx - 1, -1, -1), np.arange(1, Lx), [0]])
    zR = np.ascontiguousarray(z[tidx].T)
    zR[:, -1] = 0.0
    dR = np.ascontiguousarray(dec[tidx].T)
    dR[:, -1] = 0.0
    J = np.ascontiguousarray(np.eye(128, dtype=f32)[::-1])
    return {"h_zR": zR, "h_decR": dR.astype(f32), "h_J": J, "h_id": np.eye(128, dtype=f32), "h_ones": np.ones((64, 128), f32)}


def hyena_prepare(P, Lx, d, FM, YH, scr, fm_hy=FM_HY):
    NB = Lx // 128
    NT = 2 * Lx // 512
    HW = 2 * Lx - 128
    sb = P.sb
    cw = sb("h_cw", [128, 2, 9]); cb = sb("h_cb", [128, 2, 3]); bias = sb("h_bias", [128, 2, 2])
    Jm = sb("h_Jm", [128, 128]); idf = sb("h_idf", [128, 128]); idb = sb("h_idb", [128, 128], BF16)
    pB = P.ps("h_pB", [128, 512])
    rnb = sb("h_rnb", [128, 512])
    P.push_scope()
    w1 = sb("h_w1", [33, 64]); b1f = sb("h_b1f", [64, 8]); w2 = sb("h_w2", [64, 2, 64]); w3 = sb("h_w3", [64, 2, 2, 256])
    ones = sb("h_onesb", [64, 128])
    P.dma("sp", w1[:], d["w1"], writes=["h_w1"])
    P.dma("sp", b1f[:, 0:4], d["b1f"], writes=["h_b1f"])
    P.dma("sp", w2[:], d["w2"], writes=["h_w2"])
    P.dma("sp", w3[:], d["w3"], writes=["h_w3"])
    for lt in range(2):
        P.dma("sp", cw[:, lt, :], d["cw"][lt], writes=["h_cw"])
        P.dma("sp", cb[:, lt, :], d["cb"][lt], writes=["h_cb"])
        P.dma("sp", bias[:, lt, :], d["bias"][lt], writes=["h_bias"])
    P.dma("sp", Jm[:], d["h_J"], writes=["h_J"])
    P.dma("sp", idf[:], d["h_id"], writes=["h_idf"])
    P.dma("sp", ones[:], d["h_ones"], writes=["h_ones"])
    P.op("dve", lambda e: e.tensor_copy(out=idb[:], in_=idf[:]), reads=["h_idf"], writes=["h_idb"])
    P.op("dve", lambda e: e.tensor_scalar(out=b1f[:, 4:5], in0=b1f[:, 3:4], scalar1=1.0 / (2 * math.pi), scalar2=None, op0=ALU.mult), reads=["h_b1f"], writes=["h_A"])
    P.op("dve", lambda e: e.tensor_scalar(out=b1f[:, 5:8], in0=b1f[:, 0:3], scalar1=b1f[:, 4:5], scalar2=None, op0=ALU.mult), reads=["h_b1f", "h_A"], writes=["h_B"])
    pA = P.ps("h_pA", [64, 512])
    pR = [P.ps("h_pR%d" % i, [64, 512]) for i in range(2)]
    zt = [sb("h_zt%d" % i, [33, 512]) for i in range(2)]
    dt_ = [sb("h_dt%d" % i, [64, 4, 512]) for i in range(2)]
    u = sb("h_u", [64, 512]); ui = sb("h_ui", [64, 512], I32); hh = sb("h_hh", [64, 512]); h3 = [sb("h_h3%d" % i, [64, 512]) for i in range(2)]
    Rt = sb("h_Rt", [64, 512]); Rb = [sb("h_Rb%d" % i, [64, 512], BF16) for i in range(3)]
    asum = sb("h_asum", [64, 8, NT]); rn = sb("h_rn", [64, 16]); dg = sb("h_dg", [64, 64])
    ri = 0
    for t in range(NT):
        b = t % 2
        P.dma("sp", zt[b][:], d["h_zR"][:, t * 512:(t + 1) * 512], writes=[("zt", b)])
        P.dma("sp", dt_[b][:], d["h_decR"][:, t * 512:(t + 1) * 512].rearrange("(a p) n -> p a n", p=64), writes=[("dt", b)])
        rhs = zt[b]
        rk = ("zt", b)
        for l in range(3):
            lhsT = w1[:, :] if l == 0 else w2[:, l - 1, :]
            kk = 33 if l == 0 else 64
            P.op("pe", lambda e, lhsT=lhsT, rhs=rhs, kk=kk: e.matmul(pA[:, :], lhsT=lhsT, rhs=rhs[0:kk, :], start=True, stop=True),
                 reads=[rk, "h_w1", "h_w2"], writes=["h_pA"])
            P.op("dve", lambda e, l=l: e.tensor_scalar(out=u[:], in0=pA[:], scalar1=b1f[:, 4:5], scalar2=b1f[:, 5 + l:6 + l], op0=ALU.mult, op1=ALU.add),
                 reads=["h_pA", "h_A", "h_B"], writes=["h_u"])
            P.op("dve", lambda e: e.tensor_copy(out=ui[:], in_=u[:]), reads=["h_u"], writes=["h_ui"])
            P.op("dve", lambda e: e.tensor_tensor(out=u[:], in0=u[:], in1=ui[:], op=ALU.subtract), reads=["h_u", "h_ui"], writes=["h_u"])
            dst = hh if l < 2 else h3[b]
            dk = "h_hh" if l < 2 else ("h_h3", b)
            P.op("act", lambda e, dst=dst: e.activation(out=dst[:], in_=u[:], func=AF.Sin, scale=TWO_PI_LO), reads=["h_u"], writes=[dk])
            rhs = dst
            rk = dk
        dr = 0 if t < NT // 2 else 1
        for gi in range(8):
            o, cq = gi // 4, gi % 4
            pr = pR[gi % 2]
            prk = ("h_pR", gi % 2)
            P.op("pe", lambda e, dr=dr, o=o, cq=cq, pr=pr, b=b: e.matmul(pr[:, :], lhsT=w3[:, dr, o, cq * 64:(cq + 1) * 64], rhs=h3[b][:, :], start=True, stop=True),
                 reads=[("h_h3", b), "h_w3"], writes=[prk])
            P.op("dve", lambda e, pr=pr, cq=cq, b=b: e.tensor_tensor(out=Rt[:], in0=pr[:], in1=dt_[b][:, cq, :], op=ALU.mult), reads=[prk, ("dt", b)], writes=["h_Rt"])
            P.op("dve", lambda e, gi=gi, t=t: e.tensor_reduce(out=asum[:, gi, t:t + 1], in_=Rt[:], axis=AX.X, op=ALU.add, apply_absolute_value=True), reads=["h_Rt"], writes=[("asum", gi, t)])
            rb = ri % 3
            ri += 1
            P.op("act", lambda e, rb=rb: e.activation(out=Rb[rb][:], in_=Rt[:], func=AF.Copy), reads=["h_Rt"], writes=[("Rb", rb)])
            P.dma("sp", scr[gi * 64:(gi + 1) * 64, t * 512:(t + 1) * 512], Rb[rb][:], reads=[("Rb", rb)], writes=["scr"])
    P.op("dve", lambda e: e.tensor_reduce(out=rn[:, 0:8], in_=asum[:], axis=AX.X, op=ALU.add), reads=[("asum", gi, t) for gi in range(8) for t in range(NT)], writes=["h_rn"])
    P.op("dve", lambda e: e.tensor_scalar(out=rn[:, 0:8], in0=rn[:, 0:8], scalar1=1e-12, scalar2=None, op0=ALU.max), reads=["h_rn"], writes=["h_rn"])
    P.op("dve", lambda e: e.reciprocal(out=rn[:, 8:16], in_=rn[:, 0:8]), reads=["h_rn"], writes=["h_rn1"])
    for gi in range(8):
        P.op("dve", lambda e, gi=gi: e.tensor_scalar(out=dg[:], in0=idf[0:64, 0:64], scalar1=rn[:, 8 + gi:9 + gi], scalar2=None, op0=ALU.mult), reads=["h_idf", "h_rn1"], writes=["h_dg"])
        P.op("pe", lambda e: e.matmul(pB[:, 0:64], lhsT=ones[:, :], rhs=dg[:, :], start=True, stop=True), reads=["h_ones", "h_dg"], writes=["h_pB"])
        P.op("dve", lambda e, gi=gi: e.tensor_copy(out=rnb[:, gi * 64:(gi + 1) * 64], in_=pB[:, 0:64]), reads=["h_pB"], writes=["h_rnb"])
    P.pop_scope()
    za = sb("h_za", [128, Lx])
    zb = [sb("h_zb%d" % i, [128, 1024], BF16) for i in range(2)]
    ZT = sb("h_ZT", [128, NB, 128], BF16)
    YT = sb("h_YT", [128, NB, 128])
    Hh = [sb("h_Hh%d" % i, [128, Lx], BF16) for i in range(2)]
    xg = [sb("h_xg%d" % i, [128, 514]) for i in range(2)]
    gt = [sb("h_gt%d" % i, [128, 512]) for i in range(2)]
    yo = [sb("h_yo0", [128, 512])] * 2
    pT = P.ps("h_pT", [128, 1024], BF16)
    pC = [P.ps("h_pC%d" % i, [128, 512]) for i in range(2)]
    for b_ in range(2):
        P.op("dve", lambda e, b_=b_: e.memset(xg[b_][:], 0.0), writes=[("h_xg", b_)])
    return dict(P=P, Lx=Lx, NB=NB, HW=HW, FM=FM, YH=YH, scr=scr, fm_hy=fm_hy, cw=cw, cb=cb, bias=bias, Jm=Jm, idf=idf, idb=idb, rnb=rnb,
                za=za, zb=zb, ZT=ZT, YT=YT, Hh=Hh, xg=xg, gt=gt, yo=yo, pT=pT, pC=pC, pB=pB)


def hyena_conv_gen(st):
    P, Lx, NB, FM, YH, scr, fm_hy = st["P"], st["Lx"], st["NB"], st["FM"], st["YH"], st["scr"], st["fm_hy"]
    cw, cb, bias, Jm, idf, idb, rnb = st["cw"], st["cb"], st["bias"], st["Jm"], st["idf"], st["idb"], st["rnb"]
    za, zb, ZT, YT, Hh, xg, gt, yo, pT, pC, pB = st["za"], st["zb"], st["ZT"], st["YT"], st["Hh"], st["xg"], st["gt"], st["yo"], st["pT"], st["pC"], st["pB"]
    xr = YT[:].rearrange("p a b -> p (a b)")
    NYK = 8
    ytk = [("YT", c) for c in range(NYK)]
    cq = 0
    xq = 0
    for lt in range(2):
        idx = 0
        P.dma("sp", xr, FM[fm_hy + idx * 256 + lt * 128:fm_hy + idx * 256 + (lt + 1) * 128, :], reads=["FM"], writes=ytk)
        P.op("dve", lambda e: e.tensor_scalar(out=za[:], in0=xr, scalar1=cw[:, lt, 1:2], scalar2=cb[:, lt, 0:1], op0=ALU.mult, op1=ALU.add),
             reads=ytk + ["h_cw", "h_cb"], writes=["h_za"])
        yield
        P.op("dve", lambda e: e.scalar_tensor_tensor(out=za[:, 1:Lx], in0=xr[:, 0:Lx - 1], scalar=cw[:, lt, 0:1], in1=za[:, 1:Lx], op0=ALU.mult, op1=ALU.add),
             reads=ytk + ["h_cw", "h_za"], writes=["h_za"])
        yield
        P.op("dve", lambda e: e.scalar_tensor_tensor(out=za[:, 0:Lx - 1], in0=xr[:, 1:Lx], scalar=cw[:, lt, 2:3], in1=za[:, 0:Lx - 1], op0=ALU.mult, op1=ALU.add),
             reads=ytk + ["h_cw", "h_za"], writes=["h_za"])
        yield
        for o in range(2):
            for g in range(NB // 8):
                zbb = zb[g % 2]
                zbk = ("h_zb", g % 2)
                P.op("act", lambda e, g=g, zbb=zbb: e.activation(out=zbb[:], in_=za[:, g * 1024:(g + 1) * 1024], func=AF.Copy), reads=["h_za"], writes=[zbk])
                for jj in range(8):
                    P.op("pe", lambda e, jj=jj, zbb=zbb: e.transpose(out=pT[:, jj * 128:(jj + 1) * 128], in_=zbb[:, jj * 128:(jj + 1) * 128], identity=idb[:]),
                         reads=[zbk, "h_idb"], writes=["h_pT"])
                P.op("dve", lambda e, g=g: e.tensor_copy(out=ZT[:, g * 8:(g + 1) * 8, :].rearrange("p a b -> p (a b)"), in_=pT[:]), reads=["h_pT"], writes=[("ZT", g)])
                yield
            ztk = [("ZT", g) for g in range(NB // 8)]
            for c in range(128):
                row = o * 256 + lt * 128 + c
                P.dma("sp", Hh[0][:, :], bass.AP(scr.tensor, row * 2 * Lx, [[1, 128], [1, Lx]]), reads=["scr"], writes=[("H", 0)])
                P.dma("pool", Hh[1][:, 0:Lx - 128], bass.AP(scr.tensor, row * 2 * Lx + Lx, [[1, 128], [1, Lx - 128]]), reads=["scr"], writes=[("H", 1)])
                pcb = cq % 2
                cq += 1
                pc = pC[pcb]
                pck = ("pC", pcb)
                ds = list(range(0, NB)) + list(range(-1, -NB, -1))
                for n_, dd in enumerate(ds):
                    j0, j1 = max(0, -dd), min(NB, NB - dd)
                    x0 = Lx - 128 - 128 * dd
                    if dd >= 0:
                        lhsT = Hh[0][:, x0:x0 + 128]
                        hk = ("H", 0)
                    else:
                        lhsT = Hh[1][:, x0 - Lx:x0 - Lx + 128]
                        hk = ("H", 1)
                    P.op("pe", lambda e, dd=dd, j0=j0, j1=j1, lhsT=lhsT, n_=n_, pc=pc, c=c: e.matmul(
                        pc[:, (j0 + dd):(j1 + dd)], lhsT=lhsT, rhs=ZT[:, j0:j1, c], start=(n_ == 0), stop=(n_ == len(ds) - 1)),
                         reads=[hk] + ztk, writes=[pck])
                    if n_ % 16 == 15:
                        yield
                yk = ("YT", c % NYK)
                if c % 2 == 0:
                    P.op("act", lambda e, c=c, row=row, pc=pc: e.activation(out=YT[:, :, c], in_=pc[:, 0:NB], func=AF.Copy, scale=rnb[:, row:row + 1]),
                         reads=[pck, "h_rnb"], writes=[yk])
                else:
                    P.op("dve", lambda e, c=c, row=row, pc=pc: e.tensor_scalar(out=YT[:, :, c], in0=pc[:, 0:NB], scalar1=rnb[:, row:row + 1], scalar2=None, op0=ALU.mult),
                         reads=[pck, "h_rnb"], writes=[yk])
                yield
            gidx = 1 + o
            grow = fm_hy + gidx * 256 + lt * 128
            NG4 = NB // 4
            for g in range(NG4):
                xb = xq % 2
                xq += 1
                c0 = g * 512
                lo, hi = max(0, c0 - 1), min(Lx, c0 + 513)
                if g == 0 or g == NG4 - 1:
                    P.op("dve", lambda e, xb=xb: e.memset(xg[xb][:], 0.0), writes=[("h_xg", xb)])
                P.dma("sp", xg[xb][:, lo - (c0 - 1):hi - (c0 - 1)], FM[grow:grow + 128, lo:hi], reads=["FM"], writes=[("h_xg", xb)])
                P.op("dve", lambda e, xb=xb: e.tensor_scalar(out=gt[xb][:], in0=xg[xb][:, 1:513], scalar1=cw[:, lt, gidx * 3 + 1:gidx * 3 + 2], scalar2=cb[:, lt, gidx:gidx + 1],
                                                            op0=ALU.mult, op1=ALU.add), reads=[("h_xg", xb), "h_cw", "h_cb"], writes=[("h_gt", xb)])
                P.op("dve", lambda e, xb=xb: e.scalar_tensor_tensor(out=gt[xb][:], in0=xg[xb][:, 0:512], scalar=cw[:, lt, gidx * 3:gidx * 3 + 1], in1=gt[xb][:], op0=ALU.mult, op1=ALU.add),
                     reads=[("h_xg", xb), "h_cw", ("h_gt", xb)], writes=[("h_gt", xb)])
                P.op("dve", lambda e, xb=xb: e.scalar_tensor_tensor(out=gt[xb][:], in0=xg[xb][:, 2:514], scalar=cw[:, lt, gidx * 3 + 2:gidx * 3 + 3], in1=gt[xb][:], op0=ALU.mult, op1=ALU.add),
                     reads=[("h_xg", xb), "h_cw", ("h_gt", xb)], writes=[("h_gt", xb)])
                for ii in range(4):
                    i = g * 4 + ii
                    P.op("pe", lambda e, ii=ii, i=i: e.matmul(pB[:, ii * 128:(ii + 1) * 128], lhsT=YT[:, i, :], rhs=Jm[:, :], start=True, stop=True),
                         reads=ytk + ["h_J"], writes=["h_pB"])
                sl = slice(c0, c0 + 512)
                P.op("dve", lambda e, sl=sl, o=o: e.scalar_tensor_tensor(out=za[:, sl], in0=za[:, sl], scalar=bias[:, lt, o:o + 1], in1=pB[:, :], op0=ALU.mult, op1=ALU.add),
                     reads=["h_za", "h_bias", "h_pB"], writes=["h_za"])
                P.op("pool", lambda e, sl=sl, xb=xb: e.tensor_tensor(out=za[:, sl], in0=za[:, sl], in1=gt[xb][:], op=ALU.mult), reads=["h_za", ("h_gt", xb)], writes=["h_za"])
                yield
        for g in range(NB // 4):
            for ii in range(4):
                i = g * 4 + ii
                P.op("pe", lambda e, ii=ii, i=i: e.matmul(pB[:, ii * 128:(ii + 1) * 128], lhsT=za[:, i * 128:(i + 1) * 128], rhs=idf[:, :], start=True, stop=True),
                     reads=["h_za", "h_idf"], writes=["h_pB"])
            yb = yo[g % 2]
            P.op("act", lambda e, yb=yb: e.activation(out=yb[:], in_=pB[:, :], func=AF.Copy), reads=["h_pB"], writes=[("h_yo", 0)])
            P.dma("sp", YH[g * 512:(g + 1) * 512, lt * 128:(lt + 1) * 128].rearrange("(a p) c -> p a c", p=128), yb[:].rearrange("p (a c) -> p a c", c=128),
                  reads=[("h_yo", 0)], writes=["YH"])
            yield


def scans_gen(P, layer, d, FM, TM, OF, OB, C):
    for kind, h in [("gla", hh) for hh in range(6)] + [("hg", hh) for hh in range(6)]:
        for dr in range(2):
            O_ = OF if dr == 0 else OB
            prm = scan_params(P, C, kind, layer, h, d)
            if kind == "gla":
                yield from scan_dir(C, "gla", layer, 32, dr, FM[FM_GQ + h * 32:FM_GQ + (h + 1) * 32, :], FM[FM_GK + h * 32:FM_GK + (h + 1) * 32, :],
                                    FM[FM_GA + dr * 16:FM_GA + (dr + 1) * 16, :], TM[:, TM_GV + h * 64:TM_GV + (h + 1) * 64], O_[:, h * 64:(h + 1) * 64], prm,
                                    rkeys=["FM", "TM"])
            else:
                yield from scan_dir(C, "hg", layer, 64, dr, FM[FM_HQ + h * 64:FM_HQ + (h + 1) * 64, :], None,
                                    FM[FM_HF + dr * 384 + h * 64:FM_HF + dr * 384 + (h + 1) * 64, :], TM[:, TM_HI + h * 64:TM_HI + (h + 1) * 64],
                                    O_[:, 384 + h * 64:384 + (h + 1) * 64], prm, rkeys=["FM", "TM"])


def mixers_overlapped(P, layer, d, FM, TM, OF, OB, YH, scr, ratio=3):
    P.push_scope()
    st = hyena_prepare(P, L, d, FM, YH, scr)
    C = ScanCtx(P, L, d, cid=0)
    gh = hyena_conv_gen(st)
    gs = scans_gen(P, layer, d, FM, TM, OF, OB, C)
    alive_h, alive_s = True, True
    while alive_h or alive_s:
        if alive_s:
            try:
                next(gs)
            except StopIteration:
                alive_s = False
        for _ in range(ratio):
            if alive_h:
                try:
                    next(gh)
                except StopIteration:
                    alive_h = False
    P.pop_scope()


def stage_C2(P, d, HB, TM, OF, OB, YH, h1s, OUT, CT=16):
    sb, ps = P.sb, P.ps
    NCH = (L // 128) // CT
    CTOK = CT * 128
    P.push_scope()
    epst = sb("epst", [128, 1]); eps6 = sb("eps6", [128, 1])
    P.op("dve", lambda e: e.memset(epst[:], 1e-5), writes=["eps"])
    P.op("dve", lambda e: e.memset(eps6[:], 1e-6), writes=["eps6"])
    idf = sb("c_idf", [128, 128]); idb = sb("c_idb", [128, 128], BF16)
    P.dma("sp", idf[:], d["c_id"], writes=["idf"])
    P.op("dve", lambda e: e.tensor_copy(out=idb[:], in_=idf[:]), reads=["idf"], writes=["idb"])
    h1T = sb("h1T", [128, 8, CTOK], BF16)
    wfull = sb("wfull", [128, CT, 16])
    for ch in range(NCH):
        tb = ch * CT
        P.push_scope()
        wout = sb("woutb", [128, 8, D], BF16)
        for kc in range(8):
            P.dma("pool", wout[:, kc, :], d["wout"][kc * 128:(kc + 1) * 128, :], writes=[("wout", kc)])
        woutk = [("wout", kc) for kc in range(8)]
        gn = sb("gn", [128, 768]); ln1 = sb("ln1", [128, 2 * D]); wr = sb("wr", [128, 8, 20]); br = sb("br", [128, 20])
        P.dma("sp", gn[:], d["gn"], writes=["gn"])
        P.dma("sp", ln1[:], d["ln1"], writes=["ln1"])
        P.dma("sp", wr[:], d["wr"].rearrange("(k p) n -> p k n", p=128), writes=["wr"])
        P.dma("sp", br[:], d["br"], writes=["br"])
        NBUF = 2
        OFt = [sb("OF%d" % i, [128, 768]) for i in range(NBUF)]
        OBt = [sb("OB%d" % i, [128, 768]) for i in range(NBUF)]
        GT = [sb("GT%d" % i, [128, 768]) for i in range(NBUF)]
        mix = [sb("mix%d" % i, [128, D]) for i in range(NBUF)]
        mixb = [sb("mixb%d" % i, [128, D], BF16) for i in range(NBUF)]
        mixT = [sb("mixT%d" % i, [128, 8, 128], BF16) for i in range(NBUF)]
        ht = [sb("ht%d" % i, [128, D]) for i in range(NBUF)]
        sq = sb("sq", [128, 768]); ss = sb("ss", [128, 12]); rs = sb("rs_", [128, 12])
        st = sb("st", [128, 2, 6]); mv = sb("mv", [128, 4])
        h1T32 = sb("h1T32", [128, 8, 128])
        lg = sb("lg", [128, 20]); rt = sb("rt", [128, 64])
        pT = ps("pT", [128, 1024], BF16)
        pO = [ps("pO%d" % i, [128, 512]) for i in range(2)]
        pX = [ps("pX%d" % i, [128, 512]) for i in range(2)]
        pL = ps("pL", [128, 512])
        for tl in range(CT):
            t = tb + tl
            b = tl % NBUF
            rows = slice(t * 128, (t + 1) * 128)
            lrows = slice(tl * 128, (tl + 1) * 128)
            kOF, kOB, kGT, kmix, kmixb, kmixT, kht = ("OF", b), ("OB", b), ("GT", b), ("mix", b), ("mixb", b), ("mixT", b), ("ht", b)
            P.dma("sp", OFt[b][:], OF[rows, :], writes=[kOF])
            P.dma("sp", OBt[b][:], OB[rows, :], writes=[kOB])
            P.dma("sp", GT[b][:, 0:384], TM[rows, TM_GG:TM_GG + 384], writes=[(kGT, 0)])
            P.dma("sp", GT[b][:, 384:768], TM[rows, TM_HGT:TM_HGT + 384], writes=[(kGT, 1)])
            kGTs = [(kGT, 0), (kGT, 1)]
            P.dma("sp", mix[b][:, 768:1024], YH[rows, :], writes=[(kmix, "hy")])
            P.dma("sp", ht[b][:], HB[rows, :], writes=[kht])
            P.op("pool", lambda e, b=b: e.tensor_tensor(out=OFt[b][:], in0=OFt[b][:], in1=OBt[b][:], op=ALU.add), reads=[kOF, kOB], writes=[kOF])
            P.op("dve", lambda e, b=b: e.tensor_tensor(out=sq[:], in0=OFt[b][:], in1=OFt[b][:], op=ALU.mult), reads=[kOF], writes=["sq"])
            P.op("dve", lambda e: e.tensor_reduce(out=ss[:], in_=sq[:].rearrange("p (h v) -> p h v", v=64), axis=AX.X, op=ALU.add), reads=["sq"], writes=["ss"])
            P.op("act", lambda e: e.activation(out=rs[:], in_=ss[:], func=AF.Sqrt, bias=eps6[:, 0:1], scale=1.0 / 64.0), reads=["ss", "eps6"], writes=["rs"])
            P.op("dve", lambda e: e.reciprocal(out=rs[:], in_=rs[:]), reads=["rs"], writes=["rs"])
            P.op("dve", lambda e, b=b: e.tensor_tensor(out=OFt[b][:].rearrange("p (h v) -> p h v", v=64), in0=OFt[b][:].rearrange("p (h v) -> p h v", v=64),
                                                       in1=rs[:].unsqueeze(2).to_broadcast([128, 12, 64]), op=ALU.mult), reads=[kOF, "rs"], writes=[kOF])
            P.op("pool", lambda e, b=b: e.tensor_tensor(out=OFt[b][:], in0=OFt[b][:], in1=gn[:], op=ALU.mult), reads=[kOF, "gn"], writes=[kOF])
            P.op("act", lambda e, b=b: e.activation(out=sq[:], in_=GT[b][:], func=AF.Exp, scale=-1.0), reads=kGTs, writes=["sq"])
            P.op("dve", lambda e: e.tensor_scalar(out=sq[:], in0=sq[:], scalar1=1.0, scalar2=None, op0=ALU.add), reads=["sq"], writes=["sq"])
            P.op("dve", lambda e: e.reciprocal(out=sq[:], in_=sq[:]), reads=["sq"], writes=["sq"])
            P.op("pool", lambda e, b=b: e.tensor_tensor(out=sq[:, 0:384], in0=sq[:, 0:384], in1=GT[b][:, 0:384], op=ALU.mult), reads=["sq"] + kGTs, writes=["sq"])
            P.op("dve", lambda e, b=b: e.tensor_tensor(out=mix[b][:, 0:768], in0=OFt[b][:], in1=sq[:], op=ALU.mult), reads=[kOF, "sq"], writes=[(kmix, "a")])
            P.op("act", lambda e, b=b: e.activation(out=mixb[b][:], in_=mix[b][:], func=AF.Copy), reads=[(kmix, "a"), (kmix, "hy")], writes=[kmixb])
            for kc in range(8):
                P.op("pe", lambda e, kc=kc, b=b: e.transpose(out=pT[:, kc * 128:(kc + 1) * 128], in_=mixb[b][:, kc * 128:(kc + 1) * 128], identity=idb[:]),
                     reads=[kmixb, "idb"], writes=["pT"])
            P.op("dve", lambda e, b=b: e.tensor_copy(out=mixT[b][:].rearrange("p a b -> p (a b)"), in_=pT[:]), reads=["pT"], writes=[kmixT])
            for hf in range(2):
                for kc in range(8):
                    P.op("pe", lambda e, kc=kc, hf=hf, b=b: e.matmul(pO[hf][:, :], lhsT=mixT[b][:, kc, :], rhs=wout[:, kc, hf * 512:(hf + 1) * 512], start=(kc == 0), stop=(kc == 7)),
                         reads=[kmixT] + woutk, writes=[("pO", hf)])
                P.op("dve", lambda e, hf=hf, b=b: e.scalar_tensor_tensor(out=ht[b][:, hf * 512:(hf + 1) * 512], in0=ht[b][:, hf * 512:(hf + 1) * 512], scalar=ALPHA,
                                                                         in1=pO[hf][:, :], op0=ALU.mult, op1=ALU.add), reads=[kht, ("pO", hf)], writes=[kht])
            layer_norm_tile(P, ht[b], kht, st, mv, "st", epst, ln1, "ln1")
            P.dma("sp", h1s[rows, :], ht[b][:], reads=[kht], writes=["h1s"])
            for kc in range(8):
                P.op("pe", lambda e, kc=kc, b=b: e.matmul(pX[kc // 4][:, (kc % 4) * 128:(kc % 4 + 1) * 128], lhsT=ht[b][:, kc * 128:(kc + 1) * 128], rhs=idf[:], start=True, stop=True),
                     reads=[kht, "idf"], writes=[("pX", kc // 4)])
            for hf in range(2):
                P.op("act", lambda e, hf=hf: e.activation(out=h1T32[:, hf * 4:(hf + 1) * 4, :].rearrange("p a b -> p (a b)"), in_=pX[hf][:, :], func=AF.Copy),
                     reads=[("pX", hf)], writes=[("h1T32", hf)])
                P.op("dve", lambda e, hf=hf, lrows=lrows: e.tensor_copy(out=h1T[:, hf * 4:(hf + 1) * 4, lrows], in_=pX[hf][:, :].rearrange("p (a b) -> p a b", b=128)),
                     reads=[("pX", hf), ("h1T32", hf)], writes=[("h1T", tl)])
            for kc in range(8):
                P.op("pe", lambda e, kc=kc: e.matmul(pL[:, 0:20], lhsT=h1T32[:, kc, :], rhs=wr[:, kc, :], start=(kc == 0), stop=(kc == 7)),
                     reads=[("h1T32", kc // 4), "wr"], writes=["pL"])
            P.op("dve", lambda e: e.tensor_tensor(out=lg[:], in0=pL[:, 0:20], in1=br[:], op=ALU.add), reads=["pL", "br"], writes=["lg"])
            R_ = lambda a, b_: rt[:, a:b_]
            dv = lambda fn, r=("lg", "rt"), w=("rt",): P.op("dve", fn, reads=list(r), writes=list(w))
            dv(lambda e: e.tensor_reduce(out=R_(0, 1), in_=lg[:, 0:4], axis=AX.X, op=ALU.max))
            dv(lambda e: e.tensor_scalar(out=R_(1, 5), in0=lg[:, 0:4], scalar1=R_(0, 1), scalar2=None, op0=ALU.is_equal))
            dv(lambda e: e.tensor_scalar(out=R_(5, 9), in0=lg[:, 0:4], scalar1=R_(0, 1), scalar2=None, op0=ALU.subtract))
            P.op("act", lambda e: e.activation(out=R_(5, 9), in_=R_(5, 9), func=AF.Exp), reads=["rt"], writes=["rt"])
            dv(lambda e: e.tensor_reduce(out=R_(9, 10), in_=R_(5, 9), axis=AX.X, op=ALU.add))
            dv(lambda e: e.reciprocal(out=R_(10, 11), in_=R_(9, 10)))
            dv(lambda e: e.tensor_tensor(out=rt[:, 40:56].rearrange("p (g e) -> p g e", e=4), in0=lg[:, 4:20].rearrange("p (g e) -> p g e", e=4),
                                         in1=R_(1, 5).unsqueeze(2).to_broadcast([128, 4, 4]), op=ALU.mult))
            dv(lambda e: e.tensor_reduce(out=R_(11, 15), in_=rt[:, 40:56].rearrange("p (g e) -> p e g", e=4), axis=AX.X, op=ALU.add))
            dv(lambda e: e.tensor_reduce(out=R_(15, 16), in_=R_(11, 15), axis=AX.X, op=ALU.max))
            dv(lambda e: e.tensor_scalar(out=R_(16, 20), in0=R_(11, 15), scalar1=R_(15, 16), scalar2=None, op0=ALU.is_equal))
            dv(lambda e: e.scalar_tensor_tensor(out=R_(20, 24), in0=R_(16, 20), scalar=-1e30, in1=R_(11, 15), op0=ALU.mult, op1=ALU.add))
            dv(lambda e: e.tensor_reduce(out=R_(24, 25), in_=R_(20, 24), axis=AX.X, op=ALU.max))
            dv(lambda e: e.tensor_scalar(out=R_(28, 32), in0=R_(20, 24), scalar1=R_(24, 25), scalar2=None, op0=ALU.is_equal))
            dv(lambda e: e.tensor_tensor(out=R_(32, 33), in0=R_(24, 25), in1=R_(15, 16), op=ALU.subtract))
            P.op("act", lambda e: e.activation(out=R_(32, 33), in_=R_(32, 33), func=AF.Exp), reads=["rt"], writes=["rt"])
            dv(lambda e: e.tensor_scalar(out=R_(33, 34), in0=R_(32, 33), scalar1=1.0, scalar2=None, op0=ALU.add))
            dv(lambda e: e.reciprocal(out=R_(34, 35), in_=R_(33, 34)))
            dv(lambda e: e.tensor_tensor(out=R_(35, 36), in0=R_(32, 33), in1=R_(34, 35), op=ALU.mult))
            dv(lambda e: e.tensor_scalar(out=R_(36, 40), in0=R_(16, 20), scalar1=R_(34, 35), scalar2=None, op0=ALU.mult))
            dv(lambda e: e.scalar_tensor_tensor(out=R_(36, 40), in0=R_(28, 32), scalar=R_(35, 36), in1=R_(36, 40), op0=ALU.mult, op1=ALU.add))
            dv(lambda e: e.tensor_scalar(out=R_(36, 40), in0=R_(36, 40), scalar1=R_(10, 11), scalar2=None, op0=ALU.mult))
            dv(lambda e: e.tensor_tensor(out=rt[:, 40:56].rearrange("p (g e) -> p g e", e=4), in0=R_(1, 5).unsqueeze(2).to_broadcast([128, 4, 4]),
                                         in1=R_(36, 40).unsqueeze(1).to_broadcast([128, 4, 4]), op=ALU.mult))
            P.op("dve", lambda e, tl=tl: e.tensor_copy(out=wfull[:, tl, :], in_=rt[:, 40:56]), reads=["rt"], writes=[("wfull", tl)])
        P.pop_scope()
        P.push_scope()
        ln2 = sb("ln2", [128, 2 * D])
        P.dma("sp", ln2[:], d["ln2"], writes=["ln2"])
        yacc = sb("yacc", [128, CT, D])
        wg = [sb("wg%d" % i, [128, 8, DE], BF16) for i in range(2)]
        wu = [sb("wu%d" % i, [128, 8, DE], BF16) for i in range(2)]
        wd = [sb("wd%d" % i, [128, 4, D], BF16) for i in range(2)]
        hid = [sb("hid%d" % i, [128, 4, 512], BF16) for i in range(2)]
        sg = [sb("sg%d" % i, [128, 512]) for i in range(2)]
        st2 = sb("st2", [128, 2, 6]); mv2 = sb("mv2", [128, 4])
        xo = [sb("xo%d" % i, [128, D]) for i in range(2)]
        pG = [ps("pG%d" % i, [128, 512]) for i in range(2)]
        pU = [ps("pU%d" % i, [128, 512]) for i in range(2)]
        pD = [ps("pD%d" % i, [128, 512]) for i in range(4)]
        P.op("pool", lambda e: e.memset(yacc[:].rearrange("p a b -> p (a b)"), 0.0), writes=[("yacc", i) for i in range(CT)])
        for ex in range(NEXP):
            wb = ex % 2
            for kc in range(8):
                if NOWDMA and ex >= 2:
                    break
                P.dma("pool", wg[wb][:, kc, :], d["wg"][ex, kc * 128:(kc + 1) * 128, :], writes=[("wg", wb, kc)])
                P.dma("pool", wu[wb][:, kc, :], d["wu"][ex, kc * 128:(kc + 1) * 128, :], writes=[("wu", wb, kc)])
            for fc in range(4):
                if NOWDMA and ex >= 2:
                    break
                P.dma("pool", wd[wb][:, fc, :], d["wd"][ex, fc * 128:(fc + 1) * 128, :], writes=[("wd", wb, fc)])
            gw = min(512, CTOK)
            for g in range(CTOK // gw):
                tok0 = g * gw
                hb = g % 2
                h1k = [("h1T", tt) for tt in range(tok0 // 128, (tok0 + gw) // 128)]
                for fc in range(4):
                    pb = fc % 2
                    for kc in range(8):
                        P.op("pe", lambda e, kc=kc, fc=fc, pb=pb, wb=wb, tok0=tok0: e.matmul(pG[pb][:, 0:gw], lhsT=wg[wb][:, kc, fc * 128:(fc + 1) * 128], rhs=h1T[:, kc, tok0:tok0 + gw],
                                                                                         start=(kc == 0), stop=(kc == 7)), reads=[("wg", wb, kc)] + h1k, writes=[("pG", pb)])
                    for kc in range(8):
                        P.op("pe", lambda e, kc=kc, fc=fc, pb=pb, wb=wb, tok0=tok0: e.matmul(pU[pb][:, 0:gw], lhsT=wu[wb][:, kc, fc * 128:(fc + 1) * 128], rhs=h1T[:, kc, tok0:tok0 + gw],
                                                                                         start=(kc == 0), stop=(kc == 7)), reads=[("wu", wb, kc)] + h1k, writes=[("pU", pb)])
                    P.op("act", lambda e, pb=pb: e.activation(out=sg[pb][:, 0:gw], in_=pG[pb][:, 0:gw], func=AF.Silu), reads=[("pG", pb)], writes=[("sg", pb)])
                    P.op("dve", lambda e, pb=pb, fc=fc, hb=hb: e.tensor_tensor(out=hid[hb][:, fc, 0:gw], in0=sg[pb][:, 0:gw], in1=pU[pb][:, 0:gw], op=ALU.mult),
                         reads=[("sg", pb), ("pU", pb)], writes=[("hid", hb, fc)])
                for ts in range(gw // 128):
                    tl = tok0 // 128 + ts
                    for dh in range(2):
                        pb = (ts * 2 + dh) % 4
                        for fc in range(4):
                            P.op("pe", lambda e, fc=fc, ts=ts, dh=dh, pb=pb, hb=hb, wb=wb: e.matmul(pD[pb][:, :], lhsT=hid[hb][:, fc, ts * 128:(ts + 1) * 128], rhs=wd[wb][:, fc, dh * 512:(dh + 1) * 512],
                                                                                                   start=(fc == 0), stop=(fc == 3)), reads=[("hid", hb, fc), ("wd", wb, fc)], writes=[("pD", pb)])
                        P.op("dve", lambda e, tl=tl, dh=dh, pb=pb, ex=ex: e.scalar_tensor_tensor(out=yacc[:, tl, dh * 512:(dh + 1) * 512], in0=pD[pb][:, :], scalar=wfull[:, tl, ex:ex + 1],
                                                                                                in1=yacc[:, tl, dh * 512:(dh + 1) * 512], op0=ALU.mult, op1=ALU.add),
                             reads=[("pD", pb), ("wfull", tl), ("yacc", tl)], writes=[("yacc", tl)])
        for tl in range(CT):
            t = tb + tl
            b = tl % 2
            rows = slice(t * 128, (t + 1) * 128)
            P.dma("sp", xo[b][:], h1s[rows, :], reads=["h1s"], writes=[("xo", b)])
            P.op("dve", lambda e, tl=tl, b=b: e.scalar_tensor_tensor(out=xo[b][:], in0=xo[b][:], scalar=ALPHA, in1=yacc[:, tl, :], op0=ALU.mult, op1=ALU.add),
                 reads=[("xo", b), ("yacc", tl)], writes=[("xo", b)])
            layer_norm_tile(P, xo[b], ("xo", b), st2, mv2, "st2", epst, ln2, "ln2")
            P.dma("sp", OUT[rows, :], xo[b][:], reads=[("xo", b)], writes=["OUT"])
        P.pop_scope()
    P.pop_scope()


_prog = {}


def build_fused(ins0):
    P = Prog()
    nc = P.nc
    d = {}
    for k, v in ins0.items():
        d[k] = nc.dram_tensor(k, list(v.shape), F32, kind="ExternalInput").ap()
    out = nc.dram_tensor("out", [L, D], F32, kind="ExternalOutput").ap()
    I = lambda n, s, dt=F32: nc.dram_tensor(n, s, dt, kind="Internal").ap()
    HB0 = I("HB0", [L, D]); HB1 = I("HB1", [L, D]); FM = I("FM", [NFM, L]); TM = I("TM", [L, NTM])
    OF = I("OFs", [L, 768]); OB = I("OBs", [L, 768]); YH = I("YHs", [L, 256]); h1s = I("h1s", [L, D]); scr = I("scr", [512, 2 * L], BF16)
    for layer in range(2):
        dl = {k[3:]: v for k, v in d.items() if k.startswith("l%d_" % layer)}
        dl.update({k: v for k, v in d.items() if k.startswith("c_") or k.startswith("h_")})
        xin = d["x"] if layer == 0 else HB1
        stage_A2(P, layer == 0, xin, dl["win"], d["gb"], d["c_id"], HB0, FM, TM)
        mixers_overlapped(P, layer, dl, FM, TM, OF, OB, YH, scr)
        stage_C2(P, dl, HB0 if layer == 0 else HB1, TM, OF, OB, YH, h1s, HB1 if layer == 0 else out)
    return P.finish(["OUT"])


def host_inputs(inp, b):
    f32 = np.float32
    rep = lambda v: np.ascontiguousarray(np.tile(np.asarray(v, f32)[None, :], (128, 1)))
    m = {"x": np.ascontiguousarray(inp["x"][b]), "gb": rep(np.concatenate([inp["ln_in_g"], inp["ln_in_b"]]))}
    m.update(scan_consts())
    m.update(hyena_consts2(L))
    for l in range(2):
        p = "l%d_" % l
        w = inp["w_in"][l]
        m[p + "win"] = np.ascontiguousarray(np.concatenate([w[:, FM_COLS], w[:, TM_COLS]], 1))
        wa2, ba, lbl = inp["gla_wa2"][l], inp["gla_ba"][l], inp["hg_lb_logits"]
        m[p + "gwa"] = np.stack([np.ascontiguousarray(wa2[:, :, h * 32:(h + 1) * 32].transpose(1, 0, 2)) for h in range(6)])
        m[p + "gba"] = np.stack([np.ascontiguousarray(ba[:, h * 32:(h + 1) * 32].T) for h in range(6)])
        m[p + "hlb"] = np.stack([np.ascontiguousarray(lbl[:, :, h * 64:(h + 1) * 64].reshape(4, 64).T) for h in range(6)])
        cw, cb, hb_ = inp["hy_conv_w"][l], inp["hy_conv_b"][l], inp["hy_bias"][l]
        m[p + "cw"] = np.ascontiguousarray(np.stack([np.stack([cw[:, k * 256 + lt * 128:k * 256 + (lt + 1) * 128].T for k in range(3)], 1).reshape(128, 9) for lt in range(2)]))
        m[p + "cb"] = np.ascontiguousarray(np.stack([np.stack([cb[k * 256 + lt * 128:k * 256 + (lt + 1) * 128] for k in range(3)], 1) for lt in range(2)]))
        m[p + "bias"] = np.ascontiguousarray(np.stack([np.stack([hb_[o, lt * 128:(lt + 1) * 128] for o in range(2)], 1) for lt in range(2)]))
        m[p + "w1"] = inp["hy_w1"][l]
        m[p + "b1f"] = np.ascontiguousarray(np.stack([inp["hy_b1"][l], inp["hy_b2"][l][0], inp["hy_b2"][l][1], inp["hy_freq"][l]], 1))
        m[p + "w2"] = np.ascontiguousarray(inp["hy_w2"][l].transpose(1, 0, 2))
        m[p + "w3"] = np.ascontiguousarray(inp["hy_w3"][l].reshape(64, 2, 2, 256).transpose(0, 2, 1, 3))
        m[p + "gn"] = rep(np.concatenate([inp["gla_norm_g"][l], inp["hg_norm_g"][l]]))
        m[p + "wout"] = inp["w_out"][l]
        m[p + "ln1"] = rep(np.concatenate([inp["ln1_g"][l], inp["ln1_b"][l]]))
        m[p + "ln2"] = rep(np.concatenate([inp["ln2_g"][l], inp["ln2_b"][l]]))
        m[p + "wr"] = np.ascontiguousarray(np.concatenate([inp["moe_wr_g"][l], inp["moe_wr_e"][l]], 1))
        m[p + "br"] = rep(np.concatenate([inp["moe_br_g"][l], inp["moe_br_e"][l]]))
        m[p + "wg"], m[p + "wu"], m[p + "wd"] = inp["moe_w_gate"][l], inp["moe_w_up"][l], inp["moe_w_down"][l]
    return {k: np.ascontiguousarray(np.asarray(v, f32)) for k, v in m.items()}


def kernel(**inp):
    inp = {k: np.asarray(v) for k, v in inp.items()}
    in_maps = [host_inputs(inp, c % 4) for c in range(4)]
    in_maps = in_maps + in_maps
    if "nc" not in _prog:
        _prog["nc"] = build_fused(in_maps[0])
    res = run_bass_kernel_spmd(_prog["nc"], in_maps, core_ids=list(range(NCORES))).results
    return np.stack([res[b]["out"] for b in range(4)], 0).astype(np.float32)
```
